# Optimizing a Trainium2 kernel written in Bass

```python
import math
import jax, jax.numpy as jnp
from jax import lax
import numpy as np

D_MODEL = 1024
BATCH = 16
SEQ = 4096
DEPTH = 4

CTX_LEN = 256
GRID_W = 64
N_MIXERS = 3
EPS = 1e-6
D_FF = 4 * D_MODEL
ATTN_HEADS = 8
ATTN_HD = D_MODEL // ATTN_HEADS // 2
ATTN_VD = 2 * ATTN_HD
QK_COLS = ATTN_HEADS * 2 * ATTN_HD
V_COLS = ATTN_HEADS * ATTN_VD
Q_BLOCK = 128
ROPE_BASE = 10000.0
S5_GROUP = 16
S5_GROUPS = D_MODEL // S5_GROUP
S5_STATE = 64
S5_CHUNK = 128
LRU_BW = 256
LRU_WIDTH = ((4 * D_MODEL // 3 + LRU_BW - 1) // LRU_BW) * LRU_BW
LRU_BLOCKS = LRU_WIDTH // LRU_BW
LRU_C = 8.0
CONV_W = 4
CONV_LEFT = 2

kernel_name = "hybrid_diffattn_s5_rglru_ctxprefix"


def rmsnorm(x, g):
    xf = x.astype(jnp.float32)
    y = xf * lax.rsqrt(jnp.mean(xf * xf, axis=-1, keepdims=True) + EPS)
    return (y * g.astype(jnp.float32)).astype(x.dtype)


def _linear_combine(left, right):
    a_l, b_l = left
    a_r, b_r = right
    return a_l * a_r, a_r * b_l + b_r


def axial_rope(row, col):
    n_freq = ATTN_HD // 4
    inv = ROPE_BASE ** (-jnp.arange(n_freq, dtype=jnp.float32) / n_freq)
    ang = jnp.concatenate([row[:, None].astype(jnp.float32) * inv,
                           col[:, None].astype(jnp.float32) * inv], axis=-1)
    return jnp.cos(ang), jnp.sin(ang)


def apply_rope(t, cos, sin):
    half = ATTN_HD // 2
    tf = t.astype(jnp.float32)
    t1, t2 = tf[..., :half], tf[..., half:]
    cs = cos[:, None, None, :]
    sn = sin[:, None, None, :]
    out = jnp.concatenate([t1 * cs - t2 * sn, t1 * sn + t2 * cs], axis=-1)
    return out.astype(t.dtype)


def diff_attention(h_lat, h_ctx, w_qkv, w_o, lam_vecs, subln_g, lambda_init, rope, need_ctx):
    bsz, seq, _ = h_lat.shape
    cos, sin = rope
    scale = ATTN_HD ** -0.5

    def heads_qk(t):
        return t.reshape(t.shape[0], t.shape[1], ATTN_HEADS, 2, ATTN_HD)

    def heads_v(t):
        return t.reshape(t.shape[0], t.shape[1], ATTN_HEADS, ATTN_VD)

    q_l, k_l, v_l = jnp.split(jnp.einsum('btd,de->bte', h_lat, w_qkv), [QK_COLS, 2 * QK_COLS], axis=-1)
    q_l = apply_rope(heads_qk(q_l), cos, sin) * scale
    k_l = apply_rope(heads_qk(k_l), cos, sin)
    k_c, v_c = jnp.split(jnp.einsum('btd,de->bte', h_ctx, w_qkv[:, QK_COLS:]), [QK_COLS], axis=-1)
    k_c, v_c = heads_qk(k_c), heads_v(v_c)
    k_all = jnp.concatenate([k_c, k_l], axis=1)
    v_all = jnp.concatenate([v_c, heads_v(v_l)], axis=1)

    lf = lam_vecs.astype(jnp.float32)
    lam = jnp.exp(jnp.sum(lf[0] * lf[1])) - jnp.exp(jnp.sum(lf[2] * lf[3])) + lambda_init

    def attend(q, k, v):
        s = jnp.einsum('bqhcd,bkhcd->bhcqk', q, k).astype(jnp.float32)
        p = jax.nn.softmax(s, axis=-1)
        w = p[:, :, 0] - lam * p[:, :, 1]
        o = jnp.einsum('bhqk,bkhe->bqhe', w.astype(v.dtype), v)
        return rmsnorm(o, subln_g) * (1.0 - lambda_init)

    n_blocks = seq // Q_BLOCK
    qb = jnp.moveaxis(q_l.reshape(bsz, n_blocks, Q_BLOCK, ATTN_HEADS, 2, ATTN_HD), 1, 0)
    o_l = lax.map(lambda qq: attend(qq, k_all, v_all), qb)
    o_l = jnp.moveaxis(o_l, 0, 1).reshape(bsz, seq, V_COLS)
    y_lat = jnp.einsum('bte,ed->btd', o_l, w_o)
    y_ctx = None
    if need_ctx:
        q_c = heads_qk(jnp.einsum('btd,de->bte', h_ctx, w_qkv[:, :QK_COLS])) * scale
        o_c = attend(q_c, k_c, v_c).reshape(bsz, h_ctx.shape[1], V_COLS)
        y_ctx = jnp.einsum('bte,ed->btd', o_c, w_o)
    return y_lat, y_ctx


def s5_discretize(a_re, a_im, b_re, b_im, log_dt):
    lam = lax.complex(a_re.astype(jnp.float32), a_im.astype(jnp.float32))
    dt = jnp.exp(log_dt.astype(jnp.float32))[:, None]
    lam_bar = jnp.exp(lam * dt)
    b_mat = lax.complex(b_re.astype(jnp.float32), b_im.astype(jnp.float32))
    b_bar = ((lam_bar - 1.0) / lam)[..., None] * b_mat
    return lam_bar, b_bar


def s5_scan(u, lam_bar, b_bar, c_mat, h0):
    bsz, t_len, _ = u.shape
    n_chunks = t_len // S5_CHUNK
    ug = u.astype(jnp.float32).reshape(bsz, n_chunks, S5_CHUNK, S5_GROUPS, S5_GROUP)
    ug = jnp.moveaxis(ug, 1, 0)
    a = jnp.broadcast_to(lam_bar, (bsz, S5_CHUNK, S5_GROUPS, S5_STATE))

    def step(h, uc):
        bu = jnp.einsum('gpc,btgc->btgp', b_bar, uc.astype(jnp.complex64))
        a_cum, hs = lax.associative_scan(_linear_combine, (a, bu), axis=1)
        hs = hs + a_cum * h[:, None]
        y = jnp.real(jnp.einsum('gcp,btgp->btgc', c_mat, hs))
        return hs[:, -1], y

    h_last, ys = lax.scan(step, h0, ug)
    return jnp.moveaxis(ys, 0, 1).reshape(bsz, t_len, D_MODEL), h_last


def s5_mixer(h_lat, h_ctx, a_re, a_im, b_re, b_im, c_re, c_im, log_dt, d_skip, w_glu, need_ctx):
    bsz = h_lat.shape[0]
    lam_f, b_f = s5_discretize(a_re[0], a_im[0], b_re[0], b_im[0], log_dt[0])
    lam_b, b_b = s5_discretize(a_re[1], a_im[1], b_re[1], b_im[1], log_dt[1])
    c_f = lax.complex(c_re[0].astype(jnp.float32), c_im[0].astype(jnp.float32))
    c_b = lax.complex(c_re[1].astype(jnp.float32), c_im[1].astype(jnp.float32))
    zeros = jnp.zeros((bsz, S5_GROUPS, S5_STATE), jnp.complex64)
    y_cf, h_cf = s5_scan(h_ctx, lam_f, b_f, c_f, zeros)
    y_cb, h_cb = s5_scan(h_ctx[:, ::-1], lam_b, b_b, c_b, zeros)
    y_lf, _ = s5_scan(h_lat, lam_f, b_f, c_f, h_cf)
    y_lb, _ = s5_scan(h_lat[:, ::-1], lam_b, b_b, c_b, h_cb)
    d32 = d_skip.astype(jnp.float32)

    def glu(y, h):
        g = jax.nn.gelu(y.astype(h.dtype))
        o1, o2 = jnp.split(jnp.einsum('btd,de->bte', g, w_glu), 2, axis=-1)
        return o1 * jax.nn.sigmoid(o2)

    y_lat = glu(y_lf + y_lb[:, ::-1] + d32 * h_lat.astype(jnp.float32), h_lat)
    y_ctx = glu(y_cf + y_cb[:, ::-1] + d32 * h_ctx.astype(jnp.float32), h_ctx) if need_ctx else None
    return y_lat, y_ctx


def depthwise_conv(u, w, b):
    out = lax.conv_general_dilated(u, w[:, None, :], window_strides=(1,),
                                   padding=[(CONV_LEFT, CONV_W - 1 - CONV_LEFT)],
                                   dimension_numbers=('NWC', 'WIO', 'NWC'),
                                   feature_group_count=u.shape[-1])
    return out + b


def rglru_mixer(h_lat, h_ctx, w_in, conv_w, conv_b, w_gate, b_gate, a_param, w_out, need_ctx):
    bsz = h_lat.shape[0]

    def branches(h):
        gate_in, rec_in = jnp.split(jnp.einsum('btd,de->bte', h, w_in), 2, axis=-1)
        return jax.nn.gelu(gate_in), depthwise_conv(rec_in, conv_w, conv_b)

    def gate_terms(u, d):
        t_len = u.shape[1]
        ub = u.reshape(bsz, t_len, LRU_BLOCKS, LRU_BW)
        g = jnp.einsum('btnh,knhe->kbtne', ub, w_gate[d]).reshape(2, bsz, t_len, LRU_WIDTH)
        g = g.astype(jnp.float32) + b_gate[d][:, None, None, :].astype(jnp.float32)
        r = jax.nn.sigmoid(g[0])
        ig = jax.nn.sigmoid(g[1])
        log_a = -LRU_C * r * jax.nn.softplus(-a_param[d].astype(jnp.float32))
        a = jnp.exp(log_a)
        b = jnp.sqrt(-jnp.expm1(2.0 * log_a)) * (ig * u.astype(jnp.float32))
        return a, b

    def run(ab, h0):
        a_cum, hs = lax.associative_scan(_linear_combine, ab, axis=1)
        return hs + a_cum * h0[:, None, :]

    g_c, u_c = branches(h_ctx)
    g_l, u_l = branches(h_lat)
    zeros = jnp.zeros((bsz, LRU_WIDTH), jnp.float32)
    hc_f = run(gate_terms(u_c, 0), zeros)
    hc_b = run(gate_terms(u_c[:, ::-1], 1), zeros)
    hl_f = run(gate_terms(u_l, 0), hc_f[:, -1])
    hl_b = run(gate_terms(u_l[:, ::-1], 1), hc_b[:, -1])
    y_lat = jnp.einsum('bte,ed->btd', (hl_f + hl_b[:, ::-1]).astype(g_l.dtype) * g_l, w_out)
    y_ctx = None
    if need_ctx:
        y_ctx = jnp.einsum('bte,ed->btd', (hc_f + hc_b[:, ::-1]).astype(g_c.dtype) * g_c, w_out)
    return y_lat, y_ctx


def sqrelu_mlp(h, w1, w2):
    return jnp.einsum('btf,fd->btd', jnp.square(jax.nn.relu(jnp.einsum('btd,df->btf', h, w1))), w2)


def setup_inputs(seed: int = 0) -> dict:
    key = jax.random.key(seed)
    ks = iter(jax.random.split(key, 40))
    f32 = jnp.float32

    def nrm(shape, scale):
        return jax.random.normal(next(ks), shape, f32) * scale

    n_a = len(range(0, DEPTH, N_MIXERS))
    n_b = len(range(1, DEPTH, N_MIXERS))
    n_c = len(range(2, DEPTH, N_MIXERS))
    x = nrm((BATCH, SEQ, D_MODEL), 1.0)
    c = nrm((BATCH, D_MODEL), 1.0)
    ctx = nrm((BATCH, CTX_LEN, D_MODEL), 1.0)
    c_ctx = nrm((D_MODEL,), 1.0)
    ada_w = nrm((DEPTH, D_MODEL, 6 * D_MODEL), 0.5 * D_MODEL ** -0.5)
    ada_b = nrm((DEPTH, 6 * D_MODEL), 0.01)
    norm_g = 1.0 + nrm((DEPTH, 4, D_MODEL), 0.01)
    mlp_w1 = nrm((DEPTH, D_MODEL, D_FF), D_MODEL ** -0.5)
    mlp_w2 = nrm((DEPTH, D_FF, D_MODEL), D_FF ** -0.5)
    attn_w_qkv = nrm((n_a, D_MODEL, 2 * QK_COLS + V_COLS), D_MODEL ** -0.5)
    attn_w_o = nrm((n_a, V_COLS, D_MODEL), V_COLS ** -0.5)
    attn_lambda = nrm((n_a, 4, ATTN_HD), 0.1)
    attn_subln = 1.0 + nrm((n_a, ATTN_VD), 0.01)
    s5_a_re = -0.5 + nrm((n_b, 2, S5_GROUPS, S5_STATE), 0.01)
    s5_a_im = jnp.pi * jnp.arange(S5_STATE, dtype=f32) * jnp.ones((n_b, 2, S5_GROUPS, 1), f32)
    s5_b_re = nrm((n_b, 2, S5_GROUPS, S5_STATE, S5_GROUP), (2 * S5_GROUP) ** -0.5)
    s5_b_im = nrm((n_b, 2, S5_GROUPS, S5_STATE, S5_GROUP), (2 * S5_GROUP) ** -0.5)
    s5_c_re = nrm((n_b, 2, S5_GROUPS, S5_GROUP, S5_STATE), S5_STATE ** -0.5)
    s5_c_im = nrm((n_b, 2, S5_GROUPS, S5_GROUP, S5_STATE), S5_STATE ** -0.5)
    s5_log_dt = jax.random.uniform(next(ks), (n_b, 2, S5_GROUPS), f32, math.log(1e-3), math.log(1e-1))
    s5_d = nrm((n_b, D_MODEL), 1.0)
    s5_w_glu = nrm((n_b, D_MODEL, 2 * D_MODEL), D_MODEL ** -0.5)
    lru_w_in = nrm((n_c, D_MODEL, 2 * LRU_WIDTH), D_MODEL ** -0.5)
    lru_conv_w = nrm((n_c, CONV_W, LRU_WIDTH), CONV_W ** -0.5)
    lru_conv_b = nrm((n_c, LRU_WIDTH), 0.01)
    lru_w_gate = nrm((n_c, 2, 2, LRU_BLOCKS, LRU_BW, LRU_BW), LRU_BW ** -0.5)
    lru_b_gate = nrm((n_c, 2, 2, LRU_WIDTH), 0.01)
    a0 = jax.random.uniform(next(ks), (n_c, 2, LRU_WIDTH), f32, 0.9, 0.999)
    s = a0 ** (1.0 / LRU_C)
    lru_a_param = jnp.log(s) - jnp.log1p(-s)
    lru_w_out = nrm((n_c, LRU_WIDTH, D_MODEL), LRU_WIDTH ** -0.5)
    return {"x": x, "c": c, "ctx": ctx, "c_ctx": c_ctx,
            "ada_w": ada_w, "ada_b": ada_b, "norm_g": norm_g, "mlp_w1": mlp_w1, "mlp_w2": mlp_w2,
            "attn_w_qkv": attn_w_qkv, "attn_w_o": attn_w_o, "attn_lambda": attn_lambda, "attn_subln": attn_subln,
            "s5_a_re": s5_a_re, "s5_a_im": s5_a_im, "s5_b_re": s5_b_re, "s5_b_im": s5_b_im,
            "s5_c_re": s5_c_re, "s5_c_im": s5_c_im, "s5_log_dt": s5_log_dt, "s5_d": s5_d, "s5_w_glu": s5_w_glu,
            "lru_w_in": lru_w_in, "lru_conv_w": lru_conv_w, "lru_conv_b": lru_conv_b, "lru_w_gate": lru_w_gate,
            "lru_b_gate": lru_b_gate, "lru_a_param": lru_a_param, "lru_w_out": lru_w_out}


def reference(x, c, ctx, c_ctx, ada_w, ada_b, norm_g, mlp_w1, mlp_w2,
              attn_w_qkv, attn_w_o, attn_lambda, attn_subln,
              s5_a_re, s5_a_im, s5_b_re, s5_b_im, s5_c_re, s5_c_im, s5_log_dt, s5_d, s5_w_glu,
              lru_w_in, lru_conv_w, lru_conv_b, lru_w_gate, lru_b_gate, lru_a_param, lru_w_out):
    seq = x.shape[1]
    rows = seq // GRID_W
    row = jnp.repeat(jnp.arange(rows), GRID_W)
    col = jnp.tile(jnp.arange(GRID_W), rows)
    rope = axial_rope(row, col)
    xc = ctx
    sc = jax.nn.silu(c)
    scc = jax.nn.silu(c_ctx)
    for i in range(DEPTH):
        need_ctx = i < DEPTH - 1
        mod_l = (jnp.einsum('bd,de->be', sc, ada_w[i]) + ada_b[i])[:, None, :]
        mod_c = jnp.einsum('d,de->e', scc, ada_w[i]) + ada_b[i]
        sh1, s1, g1, sh2, s2, g2 = jnp.split(mod_l, 6, axis=-1)
        csh1, cs1, cg1, csh2, cs2, cg2 = jnp.split(mod_c, 6, axis=-1)
        h_lat = rmsnorm(x, norm_g[i, 0]) * (1.0 + s1) + sh1
        h_ctx = rmsnorm(xc, norm_g[i, 0]) * (1.0 + cs1) + csh1
        kind, j = i % N_MIXERS, i // N_MIXERS
        if kind == 0:
            lambda_init = 0.8 - 0.6 * math.exp(-0.3 * i)
            y_lat, y_ctx = diff_attention(h_lat, h_ctx, attn_w_qkv[j], attn_w_o[j], attn_lambda[j],
                                          attn_subln[j], lambda_init, rope, need_ctx)
        elif kind == 1:
            y_lat, y_ctx = s5_mixer(h_lat, h_ctx, s5_a_re[j], s5_a_im[j], s5_b_re[j], s5_b_im[j],
                                    s5_c_re[j], s5_c_im[j], s5_log_dt[j], s5_d[j], s5_w_glu[j], need_ctx)
        else:
            y_lat, y_ctx = rglru_mixer(h_lat, h_ctx, lru_w_in[j], lru_conv_w[j], lru_conv_b[j], lru_w_gate[j],
                                       lru_b_gate[j], lru_a_param[j], lru_w_out[j], need_ctx)
        x = x + g1 * rmsnorm(y_lat, norm_g[i, 1])
        h = rmsnorm(x, norm_g[i, 2]) * (1.0 + s2) + sh2
        x = x + g2 * rmsnorm(sqrelu_mlp(h, mlp_w1[i], mlp_w2[i]), norm_g[i, 3])
        if need_ctx:
            xc = xc + cg1 * rmsnorm(y_ctx, norm_g[i, 1])
            hc = rmsnorm(xc, norm_g[i, 2]) * (1.0 + cs2) + csh2
            xc = xc + cg2 * rmsnorm(sqrelu_mlp(hc, mlp_w1[i], mlp_w2[i]), norm_g[i, 3])
    return x
```

```python
from concourse.bass_utils import run_bass_kernel_spmd
import contextlib
import numpy as np
import concourse.bass as bass
import concourse.mybir as mybir

F32 = mybir.dt.float32
BF16 = mybir.dt.bfloat16
I32 = mybir.dt.int32
AF = mybir.ActivationFunctionType
ALU = mybir.AluOpType
AX = mybir.AxisListType

NSTREAM = 12


class Tok:
    __slots__ = ("w", "r", "name", "excl")

    def __init__(self, name="", excl=False):
        self.w = None
        self.r = {}
        self.name = name
        self.excl = excl


class KB:
    def __init__(self):
        nc = bass.Bass("TRN2", target_bir_lowering=False)
        self.nc = nc
        self.E = {"pe": nc.tensor, "act": nc.scalar, "dve": nc.vector,
                  "pool": nc.gpsimd, "sp": nc.sync}
        self.sem = {}
        self.cnt = {}
        for e in ("pe", "act", "dve", "pool"):
            self.sem[e] = nc.alloc_semaphore("s_" + e)
            self.cnt[e] = 0
        for j in range(NSTREAM):
            e = ("d", j)
            self.sem[e] = nc.alloc_semaphore("s_d%d" % j)
            self.cnt[e] = 0
        self.seen = {e: {} for e in ("pe", "act", "dve", "pool", "sp")}
        self.ndma = 0
        self.nins = 0
        self.nwait = 0

    def _val(self, src, c):
        return c * 16 if isinstance(src, tuple) else c

    def _wait(self, eng, src, c):
        if c <= 0:
            return
        if self.seen[eng].get(src, 0) >= c:
            return
        self.seen[eng][src] = c
        self.E[eng].wait_ge(self.sem[src], self._val(src, c))
        self.nins += 1
        self.nwait += 1

    def _waits_attach(self, eng, need, fn):
        todo = [(src, c) for src, c in need.items() if c > 0 and self.seen[eng].get(src, 0) < c]
        for src, c in todo[:-1]:
            self._wait(eng, src, c)
        ins = fn()
        if todo:
            src, c = todo[-1]
            self.seen[eng][src] = c
            ins._wait_ge(self.sem[src], self._val(src, c))
        return ins

    def _deps(self, eng, reads, writes, same_ok=False):
        need = {}

        def add(src, c):
            if same_ok and src == eng:
                return
            if need.get(src, 0) < c:
                need[src] = c
        for t in reads:
            if t.w is not None:
                add(*t.w)
            if t.excl:
                for src, c in t.r.items():
                    if src != eng:
                        add(src, c)
        for t in writes:
            if t.w is not None:
                add(*t.w)
            for src, c in t.r.items():
                add(src, c)
        return need

    def _commit(self, me, reads, writes):
        c = self.cnt[me]
        for t in reads:
            t.r[me] = c
        for t in writes:
            t.w = (me, c)
            t.r = {}

    def op(self, eng, fn, reads=(), writes=()):
        need = self._deps(eng, reads, writes, same_ok=(eng == "pe"))
        ins = self._waits_attach(eng, need, fn)
        self.cnt[eng] += 1
        ins.then_inc(self.sem[eng], 1)
        self.nins += 1
        self._commit(eng, reads, writes)
        return ins

    def dma(self, out, in_, reads=(), writes=(), q="sp", **kw):
        j = self.ndma % NSTREAM
        self.ndma += 1
        me = ("d", j)
        need = self._deps(q, reads, writes)
        if need.get(me, 0) < self.cnt[me]:
            need[me] = self.cnt[me]
        ins = self._waits_attach(q, need, lambda: self.E[q].dma_start(out=out, in_=in_, **kw))
        self.cnt[me] += 1
        ins.then_inc(self.sem[me], 16)
        self.nins += 1
        self._commit(me, reads, writes)
        return ins

    def barrier(self):
        for eng in ("pe", "act", "dve", "pool", "sp"):
            for src, c in self.cnt.items():
                if src == eng:
                    continue
                self._wait(eng, src, c)

    def finish(self):
        for src, c in self.cnt.items():
            self._wait("sp", src, c)


class Pool:
    _uid = [0]

    def __init__(self, kb):
        self.kb = kb
        self.st = contextlib.ExitStack()
        self.n = 0
        Pool._uid[0] += 1
        self.uid = Pool._uid[0]

    def __enter__(self):
        self.st.__enter__()
        return self

    def __exit__(self, *a):
        return self.st.__exit__(*a)

    def sb(self, shape, dtype, name=None):
        self.n += 1
        nm = "%s_%d_%d" % (name or "t", self.uid, self.n)
        t = self.st.enter_context(self.kb.nc.sbuf_tensor(nm, list(shape), dtype))
        return t

    def ps(self, shape, dtype, name=None):
        self.n += 1
        nm = "%s_%d_%d" % (name or "p", self.uid, self.n)
        t = self.st.enter_context(self.kb.nc.psum_tensor(nm, list(shape), dtype))
        return t

D = 1024
KC = 8
DFF = 4096
EPS = 1e-6


class Model:
    def __init__(self, NB, CTX, SEQ, layers=(0, 1, 2, 3), depth=4, NT=256):
        self.NB, self.CTX, self.SEQ = NB, CTX, SEQ
        self.TB = CTX + SEQ
        self.T = NB * self.TB
        self.layers = list(layers)
        self.depth = depth
        self.NT = NT
        self.kb = KB()
        self.nc = self.kb.nc
        self.dram = {}

    def din(self, name, shape, dtype=F32):
        t = self.nc.dram_tensor(name, list(shape), dtype, kind="ExternalInput").ap()
        self.dram[name] = t
        return t

    def dscratch(self, name, shape, dtype):
        t = self.nc.dram_tensor(name, list(shape), dtype, kind="Internal").ap()
        self.dram[name] = t
        return t

    def tiles(self, with_ctx=True, nt=None):
        nt = nt or self.NT
        out = []
        for b in range(self.NB):
            base = b * self.TB
            if with_ctx:
                for c0 in range(0, self.CTX, nt):
                    out.append((base + c0, min(nt, self.CTX - c0), 2))
            for c0 in range(0, self.SEQ, nt):
                out.append((base + self.CTX + c0, min(nt, self.SEQ - c0), b))
        return out

    def setup(self, G):
        kb, nc = self.kb, self.nc
        self.G = G
        self.onesm = G.sb([128, 128], BF16, "onesm")
        self.t_const = Tok("const")
        kb.op("dve", lambda: nc.vector.memset(self.onesm[:], 1.0 / D), writes=[self.t_const])
        self.epsv = G.sb([128, 1], F32, "epsv")
        kb.op("dve", lambda: nc.vector.memset(self.epsv[:], EPS), writes=[self.t_const])
        self.mv = G.sb([128, self.depth, 4, KC, 3], F32, "mv")
        self.modT = G.sb([128, self.depth, 48, 3], F32, "modT")
        self.t_mv = Tok("mv")

    def phase_mod(self):
        kb, nc = self.kb, self.nc
        cc, ada_w, ada_b, ng = (self.dram[k] for k in ("cc", "ada_w", "ada_b", "norm_g"))
        with Pool(kb) as P:
            sc = P.sb([128, KC, 3], F32, "sc")
            adab = P.sb([128, self.depth, 48], F32, "adab")
            ngt = P.sb([128, self.depth, 4, KC], F32, "ngt")
            tmp = P.sb([128, KC, 3], F32, "tmp")
            wt = [P.sb([128, KC, 512], F32, "adaw%d" % j) for j in range(2)]
            ps = [P.ps([128, 512], F32, "psmod%d" % j) for j in range(2)]
            t_sc, t_ab, t_ng, t_tmp = Tok(), Tok(), Tok(), Tok()
            t_wt = [Tok(), Tok()]
            t_ps = [Tok(excl=True), Tok(excl=True)]
            kb.dma(sc[:], cc, writes=[t_sc])
            kb.dma(adab[:], ada_b, writes=[t_ab])
            kb.dma(ngt[:], ng, writes=[t_ng])
            kb.op("act", lambda: nc.scalar.activation(out=sc[:], in_=sc[:], func=AF.Silu),
                  reads=[t_sc], writes=[t_sc])
            n = 0
            for i in self.layers:
                wv = ada_w[i].rearrange("(k p) f -> p k f", p=128)
                for cg in range(12):
                    j = n % 2
                    n += 1
                    kb.dma(wt[j][:], wv[:, :, cg * 512:(cg + 1) * 512], writes=[t_wt[j]])
                    for jj in range(4):
                        for k in range(KC):
                            kb.op("pe", lambda k=k, jj=jj, j=j: nc.tensor.matmul(
                                ps[j][:, jj * 3:jj * 3 + 3], wt[j][:, k, jj * 128:(jj + 1) * 128],
                                sc[:, k, :], start=(k == 0), stop=(k == KC - 1)),
                                reads=[t_wt[j], t_sc], writes=[t_ps[j]])
                    kb.op("dve", lambda j=j, cg=cg, i=i: nc.vector.tensor_tensor(
                        out=self.modT[:, i, cg * 4:(cg + 1) * 4, :],
                        in0=ps[j][:, 0:12].rearrange("p (a b) -> p a b", b=3),
                        in1=adab[:, i, cg * 4:(cg + 1) * 4].unsqueeze(2).broadcast_to([128, 4, 3]),
                        op=ALU.add), reads=[t_ps[j], t_ab], writes=[self.t_mv])
                for kind, (c0, gi, plus1) in enumerate([(8, 0, True), (16, 1, False),
                                                        (32, 2, True), (40, 3, False)]):
                    src = self.modT[:, i, c0:c0 + 8, :]
                    if plus1:
                        kb.op("dve", lambda src=src: nc.vector.tensor_scalar(
                            out=tmp[:], in0=src, scalar1=1.0, scalar2=None, op0=ALU.add),
                            reads=[self.t_mv], writes=[t_tmp])
                        src = tmp[:]
                    kb.op("dve", lambda src=src, i=i, kind=kind, gi=gi: nc.vector.tensor_tensor(
                        out=self.mv[:, i, kind, :, :], in0=src,
                        in1=ngt[:, i, gi, :].unsqueeze(2).broadcast_to([128, KC, 3]),
                        op=ALU.mult), reads=[self.t_mv, t_tmp, t_ng], writes=[self.t_mv])
            kb.barrier()

    def mvec(self, i, kind, k, m):
        return self.mv[:, i, kind, k, m:m + 1]

    def shvec(self, i, which, k, m):
        c0 = 0 if which == 0 else 24
        return self.modT[:, i, c0 + k, m:m + 1]

    def rstd_of(self, src, n, W, t_src, ps, t_ps):
        kb, nc = self.kb, self.nc
        sq, sd, rstd = W["sq"], W["sd"], W["rstd"]
        kb.op("act", lambda: nc.scalar.activation(out=sq[:, :, :n], in_=src, func=AF.Square),
              reads=[t_src], writes=[W["t_sq"]])
        for k in range(KC):
            kb.op("pe", lambda k=k: nc.tensor.matmul(ps[:, :n], self.onesm[:], sq[:, k, :n],
                                                     start=(k == 0), stop=(k == KC - 1)),
                  reads=[W["t_sq"], self.t_const], writes=[t_ps])
        kb.op("act", lambda: nc.scalar.activation(out=sd[:, :n], in_=ps[:, :n], func=AF.Sqrt,
                                                  bias=self.epsv[:], scale=1.0),
              reads=[t_ps, self.t_const], writes=[W["t_sd"]])
        kb.op("dve", lambda: nc.vector.reciprocal(out=rstd[:, :n], in_=sd[:, :n]),
              reads=[W["t_sd"]], writes=[W["t_rstd"]])
        return rstd

    def norm_work(self, P, NT):
        return dict(sq=P.sb([128, KC, NT], BF16, "sq"), sd=P.sb([128, NT], F32, "sd"),
                    rstd=P.sb([128, NT], F32, "rstd"), xh=P.sb([128, KC, NT], F32, "xh"),
                    t_sq=Tok(), t_sd=Tok(), t_rstd=Tok(), t_xh=Tok())

    def norm_pre(self, xt, t_x, n, i, which, m, h, t_h, W, ps, t_ps):
        kb, nc = self.kb, self.nc
        rstd = self.rstd_of(xt[:, :, :n], n, W, t_x, ps, t_ps)
        xh = W["xh"]
        kb.op("dve", lambda: nc.vector.tensor_tensor(
            out=xh[:, :, :n], in0=xt[:, :, :n],
            in1=rstd[:, :n].unsqueeze(1).broadcast_to([128, KC, n]), op=ALU.mult),
            reads=[t_x, W["t_rstd"]], writes=[W["t_xh"]])
        kindA = 0 if which == 0 else 2
        for k in range(KC):
            if k % 2 == 0:
                kb.op("act", lambda k=k: nc.scalar.activation(
                    out=h[:, k, :n], in_=xh[:, k, :n], func=AF.Identity, scale=self.mvec(i, kindA, k, m),
                    bias=self.shvec(i, which, k, m)), reads=[W["t_xh"], self.t_mv], writes=[t_h])
            else:
                kb.op("dve", lambda k=k: nc.vector.tensor_scalar(
                    out=h[:, k, :n], in0=xh[:, k, :n], scalar1=self.mvec(i, kindA, k, m),
                    scalar2=self.shvec(i, which, k, m), op0=ALU.mult, op1=ALU.add),
                    reads=[W["t_xh"], self.t_mv], writes=[t_h])

    def norm_post(self, y, t_y, xt, t_x, n, i, which, m, W, ps, t_ps):
        kb, nc = self.kb, self.nc
        rstd = self.rstd_of(y[:, :, :n], n, W, t_y, ps, t_ps)
        kb.op("dve", lambda: nc.vector.tensor_tensor(
            out=y[:, :, :n], in0=y[:, :, :n],
            in1=rstd[:, :n].unsqueeze(1).broadcast_to([128, KC, n]), op=ALU.mult),
            reads=[W["t_rstd"]], writes=[t_y])
        kindG = 1 if which == 0 else 3
        for k in range(KC):
            kb.op("dve", lambda k=k: nc.vector.scalar_tensor_tensor(
                out=xt[:, k, :n], in0=y[:, k, :n], scalar=self.mvec(i, kindG, k, m),
                in1=xt[:, k, :n], op0=ALU.mult, op1=ALU.add),
                reads=[t_y, self.t_mv], writes=[t_x])

    def phase_mlp(self, i, with_ctx):
        kb, nc = self.kb, self.nc
        NT = self.NT
        xT = self.dram["xT"].rearrange("(k p) t -> p k t", p=128)
        w1 = self.dram["mlp_w1"][i].rearrange("(k p) f -> p k f", p=128)
        w2 = self.dram["mlp_w2"][i]
        with Pool(kb) as P:
            w1s = P.sb([128, KC, DFF], BF16, "w1s")
            w2s = P.sb([128, KC, 32, 128], BF16, "w2s")
            t_w1 = [Tok() for _ in range(8)]
            t_w2 = [Tok() for _ in range(8)]
            for g in range(8):
                kb.dma(w1s[:, :, g * 512:(g + 1) * 512], w1[:, :, g * 512:(g + 1) * 512],
                       writes=[t_w1[g]], q="pool")
            for d in range(8):
                kb.dma(w2s[:, d, :, :], w2[d], writes=[t_w2[d]], q="pool")
            xt = [P.sb([128, KC, NT], F32, "xt%d" % j) for j in range(2)]
            t_xt = [Tok(), Tok()]
            h = [P.sb([128, KC, NT], BF16, "h%d" % j) for j in range(2)]
            t_h = [Tok(), Tok()]
            hid = P.sb([128, 32, NT], BF16, "hid")
            t_hid = [Tok(), Tok()]
            y = P.sb([128, KC, NT], F32, "y")
            t_y = Tok()
            rl = [P.sb([128, NT], F32, "rl%d" % j) for j in range(2)]
            t_rl = [Tok(), Tok()]
            W = self.norm_work(P, NT)
            W2 = dict(sq=P.sb([128, KC, NT], BF16, "sq2"), sd=P.sb([128, NT], F32, "sd2"),
                      rstd=P.sb([128, NT], F32, "rstd2"), t_sq=Tok(), t_sd=Tok(), t_rstd=Tok())
            psn = P.ps([128, 512], F32, "psn")
            t_psn = Tok(excl=True)
            psn2 = P.ps([128, 512], F32, "psn2")
            t_psn2 = Tok(excl=True)
            psu = [P.ps([128, 512], F32, "psu%d" % j) for j in range(3)]
            t_psu = [Tok(excl=True) for _ in range(3)]
            psd = [P.ps([128, 512], F32, "psd%d" % j) for j in range(2)]
            t_psd = [Tok(excl=True) for _ in range(2)]
            tl = self.tiles(with_ctx)

            def load(ti):
                c0, n, m = tl[ti]
                kb.dma(xt[ti % 2][:, :, :n], xT[:, :, c0:c0 + n], writes=[t_xt[ti % 2]])

            def pre(ti):
                c0, n, m = tl[ti]
                self.norm_pre(xt[ti % 2], t_xt[ti % 2], n, i, 1, m, h[ti % 2], t_h[ti % 2], W, psn, t_psn)
            if tl:
                load(0)
                if len(tl) > 1:
                    load(1)
                pre(0)
            nu = 0
            nd = 0
            for ti, (c0, n, m) in enumerate(tl):
                j = ti % 2
                for f in range(32):
                    pj = nu % 3
                    nu += 1
                    for k in range(KC):
                        kb.op("pe", lambda k=k, f=f, pj=pj: nc.tensor.matmul(
                            psu[pj][:, :n], w1s[:, k, f * 128:(f + 1) * 128], h[j][:, k, :n],
                            start=(k == 0), stop=(k == KC - 1)),
                            reads=[t_w1[f // 4], t_h[j]], writes=[t_psu[pj]])
                    rj = f % 2
                    kb.op("act", lambda pj=pj, rj=rj: nc.scalar.activation(
                        out=rl[rj][:, :n], in_=psu[pj][:, :n], func=AF.Relu),
                        reads=[t_psu[pj]], writes=[t_rl[rj]])
                    eng = "dve" if f % 2 == 0 else "pool"
                    E = nc.vector if eng == "dve" else nc.gpsimd
                    kb.op(eng, lambda rj=rj, f=f, E=E: E.tensor_tensor(
                        out=hid[:, f, :n], in0=rl[rj][:, :n], in1=rl[rj][:, :n], op=ALU.mult),
                        reads=[t_rl[rj]], writes=[t_hid[f % 2]])
                    if f == 8 and ti + 1 < len(tl):
                        pre(ti + 1)
                for d in range(KC):
                    pj = nd % 2
                    nd += 1
                    for f in range(32):
                        kb.op("pe", lambda d=d, f=f, pj=pj: nc.tensor.matmul(
                            psd[pj][:, :n], w2s[:, d, f, :], hid[:, f, :n],
                            start=(f == 0), stop=(f == 31)),
                            reads=[t_w2[d]] + t_hid, writes=[t_psd[pj]])
                    kb.op("act", lambda d=d, pj=pj: nc.scalar.copy(out=y[:, d, :n], in_=psd[pj][:, :n]),
                          reads=[t_psd[pj]], writes=[t_y])
                self.norm_post(y, t_y, xt[j], t_xt[j], n, i, 1, m, W2, psn2, t_psn2)
                kb.dma(xT[:, :, c0:c0 + n], xt[j][:, :, :n], reads=[t_xt[j]])
                if ti + 2 < len(tl):
                    load(ti + 2)
            kb.barrier()


HEADS = 8


def _attn_scratch(self):
    if "QT" in self.dram:
        return
    self.dscratch("QT", [self.NB, HEADS, 2, 128, self.TB], BF16)
    self.dscratch("KT", [self.NB, HEADS, 128, self.TB], BF16)
    self.dscratch("V", [self.NB, self.TB, 1024], BF16)
    self.dscratch("OT", [1536, self.T], BF16)


def phase_attn_qkv(self, i, j):
    kb, nc = self.kb, self.nc
    _attn_scratch(self)
    NT = self.NT
    xT = self.dram["xT"].rearrange("(k p) t -> p k t", p=128)
    wq = self.dram["attn_w_qkv"][j].rearrange("(k p) f -> p k f", p=128)
    rope = self.dram["rope"]
    QT, KT, V = self.dram["QT"], self.dram["KT"], self.dram["V"]
    with Pool(kb) as P:
        wqs = P.sb([128, KC, 3072], BF16, "wqs")
        t_wq = [Tok() for _ in range(6)]
        for g in range(6):
            kb.dma(wqs[:, :, g * 512:(g + 1) * 512], wq[:, :, g * 512:(g + 1) * 512],
                   writes=[t_wq[g]], q="pool")
        nrt = self.SEQ // 128
        rp = P.sb([128, nrt, 2, 32], F32, "rp")
        rpq = P.sb([128, nrt, 2, 32], F32, "rpq")
        t_rp = Tok()
        kb.dma(rp[:], rope, writes=[t_rp])
        kb.op("act", lambda: nc.scalar.mul(out=rpq[:], in_=rp[:], mul=0.125), reads=[t_rp], writes=[t_rp])
        ident = P.sb([128, 128], BF16, "ident")
        t_id = Tok()
        kb.dma(ident[:], self.dram["ident"], writes=[t_id], q="pool")
        xt = [P.sb([128, KC, NT], F32, "xt%d" % q) for q in range(2)]
        t_xt = [Tok(), Tok()]
        h = P.sb([128, KC, NT], BF16, "h")
        t_h = Tok()
        W = self.norm_work(P, NT)
        psn = P.ps([128, 512], F32, "psn")
        t_psn = Tok(excl=True)
        psqk = [P.ps([128, 1024], F32, "psqk%d" % q) for q in range(2)]
        t_psqk = [Tok(excl=True), Tok(excl=True)]
        psv = [P.ps([128, 512], F32, "psv%d" % q) for q in range(2)]
        t_psv = [Tok(excl=True), Tok(excl=True)]
        pst = P.ps([128, 8, 128], BF16, "pst")
        t_pst = Tok(excl=True)
        qk = [P.sb([128, 1024], BF16, "qk%d" % q) for q in range(2)]
        t_qk = [Tok(), Tok()]
        ta = P.sb([128, 16, 32], F32, "ta")
        tb = P.sb([128, 16, 32], F32, "tb")
        t_ta, t_tb = Tok(), Tok()
        vst = [P.sb([128, 1024], BF16, "vst%d" % q) for q in range(2)]
        t_vst = [Tok(), Tok()]
        qz = [P.sb([128, HEADS, NT], BF16, "qz%d" % c) for c in range(2)]
        t_qz = [Tok(), Tok()]
        kz = P.sb([128, HEADS, NT], BF16, "kz")
        t_kz = Tok()
        kb.op("pool", lambda: nc.gpsimd.memset(qz[0][:], 0.0), writes=[t_qz[0]])
        kb.op("pool", lambda: nc.gpsimd.memset(qz[1][:], 0.0), writes=[t_qz[1]])
        tl = self.tiles(True)
        c0, n, m = tl[0]
        kb.dma(xt[0][:, :, :n], xT[:, :, c0:c0 + n], writes=[t_xt[0]])
        nv = 0
        for ti, (c0, n, m) in enumerate(tl):
            jx = ti % 2
            if ti + 1 < len(tl):
                c1, n1, _ = tl[ti + 1]
                kb.dma(xt[1 - jx][:, :, :n1], xT[:, :, c1:c1 + n1], writes=[t_xt[1 - jx]])
            self.norm_pre(xt[jx], t_xt[jx], n, i, 0, m, h, t_h, W, psn, t_psn)
            b = c0 // self.TB
            pos0 = c0 - b * self.TB
            is_ctx = (m == 2)
            for s in range(n // 128):
                hs = lambda k: h[:, k, s * 128:(s + 1) * 128]
                for half in range(2):
                    pj = nv % 2
                    nv += 1
                    for k in range(KC):
                        kb.op("pe", lambda k=k, half=half, pj=pj: nc.tensor.matmul(
                            psv[pj][:, :], hs(k), wqs[:, k, 2048 + half * 512:2048 + (half + 1) * 512],
                            start=(k == 0), stop=(k == KC - 1)),
                            reads=[t_h, t_wq[4 + half]], writes=[t_psv[pj]])
                    kb.op("act", lambda half=half, pj=pj, s=s: nc.scalar.copy(
                        out=vst[s % 2][:, half * 512:(half + 1) * 512], in_=psv[pj][:, :]),
                        reads=[t_psv[pj]], writes=[t_vst[s % 2]])
                r0 = b * self.TB + pos0 + s * 128
                kb.dma(V[b, pos0 + s * 128:pos0 + (s + 1) * 128, :], vst[s % 2][:], reads=[t_vst[s % 2]])
                for which in range(2):
                    ps = psqk[which]
                    tps = t_psqk[which]
                    for half in range(2):
                        for k in range(KC):
                            kb.op("pe", lambda k=k, half=half, which=which, ps=ps: nc.tensor.matmul(
                                ps[:, half * 512:(half + 1) * 512], hs(k),
                                wqs[:, k, which * 1024 + half * 512:which * 1024 + (half + 1) * 512],
                                start=(k == 0), stop=(k == KC - 1)),
                                reads=[t_h, t_wq[which * 2 + half]], writes=[tps])
                    dst = qk[which]
                    if is_ctx:
                        kb.op("act", lambda ps=ps, dst=dst, which=which: nc.scalar.mul(
                            out=dst[:], in_=ps[:], mul=(0.125 if which == 0 else 1.0)),
                            reads=[tps], writes=[t_qk[which]])
                    else:
                        lt = (pos0 - self.CTX) // 128 + s
                        tab = rpq if which == 0 else rp
                        cs = tab[:, lt, 0, :].unsqueeze(1).broadcast_to([128, 16, 32])
                        sn = tab[:, lt, 1, :].unsqueeze(1).broadcast_to([128, 16, 32])
                        pv = ps[:, :].rearrange("p (a two f) -> p a two f", two=2, f=32)
                        dv = dst[:, :].rearrange("p (a two f) -> p a two f", two=2, f=32)
                        t1, t2 = pv[:, :, 0, :], pv[:, :, 1, :]
                        kb.op("dve", lambda: nc.vector.tensor_tensor(out=ta[:], in0=t1, in1=cs, op=ALU.mult),
                              reads=[tps, t_rp], writes=[t_ta])
                        kb.op("dve", lambda: nc.vector.tensor_tensor(out=tb[:], in0=t2, in1=sn, op=ALU.mult),
                              reads=[tps, t_rp], writes=[t_tb])
                        kb.op("dve", lambda: nc.vector.tensor_tensor(out=dv[:, :, 0, :], in0=ta[:], in1=tb[:],
                                                                     op=ALU.subtract),
                              reads=[t_ta, t_tb], writes=[t_qk[which]])
                        kb.op("dve", lambda: nc.vector.tensor_tensor(out=ta[:], in0=t1, in1=sn, op=ALU.mult),
                              reads=[tps, t_rp], writes=[t_ta])
                        kb.op("dve", lambda: nc.vector.tensor_tensor(out=tb[:], in0=t2, in1=cs, op=ALU.mult),
                              reads=[tps, t_rp], writes=[t_tb])
                        kb.op("dve", lambda: nc.vector.tensor_tensor(out=dv[:, :, 1, :], in0=ta[:], in1=tb[:],
                                                                     op=ALU.add),
                              reads=[t_ta, t_tb], writes=[t_qk[which]])
                    for hd in range(HEADS):
                        kb.op("pe", lambda hd=hd, dst=dst: nc.tensor.transpose(
                            out=pst[:, hd, :], in_=dst[:, hd * 128:(hd + 1) * 128], identity=ident[:]),
                            reads=[t_qk[which], t_id], writes=[t_pst])
                    sl = slice(s * 128, (s + 1) * 128)
                    if which == 0:
                        kb.op("act", lambda sl=sl: nc.scalar.copy(out=qz[0][0:64, :, sl], in_=pst[0:64, :, :]),
                              reads=[t_pst], writes=[t_qz[0]])
                        kb.op("act", lambda sl=sl: nc.scalar.copy(out=qz[1][64:128, :, sl], in_=pst[64:128, :, :]),
                              reads=[t_pst], writes=[t_qz[1]])
                    else:
                        kb.op("act", lambda sl=sl: nc.scalar.copy(out=kz[:, :, sl], in_=pst[:, :, :]),
                              reads=[t_pst], writes=[t_kz])
            for c in range(2):
                kb.dma(QT[b, :, c, :, pos0:pos0 + n].rearrange("h p t -> p h t"), qz[c][:, :, :n],
                       reads=[t_qz[c]])
            kb.dma(KT[b, :, :, pos0:pos0 + n].rearrange("h p t -> p h t"), kz[:, :, :n], reads=[t_kz])
        kb.barrier()


def phase_attn_core(self, i, j, lambda_init, need_ctx):
    kb, nc = self.kb, self.nc
    QT, KT, V, OT = self.dram["QT"], self.dram["KT"], self.dram["V"], self.dram["OT"]
    TB, CTX = self.TB, self.CTX
    nkt = TB // 128
    QG = 256
    with Pool(kb) as P:
        kts = P.sb([128, HEADS, TB], BF16, "kts")
        vs = P.sb([128, nkt, HEADS, 130], BF16, "vs")
        t_kts, t_vs = Tok(), Tok()
        ident = P.sb([128, 128], BF16, "ident")
        t_id = Tok()
        kb.dma(ident[:], self.dram["ident"], writes=[t_id], q="pool")
        lam = P.sb([128, 4, 64], F32, "lam")
        lt = P.sb([128, 2, 64], F32, "lt")
        ls = P.sb([128, 2], F32, "ls")
        nlam = P.sb([128, 1], F32, "nlam")
        gsb = P.sb([128, 128], F32, "gsb")
        t_l = Tok()
        kb.dma(lam[:], self.dram["attn_lambda"][j].partition_broadcast(128), writes=[t_l])
        kb.dma(gsb[:], self.dram["attn_subln"][j].partition_broadcast(128), writes=[t_l])
        kb.op("dve", lambda: nc.vector.tensor_tensor(out=lt[:], in0=lam[:, 0::2, :], in1=lam[:, 1::2, :],
                                                     op=ALU.mult), reads=[t_l], writes=[t_l])
        kb.op("dve", lambda: nc.vector.tensor_reduce(out=ls[:], in_=lt[:], op=ALU.add, axis=AX.X),
              reads=[t_l], writes=[t_l])
        kb.op("act", lambda: nc.scalar.activation(out=ls[:], in_=ls[:], func=AF.Exp), reads=[t_l], writes=[t_l])
        kb.op("dve", lambda: nc.vector.tensor_tensor(out=nlam[:], in0=ls[:, 1:2], in1=ls[:, 0:1], op=ALU.subtract),
              reads=[t_l], writes=[t_l])
        kb.op("dve", lambda: nc.vector.tensor_scalar(out=nlam[:], in0=nlam[:], scalar1=-lambda_init, scalar2=None,
                                                     op0=ALU.add), reads=[t_l], writes=[t_l])
        kb.op("act", lambda: nc.scalar.mul(out=gsb[:], in_=gsb[:], mul=1.0 - lambda_init), reads=[t_l], writes=[t_l])
        epsv = self.epsv
        qs_ = [P.sb([128, HEADS, 2, QG], BF16, "qs%d" % q) for q in range(2)]
        t_qs = [Tok(), Tok()]
        NPS = 3
        pss = [P.ps([128, 2, QG], F32, "pss%d" % q) for q in range(NPS)]
        t_pss = [Tok(excl=True) for _ in range(NPS)]
        accs = P.sb([128, 2, 2, 129], F32, "accs")
        t_accs = Tok()
        psa = [[P.ps([128, 512], F32, "psa%d%d" % (c, q)) for q in range(2)] for c in range(2)]
        t_psa = [[Tok(excl=True), Tok(excl=True)], [Tok(excl=True), Tok(excl=True)]]
        pst = P.ps([128, 128], BF16, "pst")
        t_pst = Tok(excl=True)
        es = [P.sb([128, 2, QG], BF16, "es%d" % q) for q in range(3)]
        t_es = [Tok() for _ in range(3)]
        mhalf = P.sb([128, 1], F32, "mhalf")
        kb.op("dve", lambda: nc.vector.memset(mhalf[:], -0.5), writes=[t_l])
        rz = P.sb([128, 2], F32, "rz")
        t_rz = Tok()
        o1 = P.sb([128, 128], F32, "o1")
        o2 = P.sb([128, 128], F32, "o2")
        junk = P.sb([128, 128], F32, "junk")
        ss = P.sb([128, 1], F32, "ss")
        on = P.sb([128, 128], BF16, "on")
        t_o1, t_o2, t_ss, t_on = Tok(), Tok(), Tok(), Tok()
        ots = [P.sb([128, HEADS, QG], BF16, "ots%d" % q) for q in range(2)]
        t_ots = [Tok(), Tok()]
        ne = 0
        ng = 0
        for b in range(self.NB):
            kb.dma(kts[:], KT[b].rearrange("h p t -> p h t"), writes=[t_kts])
            for hh in range(HEADS):
                kb.dma(vs[:, :, hh, 0:128],
                       V[b, :, hh * 128:(hh + 1) * 128].rearrange("(kt p) e -> p kt e", p=128), writes=[t_vs])
            kb.op("pool", lambda: nc.gpsimd.memset(vs[:, :, :, 128:129], 1.0), writes=[t_vs])
            groups = []
            if need_ctx:
                for q0 in range(0, CTX, QG):
                    groups.append((q0, min(QG, CTX - q0), list(range(CTX // 128))))
            for q0 in range(0, self.SEQ, QG):
                groups.append((CTX + q0, QG, list(range(nkt))))
            for (q0, nq, ktl) in groups:
                gj = ng % 2
                ng += 1
                kb.dma(qs_[gj][:, :, :, :nq], QT[b, :, :, :, q0:q0 + nq].rearrange("h c p t -> p h c t"),
                       writes=[t_qs[gj]])
                nsub = nq // 128
                its = [(hh, ki, kt) for hh in range(HEADS) for ki, kt in enumerate(ktl)]
                nk = len(ktl)

                def emit_scores(idx, gj=gj, nq=nq, its=its):
                    hh, ki, kt = its[idx]
                    pj = (ne0 + idx) % NPS
                    for c in range(2):
                        kb.op("pe", lambda c=c: nc.tensor.matmul(
                            pss[pj][:, c, :nq], kts[:, hh, kt * 128:(kt + 1) * 128], qs_[gj][:, hh, c, :nq],
                            start=True, stop=True), reads=[t_kts, t_qs[gj]], writes=[t_pss[pj]])

                def emit_exp(idx, nq=nq):
                    pj = (ne0 + idx) % NPS
                    ej = (ne0 + idx) % 3
                    kb.op("act", lambda: nc.scalar.activation(
                        out=es[ej][:, :, :nq], in_=pss[pj][:, :, :nq], func=AF.Exp),
                        reads=[t_pss[pj]], writes=[t_es[ej]])

                def emit_pv(idx, nsub=nsub, its=its, nk=nk):
                    hh, ki, kt = its[idx]
                    ej = (ne0 + idx) % 3
                    for c in range(2):
                        for sq in range(nsub):
                            kb.op("pe", lambda c=c, sq=sq: nc.tensor.matmul(
                                psa[c][sq][:, 0:129], es[ej][:, c, sq * 128:(sq + 1) * 128],
                                vs[:, kt, hh, 0:129], start=(ki == 0), stop=(ki == nk - 1)),
                                reads=[t_es[ej], t_vs], writes=[t_psa[c][sq]])

                def finalize(hh, gj=gj, nsub=nsub):
                    for c in range(2):
                        for sq in range(nsub):
                            kb.op("dve", lambda c=c, sq=sq: nc.vector.tensor_copy(
                                out=accs[:, c, sq, :], in_=psa[c][sq][:, 0:129]),
                                reads=[t_psa[c][sq]], writes=[t_accs])
                    for sq in range(nsub):
                        kb.op("dve", lambda sq=sq: nc.vector.reciprocal(out=rz[:, 0:2], in_=accs[:, :, sq, 128]),
                              reads=[t_accs], writes=[t_rz])
                        kb.op("dve", lambda: nc.vector.tensor_tensor(out=rz[:, 1:2], in0=rz[:, 1:2], in1=nlam[:],
                                                                     op=ALU.mult), reads=[t_l], writes=[t_rz])
                        kb.op("dve", lambda sq=sq: nc.vector.tensor_scalar(
                            out=o1[:], in0=accs[:, 0, sq, 0:128], scalar1=rz[:, 0:1], scalar2=None, op0=ALU.mult),
                            reads=[t_accs, t_rz], writes=[t_o1])
                        kb.op("dve", lambda sq=sq: nc.vector.scalar_tensor_tensor(
                            out=o2[:], in0=accs[:, 1, sq, 0:128], scalar=rz[:, 1:2], in1=o1[:],
                            op0=ALU.mult, op1=ALU.add), reads=[t_accs, t_rz, t_o1], writes=[t_o2])
                        kb.op("dve", lambda: nc.vector.scalar_tensor_tensor(
                            out=junk[:], in0=o2[:], scalar=1.0, in1=o2[:], op0=ALU.mult, op1=ALU.mult,
                            accum_out=ss[:]), reads=[t_o2], writes=[t_ss])
                        kb.op("dve", lambda: nc.vector.tensor_scalar(
                            out=ss[:], in0=ss[:], scalar1=1.0 / 128.0, scalar2=EPS, op0=ALU.mult, op1=ALU.add),
                            writes=[t_ss])
                        kb.op("pool", lambda: nc.gpsimd.tensor_tensor(out=ss[:], in0=ss[:], in1=mhalf[:], op=ALU.pow),
                              reads=[t_l], writes=[t_ss])
                        kb.op("dve", lambda: nc.vector.scalar_tensor_tensor(
                            out=on[:], in0=o2[:], scalar=ss[:, 0:1], in1=gsb[:], op0=ALU.mult, op1=ALU.mult),
                            reads=[t_o2, t_ss, t_l], writes=[t_on])
                        kb.op("pe", lambda: nc.tensor.transpose(out=pst[:], in_=on[:], identity=ident[:]),
                              reads=[t_on, t_id], writes=[t_pst])
                        kb.op("dve", lambda sq=sq: nc.vector.tensor_copy(
                            out=ots[gj][:, hh, sq * 128:(sq + 1) * 128], in_=pst[:]),
                            reads=[t_pst], writes=[t_ots[gj]])

                ne0 = ne
                AH = 2
                for a in range(min(AH, len(its))):
                    emit_scores(a)
                for idx in range(len(its)):
                    emit_exp(idx)
                    if idx + AH < len(its):
                        emit_scores(idx + AH)
                    emit_pv(idx)
                    if its[idx][1] == nk - 1:
                        finalize(its[idx][0])
                ne += len(its)
                col = b * TB + q0
                kb.dma(OT[0:1024, col:col + nq].rearrange("(h p) t -> p h t", p=128), ots[gj][:, :, :nq],
                       reads=[t_ots[gj]])
        kb.barrier()


def phase_proj_post(self, i, wname, widx, kc, with_ctx, glu=False):
    kb, nc = self.kb, self.nc
    NT = self.NT
    xT = self.dram["xT"].rearrange("(k p) t -> p k t", p=128)
    OT = self.dram["OT"].rearrange("(k p) t -> p k t", p=128)
    wd = self.dram[wname][widx].rearrange("(k p) d -> p k d", p=128)
    ncol = 2048 if glu else 1024
    with Pool(kb) as P:
        ws = P.sb([128, kc, ncol], BF16, "ws")
        t_w = [Tok() for _ in range(kc)]
        for k in range(kc):
            for hf in range(ncol // 1024):
                kb.dma(ws[:, k, hf * 1024:(hf + 1) * 1024], wd[:, k, hf * 1024:(hf + 1) * 1024],
                       writes=[t_w[k]], q="pool")
        xt = [P.sb([128, KC, NT], F32, "xt%d" % q) for q in range(2)]
        t_xt = [Tok(), Tok()]
        at = [P.sb([128, kc, NT], BF16, "at%d" % q) for q in range(2)]
        t_at = [Tok(), Tok()]
        y = P.sb([128, KC, NT], F32, "y")
        t_y = Tok()
        sg = P.sb([128, NT], F32, "sg")
        t_sg = Tok()
        W = self.norm_work(P, NT)
        psn = P.ps([128, 512], F32, "psn")
        t_psn = Tok(excl=True)
        psd = [P.ps([128, 512], F32, "psd%d" % q) for q in range(4)]
        t_psd = [Tok(excl=True) for _ in range(4)]
        tl = self.tiles(with_ctx)
        c0, n, m = tl[0]
        kb.dma(xt[0][:, :, :n], xT[:, :, c0:c0 + n], writes=[t_xt[0]])
        kb.dma(at[0][:, :, :n], OT[:, 0:kc, c0:c0 + n], writes=[t_at[0]])
        nd = 0
        for ti, (c0, n, m) in enumerate(tl):
            jx = ti % 2
            if ti + 1 < len(tl):
                c1, n1, _ = tl[ti + 1]
                kb.dma(xt[1 - jx][:, :, :n1], xT[:, :, c1:c1 + n1], writes=[t_xt[1 - jx]])
                kb.dma(at[1 - jx][:, :, :n1], OT[:, 0:kc, c1:c1 + n1], writes=[t_at[1 - jx]])
            for d in range(KC):
                pj = nd % 2
                nd += 1
                for k in range(kc):
                    kb.op("pe", lambda d=d, k=k, pj=pj: nc.tensor.matmul(
                        psd[pj][:, :n], ws[:, k, d * 128:(d + 1) * 128], at[jx][:, k, :n],
                        start=(k == 0), stop=(k == kc - 1)), reads=[t_w[k], t_at[jx]], writes=[t_psd[pj]])
                if glu:
                    for k in range(kc):
                        kb.op("pe", lambda d=d, k=k, pj=pj: nc.tensor.matmul(
                            psd[2 + pj][:, :n], ws[:, k, 1024 + d * 128:1024 + (d + 1) * 128], at[jx][:, k, :n],
                            start=(k == 0), stop=(k == kc - 1)), reads=[t_w[k], t_at[jx]], writes=[t_psd[2 + pj]])
                    kb.op("act", lambda pj=pj: nc.scalar.activation(out=sg[:, :n], in_=psd[2 + pj][:, :n],
                                                                    func=AF.Sigmoid),
                          reads=[t_psd[2 + pj]], writes=[t_sg])
                    kb.op("dve", lambda d=d, pj=pj: nc.vector.tensor_tensor(
                        out=y[:, d, :n], in0=psd[pj][:, :n], in1=sg[:, :n], op=ALU.mult),
                        reads=[t_psd[pj], t_sg], writes=[t_y])
                else:
                    kb.op("act", lambda d=d, pj=pj: nc.scalar.copy(out=y[:, d, :n], in_=psd[pj][:, :n]),
                          reads=[t_psd[pj]], writes=[t_y])
            self.norm_post(y, t_y, xt[jx], t_xt[jx], n, i, 0, m, W, psn, t_psn)
            kb.dma(xT[:, :, c0:c0 + n], xt[jx][:, :, :n], reads=[t_xt[jx]])
        kb.barrier()


Model.phase_attn_qkv = phase_attn_qkv
Model.phase_attn_core = phase_attn_core
Model.phase_proj_post = phase_proj_post


LW = 1536
LC = 12


def gelu_tanh(self, dst, src, t_src, shape, Wg, t_dst):
    kb, nc = self.kb, self.nc
    a, b_ = Wg["a"], Wg["b"]
    va = a[:, :shape[1]] if len(shape) == 2 else a[:, :shape[1], :shape[2]]
    vb = b_[:, :shape[1]] if len(shape) == 2 else b_[:, :shape[1], :shape[2]]
    kb.op("act", lambda: nc.scalar.activation(out=va, in_=src, func=AF.Square), reads=[t_src], writes=[Wg["ta"]])
    kb.op("dve", lambda: nc.vector.tensor_scalar(out=va, in0=va, scalar1=0.044715, scalar2=1.0,
                                                 op0=ALU.mult, op1=ALU.add), writes=[Wg["ta"]])
    kb.op("dve", lambda: nc.vector.tensor_tensor(out=va, in0=va, in1=src, op=ALU.mult),
          reads=[t_src], writes=[Wg["ta"]])
    kb.op("act", lambda: nc.scalar.activation(out=vb, in_=va, func=AF.Sigmoid, scale=1.5957691216057308),
          reads=[Wg["ta"]], writes=[Wg["tb"]])
    kb.op("dve", lambda: nc.vector.tensor_tensor(out=dst, in0=vb, in1=src, op=ALU.mult),
          reads=[Wg["tb"], t_src], writes=[t_dst])


def _lru_scratch(self):
    _attn_scratch(self)
    if "RT" not in self.dram:
        self.dscratch("RT", [LW, self.T], F32)
        self.dscratch("GT", [LW, self.T], BF16)


def phase_lru_in(self, i, j):
    kb, nc = self.kb, self.nc
    _lru_scratch(self)
    NT = self.NT
    xT = self.dram["xT"].rearrange("(k p) t -> p k t", p=128)
    win = self.dram["lru_w_in"][j].rearrange("(k p) f -> p k f", p=128)
    RT = self.dram["RT"].rearrange("(k p) t -> p k t", p=128)
    GT = self.dram["GT"].rearrange("(k p) t -> p k t", p=128)
    with Pool(kb) as P:
        ws = P.sb([128, KC, 2 * LW], BF16, "wins")
        t_w = [Tok() for _ in range(6)]
        for g in range(6):
            kb.dma(ws[:, :, g * 512:(g + 1) * 512], win[:, :, g * 512:(g + 1) * 512], writes=[t_w[g]], q="pool")
        xt = [P.sb([128, KC, NT], F32, "xt%d" % q) for q in range(2)]
        t_xt = [Tok(), Tok()]
        h = P.sb([128, KC, NT], BF16, "h")
        t_h = Tok()
        W = self.norm_work(P, NT)
        psn = P.ps([128, 512], F32, "psn")
        t_psn = Tok(excl=True)
        pso = [P.ps([128, 2, 256], F32, "pso%d" % q) for q in range(3)]
        t_pso = [Tok(excl=True) for _ in range(3)]
        Wg = dict(a=P.sb([128, 2, 256], F32, "ga"), b=P.sb([128, 2, 256], F32, "gb"), ta=Tok(), tb=Tok())
        gs = [P.sb([128, LC, NT], BF16, "gs%d" % q) for q in range(2)]
        t_gs = [Tok(), Tok()]
        rs = [P.sb([128, LC, NT], F32, "rs%d" % q) for q in range(2)]
        t_rs = [Tok(), Tok()]
        tl = self.tiles(True)
        c0, n, m = tl[0]
        kb.dma(xt[0][:, :, :n], xT[:, :, c0:c0 + n], writes=[t_xt[0]])
        np_ = 0
        for ti, (c0, n, m) in enumerate(tl):
            jx = ti % 2
            if ti + 1 < len(tl):
                c1, n1, _ = tl[ti + 1]
                kb.dma(xt[1 - jx][:, :, :n1], xT[:, :, c1:c1 + n1], writes=[t_xt[1 - jx]])
            self.norm_pre(xt[jx], t_xt[jx], n, i, 0, m, h, t_h, W, psn, t_psn)
            for op in range(LC):
                pj = np_ % 3
                np_ += 1
                for q2 in range(2):
                    oc = op * 2 + q2
                    for k in range(KC):
                        kb.op("pe", lambda k=k, oc=oc, q2=q2, pj=pj: nc.tensor.matmul(
                            pso[pj][:, q2, :n], ws[:, k, oc * 128:(oc + 1) * 128], h[:, k, :n],
                            start=(k == 0), stop=(k == KC - 1)), reads=[t_w[oc // 4], t_h], writes=[t_pso[pj]])
                if op < 6:
                    gelu_tanh(self, gs[jx][:, 2 * op:2 * op + 2, :n], pso[pj][:, :, :n], t_pso[pj],
                              [128, 2, n], Wg, t_gs[jx])
                else:
                    kb.op("act", lambda op=op, pj=pj: nc.scalar.copy(
                        out=rs[jx][:, 2 * (op - 6):2 * (op - 6) + 2, :n], in_=pso[pj][:, :, :n]),
                        reads=[t_pso[pj]], writes=[t_rs[jx]])
            kb.dma(GT[:, :, c0:c0 + n], gs[jx][:, :, :n], reads=[t_gs[jx]])
            kb.dma(RT[:, :, c0:c0 + n], rs[jx][:, :, :n], reads=[t_rs[jx]])
        kb.barrier()


def phase_lru_scan(self, i, j):
    kb, nc = self.kb, self.nc
    TB, CTX, SEQ = self.TB, self.CTX, self.SEQ
    RT = self.dram["RT"].rearrange("(k p) t -> p k t", p=128)
    GT = self.dram["GT"].rearrange("(k p) t -> p k t", p=128)
    OT = self.dram["OT"].rearrange("(k p) t -> p k t", p=128)
    CW = 512
    with Pool(kb) as P:
        wg = P.sb([128, 2, 2, 6, 2, 256], BF16, "wg")
        cw = P.sb([128, LC, 4], F32, "cw")
        cb = P.sb([128, LC], F32, "cb")
        bg = P.sb([128, 2, 2, LC], F32, "bg")
        ap_ = P.sb([128, 2, LC], F32, "ap")
        cv = P.sb([128, 2, LC], F32, "cv")
        cv2 = P.sb([128, 2, LC], F32, "cv2")
        one = P.sb([128, 1], F32, "one")
        t_s = Tok()
        kb.dma(wg[:], self.dram["lru_w_gate"][j], writes=[t_s], q="pool")
        kb.dma(cw[:], self.dram["lru_conv_w"][j], writes=[t_s])
        kb.dma(cb[:], self.dram["lru_conv_b"][j], writes=[t_s])
        kb.dma(bg[:], self.dram["lru_b_gate"][j], writes=[t_s])
        kb.dma(ap_[:], self.dram["lru_a_param"][j], writes=[t_s])
        kb.op("dve", lambda: nc.vector.memset(one[:], 1.0), writes=[t_s])
        kb.op("act", lambda: nc.scalar.activation(out=cv[:], in_=ap_[:], func=AF.Exp, scale=-1.0),
              reads=[t_s], writes=[t_s])
        kb.op("act", lambda: nc.scalar.activation(out=cv[:], in_=cv[:], func=AF.Ln, bias=one[:], scale=1.0),
              reads=[t_s], writes=[t_s])
        kb.op("dve", lambda: nc.vector.tensor_scalar(out=cv2[:], in0=cv[:], scalar1=-16.0, scalar2=None,
                                                     op0=ALU.mult), reads=[t_s], writes=[t_s])
        kb.op("dve", lambda: nc.vector.tensor_scalar(out=cv[:], in0=cv[:], scalar1=-8.0, scalar2=None,
                                                     op0=ALU.mult), reads=[t_s], writes=[t_s])
        rt = P.sb([128, TB], F32, "rt")
        t_rt = Tok()
        u = P.sb([128, 2, TB], F32, "u")
        t_u = [Tok(), Tok()]
        ub = P.sb([128, 2, TB], BF16, "ub")
        t_ub = [Tok(), Tok()]
        av = P.sb([128, TB], F32, "av")
        bv = P.sb([128, TB], F32, "bv")
        t_av, t_bv = Tok(), Tok()
        hf = P.sb([128, TB], F32, "hf")
        hb = P.sb([128, TB], F32, "hb")
        t_hf, t_hb = Tok(), Tok()
        gt = P.sb([128, TB], BF16, "gt")
        t_gt = Tok()
        mo = P.sb([128, TB], BF16, "mo")
        t_mo = Tok()
        psg = [P.ps([128, 2, CW], F32, "psg%d" % q) for q in range(2)]
        t_psg = [Tok(excl=True), Tok(excl=True)]
        sr = [P.sb([128, 2, CW], F32, "sr%d" % q) for q in range(2)]
        t_sr = [Tok(), Tok()]
        a2 = [P.sb([128, CW], F32, "a2%d" % q) for q in range(2)]
        t_a2 = [Tok(), Tok()]
        segs = [(0, CTX), (CTX, TB)]
        ng = 0
        for b in range(self.NB):
            cb0 = b * TB
            for n6 in range(6):
                for q2 in range(2):
                    ch = n6 * 2 + q2
                    kb.dma(rt[:], RT[:, ch, cb0:cb0 + TB], writes=[t_rt])
                    for (s0, s1) in segs:
                        kb.op("dve", lambda s0=s0, s1=s1, ch=ch, q2=q2: nc.vector.tensor_scalar(
                            out=u[:, q2, s0:s1], in0=rt[:, s0:s1], scalar1=cw[:, ch, 2:3], scalar2=cb[:, ch:ch + 1],
                            op0=ALU.mult, op1=ALU.add), reads=[t_rt, t_s], writes=[t_u[q2]])
                        for (tap, off) in ((0, -2), (1, -1), (3, 1)):
                            if off < 0:
                                o_sl, i_sl = slice(s0 - off, s1), slice(s0, s1 + off)
                            else:
                                o_sl, i_sl = slice(s0, s1 - off), slice(s0 + off, s1)
                            kb.op("dve", lambda o_sl=o_sl, i_sl=i_sl, ch=ch, q2=q2, tap=tap:
                                  nc.vector.scalar_tensor_tensor(
                                      out=u[:, q2, o_sl], in0=rt[:, i_sl], scalar=cw[:, ch, tap:tap + 1],
                                      in1=u[:, q2, o_sl], op0=ALU.mult, op1=ALU.add),
                                  reads=[t_rt, t_s], writes=[t_u[q2]])
                    kb.op("act", lambda q2=q2: nc.scalar.copy(out=ub[:, q2, :], in_=u[:, q2, :]),
                          reads=[t_u[q2]], writes=[t_ub[q2]])
                for q2 in range(2):
                    ch = n6 * 2 + q2
                    kb.dma(gt[:], GT[:, ch, cb0:cb0 + TB], writes=[t_gt])
                    for d in range(2):
                        for c0 in range(0, TB, CW):
                            n = min(CW, TB - c0)
                            pj = ng % 2
                            ng += 1
                            for kk in range(2):
                                for kc in range(2):
                                    kb.op("pe", lambda kk=kk, kc=kc, d=d, c0=c0, n=n, pj=pj: nc.tensor.matmul(
                                        psg[pj][:, kk, :n], wg[:, d, kk, n6, kc, q2 * 128:(q2 + 1) * 128],
                                        ub[:, kc, c0:c0 + n], start=(kc == 0), stop=(kc == 1)),
                                        reads=[t_s] + t_ub, writes=[t_psg[pj]])
                            for kk in range(2):
                                kb.op("act", lambda kk=kk, d=d, n=n, pj=pj: nc.scalar.activation(
                                    out=sr[pj][:, kk, :n], in_=psg[pj][:, kk, :n], func=AF.Sigmoid,
                                    bias=bg[:, d, kk, ch:ch + 1], scale=1.0),
                                    reads=[t_psg[pj], t_s], writes=[t_sr[pj]])
                            kb.op("act", lambda d=d, c0=c0, n=n, pj=pj: nc.scalar.activation(
                                out=av[:, c0:c0 + n], in_=sr[pj][:, 0, :n], func=AF.Exp, scale=cv[:, d, ch:ch + 1]),
                                reads=[t_sr[pj], t_s], writes=[t_av])
                            kb.op("act", lambda d=d, n=n, pj=pj: nc.scalar.activation(
                                out=a2[pj][:, :n], in_=sr[pj][:, 0, :n], func=AF.Exp, scale=cv2[:, d, ch:ch + 1]),
                                reads=[t_sr[pj], t_s], writes=[t_a2[pj]])
                            kb.op("act", lambda n=n, pj=pj: nc.scalar.activation(
                                out=a2[pj][:, :n], in_=a2[pj][:, :n], func=AF.Sqrt, bias=one[:], scale=-1.0),
                                reads=[t_s], writes=[t_a2[pj]])
                            kb.op("dve", lambda n=n, pj=pj, c0=c0: nc.vector.tensor_tensor(
                                out=sr[pj][:, 1, :n], in0=sr[pj][:, 1, :n], in1=u[:, q2, c0:c0 + n], op=ALU.mult),
                                reads=[t_u[q2]], writes=[t_sr[pj]])
                            kb.op("dve", lambda n=n, pj=pj, c0=c0: nc.vector.tensor_tensor(
                                out=bv[:, c0:c0 + n], in0=sr[pj][:, 1, :n], in1=a2[pj][:, :n], op=ALU.mult),
                                reads=[t_sr[pj], t_a2[pj]], writes=[t_bv])
                        if d == 0:
                            kb.op("dve", lambda: nc.vector.tensor_tensor_scan(
                                out=hf[:, :], data0=av[:, :], data1=bv[:, :], initial=0.0,
                                op0=ALU.mult, op1=ALU.add), reads=[t_av, t_bv], writes=[t_hf])
                        else:
                            kb.op("dve", lambda: nc.vector.tensor_tensor_scan(
                                out=hb[:, 0:CTX][:, ::-1], data0=av[:, 0:CTX][:, ::-1], data1=bv[:, 0:CTX][:, ::-1],
                                initial=0.0, op0=ALU.mult, op1=ALU.add), reads=[t_av, t_bv], writes=[t_hb])
                            kb.op("dve", lambda: nc.vector.tensor_tensor_scan(
                                out=hb[:, CTX:TB][:, ::-1], data0=av[:, CTX:TB][:, ::-1],
                                data1=bv[:, CTX:TB][:, ::-1], initial=hb[:, 0:1], op0=ALU.mult, op1=ALU.add),
                                reads=[t_av, t_bv], writes=[t_hb])
                    kb.op("dve", lambda: nc.vector.tensor_tensor(out=hf[:], in0=hf[:], in1=hb[:], op=ALU.add),
                          reads=[t_hb], writes=[t_hf])
                    kb.op("dve", lambda: nc.vector.tensor_tensor(out=mo[:], in0=hf[:], in1=gt[:], op=ALU.mult),
                          reads=[t_hf, t_gt], writes=[t_mo])
                    kb.dma(OT[:, ch, cb0:cb0 + TB], mo[:], reads=[t_mo])
        kb.barrier()


Model.phase_lru_in = phase_lru_in
Model.phase_lru_scan = phase_lru_scan


import math as _math
NG = 64
LL = 8


def _s5_scratch(self):
    _attn_scratch(self)
    if "UT" not in self.dram:
        self.dscratch("UT", [1024, self.T], BF16)
        self.dscratch("YF", [2, 1024, self.T], F32)


def phase_s5_in(self, i):
    kb, nc = self.kb, self.nc
    _s5_scratch(self)
    NT = self.NT
    xT = self.dram["xT"].rearrange("(k p) t -> p k t", p=128)
    UT = self.dram["UT"].rearrange("(k p) t -> p k t", p=128)
    with Pool(kb) as P:
        xt = [P.sb([128, KC, NT], F32, "xt%d" % q) for q in range(2)]
        t_xt = [Tok(), Tok()]
        h = [P.sb([128, KC, NT], BF16, "h%d" % q) for q in range(2)]
        t_h = [Tok(), Tok()]
        W = self.norm_work(P, NT)
        psn = P.ps([128, 512], F32, "psn")
        t_psn = Tok(excl=True)
        tl = self.tiles(True)
        c0, n, m = tl[0]
        kb.dma(xt[0][:, :, :n], xT[:, :, c0:c0 + n], writes=[t_xt[0]])
        for ti, (c0, n, m) in enumerate(tl):
            jx = ti % 2
            if ti + 1 < len(tl):
                c1, n1, _ = tl[ti + 1]
                kb.dma(xt[1 - jx][:, :, :n1], xT[:, :, c1:c1 + n1], writes=[t_xt[1 - jx]])
            self.norm_pre(xt[jx], t_xt[jx], n, i, 0, m, h[jx], t_h[jx], W, psn, t_psn)
            kb.dma(UT[:, :, c0:c0 + n], h[jx][:, :, :n], reads=[t_h[jx]])
        kb.barrier()


def phase_s5_scan(self, i, jb):
    kb, nc = self.kb, self.nc
    TB, CTX, SEQ = self.TB, self.CTX, self.SEQ
    J = TB // LL
    npass = 0
    while (1 << npass) < J:
        npass += 1
    pmax = getattr(Model, "s5_piece_max", 512)
    pieces = [(0, J)]
    if J > pmax:
        hh = (J + 1) // 2
        pieces = [(0, hh), (hh, J)]
    UT = self.dram["UT"].rearrange("(k p) t -> p k t", p=128)
    YF = self.dram["YF"]
    TWO_PI = 2.0 * _math.pi
    V = nc.vector
    for d in getattr(self, 's5_dirs', (0, 1)):
        with Pool(kb) as P:
            t_st = Tok()

            def tt(o, a, b, op):
                kb.op("dve", lambda: V.tensor_tensor(out=o, in0=a, in1=b, op=op), writes=[t_st])

            def ts(o, a, s1, op0, s2=None, op1=None):
                if op1 is None:
                    kb.op("dve", lambda: V.tensor_scalar(out=o, in0=a, scalar1=s1, scalar2=None, op0=op0),
                          writes=[t_st])
                else:
                    kb.op("dve", lambda: V.tensor_scalar(out=o, in0=a, scalar1=s1, scalar2=s2, op0=op0, op1=op1),
                          writes=[t_st])

            def act(o, a, func, scale=1.0, bias=None):
                if bias is None:
                    kb.op("act", lambda: nc.scalar.activation(out=o, in_=a, func=func, scale=scale), writes=[t_st])
                else:
                    kb.op("act", lambda: nc.scalar.activation(out=o, in_=a, func=func, scale=scale, bias=bias),
                          writes=[t_st])
            BS = P.sb([128, 8, NG, 16], BF16, "BS")
            CS = P.sb([128, 9, NG, 16], BF16, "CS")
            PW = P.sb([128, 9, 2, NG], F32, "PW")
            QW = P.sb([128, npass, 2, NG], F32, "QW")
            qos = P.sb([128, npass, NG], F32, "qos")
            identb = P.sb([128, 128], BF16, "identb")
            identf = P.sb([128, 128], F32, "identf")
            shf = P.sb([128, 128], F32, "shf")
            dsk = P.sb([128, KC], F32, "dsk")
            kb.dma(identb[:], self.dram["ident"], writes=[t_st], q="pool")
            kb.dma(identf[:], self.dram["ident"], writes=[t_st])
            kb.dma(shf[:], self.dram["shift64"], writes=[t_st])
            with Pool(kb) as PS:
                pg = lambda nm: PS.sb([128, NG], F32, nm)
                are, aim, dt, xr, xi, mag, yy, ff, mm, sn, cs, lr, li = [pg("pg%d" % q) for q in range(13)]
                nr, den, cr, ci, t1, t2, t3, t4 = [pg("pgb%d" % q) for q in range(8)]
                ki = PS.sb([128, NG], I32, "ki")
                pgc = lambda nm: PS.sb([128, NG, 16], F32, nm)
                Bre, Bim, Cre, Cim, Bbr, Bbi, T1, T2, T3, T4 = [pgc("pgc%d" % q) for q in range(10)]
                kb.dma(are[:], self.dram["s5_a_re"][jb, d], writes=[t_st])
                kb.dma(aim[:], self.dram["s5_a_im"][jb, d], writes=[t_st])
                kb.dma(dt[:], self.dram["s5_log_dt"][jb, d].partition_broadcast(128), writes=[t_st])
                kb.dma(Bre[:], self.dram["s5_b_re"][jb, d], writes=[t_st])
                kb.dma(Bim[:], self.dram["s5_b_im"][jb, d], writes=[t_st])
                kb.dma(Cre[:], self.dram["s5_c_re"][jb, d], writes=[t_st])
                kb.dma(Cim[:], self.dram["s5_c_im"][jb, d], writes=[t_st])
                act(dt[:], dt[:], AF.Exp)
                tt(xr[:], are[:], dt[:], ALU.mult)
                tt(xi[:], aim[:], dt[:], ALU.mult)
                ts(yy[:], xi[:], 1.0 / TWO_PI, ALU.mult)
                kb.op("dve", lambda: V.tensor_copy(out=ki[:], in_=yy[:]), writes=[t_st])
                kb.op("dve", lambda: V.tensor_copy(out=ff[:], in_=ki[:]), writes=[t_st])
                tt(ff[:], yy[:], ff[:], ALU.subtract)
                ts(mm[:], ff[:], 0.5, ALU.is_gt)
                tt(ff[:], ff[:], mm[:], ALU.subtract)
                ts(mm[:], ff[:], -0.5, ALU.is_lt)
                tt(ff[:], ff[:], mm[:], ALU.add)
                ts(ff[:], ff[:], TWO_PI / 16.0, ALU.mult)
                tt(yy[:], ff[:], ff[:], ALU.mult)

                def horner(o, z, coefs):
                    kb.op("dve", lambda: V.memset(o, 0.0), writes=[t_st])
                    for c in reversed(coefs[1:]):
                        kb.op("dve", lambda c=c: V.scalar_tensor_tensor(out=o, in0=o, scalar=float(c), in1=z,
                                                                        op0=ALU.add, op1=ALU.mult), writes=[t_st])
                    ts(o, o, float(coefs[0]), ALU.add)
                fact = [1.0]
                for q in range(1, 16):
                    fact.append(fact[-1] * q)
                horner(cs[:], yy[:], [(-1) ** q / fact[2 * q] for q in range(6)])
                horner(sn[:], yy[:], [(-1) ** q / fact[2 * q + 1] for q in range(6)])
                tt(sn[:], sn[:], ff[:], ALU.mult)
                ts(mm[:], xr[:], 1.0 / 16.0, ALU.mult)
                horner(mag[:], mm[:], [1.0 / fact[q] for q in range(10)])
                tt(lr[:], mag[:], cs[:], ALU.mult)
                tt(li[:], mag[:], sn[:], ALU.mult)
                for _sq in range(4):
                    tt(t1[:], lr[:], lr[:], ALU.mult)
                    tt(t2[:], li[:], li[:], ALU.mult)
                    tt(t3[:], lr[:], li[:], ALU.mult)
                    tt(lr[:], t1[:], t2[:], ALU.subtract)
                    ts(li[:], t3[:], 2.0, ALU.mult)
                ts(nr[:], lr[:], -1.0, ALU.add)
                tt(t1[:], are[:], are[:], ALU.mult)
                tt(t2[:], aim[:], aim[:], ALU.mult)
                tt(den[:], t1[:], t2[:], ALU.add)
                kb.op("dve", lambda: V.reciprocal(out=den[:], in_=den[:]), writes=[t_st])
                tt(t1[:], nr[:], are[:], ALU.mult)
                tt(t2[:], li[:], aim[:], ALU.mult)
                tt(cr[:], t1[:], t2[:], ALU.add)
                tt(cr[:], cr[:], den[:], ALU.mult)
                tt(t1[:], li[:], are[:], ALU.mult)
                tt(t2[:], nr[:], aim[:], ALU.mult)
                tt(ci[:], t1[:], t2[:], ALU.subtract)
                tt(ci[:], ci[:], den[:], ALU.mult)
                bc = lambda v: v.unsqueeze(2).broadcast_to([128, NG, 16])
                tt(T1[:], Bre[:], bc(cr[:]), ALU.mult)
                tt(T2[:], Bim[:], bc(ci[:]), ALU.mult)
                tt(Bbr[:], T1[:], T2[:], ALU.subtract)
                tt(T1[:], Bim[:], bc(cr[:]), ALU.mult)
                tt(T2[:], Bre[:], bc(ci[:]), ALU.mult)
                tt(Bbi[:], T1[:], T2[:], ALU.add)
                kb.op("dve", lambda: V.memset(PW[:, 0, 0, :], 1.0), writes=[t_st])
                kb.op("dve", lambda: V.memset(PW[:, 0, 1, :], 0.0), writes=[t_st])
                kb.op("dve", lambda: V.tensor_copy(out=PW[:, 1, 0, :], in_=lr[:]), writes=[t_st])
                kb.op("dve", lambda: V.tensor_copy(out=PW[:, 1, 1, :], in_=li[:]), writes=[t_st])

                def cmul(o_r, o_i, a_r, a_i, b_r, b_i):
                    tt(t1[:], a_r, b_r, ALU.mult)
                    tt(t2[:], a_i, b_i, ALU.mult)
                    tt(t3[:], a_r, b_i, ALU.mult)
                    tt(t4[:], a_i, b_r, ALU.mult)
                    tt(o_r, t1[:], t2[:], ALU.subtract)
                    tt(o_i, t3[:], t4[:], ALU.add)
                for e in range(2, 9):
                    cmul(PW[:, e, 0, :], PW[:, e, 1, :], PW[:, e - 1, 0, :], PW[:, e - 1, 1, :], lr[:], li[:])
                kb.op("dve", lambda: V.tensor_copy(out=QW[:, 0, :, :], in_=PW[:, 8, :, :]), writes=[t_st])
                for k in range(1, npass):
                    cmul(QW[:, k, 0, :], QW[:, k, 1, :], QW[:, k - 1, 0, :], QW[:, k - 1, 1, :],
                         QW[:, k - 1, 0, :], QW[:, k - 1, 1, :])
                kb.op("dve", lambda: V.tensor_copy(out=qos[0:64, :, :], in_=QW[0:64, :, 1, :]), writes=[t_st])
                kb.op("dve", lambda: V.tensor_scalar(out=qos[64:128, :, :], in0=QW[64:128, :, 1, :], scalar1=-1.0,
                                                     scalar2=None, op0=ALU.mult), writes=[t_st])
                for e in range(9):
                    p_r, p_i = bc(PW[:, e, 0, :]), bc(PW[:, e, 1, :])
                    if e < 8:
                        tt(T1[:], Bbr[:], p_r, ALU.mult)
                        tt(T2[:], Bbi[:], p_i, ALU.mult)
                        tt(T3[:], Bbi[:], p_r, ALU.mult)
                        tt(T4[:], Bbr[:], p_i, ALU.mult)
                        tt(BS[0:64, e], T1[0:64], T2[0:64], ALU.subtract)
                        tt(BS[64:128, e], T3[64:128], T4[64:128], ALU.add)
                    tt(T1[:], Cre[:], p_r, ALU.mult)
                    tt(T2[:], Cim[:], p_i, ALU.mult)
                    tt(T3[:], Cre[:], p_i, ALU.mult)
                    tt(T4[:], Cim[:], p_r, ALU.mult)
                    tt(CS[0:64, e], T1[0:64], T2[0:64], ALU.subtract)
                    kb.op("dve", lambda: V.scalar_tensor_tensor(
                        out=CS[64:128, e], in0=T3[64:128], scalar=-1.0, in1=T4[64:128],
                        op0=ALU.mult, op1=ALU.subtract), writes=[t_st])
            if getattr(self, "s5_stop", 0) == 1:
                if d == getattr(self, "s5_dbg_dir", 0):
                    kb.dma(self.dram["dbgBS"], BS[:], reads=[t_st])
                    kb.dma(self.dram["dbgCS"], CS[:], reads=[t_st])
                    kb.dma(self.dram["dbgPW"], PW[:], reads=[t_st])
                    kb.dma(self.dram["dbgQW"], QW[:], reads=[t_st])
                kb.barrier()
                continue
            ECt = P.sb([128, 9, 1152], BF16, "ECt")
            LBt = P.sb([128, 8, 8, 128], BF16, "LBt")
            Kdt = P.sb([128, 8, 128], BF16, "Kdt")
            AMt = P.sb([128, npass, 8, 128], BF16, "AMt")
            t_EC, t_LB, t_Kd, t_AM = Tok(), Tok(), Tok(), Tok()
            kb.op("pool", lambda: nc.gpsimd.memset(ECt[:], 0.0), reads=[t_st], writes=[t_EC])
            ub = [P.sb([128, TB + CTX], BF16, "ub%d" % q) for q in range(2)]
            t_ub = [Tok(), Tok()]
            S32 = P.sb([128, 8, J], F32, "S32")
            Sb = P.sb([128, 8, J], BF16, "Sb")
            Hb = P.sb([128, 8, J], BF16, "Hb")
            t_S32 = [Tok() for _ in range(8)]
            t_Sb = [Tok() for _ in range(8)]
            t_Hb = Tok()
            kb.op("pool", lambda: nc.gpsimd.memset(Hb[:, :, 0:1], 0.0), writes=[t_Hb])
            yb1 = P.sb([128, TB], F32, "yb")
            yb = [yb1, yb1]
            t_yb1 = Tok()
            t_yb = [t_yb1, t_yb1]
            amt = [P.sb([128, 128], F32, "amt%d" % q) for q in range(2)]
            t_amt = [Tok(), Tok()]
            nu = 0
            for cidx in getattr(self, 's5_chunks', range(KC)):
                g0 = cidx * 8
                with Pool(kb) as PC:
                    EBt = PC.sb([128, 8, 1152], BF16, "EBt")
                    t_EB = Tok()
                    pst = [PC.ps([128, 8, 128], BF16, "pst%d" % q) for q in range(2)]
                    t_pst = [Tok(excl=True), Tok(excl=True)]
                    psk = [PC.ps([128, 4, 128], F32, "psk%d" % q) for q in range(2)]
                    t_psk = [Tok(excl=True), Tok(excl=True)]
                    kb.op("pool", lambda: nc.gpsimd.memset(EBt[:], 0.0), writes=[t_EB])
                    for e in range(9):
                        if e < 8:
                            kb.op("dve", lambda e=e: V.tensor_copy(
                                out=EBt[:, e, :].rearrange("p (g s) -> p g s", s=144)[:, :, 0:16],
                                in_=BS[:, e, g0:g0 + 8, :]), reads=[t_st], writes=[t_EB])
                        kb.op("dve", lambda e=e: V.tensor_copy(
                            out=ECt[:, e, :].rearrange("p (g s) -> p g s", s=144)[:, :, 0:16],
                            in_=CS[:, e, g0:g0 + 8, :]), reads=[t_st], writes=[t_EC])
                    for e in range(8):
                        pj = e % 2
                        for g in range(8):
                            kb.op("pe", lambda e=e, g=g, pj=pj: nc.tensor.transpose(
                                out=pst[pj][:, g, :], in_=EBt[:, e, g * 128:(g + 1) * 128], identity=identb[:]),
                                reads=[t_EB, t_st], writes=[t_pst[pj]])
                        kb.op("act", lambda e=e, pj=pj: nc.scalar.copy(out=LBt[:, e, :, :], in_=pst[pj][:, :, :]),
                              reads=[t_pst[pj]], writes=[t_LB])
                    for hb_ in range(2):
                        for e4 in range(4):
                            e = hb_ * 4 + e4
                            for g in range(8):
                                kb.op("pe", lambda e=e, e4=e4, g=g, hb_=hb_: nc.tensor.matmul(
                                    psk[hb_][:, e4, :], EBt[:, e, g * 128:(g + 1) * 128],
                                    ECt[:, 0, g * 128:(g + 1) * 128], start=(g == 0), stop=(g == 7)),
                                    reads=[t_EB, t_EC], writes=[t_psk[hb_]])
                        kb.op("act", lambda hb_=hb_: nc.scalar.copy(out=Kdt[:, hb_ * 4:(hb_ + 1) * 4, :],
                                                                    in_=psk[hb_][:, :, :]),
                              reads=[t_psk[hb_]], writes=[t_Kd])
                    na = 0
                    for k in range(npass):
                        for g in range(8):
                            aj = na % 2
                            na += 1
                            kb.op("pool", lambda k=k, g=g, aj=aj: nc.gpsimd.tensor_scalar(
                                out=amt[aj][:], in0=identf[:], scalar1=QW[:, k, 0, g0 + g:g0 + g + 1], scalar2=None,
                                op0=ALU.mult), reads=[t_st], writes=[t_amt[aj]])
                            kb.op("dve", lambda k=k, g=g, aj=aj: V.scalar_tensor_tensor(
                                out=AMt[:, k, g, :], in0=shf[:], scalar=qos[:, k, g0 + g:g0 + g + 1], in1=amt[aj][:],
                                op0=ALU.mult, op1=ALU.add), reads=[t_st, t_amt[aj]], writes=[t_AM])
                if getattr(self, "s5_stop", 0) == 2:
                    if d == 0 and cidx == 0:
                        kb.dma(self.dram["dbgLB"], LBt[:], reads=[t_LB])
                        kb.dma(self.dram["dbgKd"], Kdt[:], reads=[t_Kd])
                        kb.dma(self.dram["dbgAM"], AMt[:], reads=[t_AM])
                    continue
                with Pool(kb) as PM:
                    psv = [PM.ps([128, 1024], F32, "psv%d" % q) for q in range(2)]
                    t_psv = [Tok(excl=True), Tok(excl=True)]
                    psy = [PM.ps([128, 1024], F32, "psy%d" % q) for q in range(2)]
                    t_psy = [Tok(excl=True), Tok(excl=True)]
                    npv = 0
                    npy = 0
                    for b in range(self.NB):
                        uj = nu % 2
                        nu += 1
                        ubt = ub[uj]
                        kb.dma(ubt[:, 0:TB], UT[:, cidx, b * TB:(b + 1) * TB], writes=[t_ub[uj]])
                        kb.dma(ubt[:, TB:TB + CTX], UT[:, cidx, b * TB:b * TB + CTX], writes=[t_ub[uj]])
                        ybt = yb[uj]
                        if d == 0:
                            useq = lambda s: ubt[:, s:TB:LL]
                            yseq = lambda r: ybt[:, r:TB:LL]
                        else:
                            useq = lambda s: ubt[:, CTX:CTX + TB][:, (TB - 1 - s)::-LL]
                            yseq = lambda r: ybt[:, (TB - 1 - r)::-LL]
                        for g in range(8):
                            pj = npv % 2
                            npv += 1
                            for (c0, c1) in pieces:
                                pc = pieces.index((c0, c1))
                                for s in range(LL):
                                    kb.op("pe", lambda s=s, g=g, c0=c0, c1=c1, pc=pc, pj=pj: nc.tensor.matmul(
                                        psv[pj][:, pc * 512:pc * 512 + (c1 - c0)], LBt[:, LL - 1 - s, g, :],
                                        useq(s)[:, c0:c1], start=(s == 0), stop=(s == LL - 1)),
                                        reads=[t_LB, t_ub[uj]], writes=[t_psv[pj]])
                            for (c0, c1) in pieces:
                                pc = pieces.index((c0, c1))
                                kb.op("act", lambda g=g, c0=c0, c1=c1, pc=pc, pj=pj: nc.scalar.copy(
                                    out=S32[:, g, c0:c1], in_=psv[pj][:, pc * 512:pc * 512 + (c1 - c0)]),
                                    reads=[t_psv[pj]], writes=[t_S32[g]])
                                kb.op("dve", lambda g=g, c0=c0, c1=c1: V.tensor_copy(
                                    out=Sb[:, g, c0:c1], in_=S32[:, g, c0:c1]),
                                    reads=[t_S32[g]], writes=[t_Sb[g]])
                        _stop = getattr(self, "s5_stop", 0)
                        if _stop == 3:
                            kb.dma(self.dram["dbgS32"], S32[:], reads=t_S32)
                            continue
                        for k in range(npass):
                            dd = 1 << k
                            for g in range(8):
                                pj = npv % 2
                                npv += 1
                                segs = []
                                for (c0, c1) in pieces:
                                    if c1 <= dd:
                                        continue
                                    lo = max(c0, dd)
                                    pc = pieces.index((c0, c1))
                                    segs.append((lo, c1, pc * 512 + (lo - c0)))
                                for (lo, c1, po) in segs:
                                    kb.op("pe", lambda k=k, g=g, lo=lo, c1=c1, po=po, pj=pj, dd=dd: nc.tensor.matmul(
                                        psv[pj][:, po:po + (c1 - lo)], AMt[:, k, g, :], Sb[:, g, lo - dd:c1 - dd],
                                        start=True, stop=True), reads=[t_AM, t_Sb[g]], writes=[t_psv[pj]])
                                for (lo, c1, po) in segs:
                                    kb.op("dve", lambda g=g, lo=lo, c1=c1, po=po, pj=pj: V.tensor_tensor(
                                        out=S32[:, g, lo:c1], in0=S32[:, g, lo:c1], in1=psv[pj][:, po:po + (c1 - lo)],
                                        op=ALU.add), reads=[t_psv[pj]], writes=[t_S32[g]])
                                kb.op("act", lambda g=g, dd=dd: nc.scalar.copy(out=Sb[:, g, dd:J], in_=S32[:, g, dd:J]),
                                      reads=[t_S32[g]], writes=[t_Sb[g]])
                        kb.op("act", lambda: nc.scalar.copy(out=Hb[:, :, 1:J], in_=S32[:, :, 0:J - 1]),
                              reads=t_S32, writes=[t_Hb])
                        if _stop == 4:
                            kb.dma(self.dram["dbgS32"], S32[:], reads=t_S32)
                            continue
                        for r in range(LL):
                            pj = npy % 2
                            npy += 1
                            for (c0, c1) in pieces:
                                pc = pieces.index((c0, c1))
                                o_ap = psy[pj][:, pc * 512:pc * 512 + (c1 - c0)]
                                nmm = (r + 1) + 8
                                im = 0
                                for q in range(r + 1):
                                    kb.op("pe", lambda q=q, r=r, c0=c0, c1=c1, o_ap=o_ap, im=im, nmm=nmm:
                                          nc.tensor.matmul(o_ap, Kdt[:, q, :], useq(r - q)[:, c0:c1],
                                                           start=(im == 0), stop=(im == nmm - 1)),
                                          reads=[t_Kd, t_ub[uj]], writes=[t_psy[pj]])
                                    im += 1
                                for g in range(8):
                                    kb.op("pe", lambda g=g, r=r, c0=c0, c1=c1, o_ap=o_ap, im=im, nmm=nmm:
                                          nc.tensor.matmul(o_ap, ECt[:, r + 1, g * 128:(g + 1) * 128], Hb[:, g, c0:c1],
                                                           start=(im == 0), stop=(im == nmm - 1)),
                                          reads=[t_EC, t_Hb], writes=[t_psy[pj]])
                                    im += 1
                            for (c0, c1) in pieces:
                                pc = pieces.index((c0, c1))
                                kb.op("act", lambda r=r, c0=c0, c1=c1, pc=pc, pj=pj: nc.scalar.copy(
                                    out=yseq(r)[:, c0:c1], in_=psy[pj][:, pc * 512:pc * 512 + (c1 - c0)]),
                                    reads=[t_psy[pj]], writes=[t_yb[uj]])
                        col = b * TB
                        if d == 0:
                            kb.dma(YF[0, cidx * 128:(cidx + 1) * 128, col:col + TB], ybt[:, :], reads=[t_yb[uj]])
                        else:
                            kb.dma(YF[1, cidx * 128:(cidx + 1) * 128, col + CTX:col + TB], ybt[:, 0:SEQ],
                                   reads=[t_yb[uj]])
                            kb.dma(YF[1, cidx * 128:(cidx + 1) * 128, col:col + CTX], ybt[:, SEQ:TB],
                                   reads=[t_yb[uj]])
            kb.barrier()


def phase_s5_out(self, i, jb, with_ctx):
    kb, nc = self.kb, self.nc
    NT = self.NT
    xT = self.dram["xT"].rearrange("(k p) t -> p k t", p=128)
    UT = self.dram["UT"].rearrange("(k p) t -> p k t", p=128)
    YF0 = self.dram["YF"][0].rearrange("(k p) t -> p k t", p=128)
    YF1 = self.dram["YF"][1].rearrange("(k p) t -> p k t", p=128)
    wd = self.dram["s5_w_glu"][jb].rearrange("(k p) d -> p k d", p=128)
    with Pool(kb) as P:
        ws = P.sb([128, KC, 2048], BF16, "ws")
        t_w = [Tok() for _ in range(KC)]
        for k in range(KC):
            for hf in range(2):
                kb.dma(ws[:, k, hf * 1024:(hf + 1) * 1024], wd[:, k, hf * 1024:(hf + 1) * 1024],
                       writes=[t_w[k]], q="pool")
        dsk = P.sb([128, KC], F32, "dsk")
        t_d = Tok()
        kb.dma(dsk[:], self.dram["s5_d"][jb], writes=[t_d])
        xt = [P.sb([128, KC, NT], F32, "xt%d" % q) for q in range(2)]
        t_xt = [Tok(), Tok()]
        y0 = [P.sb([128, KC, NT], F32, "y0%d" % q) for q in range(2)]
        y1 = [P.sb([128, KC, NT], F32, "y1%d" % q) for q in range(2)]
        ut = [P.sb([128, KC, NT], BF16, "ut%d" % q) for q in range(2)]
        t_in = [Tok(), Tok()]
        at = P.sb([128, KC, NT], BF16, "at")
        t_at = Tok()
        Wg = dict(a=P.sb([128, KC, NT], F32, "ga"), b=P.sb([128, KC, NT], F32, "gb"), ta=Tok(), tb=Tok())
        y = P.sb([128, KC, NT], F32, "y")
        t_y = Tok()
        sg = P.sb([128, NT], F32, "sg")
        t_sg = Tok()
        W = self.norm_work(P, NT)
        psn = P.ps([128, 512], F32, "psn")
        t_psn = Tok(excl=True)
        psd = [P.ps([128, 512], F32, "psd%d" % q) for q in range(4)]
        t_psd = [Tok(excl=True) for _ in range(4)]
        tl = self.tiles(with_ctx)

        def loads(jx, c0, n):
            kb.dma(xt[jx][:, :, :n], xT[:, :, c0:c0 + n], writes=[t_xt[jx]])
            kb.dma(y0[jx][:, :, :n], YF0[:, :, c0:c0 + n], writes=[t_in[jx]])
            kb.dma(y1[jx][:, :, :n], YF1[:, :, c0:c0 + n], writes=[t_in[jx]])
            kb.dma(ut[jx][:, :, :n], UT[:, :, c0:c0 + n], writes=[t_in[jx]])
        loads(0, tl[0][0], tl[0][1])
        nd = 0
        for ti, (c0, n, m) in enumerate(tl):
            jx = ti % 2
            if ti + 1 < len(tl):
                loads(1 - jx, tl[ti + 1][0], tl[ti + 1][1])
            kb.op("dve", lambda jx=jx, n=n: nc.vector.tensor_tensor(
                out=y0[jx][:, :, :n], in0=y0[jx][:, :, :n], in1=y1[jx][:, :, :n], op=ALU.add),
                writes=[t_in[jx]])
            for k in range(KC):
                kb.op("dve", lambda jx=jx, n=n, k=k: nc.vector.scalar_tensor_tensor(
                    out=y0[jx][:, k, :n], in0=ut[jx][:, k, :n], scalar=dsk[:, k:k + 1], in1=y0[jx][:, k, :n],
                    op0=ALU.mult, op1=ALU.add), reads=[t_d], writes=[t_in[jx]])
            gelu_tanh(self, at[:, :, :n], y0[jx][:, :, :n], t_in[jx], [128, KC, n], Wg, t_at)
            for dch in range(KC):
                pj = nd % 2
                nd += 1
                for k in range(KC):
                    kb.op("pe", lambda dch=dch, k=k, pj=pj: nc.tensor.matmul(
                        psd[pj][:, :n], ws[:, k, dch * 128:(dch + 1) * 128], at[:, k, :n],
                        start=(k == 0), stop=(k == KC - 1)), reads=[t_w[k], t_at], writes=[t_psd[pj]])
                for k in range(KC):
                    kb.op("pe", lambda dch=dch, k=k, pj=pj: nc.tensor.matmul(
                        psd[2 + pj][:, :n], ws[:, k, 1024 + dch * 128:1024 + (dch + 1) * 128], at[:, k, :n],
                        start=(k == 0), stop=(k == KC - 1)), reads=[t_w[k], t_at], writes=[t_psd[2 + pj]])
                kb.op("act", lambda pj=pj: nc.scalar.activation(out=sg[:, :n], in_=psd[2 + pj][:, :n],
                                                                func=AF.Sigmoid),
                      reads=[t_psd[2 + pj]], writes=[t_sg])
                kb.op("dve", lambda dch=dch, pj=pj: nc.vector.tensor_tensor(
                    out=y[:, dch, :n], in0=psd[pj][:, :n], in1=sg[:, :n], op=ALU.mult),
                    reads=[t_psd[pj], t_sg], writes=[t_y])
            self.norm_post(y, t_y, xt[jx], t_xt[jx], n, i, 0, m, W, psn, t_psn)
            kb.dma(xT[:, :, c0:c0 + n], xt[jx][:, :, :n], reads=[t_xt[jx]])
        kb.barrier()


Model.phase_s5_in = phase_s5_in
Model.phase_s5_scan = phase_s5_scan
Model.phase_s5_out = phase_s5_out

def arr_vec(v):
    v = np.asarray(v)
    lead = v.shape[:-1]
    n = v.shape[-1] // 128
    return np.ascontiguousarray(np.moveaxis(v.reshape(lead + (n, 128)), -1, 0))

def host_common_x(inp, bs):
    f = np.float32
    x, ctx, c, c_ctx = inp["x"], inp["ctx"], inp["c"], inp["c_ctx"]
    cols = []
    for b in bs:
        cols.append(ctx[b].T)
        cols.append(x[b].T)
    xin = np.ascontiguousarray(np.concatenate(cols, axis=1), dtype=f)
    cvecs = [c[b] for b in bs]
    while len(cvecs) < 2:
        cvecs.append(np.zeros_like(c_ctx))
    cvecs.append(c_ctx)
    cc = np.stack(cvecs, axis=-1)
    cc = np.ascontiguousarray(cc.reshape(8, 128, 3).transpose(1, 0, 2), dtype=f)
    return {"xin": xin, "cc": cc}


def host_common(inp, bs, CTX, SEQ):
    f = np.float32
    d = host_common_x(inp, bs)
    d.update({
         "ada_w": np.ascontiguousarray(inp["ada_w"], dtype=f),
         "ada_b": arr_vec(inp["ada_b"]).astype(f),
         "norm_g": arr_vec(inp["norm_g"]).astype(f),
         "mlp_w1": np.ascontiguousarray(inp["mlp_w1"], dtype=f),
         "mlp_w2": np.ascontiguousarray(
             np.asarray(inp["mlp_w2"]).reshape(-1, 32, 128, 8, 128).transpose(0, 3, 2, 1, 4), dtype=f),
         })
    return d

def rope_table(SEQ, GRID_W=64):
    t = np.arange(SEQ)
    row, col = t // GRID_W, t % GRID_W
    inv = (10000.0 ** (-np.arange(16, dtype=np.float32) / 16)).astype(np.float32)
    ang = np.concatenate([row[:, None].astype(np.float32) * inv, col[:, None].astype(np.float32) * inv], axis=-1)
    tab = np.stack([np.cos(ang), np.sin(ang)], axis=1).astype(np.float32)
    return np.ascontiguousarray(tab.reshape(SEQ // 128, 128, 2, 32).transpose(1, 0, 2, 3))

def host_attn(inp, SEQ):
    f = np.float32
    return {"attn_w_qkv": np.ascontiguousarray(inp["attn_w_qkv"], dtype=f),
            "attn_w_o": np.ascontiguousarray(inp["attn_w_o"], dtype=f),
            "attn_lambda": np.ascontiguousarray(inp["attn_lambda"], dtype=f),
            "attn_subln": np.ascontiguousarray(inp["attn_subln"], dtype=f),
            "rope": rope_table(SEQ), "ident": np.eye(128, dtype=f)}

def host_lru(inp):
    f = np.float32
    wg = np.asarray(inp["lru_w_gate"])
    nc_ = wg.shape[0]
    wg = wg.reshape(nc_, 2, 2, 6, 2, 128, 256).transpose(0, 5, 1, 2, 3, 4, 6)
    return {"lru_w_in": np.ascontiguousarray(inp["lru_w_in"], dtype=f),
            "lru_w_out": np.ascontiguousarray(inp["lru_w_out"], dtype=f),
            "lru_w_gate": np.ascontiguousarray(wg, dtype=f),
            "lru_conv_w": np.ascontiguousarray(
                np.asarray(inp["lru_conv_w"]).reshape(nc_, 4, 12, 128).transpose(0, 3, 2, 1), dtype=f),
            "lru_conv_b": np.ascontiguousarray(
                np.asarray(inp["lru_conv_b"]).reshape(nc_, 12, 128).transpose(0, 2, 1), dtype=f),
            "lru_b_gate": np.ascontiguousarray(
                np.asarray(inp["lru_b_gate"]).reshape(nc_, 2, 2, 12, 128).transpose(0, 4, 1, 2, 3), dtype=f),
            "lru_a_param": np.ascontiguousarray(
                np.asarray(inp["lru_a_param"]).reshape(nc_, 2, 12, 128).transpose(0, 3, 1, 2), dtype=f)}

def host_s5(inp):
    f = np.float32
    def dup(a):
        return np.concatenate([a, a], axis=2)
    a_re = np.asarray(inp["s5_a_re"]).transpose(0, 1, 3, 2)
    a_im = np.asarray(inp["s5_a_im"]).transpose(0, 1, 3, 2)
    b_re = np.asarray(inp["s5_b_re"]).transpose(0, 1, 3, 2, 4)
    b_im = np.asarray(inp["s5_b_im"]).transpose(0, 1, 3, 2, 4)
    c_re = np.asarray(inp["s5_c_re"]).transpose(0, 1, 4, 2, 3)
    c_im = np.asarray(inp["s5_c_im"]).transpose(0, 1, 4, 2, 3)
    sh = np.roll(np.eye(128, dtype=f), 64, axis=1)
    return {"s5_a_re": np.ascontiguousarray(dup(a_re), dtype=f), "s5_a_im": np.ascontiguousarray(dup(a_im), dtype=f),
            "s5_b_re": np.ascontiguousarray(dup(b_re), dtype=f), "s5_b_im": np.ascontiguousarray(dup(b_im), dtype=f),
            "s5_c_re": np.ascontiguousarray(dup(c_re), dtype=f), "s5_c_im": np.ascontiguousarray(dup(c_im), dtype=f),
            "s5_log_dt": np.ascontiguousarray(inp["s5_log_dt"], dtype=f),
            "s5_d": np.ascontiguousarray(np.asarray(inp["s5_d"]).reshape(-1, 8, 128).transpose(0, 2, 1), dtype=f),
            "s5_w_glu": np.ascontiguousarray(inp["s5_w_glu"], dtype=f),
            "shift64": sh, "ident": np.eye(128, dtype=f)}


import math


def build_program(NB, CTX, SEQ, shapes, depth=4, layer_list=None):
    M = Model(NB, CTX, SEQ, layers=list(range(depth)) if layer_list is None else layer_list, depth=4)
    nc = M.nc
    for k, shp in shapes.items():
        M.din(k, shp)
    M.dram["xT"] = nc.dram_tensor("xT", [1024, M.T], F32, kind="ExternalOutput").ap()
    with Pool(M.kb) as G:
        M.setup(G)
        M.kb.dma(M.dram["xT"], M.dram["xin"])
        M.kb.barrier()
        M.phase_mod()
        for i in M.layers:
            need_ctx = i < depth - 1
            kind, j = i % 3, i // 3
            if kind == 0:
                lambda_init = 0.8 - 0.6 * math.exp(-0.3 * i)
                M.phase_attn_qkv(i, j)
                M.phase_attn_core(i, j, lambda_init, need_ctx)
                M.phase_proj_post(i, "attn_w_o", j, 8, need_ctx)
            elif kind == 1:
                M.phase_s5_in(i)
                M.phase_s5_scan(i, j)
                M.phase_s5_out(i, j, need_ctx)
            else:
                M.phase_lru_in(i, j)
                M.phase_lru_scan(i, j)
                M.phase_proj_post(i, "lru_w_out", j, 12, need_ctx)
            M.phase_mlp(i, need_ctx)
        M.kb.finish()
    return M


def host_all(inp, bs, CTX, SEQ):
    d = host_common(inp, bs, CTX, SEQ)
    d.update(host_attn(inp, SEQ))
    d.update(host_s5(inp))
    d.update(host_lru(inp))
    return d


def kernel(**inputs):
    inp = {k: np.asarray(v) for k, v in inputs.items()}
    B, SEQ, _ = inp["x"].shape
    CTX = inp["ctx"].shape[1]
    ncores = 8
    NB = B // ncores
    shared = None
    in_maps = []
    for c in range(ncores):
        bs = list(range(c * NB, (c + 1) * NB))
        if shared is None:
            d = host_all(inp, bs, CTX, SEQ)
            shared = {k: v for k, v in d.items() if k not in ("xin", "cc")}
        else:
            d = dict(shared)
            d.update({k: v for k, v in host_common_x(inp, bs).items()})
        in_maps.append(d)
    shapes = {k: v.shape for k, v in in_maps[0].items()}
    M = build_program(NB, CTX, SEQ, shapes)
    res = run_bass_kernel_spmd(M.nc, in_maps, core_ids=list(range(ncores)))
    TB = CTX + SEQ
    out = np.empty((B, SEQ, 1024), np.float32)
    for c in range(ncores):
        o = np.asarray(res.results[c]["xT"])
        for bl in range(NB):
            out[c * NB + bl] = o[:, bl * TB + CTX:(bl + 1) * TB].T
    return out
```

```python
from concourse.bass_utils import run_bass_kernel_spmd
import contextlib
import numpy as np
import concourse.bass as bass
import concourse.mybir as mybir

F32 = mybir.dt.float32
BF16 = mybir.dt.bfloat16
I32 = mybir.dt.int32
AF = mybir.ActivationFunctionType
ALU = mybir.AluOpType
AX = mybir.AxisListType

NSTREAM = 12


class Tok:
    __slots__ = ("w", "r", "name", "excl")

    def __init__(self, name="", excl=False):
        self.w = None
        self.r = {}
        self.name = name
        self.excl = excl


class KB:
    def __init__(self):
        nc = bass.Bass("TRN2", target_bir_lowering=False)
        self.nc = nc
        self.E = {"pe": nc.tensor, "act": nc.scalar, "dve": nc.vector,
                  "pool": nc.gpsimd, "sp": nc.sync}
        self.sem = {}
        self.cnt = {}
        for e in ("pe", "act", "dve", "pool"):
            self.sem[e] = nc.alloc_semaphore("s_" + e)
            self.cnt[e] = 0
        for j in range(NSTREAM):
            e = ("d", j)
            self.sem[e] = nc.alloc_semaphore("s_d%d" % j)
            self.cnt[e] = 0
        self.seen = {e: {} for e in ("pe", "act", "dve", "pool", "sp")}
        self.ndma = 0
        self.nins = 0
        self.nwait = 0

    def _val(self, src, c):
        return c * 16 if isinstance(src, tuple) else c

    def _wait(self, eng, src, c):
        if c <= 0:
            return
        if self.seen[eng].get(src, 0) >= c:
            return
        self.seen[eng][src] = c
        self.E[eng].wait_ge(self.sem[src], self._val(src, c))
        self.nins += 1
        self.nwait += 1

    def _waits_attach(self, eng, need, fn):
        todo = [(src, c) for src, c in need.items() if c > 0 and self.seen[eng].get(src, 0) < c]
        for src, c in todo[:-1]:
            self._wait(eng, src, c)
        ins = fn()
        if todo:
            src, c = todo[-1]
            self.seen[eng][src] = c
            ins._wait_ge(self.sem[src], self._val(src, c))
        return ins

    def _deps(self, eng, reads, writes, same_ok=False):
        need = {}

        def add(src, c):
            if same_ok and src == eng:
                return
            if need.get(src, 0) < c:
                need[src] = c
        for t in reads:
            if t.w is not None:
                add(*t.w)
            if t.excl:
                for src, c in t.r.items():
                    if src != eng:
                        add(src, c)
        for t in writes:
            if t.w is not None:
                add(*t.w)
            for src, c in t.r.items():
                add(src, c)
        return need

    def _commit(self, me, reads, writes):
        c = self.cnt[me]
        for t in reads:
            t.r[me] = c
        for t in writes:
            t.w = (me, c)
            t.r = {}

    def op(self, eng, fn, reads=(), writes=()):
        need = self._deps(eng, reads, writes, same_ok=(eng == "pe"))
        ins = self._waits_attach(eng, need, fn)
        self.cnt[eng] += 1
        ins.then_inc(self.sem[eng], 1)
        self.nins += 1
        self._commit(eng, reads, writes)
        return ins

    def dma(self, out, in_, reads=(), writes=(), q="sp", **kw):
        j = self.ndma % NSTREAM
        self.ndma += 1
        me = ("d", j)
        need = self._deps(q, reads, writes)
        if need.get(me, 0) < self.cnt[me]:
            need[me] = self.cnt[me]
        ins = self._waits_attach(q, need, lambda: self.E[q].dma_start(out=out, in_=in_, **kw))
        self.cnt[me] += 1
        ins.then_inc(self.sem[me], 16)
        self.nins += 1
        self._commit(me, reads, writes)
        return ins

    def barrier(self):
        for eng in ("pe", "act", "dve", "pool", "sp"):
            for src, c in self.cnt.items():
                if src == eng:
                    continue
                self._wait(eng, src, c)

    def finish(self):
        for src, c in self.cnt.items():
            self._wait("sp", src, c)


class Pool:
    _uid = [0]

    def __init__(self, kb):
        self.kb = kb
        self.st = contextlib.ExitStack()
        self.n = 0
        Pool._uid[0] += 1
        self.uid = Pool._uid[0]

    def __enter__(self):
        self.st.__enter__()
        return self

    def __exit__(self, *a):
        return self.st.__exit__(*a)

    def sb(self, shape, dtype, name=None):
        self.n += 1
        nm = "%s_%d_%d" % (name or "t", self.uid, self.n)
        t = self.st.enter_context(self.kb.nc.sbuf_tensor(nm, list(shape), dtype))
        return t

    def ps(self, shape, dtype, name=None):
        self.n += 1
        nm = "%s_%d_%d" % (name or "p", self.uid, self.n)
        t = self.st.enter_context(self.kb.nc.psum_tensor(nm, list(shape), dtype))
        return t

D = 1024
KC = 8
DFF = 4096
EPS = 1e-6


class Model:
    def __init__(self, NB, CTX, SEQ, layers=(0, 1, 2, 3), depth=4, NT=256):
        self.NB, self.CTX, self.SEQ = NB, CTX, SEQ
        self.TB = CTX + SEQ
        self.T = NB * self.TB
        self.layers = list(layers)
        self.depth = depth
        self.NT = NT
        self.kb = KB()
        self.nc = self.kb.nc
        self.dram = {}

    def din(self, name, shape, dtype=F32):
        t = self.nc.dram_tensor(name, list(shape), dtype, kind="ExternalInput").ap()
        self.dram[name] = t
        return t

    def dscratch(self, name, shape, dtype):
        t = self.nc.dram_tensor(name, list(shape), dtype, kind="Internal").ap()
        self.dram[name] = t
        return t

    def tiles(self, with_ctx=True, nt=None):
        nt = nt or self.NT
        out = []
        for b in range(self.NB):
            base = b * self.TB
            if with_ctx:
                for c0 in range(0, self.CTX, nt):
                    out.append((base + c0, min(nt, self.CTX - c0), 2))
            for c0 in range(0, self.SEQ, nt):
                out.append((base + self.CTX + c0, min(nt, self.SEQ - c0), b))
        return out

    def setup(self, G):
        kb, nc = self.kb, self.nc
        self.G = G
        self.onesm = G.sb([128, 128], BF16, "onesm")
        self.t_const = Tok("const")
        kb.op("dve", lambda: nc.vector.memset(self.onesm[:], 1.0 / D), writes=[self.t_const])
        self.epsv = G.sb([128, 1], F32, "epsv")
        kb.op("dve", lambda: nc.vector.memset(self.epsv[:], EPS), writes=[self.t_const])
        self.mv = G.sb([128, self.depth, 4, KC, 3], F32, "mv")
        self.modT = G.sb([128, self.depth, 48, 3], F32, "modT")
        self.t_mv = Tok("mv")

    def phase_mod(self):
        kb, nc = self.kb, self.nc
        cc, ada_w, ada_b, ng = (self.dram[k] for k in ("cc", "ada_w", "ada_b", "norm_g"))
        with Pool(kb) as P:
            sc = P.sb([128, KC, 3], F32, "sc")
            adab = P.sb([128, self.depth, 48], F32, "adab")
            ngt = P.sb([128, self.depth, 4, KC], F32, "ngt")
            tmp = P.sb([128, KC, 3], F32, "tmp")
            wt = [P.sb([128, KC, 512], F32, "adaw%d" % j) for j in range(2)]
            ps = [P.ps([128, 512], F32, "psmod%d" % j) for j in range(2)]
            t_sc, t_ab, t_ng, t_tmp = Tok(), Tok(), Tok(), Tok()
            t_wt = [Tok(), Tok()]
            t_ps = [Tok(excl=True), Tok(excl=True)]
            kb.dma(sc[:], cc, writes=[t_sc])
            kb.dma(adab[:], ada_b, writes=[t_ab])
            kb.dma(ngt[:], ng, writes=[t_ng])
            kb.op("act", lambda: nc.scalar.activation(out=sc[:], in_=sc[:], func=AF.Silu),
                  reads=[t_sc], writes=[t_sc])
            n = 0
            for i in self.layers:
                wv = ada_w[i].rearrange("(k p) f -> p k f", p=128)
                for cg in range(12):
                    j = n % 2
                    n += 1
                    kb.dma(wt[j][:], wv[:, :, cg * 512:(cg + 1) * 512], writes=[t_wt[j]])
                    for jj in range(4):
                        for k in range(KC):
                            kb.op("pe", lambda k=k, jj=jj, j=j: nc.tensor.matmul(
                                ps[j][:, jj * 3:jj * 3 + 3], wt[j][:, k, jj * 128:(jj + 1) * 128],
                                sc[:, k, :], start=(k == 0), stop=(k == KC - 1)),
                                reads=[t_wt[j], t_sc], writes=[t_ps[j]])
                    kb.op("dve", lambda j=j, cg=cg, i=i: nc.vector.tensor_tensor(
                        out=self.modT[:, i, cg * 4:(cg + 1) * 4, :],
                        in0=ps[j][:, 0:12].rearrange("p (a b) -> p a b", b=3),
                        in1=adab[:, i, cg * 4:(cg + 1) * 4].unsqueeze(2).broadcast_to([128, 4, 3]),
                        op=ALU.add), reads=[t_ps[j], t_ab], writes=[self.t_mv])
                for kind, (c0, gi, plus1) in enumerate([(8, 0, True), (16, 1, False),
                                                        (32, 2, True), (40, 3, False)]):
                    src = self.modT[:, i, c0:c0 + 8, :]
                    if plus1:
                        kb.op("dve", lambda src=src: nc.vector.tensor_scalar(
                            out=tmp[:], in0=src, scalar1=1.0, scalar2=None, op0=ALU.add),
                            reads=[self.t_mv], writes=[t_tmp])
                        src = tmp[:]
                    kb.op("dve", lambda src=src, i=i, kind=kind, gi=gi: nc.vector.tensor_tensor(
                        out=self.mv[:, i, kind, :, :], in0=src,
                        in1=ngt[:, i, gi, :].unsqueeze(2).broadcast_to([128, KC, 3]),
                        op=ALU.mult), reads=[self.t_mv, t_tmp, t_ng], writes=[self.t_mv])
            kb.barrier()

    def mvec(self, i, kind, k, m):
        return self.mv[:, i, kind, k, m:m + 1]

    def shvec(self, i, which, k, m):
        c0 = 0 if which == 0 else 24
        return self.modT[:, i, c0 + k, m:m + 1]

    def stats_a(self, src, n, W, t_src):
        kb, nc = self.kb, self.nc
        sq = W["sq"]
        kb.op("act", lambda: nc.scalar.activation(out=sq[:, :, :n], in_=src, func=AF.Square),
              reads=[t_src], writes=[W["t_sq"]])

    def stats_b(self, n, W, ps, t_ps):
        kb, nc = self.kb, self.nc
        sq = W["sq"]
        for k in range(KC):
            kb.op("pe", lambda k=k: nc.tensor.matmul(ps[:, :n], self.onesm[:], sq[:, k, :n],
                                                     start=(k == 0), stop=(k == KC - 1)),
                  reads=[W["t_sq"], self.t_const], writes=[t_ps])

    def stats_c(self, n, W, ps, t_ps):
        kb, nc = self.kb, self.nc
        sd, rstd = W["sd"], W["rstd"]
        kb.op("act", lambda: nc.scalar.activation(out=sd[:, :n], in_=ps[:, :n], func=AF.Sqrt,
                                                  bias=self.epsv[:], scale=1.0),
              reads=[t_ps, self.t_const], writes=[W["t_sd"]])
        kb.op("dve", lambda: nc.vector.reciprocal(out=rstd[:, :n], in_=sd[:, :n]),
              reads=[W["t_sd"]], writes=[W["t_rstd"]])
        return rstd

    def rstd_of(self, src, n, W, t_src, ps, t_ps):
        self.stats_a(src, n, W, t_src)
        self.stats_b(n, W, ps, t_ps)
        return self.stats_c(n, W, ps, t_ps)

    def norm_work(self, P, NT):
        return dict(sq=P.sb([128, KC, NT], BF16, "sq"), sd=P.sb([128, NT], F32, "sd"),
                    rstd=P.sb([128, NT], F32, "rstd"), xh=P.sb([128, KC, NT], F32, "xh"),
                    t_sq=Tok(), t_sd=Tok(), t_rstd=Tok(), t_xh=Tok())

    def norm_pre(self, xt, t_x, n, i, which, m, h, t_h, W, ps, t_ps, stats_done=False):
        kb, nc = self.kb, self.nc
        rstd = W["rstd"] if stats_done else self.rstd_of(xt[:, :, :n], n, W, t_x, ps, t_ps)
        xh = W["xh"]
        kb.op("dve", lambda: nc.vector.tensor_tensor(
            out=xh[:, :, :n], in0=xt[:, :, :n],
            in1=rstd[:, :n].unsqueeze(1).broadcast_to([128, KC, n]), op=ALU.mult),
            reads=[t_x, W["t_rstd"]], writes=[W["t_xh"]])
        kindA = 0 if which == 0 else 2
        for k in range(KC):
            if k % 2 == 0:
                kb.op("act", lambda k=k: nc.scalar.activation(
                    out=h[:, k, :n], in_=xh[:, k, :n], func=AF.Identity, scale=self.mvec(i, kindA, k, m),
                    bias=self.shvec(i, which, k, m)), reads=[W["t_xh"], self.t_mv], writes=[t_h])
            else:
                kb.op("dve", lambda k=k: nc.vector.tensor_scalar(
                    out=h[:, k, :n], in0=xh[:, k, :n], scalar1=self.mvec(i, kindA, k, m),
                    scalar2=self.shvec(i, which, k, m), op0=ALU.mult, op1=ALU.add),
                    reads=[W["t_xh"], self.t_mv], writes=[t_h])

    def norm_post(self, y, t_y, xt, t_x, n, i, which, m, W, ps, t_ps, stats_done=False):
        kb, nc = self.kb, self.nc
        rstd = W["rstd"] if stats_done else self.rstd_of(y[:, :, :n], n, W, t_y, ps, t_ps)
        kb.op("dve", lambda: nc.vector.tensor_tensor(
            out=y[:, :, :n], in0=y[:, :, :n],
            in1=rstd[:, :n].unsqueeze(1).broadcast_to([128, KC, n]), op=ALU.mult),
            reads=[W["t_rstd"]], writes=[t_y])
        kindG = 1 if which == 0 else 3
        for k in range(KC):
            kb.op("dve", lambda k=k: nc.vector.scalar_tensor_tensor(
                out=xt[:, k, :n], in0=y[:, k, :n], scalar=self.mvec(i, kindG, k, m),
                in1=xt[:, k, :n], op0=ALU.mult, op1=ALU.add),
                reads=[t_y, self.t_mv], writes=[t_x])

    def phase_mlp(self, i, with_ctx):
        kb, nc = self.kb, self.nc
        NT = self.NT
        xT = self.dram["xT"].rearrange("(k p) t -> p k t", p=128)
        w1 = self.dram["mlp_w1"][i].rearrange("(k p) f -> p k f", p=128)
        w2 = self.dram["mlp_w2"][i]
        with Pool(kb) as P:
            w1s = P.sb([128, KC, DFF], BF16, "w1s")
            w2s = P.sb([128, KC, 32, 128], BF16, "w2s")
            t_w1 = [Tok() for _ in range(8)]
            t_w2 = [Tok() for _ in range(8)]
            for g in range(8):
                kb.dma(w1s[:, :, g * 512:(g + 1) * 512], w1[:, :, g * 512:(g + 1) * 512],
                       writes=[t_w1[g]], q="pool")
            for d in range(8):
                kb.dma(w2s[:, d, :, :], w2[d], writes=[t_w2[d]], q="pool")
            xt = [P.sb([128, KC, NT], F32, "xt%d" % j) for j in range(2)]
            t_xt = [Tok(), Tok()]
            h = [P.sb([128, KC, NT], BF16, "h%d" % j) for j in range(2)]
            t_h = [Tok(), Tok()]
            hid = P.sb([128, 32, NT], BF16, "hid")
            t_hid = [Tok() for _ in range(32)]
            y1 = P.sb([128, KC, NT], F32, "y")
            y = [y1, y1]
            t_y1 = Tok()
            t_y = [t_y1, t_y1]
            rl = [P.sb([128, NT], F32, "rl%d" % j) for j in range(2)]
            t_rl = [Tok(), Tok()]
            W = self.norm_work(P, NT)
            W2 = dict(sq=P.sb([128, KC, NT], BF16, "sq2"), sd=P.sb([128, NT], F32, "sd2"),
                      rstd=P.sb([128, NT], F32, "rstd2"), t_sq=Tok(), t_sd=Tok(), t_rstd=Tok())
            psn = P.ps([128, 512], F32, "psn")
            t_psn = Tok(excl=True)
            psn2 = P.ps([128, 512], F32, "psn2")
            t_psn2 = Tok(excl=True)
            psu = [P.ps([128, 512], F32, "psu%d" % j) for j in range(3)]
            t_psu = [Tok(excl=True) for _ in range(3)]
            psd = [P.ps([128, 512], F32, "psd%d" % j) for j in range(2)]
            t_psd = [Tok(excl=True) for _ in range(2)]
            tl = self.tiles(with_ctx)

            def load(ti):
                c0, n, m = tl[ti]
                kb.dma(xt[ti % 2][:, :, :n], xT[:, :, c0:c0 + n], writes=[t_xt[ti % 2]])

            def pre_a(ti):
                c0, n, m = tl[ti]
                self.stats_a(xt[ti % 2][:, :, :n], n, W, t_xt[ti % 2])

            def pre_bc(ti):
                c0, n, m = tl[ti]
                self.stats_b(n, W, psn, t_psn)
                self.stats_c(n, W, psn, t_psn)
                self.norm_pre(xt[ti % 2], t_xt[ti % 2], n, i, 1, m, h[ti % 2], t_h[ti % 2], W, psn, t_psn,
                              stats_done=True)

            def post_bc(ti):
                c0, n, m = tl[ti]
                self.stats_b(n, W2, psn2, t_psn2)
                self.stats_c(n, W2, psn2, t_psn2)
                self.norm_post(y[ti % 2], t_y[ti % 2], xt[ti % 2], t_xt[ti % 2], n, i, 1, m, W2, psn2, t_psn2,
                               stats_done=True)
                kb.dma(xT[:, :, c0:c0 + n], xt[ti % 2][:, :, :n], reads=[t_xt[ti % 2]])
            if tl:
                load(0)
                pre_a(0)
                pre_bc(0)
                if len(tl) > 1:
                    load(1)
            nu = 0
            nd = 0
            for ti, (c0, n, m) in enumerate(tl):
                j = ti % 2
                for f in range(32):
                    pj = nu % 3
                    nu += 1
                    for k in range(KC):
                        kb.op("pe", lambda k=k, f=f, pj=pj: nc.tensor.matmul(
                            psu[pj][:, :n], w1s[:, k, f * 128:(f + 1) * 128], h[j][:, k, :n],
                            start=(k == 0), stop=(k == KC - 1)),
                            reads=[t_w1[f // 4], t_h[j]], writes=[t_psu[pj]])
                    rj = f % 2
                    kb.op("act", lambda pj=pj, rj=rj: nc.scalar.activation(
                        out=rl[rj][:, :n], in_=psu[pj][:, :n], func=AF.Relu),
                        reads=[t_psu[pj]], writes=[t_rl[rj]])
                    eng = "dve" if f % 2 == 0 else "pool"
                    E = nc.vector if eng == "dve" else nc.gpsimd
                    kb.op(eng, lambda rj=rj, f=f, E=E: E.tensor_tensor(
                        out=hid[:, f, :n], in0=rl[rj][:, :n], in1=rl[rj][:, :n], op=ALU.mult),
                        reads=[t_rl[rj]], writes=[t_hid[f]])
                    if f == 3 and ti >= 1:
                        post_bc(ti - 1)
                        if ti + 1 < len(tl):
                            load(ti + 1)
                    if f == 14 and ti + 1 < len(tl):
                        pre_a(ti + 1)
                    if f == 22 and ti + 1 < len(tl):
                        pre_bc(ti + 1)
                for d in range(KC):
                    pj = nd % 2
                    nd += 1
                    for f in range(32):
                        kb.op("pe", lambda d=d, f=f, pj=pj: nc.tensor.matmul(
                            psd[pj][:, :n], w2s[:, d, f, :], hid[:, f, :n],
                            start=(f == 0), stop=(f == 31)),
                            reads=[t_w2[d], t_hid[f]], writes=[t_psd[pj]])
                    kb.op("act", lambda d=d, pj=pj, j=j: nc.scalar.copy(out=y[j][:, d, :n], in_=psd[pj][:, :n]),
                          reads=[t_psd[pj]], writes=[t_y[j]])
                self.stats_a(y[j][:, :, :n], n, W2, t_y[j])
            if tl:
                post_bc(len(tl) - 1)
            kb.barrier()


HEADS = 8


def _attn_scratch(self):
    if "QT" in self.dram:
        return
    self.dscratch("QT", [self.NB, HEADS, 2, 128, self.TB], BF16)
    self.dscratch("KT", [self.NB, HEADS, 128, self.TB], BF16)
    self.dscratch("V", [self.NB, self.TB, 1024], BF16)
    self.dscratch("OT", [1536, self.T], BF16)


def phase_attn_qkv(self, i, j):
    kb, nc = self.kb, self.nc
    _attn_scratch(self)
    NT = self.NT
    xT = self.dram["xT"].rearrange("(k p) t -> p k t", p=128)
    wq = self.dram["attn_w_qkv"][j].rearrange("(k p) f -> p k f", p=128)
    rope = self.dram["rope"]
    QT, KT, V = self.dram["QT"], self.dram["KT"], self.dram["V"]
    with Pool(kb) as P:
        wqs = P.sb([128, KC, 3072], BF16, "wqs")
        t_wq = [Tok() for _ in range(6)]
        for g in range(6):
            kb.dma(wqs[:, :, g * 512:(g + 1) * 512], wq[:, :, g * 512:(g + 1) * 512],
                   writes=[t_wq[g]], q="pool")
        nrt = self.SEQ // 128
        rp = P.sb([128, nrt, 2, 32], F32, "rp")
        rpq = P.sb([128, nrt, 2, 32], F32, "rpq")
        t_rp = Tok()
        kb.dma(rp[:], rope, writes=[t_rp])
        kb.op("act", lambda: nc.scalar.mul(out=rpq[:], in_=rp[:], mul=0.125), reads=[t_rp], writes=[t_rp])
        ident = P.sb([128, 128], BF16, "ident")
        t_id = Tok()
        kb.dma(ident[:], self.dram["ident"], writes=[t_id], q="pool")
        xt = [P.sb([128, KC, NT], F32, "xt%d" % q) for q in range(2)]
        t_xt = [Tok(), Tok()]
        h = P.sb([128, KC, NT], BF16, "h")
        t_h = Tok()
        W = self.norm_work(P, NT)
        psn = P.ps([128, 512], F32, "psn")
        t_psn = Tok(excl=True)
        psqk = [P.ps([128, 1024], F32, "psqk%d" % q) for q in range(2)]
        t_psqk = [Tok(excl=True), Tok(excl=True)]
        psv = [P.ps([128, 512], F32, "psv%d" % q) for q in range(2)]
        t_psv = [Tok(excl=True), Tok(excl=True)]
        pst = P.ps([128, 8, 128], BF16, "pst")
        t_pst = Tok(excl=True)
        qk = [P.sb([128, 1024], BF16, "qk%d" % q) for q in range(2)]
        t_qk = [Tok(), Tok()]
        ta = P.sb([128, 16, 32], F32, "ta")
        tb = P.sb([128, 16, 32], F32, "tb")
        t_ta, t_tb = Tok(), Tok()
        vst = [P.sb([128, 1024], BF16, "vst%d" % q) for q in range(2)]
        t_vst = [Tok(), Tok()]
        qz = [P.sb([128, HEADS, NT], BF16, "qz%d" % c) for c in range(2)]
        t_qz = [Tok(), Tok()]
        kz = P.sb([128, HEADS, NT], BF16, "kz")
        t_kz = Tok()
        kb.op("pool", lambda: nc.gpsimd.memset(qz[0][:], 0.0), writes=[t_qz[0]])
        kb.op("pool", lambda: nc.gpsimd.memset(qz[1][:], 0.0), writes=[t_qz[1]])
        tl = self.tiles(True)
        c0, n, m = tl[0]
        kb.dma(xt[0][:, :, :n], xT[:, :, c0:c0 + n], writes=[t_xt[0]])
        nv = 0
        for ti, (c0, n, m) in enumerate(tl):
            jx = ti % 2
            if ti + 1 < len(tl):
                c1, n1, _ = tl[ti + 1]
                kb.dma(xt[1 - jx][:, :, :n1], xT[:, :, c1:c1 + n1], writes=[t_xt[1 - jx]])
            self.norm_pre(xt[jx], t_xt[jx], n, i, 0, m, h, t_h, W, psn, t_psn)
            b = c0 // self.TB
            pos0 = c0 - b * self.TB
            is_ctx = (m == 2)
            for s in range(n // 128):
                hs = lambda k: h[:, k, s * 128:(s + 1) * 128]
                for half in range(2):
                    pj = nv % 2
                    nv += 1
                    for k in range(KC):
                        kb.op("pe", lambda k=k, half=half, pj=pj: nc.tensor.matmul(
                            psv[pj][:, :], hs(k), wqs[:, k, 2048 + half * 512:2048 + (half + 1) * 512],
                            start=(k == 0), stop=(k == KC - 1)),
                            reads=[t_h, t_wq[4 + half]], writes=[t_psv[pj]])
                    kb.op("act", lambda half=half, pj=pj, s=s: nc.scalar.copy(
                        out=vst[s % 2][:, half * 512:(half + 1) * 512], in_=psv[pj][:, :]),
                        reads=[t_psv[pj]], writes=[t_vst[s % 2]])
                r0 = b * self.TB + pos0 + s * 128
                kb.dma(V[b, pos0 + s * 128:pos0 + (s + 1) * 128, :], vst[s % 2][:], reads=[t_vst[s % 2]])
                for which in range(2):
                    ps = psqk[which]
                    tps = t_psqk[which]
                    for half in range(2):
                        for k in range(KC):
                            kb.op("pe", lambda k=k, half=half, which=which, ps=ps: nc.tensor.matmul(
                                ps[:, half * 512:(half + 1) * 512], hs(k),
                                wqs[:, k, which * 1024 + half * 512:which * 1024 + (half + 1) * 512],
                                start=(k == 0), stop=(k == KC - 1)),
                                reads=[t_h, t_wq[which * 2 + half]], writes=[tps])
                    dst = qk[which]
                    if is_ctx:
                        kb.op("act", lambda ps=ps, dst=dst, which=which: nc.scalar.mul(
                            out=dst[:], in_=ps[:], mul=(0.125 if which == 0 else 1.0)),
                            reads=[tps], writes=[t_qk[which]])
                    else:
                        lt = (pos0 - self.CTX) // 128 + s
                        tab = rpq if which == 0 else rp
                        cs = tab[:, lt, 0, :].unsqueeze(1).broadcast_to([128, 16, 32])
                        sn = tab[:, lt, 1, :].unsqueeze(1).broadcast_to([128, 16, 32])
                        pv = ps[:, :].rearrange("p (a two f) -> p a two f", two=2, f=32)
                        dv = dst[:, :].rearrange("p (a two f) -> p a two f", two=2, f=32)
                        t1, t2 = pv[:, :, 0, :], pv[:, :, 1, :]
                        kb.op("dve", lambda: nc.vector.tensor_tensor(out=ta[:], in0=t1, in1=cs, op=ALU.mult),
                              reads=[tps, t_rp], writes=[t_ta])
                        kb.op("dve", lambda: nc.vector.tensor_tensor(out=tb[:], in0=t2, in1=sn, op=ALU.mult),
                              reads=[tps, t_rp], writes=[t_tb])
                        kb.op("dve", lambda: nc.vector.tensor_tensor(out=dv[:, :, 0, :], in0=ta[:], in1=tb[:],
                                                                     op=ALU.subtract),
                              reads=[t_ta, t_tb], writes=[t_qk[which]])
                        kb.op("dve", lambda: nc.vector.tensor_tensor(out=ta[:], in0=t1, in1=sn, op=ALU.mult),
                              reads=[tps, t_rp], writes=[t_ta])
                        kb.op("dve", lambda: nc.vector.tensor_tensor(out=tb[:], in0=t2, in1=cs, op=ALU.mult),
                              reads=[tps, t_rp], writes=[t_tb])
                        kb.op("dve", lambda: nc.vector.tensor_tensor(out=dv[:, :, 1, :], in0=ta[:], in1=tb[:],
                                                                     op=ALU.add),
                              reads=[t_ta, t_tb], writes=[t_qk[which]])
                    for hd in range(HEADS):
                        kb.op("pe", lambda hd=hd, dst=dst: nc.tensor.transpose(
                            out=pst[:, hd, :], in_=dst[:, hd * 128:(hd + 1) * 128], identity=ident[:]),
                            reads=[t_qk[which], t_id], writes=[t_pst])
                    sl = slice(s * 128, (s + 1) * 128)
                    if which == 0:
                        kb.op("act", lambda sl=sl: nc.scalar.copy(out=qz[0][0:64, :, sl], in_=pst[0:64, :, :]),
                              reads=[t_pst], writes=[t_qz[0]])
                        kb.op("act", lambda sl=sl: nc.scalar.copy(out=qz[1][64:128, :, sl], in_=pst[64:128, :, :]),
                              reads=[t_pst], writes=[t_qz[1]])
                    else:
                        kb.op("act", lambda sl=sl: nc.scalar.copy(out=kz[:, :, sl], in_=pst[:, :, :]),
                              reads=[t_pst], writes=[t_kz])
            for c in range(2):
                kb.dma(QT[b, :, c, :, pos0:pos0 + n].rearrange("h p t -> p h t"), qz[c][:, :, :n],
                       reads=[t_qz[c]])
            kb.dma(KT[b, :, :, pos0:pos0 + n].rearrange("h p t -> p h t"), kz[:, :, :n], reads=[t_kz])
        kb.barrier()


def phase_attn_core(self, i, j, lambda_init, need_ctx):
    kb, nc = self.kb, self.nc
    QT, KT, V, OT = self.dram["QT"], self.dram["KT"], self.dram["V"], self.dram["OT"]
    TB, CTX = self.TB, self.CTX
    nkt = TB // 128
    QG = 256
    with Pool(kb) as P:
        kts = P.sb([128, HEADS, TB], BF16, "kts")
        vs = P.sb([128, nkt, HEADS, 130], BF16, "vs")
        t_kts, t_vs = Tok(), Tok()
        ident = P.sb([128, 128], BF16, "ident")
        t_id = Tok()
        kb.dma(ident[:], self.dram["ident"], writes=[t_id], q="pool")
        lam = P.sb([128, 4, 64], F32, "lam")
        lt = P.sb([128, 2, 64], F32, "lt")
        ls = P.sb([128, 2], F32, "ls")
        nlam = P.sb([128, 1], F32, "nlam")
        gsb = P.sb([128, 128], F32, "gsb")
        t_l = Tok()
        kb.dma(lam[:], self.dram["attn_lambda"][j].partition_broadcast(128), writes=[t_l])
        kb.dma(gsb[:], self.dram["attn_subln"][j].partition_broadcast(128), writes=[t_l])
        kb.op("dve", lambda: nc.vector.tensor_tensor(out=lt[:], in0=lam[:, 0::2, :], in1=lam[:, 1::2, :],
                                                     op=ALU.mult), reads=[t_l], writes=[t_l])
        kb.op("dve", lambda: nc.vector.tensor_reduce(out=ls[:], in_=lt[:], op=ALU.add, axis=AX.X),
              reads=[t_l], writes=[t_l])
        kb.op("act", lambda: nc.scalar.activation(out=ls[:], in_=ls[:], func=AF.Exp), reads=[t_l], writes=[t_l])
        kb.op("dve", lambda: nc.vector.tensor_tensor(out=nlam[:], in0=ls[:, 1:2], in1=ls[:, 0:1], op=ALU.subtract),
              reads=[t_l], writes=[t_l])
        kb.op("dve", lambda: nc.vector.tensor_scalar(out=nlam[:], in0=nlam[:], scalar1=-lambda_init, scalar2=None,
                                                     op0=ALU.add), reads=[t_l], writes=[t_l])
        kb.op("act", lambda: nc.scalar.mul(out=gsb[:], in_=gsb[:], mul=1.0 - lambda_init), reads=[t_l], writes=[t_l])
        epsv = self.epsv
        qs_ = [P.sb([128, HEADS, 2, QG], BF16, "qs%d" % q) for q in range(2)]
        t_qs = [Tok(), Tok()]
        NPS = 3
        pss = [P.ps([128, 2, QG], F32, "pss%d" % q) for q in range(NPS)]
        t_pss = [Tok(excl=True) for _ in range(NPS)]
        accs = P.sb([128, 2, 2, 129], F32, "accs")
        t_accs = Tok()
        psa = [[P.ps([128, 512], F32, "psa%d%d" % (c, q)) for q in range(2)] for c in range(2)]
        t_psa = [[Tok(excl=True), Tok(excl=True)], [Tok(excl=True), Tok(excl=True)]]
        pst = P.ps([128, 128], BF16, "pst")
        t_pst = Tok(excl=True)
        es = [P.sb([128, 2, QG], BF16, "es%d" % q) for q in range(3)]
        t_es = [Tok() for _ in range(3)]
        mhalf = P.sb([128, 1], F32, "mhalf")
        kb.op("dve", lambda: nc.vector.memset(mhalf[:], -0.5), writes=[t_l])
        rz = P.sb([128, 2], F32, "rz")
        t_rz = Tok()
        o1 = P.sb([128, 128], F32, "o1")
        o2 = P.sb([128, 128], F32, "o2")
        junk = P.sb([128, 128], F32, "junk")
        ss = P.sb([128, 1], F32, "ss")
        on = P.sb([128, 128], BF16, "on")
        t_o1, t_o2, t_ss, t_on = Tok(), Tok(), Tok(), Tok()
        ots = [P.sb([128, HEADS, QG], BF16, "ots%d" % q) for q in range(2)]
        t_ots = [Tok(), Tok()]
        ne = 0
        ng = 0
        for b in range(self.NB):
            kb.dma(kts[:], KT[b].rearrange("h p t -> p h t"), writes=[t_kts])
            for hh in range(HEADS):
                kb.dma(vs[:, :, hh, 0:128],
                       V[b, :, hh * 128:(hh + 1) * 128].rearrange("(kt p) e -> p kt e", p=128), writes=[t_vs])
            kb.op("pool", lambda: nc.gpsimd.memset(vs[:, :, :, 128:129], 1.0), writes=[t_vs])
            groups = []
            if need_ctx:
                for q0 in range(0, CTX, QG):
                    groups.append((q0, min(QG, CTX - q0), list(range(CTX // 128))))
            for q0 in range(0, self.SEQ, QG):
                groups.append((CTX + q0, QG, list(range(nkt))))
            for (q0, nq, ktl) in groups:
                gj = ng % 2
                ng += 1
                kb.dma(qs_[gj][:, :, :, :nq], QT[b, :, :, :, q0:q0 + nq].rearrange("h c p t -> p h c t"),
                       writes=[t_qs[gj]])
                nsub = nq // 128
                its = [(hh, ki, kt) for hh in range(HEADS) for ki, kt in enumerate(ktl)]
                nk = len(ktl)

                def emit_scores(idx, gj=gj, nq=nq, its=its):
                    hh, ki, kt = its[idx]
                    pj = (ne0 + idx) % NPS
                    for c in range(2):
                        kb.op("pe", lambda c=c: nc.tensor.matmul(
                            pss[pj][:, c, :nq], kts[:, hh, kt * 128:(kt + 1) * 128], qs_[gj][:, hh, c, :nq],
                            start=True, stop=True), reads=[t_kts, t_qs[gj]], writes=[t_pss[pj]])

                def emit_exp(idx, nq=nq):
                    pj = (ne0 + idx) % NPS
                    ej = (ne0 + idx) % 3
                    kb.op("act", lambda: nc.scalar.activation(
                        out=es[ej][:, :, :nq], in_=pss[pj][:, :, :nq], func=AF.Exp),
                        reads=[t_pss[pj]], writes=[t_es[ej]])

                def emit_pv(idx, nsub=nsub, its=its, nk=nk):
                    hh, ki, kt = its[idx]
                    ej = (ne0 + idx) % 3
                    for c in range(2):
                        for sq in range(nsub):
                            kb.op("pe", lambda c=c, sq=sq: nc.tensor.matmul(
                                psa[c][sq][:, 0:129], es[ej][:, c, sq * 128:(sq + 1) * 128],
                                vs[:, kt, hh, 0:129], start=(ki == 0), stop=(ki == nk - 1)),
                                reads=[t_es[ej], t_vs], writes=[t_psa[c][sq]])

                def finalize(hh, gj=gj, nsub=nsub):
                    for c in range(2):
                        for sq in range(nsub):
                            kb.op("dve", lambda c=c, sq=sq: nc.vector.tensor_copy(
                                out=accs[:, c, sq, :], in_=psa[c][sq][:, 0:129]),
                                reads=[t_psa[c][sq]], writes=[t_accs])
                    for sq in range(nsub):
                        kb.op("dve", lambda sq=sq: nc.vector.reciprocal(out=rz[:, 0:2], in_=accs[:, :, sq, 128]),
                              reads=[t_accs], writes=[t_rz])
                        kb.op("dve", lambda: nc.vector.tensor_tensor(out=rz[:, 1:2], in0=rz[:, 1:2], in1=nlam[:],
                                                                     op=ALU.mult), reads=[t_l], writes=[t_rz])
                        kb.op("dve", lambda sq=sq: nc.vector.tensor_scalar(
                            out=o1[:], in0=accs[:, 0, sq, 0:128], scalar1=rz[:, 0:1], scalar2=None, op0=ALU.mult),
                            reads=[t_accs, t_rz], writes=[t_o1])
                        kb.op("dve", lambda sq=sq: nc.vector.scalar_tensor_tensor(
                            out=o2[:], in0=accs[:, 1, sq, 0:128], scalar=rz[:, 1:2], in1=o1[:],
                            op0=ALU.mult, op1=ALU.add), reads=[t_accs, t_rz, t_o1], writes=[t_o2])
                        kb.op("dve", lambda: nc.vector.scalar_tensor_tensor(
                            out=junk[:], in0=o2[:], scalar=1.0, in1=o2[:], op0=ALU.mult, op1=ALU.mult,
                            accum_out=ss[:]), reads=[t_o2], writes=[t_ss])
                        kb.op("dve", lambda: nc.vector.tensor_scalar(
                            out=ss[:], in0=ss[:], scalar1=1.0 / 128.0, scalar2=EPS, op0=ALU.mult, op1=ALU.add),
                            writes=[t_ss])
                        kb.op("pool", lambda: nc.gpsimd.tensor_tensor(out=ss[:], in0=ss[:], in1=mhalf[:], op=ALU.pow),
                              reads=[t_l], writes=[t_ss])
                        kb.op("dve", lambda: nc.vector.scalar_tensor_tensor(
                            out=on[:], in0=o2[:], scalar=ss[:, 0:1], in1=gsb[:], op0=ALU.mult, op1=ALU.mult),
                            reads=[t_o2, t_ss, t_l], writes=[t_on])
                        kb.op("pe", lambda: nc.tensor.transpose(out=pst[:], in_=on[:], identity=ident[:]),
                              reads=[t_on, t_id], writes=[t_pst])
                        kb.op("dve", lambda sq=sq: nc.vector.tensor_copy(
                            out=ots[gj][:, hh, sq * 128:(sq + 1) * 128], in_=pst[:]),
                            reads=[t_pst], writes=[t_ots[gj]])

                ne0 = ne
                AH = 2
                for a in range(min(AH, len(its))):
                    emit_scores(a)
                for idx in range(len(its)):
                    emit_exp(idx)
                    if idx + AH < len(its):
                        emit_scores(idx + AH)
                    emit_pv(idx)
                    if its[idx][1] == nk - 1:
                        finalize(its[idx][0])
                ne += len(its)
                col = b * TB + q0
                kb.dma(OT[0:1024, col:col + nq].rearrange("(h p) t -> p h t", p=128), ots[gj][:, :, :nq],
                       reads=[t_ots[gj]])
        kb.barrier()


def phase_proj_post(self, i, wname, widx, kc, with_ctx, glu=False):
    kb, nc = self.kb, self.nc
    NT = self.NT
    xT = self.dram["xT"].rearrange("(k p) t -> p k t", p=128)
    OT = self.dram["OT"].rearrange("(k p) t -> p k t", p=128)
    wd = self.dram[wname][widx].rearrange("(k p) d -> p k d", p=128)
    ncol = 2048 if glu else 1024
    with Pool(kb) as P:
        ws = P.sb([128, kc, ncol], BF16, "ws")
        t_w = [Tok() for _ in range(kc)]
        for k in range(kc):
            for hf in range(ncol // 1024):
                kb.dma(ws[:, k, hf * 1024:(hf + 1) * 1024], wd[:, k, hf * 1024:(hf + 1) * 1024],
                       writes=[t_w[k]], q="pool")
        xt = [P.sb([128, KC, NT], F32, "xt%d" % q) for q in range(2)]
        t_xt = [Tok(), Tok()]
        at = [P.sb([128, kc, NT], BF16, "at%d" % q) for q in range(2)]
        t_at = [Tok(), Tok()]
        y = P.sb([128, KC, NT], F32, "y")
        t_y = Tok()
        sg = P.sb([128, NT], F32, "sg")
        t_sg = Tok()
        W = self.norm_work(P, NT)
        psn = P.ps([128, 512], F32, "psn")
        t_psn = Tok(excl=True)
        psd = [P.ps([128, 512], F32, "psd%d" % q) for q in range(4)]
        t_psd = [Tok(excl=True) for _ in range(4)]
        tl = self.tiles(with_ctx)
        c0, n, m = tl[0]
        kb.dma(xt[0][:, :, :n], xT[:, :, c0:c0 + n], writes=[t_xt[0]])
        kb.dma(at[0][:, :, :n], OT[:, 0:kc, c0:c0 + n], writes=[t_at[0]])
        nd = 0
        for ti, (c0, n, m) in enumerate(tl):
            jx = ti % 2
            if ti + 1 < len(tl):
                c1, n1, _ = tl[ti + 1]
                kb.dma(xt[1 - jx][:, :, :n1], xT[:, :, c1:c1 + n1], writes=[t_xt[1 - jx]])
                kb.dma(at[1 - jx][:, :, :n1], OT[:, 0:kc, c1:c1 + n1], writes=[t_at[1 - jx]])
            for d in range(KC):
                pj = nd % 2
                nd += 1
                for k in range(kc):
                    kb.op("pe", lambda d=d, k=k, pj=pj: nc.tensor.matmul(
                        psd[pj][:, :n], ws[:, k, d * 128:(d + 1) * 128], at[jx][:, k, :n],
                        start=(k == 0), stop=(k == kc - 1)), reads=[t_w[k], t_at[jx]], writes=[t_psd[pj]])
                if glu:
                    for k in range(kc):
                        kb.op("pe", lambda d=d, k=k, pj=pj: nc.tensor.matmul(
                            psd[2 + pj][:, :n], ws[:, k, 1024 + d * 128:1024 + (d + 1) * 128], at[jx][:, k, :n],
                            start=(k == 0), stop=(k == kc - 1)), reads=[t_w[k], t_at[jx]], writes=[t_psd[2 + pj]])
                    kb.op("act", lambda pj=pj: nc.scalar.activation(out=sg[:, :n], in_=psd[2 + pj][:, :n],
                                                                    func=AF.Sigmoid),
                          reads=[t_psd[2 + pj]], writes=[t_sg])
                    kb.op("dve", lambda d=d, pj=pj: nc.vector.tensor_tensor(
                        out=y[:, d, :n], in0=psd[pj][:, :n], in1=sg[:, :n], op=ALU.mult),
                        reads=[t_psd[pj], t_sg], writes=[t_y])
                else:
                    kb.op("act", lambda d=d, pj=pj: nc.scalar.copy(out=y[:, d, :n], in_=psd[pj][:, :n]),
                          reads=[t_psd[pj]], writes=[t_y])
            self.norm_post(y, t_y, xt[jx], t_xt[jx], n, i, 0, m, W, psn, t_psn)
            kb.dma(xT[:, :, c0:c0 + n], xt[jx][:, :, :n], reads=[t_xt[jx]])
        kb.barrier()


Model.phase_attn_qkv = phase_attn_qkv
Model.phase_attn_core = phase_attn_core
Model.phase_proj_post = phase_proj_post


LW = 1536
LC = 12


def gelu_tanh(self, dst, src, t_src, shape, Wg, t_dst):
    kb, nc = self.kb, self.nc
    a, b_ = Wg["a"], Wg["b"]
    va = a[:, :shape[1]] if len(shape) == 2 else a[:, :shape[1], :shape[2]]
    vb = b_[:, :shape[1]] if len(shape) == 2 else b_[:, :shape[1], :shape[2]]
    kb.op("act", lambda: nc.scalar.activation(out=va, in_=src, func=AF.Square), reads=[t_src], writes=[Wg["ta"]])
    kb.op("dve", lambda: nc.vector.tensor_scalar(out=va, in0=va, scalar1=0.044715, scalar2=1.0,
                                                 op0=ALU.mult, op1=ALU.add), writes=[Wg["ta"]])
    kb.op("dve", lambda: nc.vector.tensor_tensor(out=va, in0=va, in1=src, op=ALU.mult),
          reads=[t_src], writes=[Wg["ta"]])
    kb.op("act", lambda: nc.scalar.activation(out=vb, in_=va, func=AF.Sigmoid, scale=1.5957691216057308),
          reads=[Wg["ta"]], writes=[Wg["tb"]])
    kb.op("dve", lambda: nc.vector.tensor_tensor(out=dst, in0=vb, in1=src, op=ALU.mult),
          reads=[Wg["tb"], t_src], writes=[t_dst])


def _lru_scratch(self):
    _attn_scratch(self)
    if "RT" not in self.dram:
        self.dscratch("RT", [LW, self.T], F32)
        self.dscratch("GT", [LW, self.T], BF16)


def phase_lru_in(self, i, j):
    kb, nc = self.kb, self.nc
    _lru_scratch(self)
    NT = self.NT
    xT = self.dram["xT"].rearrange("(k p) t -> p k t", p=128)
    win = self.dram["lru_w_in"][j].rearrange("(k p) f -> p k f", p=128)
    RT = self.dram["RT"].rearrange("(k p) t -> p k t", p=128)
    GT = self.dram["GT"].rearrange("(k p) t -> p k t", p=128)
    with Pool(kb) as P:
        ws = P.sb([128, KC, 2 * LW], BF16, "wins")
        t_w = [Tok() for _ in range(6)]
        for g in range(6):
            kb.dma(ws[:, :, g * 512:(g + 1) * 512], win[:, :, g * 512:(g + 1) * 512], writes=[t_w[g]], q="pool")
        xt = [P.sb([128, KC, NT], F32, "xt%d" % q) for q in range(2)]
        t_xt = [Tok(), Tok()]
        h = P.sb([128, KC, NT], BF16, "h")
        t_h = Tok()
        W = self.norm_work(P, NT)
        psn = P.ps([128, 512], F32, "psn")
        t_psn = Tok(excl=True)
        pso = [P.ps([128, 2, 256], F32, "pso%d" % q) for q in range(3)]
        t_pso = [Tok(excl=True) for _ in range(3)]
        Wg = dict(a=P.sb([128, 2, 256], F32, "ga"), b=P.sb([128, 2, 256], F32, "gb"), ta=Tok(), tb=Tok())
        gs = [P.sb([128, LC, NT], BF16, "gs%d" % q) for q in range(2)]
        t_gs = [Tok(), Tok()]
        rs = [P.sb([128, LC, NT], F32, "rs%d" % q) for q in range(2)]
        t_rs = [Tok(), Tok()]
        tl = self.tiles(True)
        c0, n, m = tl[0]
        kb.dma(xt[0][:, :, :n], xT[:, :, c0:c0 + n], writes=[t_xt[0]])
        np_ = 0
        for ti, (c0, n, m) in enumerate(tl):
            jx = ti % 2
            if ti + 1 < len(tl):
                c1, n1, _ = tl[ti + 1]
                kb.dma(xt[1 - jx][:, :, :n1], xT[:, :, c1:c1 + n1], writes=[t_xt[1 - jx]])
            self.norm_pre(xt[jx], t_xt[jx], n, i, 0, m, h, t_h, W, psn, t_psn)
            for op in range(LC):
                pj = np_ % 3
                np_ += 1
                for q2 in range(2):
                    oc = op * 2 + q2
                    for k in range(KC):
                        kb.op("pe", lambda k=k, oc=oc, q2=q2, pj=pj: nc.tensor.matmul(
                            pso[pj][:, q2, :n], ws[:, k, oc * 128:(oc + 1) * 128], h[:, k, :n],
                            start=(k == 0), stop=(k == KC - 1)), reads=[t_w[oc // 4], t_h], writes=[t_pso[pj]])
                if op < 6:
                    gelu_tanh(self, gs[jx][:, 2 * op:2 * op + 2, :n], pso[pj][:, :, :n], t_pso[pj],
                              [128, 2, n], Wg, t_gs[jx])
                else:
                    kb.op("act", lambda op=op, pj=pj: nc.scalar.copy(
                        out=rs[jx][:, 2 * (op - 6):2 * (op - 6) + 2, :n], in_=pso[pj][:, :, :n]),
                        reads=[t_pso[pj]], writes=[t_rs[jx]])
            kb.dma(GT[:, :, c0:c0 + n], gs[jx][:, :, :n], reads=[t_gs[jx]])
            kb.dma(RT[:, :, c0:c0 + n], rs[jx][:, :, :n], reads=[t_rs[jx]])
        kb.barrier()


def phase_lru_scan(self, i, j):
    kb, nc = self.kb, self.nc
    TB, CTX, SEQ = self.TB, self.CTX, self.SEQ
    RT = self.dram["RT"].rearrange("(k p) t -> p k t", p=128)
    GT = self.dram["GT"].rearrange("(k p) t -> p k t", p=128)
    OT = self.dram["OT"].rearrange("(k p) t -> p k t", p=128)
    CW = 512
    with Pool(kb) as P:
        wg = P.sb([128, 2, 2, 6, 2, 256], BF16, "wg")
        cw = P.sb([128, LC, 4], F32, "cw")
        cb = P.sb([128, LC], F32, "cb")
        bg = P.sb([128, 2, 2, LC], F32, "bg")
        ap_ = P.sb([128, 2, LC], F32, "ap")
        cv = P.sb([128, 2, LC], F32, "cv")
        cv2 = P.sb([128, 2, LC], F32, "cv2")
        one = P.sb([128, 1], F32, "one")
        t_s = Tok()
        kb.dma(wg[:], self.dram["lru_w_gate"][j], writes=[t_s], q="pool")
        kb.dma(cw[:], self.dram["lru_conv_w"][j], writes=[t_s])
        kb.dma(cb[:], self.dram["lru_conv_b"][j], writes=[t_s])
        kb.dma(bg[:], self.dram["lru_b_gate"][j], writes=[t_s])
        kb.dma(ap_[:], self.dram["lru_a_param"][j], writes=[t_s])
        kb.op("dve", lambda: nc.vector.memset(one[:], 1.0), writes=[t_s])
        kb.op("act", lambda: nc.scalar.activation(out=cv[:], in_=ap_[:], func=AF.Exp, scale=-1.0),
              reads=[t_s], writes=[t_s])
        kb.op("act", lambda: nc.scalar.activation(out=cv[:], in_=cv[:], func=AF.Ln, bias=one[:], scale=1.0),
              reads=[t_s], writes=[t_s])
        cvh = cv2
        bgh = P.sb([128, 2, 2, LC], F32, "bgh")
        kb.op("dve", lambda: nc.vector.tensor_scalar(out=cvh[:], in0=cv[:], scalar1=-4.0, scalar2=None,
                                                     op0=ALU.mult), reads=[t_s], writes=[t_s])
        kb.op("dve", lambda: nc.vector.tensor_scalar(out=cv[:], in0=cv[:], scalar1=-8.0, scalar2=None,
                                                     op0=ALU.mult), reads=[t_s], writes=[t_s])
        kb.op("dve", lambda: nc.vector.tensor_scalar(out=bgh[:], in0=bg[:], scalar1=0.5, scalar2=None,
                                                     op0=ALU.mult), reads=[t_s], writes=[t_s])
        rt = P.sb([128, TB], F32, "rt")
        t_rt = Tok()
        u = P.sb([128, 2, TB], F32, "u")
        t_u = [Tok(), Tok()]
        ub = P.sb([128, 2, TB], BF16, "ub")
        t_ub = [Tok(), Tok()]
        av = P.sb([128, TB], F32, "av")
        bv = P.sb([128, TB], F32, "bv")
        t_av, t_bv = Tok(), Tok()
        hf = P.sb([128, TB], F32, "hf")
        hb = P.sb([128, TB], F32, "hb")
        t_hf, t_hb = Tok(), Tok()
        gt = P.sb([128, TB], BF16, "gt")
        t_gt = Tok()
        mo = P.sb([128, TB], BF16, "mo")
        t_mo = Tok()
        psg = [P.ps([128, 2, CW], F32, "psg%d" % q) for q in range(2)]
        t_psg = [Tok(excl=True), Tok(excl=True)]
        sr = [P.sb([128, 2, CW], F32, "sr%d" % q) for q in range(2)]
        t_sr = [Tok(), Tok()]
        a2 = [P.sb([128, CW], F32, "a2%d" % q) for q in range(2)]
        t_a2 = [Tok(), Tok()]
        segs = [(0, CTX), (CTX, TB)]
        ng = 0
        for b in range(self.NB):
            cb0 = b * TB
            for n6 in range(6):
                for q2 in range(2):
                    ch = n6 * 2 + q2
                    kb.dma(rt[:], RT[:, ch, cb0:cb0 + TB], writes=[t_rt])
                    for (s0, s1) in segs:
                        kb.op("dve", lambda s0=s0, s1=s1, ch=ch, q2=q2: nc.vector.tensor_scalar(
                            out=u[:, q2, s0:s1], in0=rt[:, s0:s1], scalar1=cw[:, ch, 2:3], scalar2=cb[:, ch:ch + 1],
                            op0=ALU.mult, op1=ALU.add), reads=[t_rt, t_s], writes=[t_u[q2]])
                        for (tap, off) in ((0, -2), (1, -1), (3, 1)):
                            if off < 0:
                                o_sl, i_sl = slice(s0 - off, s1), slice(s0, s1 + off)
                            else:
                                o_sl, i_sl = slice(s0, s1 - off), slice(s0 + off, s1)
                            kb.op("dve", lambda o_sl=o_sl, i_sl=i_sl, ch=ch, q2=q2, tap=tap:
                                  nc.vector.scalar_tensor_tensor(
                                      out=u[:, q2, o_sl], in0=rt[:, i_sl], scalar=cw[:, ch, tap:tap + 1],
                                      in1=u[:, q2, o_sl], op0=ALU.mult, op1=ALU.add),
                                  reads=[t_rt, t_s], writes=[t_u[q2]])
                    kb.op("act", lambda q2=q2: nc.scalar.copy(out=ub[:, q2, :], in_=u[:, q2, :]),
                          reads=[t_u[q2]], writes=[t_ub[q2]])
                    kb.op("pool", lambda q2=q2: nc.gpsimd.tensor_scalar(
                        out=u[:, q2, :], in0=u[:, q2, :], scalar1=0.5, scalar2=None, op0=ALU.mult),
                        reads=[t_ub[q2]], writes=[t_u[q2]])
                for q2 in range(2):
                    ch = n6 * 2 + q2
                    kb.dma(gt[:], GT[:, ch, cb0:cb0 + TB], writes=[t_gt])
                    for d in range(2):
                        for c0 in range(0, TB, CW):
                            n = min(CW, TB - c0)
                            pj = ng % 2
                            ng += 1
                            for kk in range(2):
                                for kc in range(2):
                                    kb.op("pe", lambda kk=kk, kc=kc, d=d, c0=c0, n=n, pj=pj: nc.tensor.matmul(
                                        psg[pj][:, kk, :n], wg[:, d, kk, n6, kc, q2 * 128:(q2 + 1) * 128],
                                        ub[:, kc, c0:c0 + n], start=(kc == 0), stop=(kc == 1)),
                                        reads=[t_s] + t_ub, writes=[t_psg[pj]])
                            for kk in range(2):
                                kb.op("act", lambda kk=kk, d=d, n=n, pj=pj: nc.scalar.activation(
                                    out=sr[pj][:, kk, :n], in_=psg[pj][:, kk, :n], func=AF.Tanh,
                                    bias=bgh[:, d, kk, ch:ch + 1], scale=0.5),
                                    reads=[t_psg[pj], t_s], writes=[t_sr[pj]])
                            kb.op("act", lambda d=d, c0=c0, n=n, pj=pj: nc.scalar.activation(
                                out=av[:, c0:c0 + n], in_=sr[pj][:, 0, :n], func=AF.Exp,
                                scale=cvh[:, d, ch:ch + 1], bias=cvh[:, d, ch:ch + 1]),
                                reads=[t_sr[pj], t_s], writes=[t_av])
                            kb.op("act", lambda d=d, n=n, pj=pj: nc.scalar.activation(
                                out=a2[pj][:, :n], in_=sr[pj][:, 0, :n], func=AF.Exp,
                                scale=cv[:, d, ch:ch + 1], bias=cv[:, d, ch:ch + 1]),
                                reads=[t_sr[pj], t_s], writes=[t_a2[pj]])
                            kb.op("pool", lambda n=n, pj=pj, c0=c0: nc.gpsimd.tensor_scalar(
                                out=bv[:, c0:c0 + n], in0=a2[pj][:, :n], scalar1=-1.0, scalar2=1.0,
                                op0=ALU.mult, op1=ALU.add), reads=[t_a2[pj]], writes=[t_bv])
                            kb.op("dve", lambda n=n, pj=pj, c0=c0: nc.vector.scalar_tensor_tensor(
                                out=rt[:, c0:c0 + n], in0=sr[pj][:, 1, :n], scalar=1.0, in1=u[:, q2, c0:c0 + n],
                                op0=ALU.add, op1=ALU.mult), reads=[t_sr[pj], t_u[q2]], writes=[t_rt])
                        kb.op("act", lambda: nc.scalar.activation(out=bv[:, :], in_=bv[:, :], func=AF.Sqrt),
                              writes=[t_bv])
                        kb.op("dve", lambda: nc.vector.tensor_tensor(out=bv[:, :], in0=bv[:, :], in1=rt[:, :],
                                                                     op=ALU.mult), reads=[t_rt], writes=[t_bv])
                        if d == 0:
                            kb.op("dve", lambda: nc.vector.tensor_tensor_scan(
                                out=hf[:, :], data0=av[:, :], data1=bv[:, :], initial=0.0,
                                op0=ALU.mult, op1=ALU.add), reads=[t_av, t_bv], writes=[t_hf])
                        else:
                            kb.op("dve", lambda: nc.vector.tensor_tensor_scan(
                                out=hb[:, 0:CTX][:, ::-1], data0=av[:, 0:CTX][:, ::-1], data1=bv[:, 0:CTX][:, ::-1],
                                initial=0.0, op0=ALU.mult, op1=ALU.add), reads=[t_av, t_bv], writes=[t_hb])
                            kb.op("dve", lambda: nc.vector.tensor_tensor_scan(
                                out=hb[:, CTX:TB][:, ::-1], data0=av[:, CTX:TB][:, ::-1],
                                data1=bv[:, CTX:TB][:, ::-1], initial=hb[:, 0:1], op0=ALU.mult, op1=ALU.add),
                                reads=[t_av, t_bv], writes=[t_hb])
                    kb.op("dve", lambda: nc.vector.tensor_tensor(out=hf[:], in0=hf[:], in1=hb[:], op=ALU.add),
                          reads=[t_hb], writes=[t_hf])
                    kb.op("dve", lambda: nc.vector.tensor_tensor(out=mo[:], in0=hf[:], in1=gt[:], op=ALU.mult),
                          reads=[t_hf, t_gt], writes=[t_mo])
                    kb.dma(OT[:, ch, cb0:cb0 + TB], mo[:], reads=[t_mo])
        kb.barrier()


Model.phase_lru_in = phase_lru_in
Model.phase_lru_scan = phase_lru_scan


import math as _math
NG = 64
LL = 8


def _s5_scratch(self):
    _attn_scratch(self)
    if "UT" not in self.dram:
        self.dscratch("UT", [1024, self.T], BF16)
        self.dscratch("YF", [2, 1024, self.T], F32)


def phase_s5_in(self, i):
    kb, nc = self.kb, self.nc
    _s5_scratch(self)
    NT = self.NT
    xT = self.dram["xT"].rearrange("(k p) t -> p k t", p=128)
    UT = self.dram["UT"].rearrange("(k p) t -> p k t", p=128)
    with Pool(kb) as P:
        xt = [P.sb([128, KC, NT], F32, "xt%d" % q) for q in range(2)]
        t_xt = [Tok(), Tok()]
        h = [P.sb([128, KC, NT], BF16, "h%d" % q) for q in range(2)]
        t_h = [Tok(), Tok()]
        W = self.norm_work(P, NT)
        psn = P.ps([128, 512], F32, "psn")
        t_psn = Tok(excl=True)
        tl = self.tiles(True)
        c0, n, m = tl[0]
        kb.dma(xt[0][:, :, :n], xT[:, :, c0:c0 + n], writes=[t_xt[0]])
        for ti, (c0, n, m) in enumerate(tl):
            jx = ti % 2
            if ti + 1 < len(tl):
                c1, n1, _ = tl[ti + 1]
                kb.dma(xt[1 - jx][:, :, :n1], xT[:, :, c1:c1 + n1], writes=[t_xt[1 - jx]])
            self.norm_pre(xt[jx], t_xt[jx], n, i, 0, m, h[jx], t_h[jx], W, psn, t_psn)
            kb.dma(UT[:, :, c0:c0 + n], h[jx][:, :, :n], reads=[t_h[jx]])
        kb.barrier()


def phase_s5_scan(self, i, jb):
    kb, nc = self.kb, self.nc
    TB, CTX, SEQ = self.TB, self.CTX, self.SEQ
    J = TB // LL
    npass = 0
    while (1 << npass) < J:
        npass += 1
    pmax = getattr(Model, "s5_piece_max", 512)
    pieces = [(0, J)]
    if J > pmax:
        hh = (J + 1) // 2
        pieces = [(0, hh), (hh, J)]
    UT = self.dram["UT"].rearrange("(k p) t -> p k t", p=128)
    YF = self.dram["YF"]
    TWO_PI = 2.0 * _math.pi
    V = nc.vector
    for d in getattr(self, 's5_dirs', (0, 1)):
        with Pool(kb) as P:
            t_st = Tok()

            def tt(o, a, b, op):
                kb.op("dve", lambda: V.tensor_tensor(out=o, in0=a, in1=b, op=op), writes=[t_st])

            def ts(o, a, s1, op0, s2=None, op1=None):
                if op1 is None:
                    kb.op("dve", lambda: V.tensor_scalar(out=o, in0=a, scalar1=s1, scalar2=None, op0=op0),
                          writes=[t_st])
                else:
                    kb.op("dve", lambda: V.tensor_scalar(out=o, in0=a, scalar1=s1, scalar2=s2, op0=op0, op1=op1),
                          writes=[t_st])

            def act(o, a, func, scale=1.0, bias=None):
                if bias is None:
                    kb.op("act", lambda: nc.scalar.activation(out=o, in_=a, func=func, scale=scale), writes=[t_st])
                else:
                    kb.op("act", lambda: nc.scalar.activation(out=o, in_=a, func=func, scale=scale, bias=bias),
                          writes=[t_st])
            BS = P.sb([128, 8, NG, 16], BF16, "BS")
            CS = P.sb([128, 9, NG, 16], BF16, "CS")
            PW = P.sb([128, 9, 2, NG], F32, "PW")
            QW = P.sb([128, npass, 2, NG], F32, "QW")
            qos = P.sb([128, npass, NG], F32, "qos")
            identb = P.sb([128, 128], BF16, "identb")
            identf = P.sb([128, 128], F32, "identf")
            shf = P.sb([128, 128], F32, "shf")
            dsk = P.sb([128, KC], F32, "dsk")
            kb.dma(identb[:], self.dram["ident"], writes=[t_st], q="pool")
            kb.dma(identf[:], self.dram["ident"], writes=[t_st])
            kb.dma(shf[:], self.dram["shift64"], writes=[t_st])
            with Pool(kb) as PS:
                pg = lambda nm: PS.sb([128, NG], F32, nm)
                are, aim, dt, xr, xi, mag, yy, ff, mm, sn, cs, lr, li = [pg("pg%d" % q) for q in range(13)]
                nr, den, cr, ci, t1, t2, t3, t4 = [pg("pgb%d" % q) for q in range(8)]
                ki = PS.sb([128, NG], I32, "ki")
                pgc = lambda nm: PS.sb([128, NG, 16], F32, nm)
                Bre, Bim, Cre, Cim, Bbr, Bbi, T1, T2, T3, T4 = [pgc("pgc%d" % q) for q in range(10)]
                kb.dma(are[:], self.dram["s5_a_re"][jb, d], writes=[t_st])
                kb.dma(aim[:], self.dram["s5_a_im"][jb, d], writes=[t_st])
                kb.dma(dt[:], self.dram["s5_log_dt"][jb, d].partition_broadcast(128), writes=[t_st])
                kb.dma(Bre[:], self.dram["s5_b_re"][jb, d], writes=[t_st])
                kb.dma(Bim[:], self.dram["s5_b_im"][jb, d], writes=[t_st])
                kb.dma(Cre[:], self.dram["s5_c_re"][jb, d], writes=[t_st])
                kb.dma(Cim[:], self.dram["s5_c_im"][jb, d], writes=[t_st])
                act(dt[:], dt[:], AF.Exp)
                tt(xr[:], are[:], dt[:], ALU.mult)
                tt(xi[:], aim[:], dt[:], ALU.mult)
                ts(yy[:], xi[:], 1.0 / TWO_PI, ALU.mult)
                kb.op("dve", lambda: V.tensor_copy(out=ki[:], in_=yy[:]), writes=[t_st])
                kb.op("dve", lambda: V.tensor_copy(out=ff[:], in_=ki[:]), writes=[t_st])
                tt(ff[:], yy[:], ff[:], ALU.subtract)
                ts(mm[:], ff[:], 0.5, ALU.is_gt)
                tt(ff[:], ff[:], mm[:], ALU.subtract)
                ts(mm[:], ff[:], -0.5, ALU.is_lt)
                tt(ff[:], ff[:], mm[:], ALU.add)
                ts(ff[:], ff[:], TWO_PI / 16.0, ALU.mult)
                tt(yy[:], ff[:], ff[:], ALU.mult)

                def horner(o, z, coefs):
                    kb.op("dve", lambda: V.memset(o, 0.0), writes=[t_st])
                    for c in reversed(coefs[1:]):
                        kb.op("dve", lambda c=c: V.scalar_tensor_tensor(out=o, in0=o, scalar=float(c), in1=z,
                                                                        op0=ALU.add, op1=ALU.mult), writes=[t_st])
                    ts(o, o, float(coefs[0]), ALU.add)
                fact = [1.0]
                for q in range(1, 16):
                    fact.append(fact[-1] * q)
                horner(cs[:], yy[:], [(-1) ** q / fact[2 * q] for q in range(6)])
                horner(sn[:], yy[:], [(-1) ** q / fact[2 * q + 1] for q in range(6)])
                tt(sn[:], sn[:], ff[:], ALU.mult)
                ts(mm[:], xr[:], 1.0 / 16.0, ALU.mult)
                horner(mag[:], mm[:], [1.0 / fact[q] for q in range(10)])
                tt(lr[:], mag[:], cs[:], ALU.mult)
                tt(li[:], mag[:], sn[:], ALU.mult)
                for _sq in range(4):
                    tt(t1[:], lr[:], lr[:], ALU.mult)
                    tt(t2[:], li[:], li[:], ALU.mult)
                    tt(t3[:], lr[:], li[:], ALU.mult)
                    tt(lr[:], t1[:], t2[:], ALU.subtract)
                    ts(li[:], t3[:], 2.0, ALU.mult)
                ts(nr[:], lr[:], -1.0, ALU.add)
                tt(t1[:], are[:], are[:], ALU.mult)
                tt(t2[:], aim[:], aim[:], ALU.mult)
                tt(den[:], t1[:], t2[:], ALU.add)
                kb.op("dve", lambda: V.reciprocal(out=den[:], in_=den[:]), writes=[t_st])
                tt(t1[:], nr[:], are[:], ALU.mult)
                tt(t2[:], li[:], aim[:], ALU.mult)
                tt(cr[:], t1[:], t2[:], ALU.add)
                tt(cr[:], cr[:], den[:], ALU.mult)
                tt(t1[:], li[:], are[:], ALU.mult)
                tt(t2[:], nr[:], aim[:], ALU.mult)
                tt(ci[:], t1[:], t2[:], ALU.subtract)
                tt(ci[:], ci[:], den[:], ALU.mult)
                bc = lambda v: v.unsqueeze(2).broadcast_to([128, NG, 16])
                tt(T1[:], Bre[:], bc(cr[:]), ALU.mult)
                tt(T2[:], Bim[:], bc(ci[:]), ALU.mult)
                tt(Bbr[:], T1[:], T2[:], ALU.subtract)
                tt(T1[:], Bim[:], bc(cr[:]), ALU.mult)
                tt(T2[:], Bre[:], bc(ci[:]), ALU.mult)
                tt(Bbi[:], T1[:], T2[:], ALU.add)
                kb.op("dve", lambda: V.memset(PW[:, 0, 0, :], 1.0), writes=[t_st])
                kb.op("dve", lambda: V.memset(PW[:, 0, 1, :], 0.0), writes=[t_st])
                kb.op("dve", lambda: V.tensor_copy(out=PW[:, 1, 0, :], in_=lr[:]), writes=[t_st])
                kb.op("dve", lambda: V.tensor_copy(out=PW[:, 1, 1, :], in_=li[:]), writes=[t_st])

                def cmul(o_r, o_i, a_r, a_i, b_r, b_i):
                    tt(t1[:], a_r, b_r, ALU.mult)
                    tt(t2[:], a_i, b_i, ALU.mult)
                    tt(t3[:], a_r, b_i, ALU.mult)
                    tt(t4[:], a_i, b_r, ALU.mult)
                    tt(o_r, t1[:], t2[:], ALU.subtract)
                    tt(o_i, t3[:], t4[:], ALU.add)
                for e in range(2, 9):
                    cmul(PW[:, e, 0, :], PW[:, e, 1, :], PW[:, e - 1, 0, :], PW[:, e - 1, 1, :], lr[:], li[:])
                kb.op("dve", lambda: V.tensor_copy(out=QW[:, 0, :, :], in_=PW[:, 8, :, :]), writes=[t_st])
                for k in range(1, npass):
                    cmul(QW[:, k, 0, :], QW[:, k, 1, :], QW[:, k - 1, 0, :], QW[:, k - 1, 1, :],
                         QW[:, k - 1, 0, :], QW[:, k - 1, 1, :])
                kb.op("dve", lambda: V.tensor_copy(out=qos[0:64, :, :], in_=QW[0:64, :, 1, :]), writes=[t_st])
                kb.op("dve", lambda: V.tensor_scalar(out=qos[64:128, :, :], in0=QW[64:128, :, 1, :], scalar1=-1.0,
                                                     scalar2=None, op0=ALU.mult), writes=[t_st])
                for e in range(9):
                    p_r, p_i = bc(PW[:, e, 0, :]), bc(PW[:, e, 1, :])
                    if e < 8:
                        tt(T1[:], Bbr[:], p_r, ALU.mult)
                        tt(T2[:], Bbi[:], p_i, ALU.mult)
                        tt(T3[:], Bbi[:], p_r, ALU.mult)
                        tt(T4[:], Bbr[:], p_i, ALU.mult)
                        tt(BS[0:64, e], T1[0:64], T2[0:64], ALU.subtract)
                        tt(BS[64:128, e], T3[64:128], T4[64:128], ALU.add)
                    tt(T1[:], Cre[:], p_r, ALU.mult)
                    tt(T2[:], Cim[:], p_i, ALU.mult)
                    tt(T3[:], Cre[:], p_i, ALU.mult)
                    tt(T4[:], Cim[:], p_r, ALU.mult)
                    tt(CS[0:64, e], T1[0:64], T2[0:64], ALU.subtract)
                    kb.op("dve", lambda: V.scalar_tensor_tensor(
                        out=CS[64:128, e], in0=T3[64:128], scalar=-1.0, in1=T4[64:128],
                        op0=ALU.mult, op1=ALU.subtract), writes=[t_st])
            if getattr(self, "s5_stop", 0) == 1:
                if d == getattr(self, "s5_dbg_dir", 0):
                    kb.dma(self.dram["dbgBS"], BS[:], reads=[t_st])
                    kb.dma(self.dram["dbgCS"], CS[:], reads=[t_st])
                    kb.dma(self.dram["dbgPW"], PW[:], reads=[t_st])
                    kb.dma(self.dram["dbgQW"], QW[:], reads=[t_st])
                kb.barrier()
                continue
            ECt = P.sb([128, 9, 1152], BF16, "ECt")
            LBt = P.sb([128, 8, 8, 128], BF16, "LBt")
            Kdt = P.sb([128, 8, 128], BF16, "Kdt")
            AMt = P.sb([128, npass, 8, 128], BF16, "AMt")
            t_EC, t_LB, t_Kd, t_AM = Tok(), Tok(), Tok(), Tok()
            kb.op("pool", lambda: nc.gpsimd.memset(ECt[:], 0.0), reads=[t_st], writes=[t_EC])
            ub = [P.sb([128, TB + CTX], BF16, "ub%d" % q) for q in range(2)]
            t_ub = [Tok(), Tok()]
            S32 = P.sb([128, 8, J], F32, "S32")
            Sb = P.sb([128, 8, J], BF16, "Sb")
            Hb = P.sb([128, 8, J], BF16, "Hb")
            t_S32 = [Tok() for _ in range(8)]
            t_Sb = [Tok() for _ in range(8)]
            t_Hb = Tok()
            kb.op("pool", lambda: nc.gpsimd.memset(Hb[:, :, 0:1], 0.0), writes=[t_Hb])
            yb1 = P.sb([128, TB], F32, "yb")
            yb = [yb1, yb1]
            t_yb1 = Tok()
            t_yb = [t_yb1, t_yb1]
            amt = [P.sb([128, 128], F32, "amt%d" % q) for q in range(2)]
            t_amt = [Tok(), Tok()]
            nu = 0
            for cidx in getattr(self, 's5_chunks', range(KC)):
                g0 = cidx * 8
                with Pool(kb) as PC:
                    EBt = PC.sb([128, 8, 1152], BF16, "EBt")
                    t_EB = Tok()
                    pst = [PC.ps([128, 8, 128], BF16, "pst%d" % q) for q in range(2)]
                    t_pst = [Tok(excl=True), Tok(excl=True)]
                    psk = [PC.ps([128, 4, 128], F32, "psk%d" % q) for q in range(2)]
                    t_psk = [Tok(excl=True), Tok(excl=True)]
                    kb.op("pool", lambda: nc.gpsimd.memset(EBt[:], 0.0), writes=[t_EB])
                    for e in range(9):
                        if e < 8:
                            kb.op("dve", lambda e=e: V.tensor_copy(
                                out=EBt[:, e, :].rearrange("p (g s) -> p g s", s=144)[:, :, 0:16],
                                in_=BS[:, e, g0:g0 + 8, :]), reads=[t_st], writes=[t_EB])
                        kb.op("dve", lambda e=e: V.tensor_copy(
                            out=ECt[:, e, :].rearrange("p (g s) -> p g s", s=144)[:, :, 0:16],
                            in_=CS[:, e, g0:g0 + 8, :]), reads=[t_st], writes=[t_EC])
                    for e in range(8):
                        pj = e % 2
                        for g in range(8):
                            kb.op("pe", lambda e=e, g=g, pj=pj: nc.tensor.transpose(
                                out=pst[pj][:, g, :], in_=EBt[:, e, g * 128:(g + 1) * 128], identity=identb[:]),
                                reads=[t_EB, t_st], writes=[t_pst[pj]])
                        kb.op("act", lambda e=e, pj=pj: nc.scalar.copy(out=LBt[:, e, :, :], in_=pst[pj][:, :, :]),
                              reads=[t_pst[pj]], writes=[t_LB])
                    for hb_ in range(2):
                        for e4 in range(4):
                            e = hb_ * 4 + e4
                            for g in range(8):
                                kb.op("pe", lambda e=e, e4=e4, g=g, hb_=hb_: nc.tensor.matmul(
                                    psk[hb_][:, e4, :], EBt[:, e, g * 128:(g + 1) * 128],
                                    ECt[:, 0, g * 128:(g + 1) * 128], start=(g == 0), stop=(g == 7)),
                                    reads=[t_EB, t_EC], writes=[t_psk[hb_]])
                        kb.op("act", lambda hb_=hb_: nc.scalar.copy(out=Kdt[:, hb_ * 4:(hb_ + 1) * 4, :],
                                                                    in_=psk[hb_][:, :, :]),
                              reads=[t_psk[hb_]], writes=[t_Kd])
                    na = 0
                    for k in range(npass):
                        for g in range(8):
                            aj = na % 2
                            na += 1
                            kb.op("pool", lambda k=k, g=g, aj=aj: nc.gpsimd.tensor_scalar(
                                out=amt[aj][:], in0=identf[:], scalar1=QW[:, k, 0, g0 + g:g0 + g + 1], scalar2=None,
                                op0=ALU.mult), reads=[t_st], writes=[t_amt[aj]])
                            kb.op("dve", lambda k=k, g=g, aj=aj: V.scalar_tensor_tensor(
                                out=AMt[:, k, g, :], in0=shf[:], scalar=qos[:, k, g0 + g:g0 + g + 1], in1=amt[aj][:],
                                op0=ALU.mult, op1=ALU.add), reads=[t_st, t_amt[aj]], writes=[t_AM])
                if getattr(self, "s5_stop", 0) == 2:
                    if d == 0 and cidx == 0:
                        kb.dma(self.dram["dbgLB"], LBt[:], reads=[t_LB])
                        kb.dma(self.dram["dbgKd"], Kdt[:], reads=[t_Kd])
                        kb.dma(self.dram["dbgAM"], AMt[:], reads=[t_AM])
                    continue
                with Pool(kb) as PM:
                    psv = [PM.ps([128, 1024], F32, "psv%d" % q) for q in range(2)]
                    t_psv = [Tok(excl=True), Tok(excl=True)]
                    psy = [PM.ps([128, 1024], F32, "psy%d" % q) for q in range(2)]
                    t_psy = [Tok(excl=True), Tok(excl=True)]
                    npv = 0
                    npy = 0
                    for b in range(self.NB):
                        uj = nu % 2
                        nu += 1
                        ubt = ub[uj]
                        kb.dma(ubt[:, 0:TB], UT[:, cidx, b * TB:(b + 1) * TB], writes=[t_ub[uj]])
                        kb.dma(ubt[:, TB:TB + CTX], UT[:, cidx, b * TB:b * TB + CTX], writes=[t_ub[uj]])
                        ybt = yb[uj]
                        if d == 0:
                            useq = lambda s: ubt[:, s:TB:LL]
                            yseq = lambda r: ybt[:, r:TB:LL]
                        else:
                            useq = lambda s: ubt[:, CTX:CTX + TB][:, (TB - 1 - s)::-LL]
                            yseq = lambda r: ybt[:, (TB - 1 - r)::-LL]
                        for g in range(8):
                            pj = npv % 2
                            npv += 1
                            for (c0, c1) in pieces:
                                pc = pieces.index((c0, c1))
                                for s in range(LL):
                                    kb.op("pe", lambda s=s, g=g, c0=c0, c1=c1, pc=pc, pj=pj: nc.tensor.matmul(
                                        psv[pj][:, pc * 512:pc * 512 + (c1 - c0)], LBt[:, LL - 1 - s, g, :],
                                        useq(s)[:, c0:c1], start=(s == 0), stop=(s == LL - 1)),
                                        reads=[t_LB, t_ub[uj]], writes=[t_psv[pj]])
                            for (c0, c1) in pieces:
                                pc = pieces.index((c0, c1))
                                kb.op("act", lambda g=g, c0=c0, c1=c1, pc=pc, pj=pj: nc.scalar.copy(
                                    out=S32[:, g, c0:c1], in_=psv[pj][:, pc * 512:pc * 512 + (c1 - c0)]),
                                    reads=[t_psv[pj]], writes=[t_S32[g]])
                                kb.op("dve", lambda g=g, c0=c0, c1=c1: V.tensor_copy(
                                    out=Sb[:, g, c0:c1], in_=S32[:, g, c0:c1]),
                                    reads=[t_S32[g]], writes=[t_Sb[g]])
                        _stop = getattr(self, "s5_stop", 0)
                        if _stop == 3:
                            kb.dma(self.dram["dbgS32"], S32[:], reads=t_S32)
                            continue
                        for k in range(npass):
                            dd = 1 << k
                            for g in range(8):
                                pj = npv % 2
                                npv += 1
                                segs = []
                                for (c0, c1) in pieces:
                                    if c1 <= dd:
                                        continue
                                    lo = max(c0, dd)
                                    pc = pieces.index((c0, c1))
                                    segs.append((lo, c1, pc * 512 + (lo - c0)))
                                for (lo, c1, po) in segs:
                                    kb.op("pe", lambda k=k, g=g, lo=lo, c1=c1, po=po, pj=pj, dd=dd: nc.tensor.matmul(
                                        psv[pj][:, po:po + (c1 - lo)], AMt[:, k, g, :], Sb[:, g, lo - dd:c1 - dd],
                                        start=True, stop=True), reads=[t_AM, t_Sb[g]], writes=[t_psv[pj]])
                                for (lo, c1, po) in segs:
                                    kb.op("dve", lambda g=g, lo=lo, c1=c1, po=po, pj=pj: V.tensor_tensor(
                                        out=S32[:, g, lo:c1], in0=S32[:, g, lo:c1], in1=psv[pj][:, po:po + (c1 - lo)],
                                        op=ALU.add), reads=[t_psv[pj]], writes=[t_S32[g]])
                                kb.op("act", lambda g=g, dd=dd: nc.scalar.copy(out=Sb[:, g, dd:J], in_=S32[:, g, dd:J]),
                                      reads=[t_S32[g]], writes=[t_Sb[g]])
                        kb.op("act", lambda: nc.scalar.copy(out=Hb[:, :, 1:J], in_=S32[:, :, 0:J - 1]),
                              reads=t_S32, writes=[t_Hb])
                        if _stop == 4:
                            kb.dma(self.dram["dbgS32"], S32[:], reads=t_S32)
                            continue
                        for r in range(LL):
                            pj = npy % 2
                            npy += 1
                            for (c0, c1) in pieces:
                                pc = pieces.index((c0, c1))
                                o_ap = psy[pj][:, pc * 512:pc * 512 + (c1 - c0)]
                                nmm = (r + 1) + 8
                                im = 0
                                for q in range(r + 1):
                                    kb.op("pe", lambda q=q, r=r, c0=c0, c1=c1, o_ap=o_ap, im=im, nmm=nmm:
                                          nc.tensor.matmul(o_ap, Kdt[:, q, :], useq(r - q)[:, c0:c1],
                                                           start=(im == 0), stop=(im == nmm - 1)),
                                          reads=[t_Kd, t_ub[uj]], writes=[t_psy[pj]])
                                    im += 1
                                for g in range(8):
                                    kb.op("pe", lambda g=g, r=r, c0=c0, c1=c1, o_ap=o_ap, im=im, nmm=nmm:
                                          nc.tensor.matmul(o_ap, ECt[:, r + 1, g * 128:(g + 1) * 128], Hb[:, g, c0:c1],
                                                           start=(im == 0), stop=(im == nmm - 1)),
                                          reads=[t_EC, t_Hb], writes=[t_psy[pj]])
                                    im += 1
                            for (c0, c1) in pieces:
                                pc = pieces.index((c0, c1))
                                kb.op("act", lambda r=r, c0=c0, c1=c1, pc=pc, pj=pj: nc.scalar.copy(
                                    out=yseq(r)[:, c0:c1], in_=psy[pj][:, pc * 512:pc * 512 + (c1 - c0)]),
                                    reads=[t_psy[pj]], writes=[t_yb[uj]])
                        col = b * TB
                        if d == 0:
                            kb.dma(YF[0, cidx * 128:(cidx + 1) * 128, col:col + TB], ybt[:, :], reads=[t_yb[uj]])
                        else:
                            kb.dma(YF[1, cidx * 128:(cidx + 1) * 128, col + CTX:col + TB], ybt[:, 0:SEQ],
                                   reads=[t_yb[uj]])
                            kb.dma(YF[1, cidx * 128:(cidx + 1) * 128, col:col + CTX], ybt[:, SEQ:TB],
                                   reads=[t_yb[uj]])
            kb.barrier()


def phase_s5_out(self, i, jb, with_ctx):
    kb, nc = self.kb, self.nc
    NT = self.NT
    xT = self.dram["xT"].rearrange("(k p) t -> p k t", p=128)
    UT = self.dram["UT"].rearrange("(k p) t -> p k t", p=128)
    YF0 = self.dram["YF"][0].rearrange("(k p) t -> p k t", p=128)
    YF1 = self.dram["YF"][1].rearrange("(k p) t -> p k t", p=128)
    wd = self.dram["s5_w_glu"][jb].rearrange("(k p) d -> p k d", p=128)
    with Pool(kb) as P:
        ws = P.sb([128, KC, 2048], BF16, "ws")
        t_w = [Tok() for _ in range(KC)]
        for k in range(KC):
            for hf in range(2):
                kb.dma(ws[:, k, hf * 1024:(hf + 1) * 1024], wd[:, k, hf * 1024:(hf + 1) * 1024],
                       writes=[t_w[k]], q="pool")
        dsk = P.sb([128, KC], F32, "dsk")
        t_d = Tok()
        kb.dma(dsk[:], self.dram["s5_d"][jb], writes=[t_d])
        xt = [P.sb([128, KC, NT], F32, "xt%d" % q) for q in range(2)]
        t_xt = [Tok(), Tok()]
        y0 = [P.sb([128, KC, NT], F32, "y0%d" % q) for q in range(2)]
        y1 = [P.sb([128, KC, NT], F32, "y1%d" % q) for q in range(2)]
        ut = [P.sb([128, KC, NT], BF16, "ut%d" % q) for q in range(2)]
        t_in = [Tok(), Tok()]
        at = P.sb([128, KC, NT], BF16, "at")
        t_at = Tok()
        Wg = dict(a=P.sb([128, KC, NT], F32, "ga"), b=P.sb([128, KC, NT], F32, "gb"), ta=Tok(), tb=Tok())
        y = P.sb([128, KC, NT], F32, "y")
        t_y = Tok()
        sg = P.sb([128, NT], F32, "sg")
        t_sg = Tok()
        W = self.norm_work(P, NT)
        psn = P.ps([128, 512], F32, "psn")
        t_psn = Tok(excl=True)
        psd = [P.ps([128, 512], F32, "psd%d" % q) for q in range(4)]
        t_psd = [Tok(excl=True) for _ in range(4)]
        tl = self.tiles(with_ctx)

        def loads(jx, c0, n):
            kb.dma(xt[jx][:, :, :n], xT[:, :, c0:c0 + n], writes=[t_xt[jx]])
            kb.dma(y0[jx][:, :, :n], YF0[:, :, c0:c0 + n], writes=[t_in[jx]])
            kb.dma(y1[jx][:, :, :n], YF1[:, :, c0:c0 + n], writes=[t_in[jx]])
            kb.dma(ut[jx][:, :, :n], UT[:, :, c0:c0 + n], writes=[t_in[jx]])
        loads(0, tl[0][0], tl[0][1])
        nd = 0
        for ti, (c0, n, m) in enumerate(tl):
            jx = ti % 2
            if ti + 1 < len(tl):
                loads(1 - jx, tl[ti + 1][0], tl[ti + 1][1])
            kb.op("dve", lambda jx=jx, n=n: nc.vector.tensor_tensor(
                out=y0[jx][:, :, :n], in0=y0[jx][:, :, :n], in1=y1[jx][:, :, :n], op=ALU.add),
                writes=[t_in[jx]])
            for k in range(KC):
                kb.op("dve", lambda jx=jx, n=n, k=k: nc.vector.scalar_tensor_tensor(
                    out=y0[jx][:, k, :n], in0=ut[jx][:, k, :n], scalar=dsk[:, k:k + 1], in1=y0[jx][:, k, :n],
                    op0=ALU.mult, op1=ALU.add), reads=[t_d], writes=[t_in[jx]])
            gelu_tanh(self, at[:, :, :n], y0[jx][:, :, :n], t_in[jx], [128, KC, n], Wg, t_at)
            for dch in range(KC):
                pj = nd % 2
                nd += 1
                for k in range(KC):
                    kb.op("pe", lambda dch=dch, k=k, pj=pj: nc.tensor.matmul(
                        psd[pj][:, :n], ws[:, k, dch * 128:(dch + 1) * 128], at[:, k, :n],
                        start=(k == 0), stop=(k == KC - 1)), reads=[t_w[k], t_at], writes=[t_psd[pj]])
                for k in range(KC):
                    kb.op("pe", lambda dch=dch, k=k, pj=pj: nc.tensor.matmul(
                        psd[2 + pj][:, :n], ws[:, k, 1024 + dch * 128:1024 + (dch + 1) * 128], at[:, k, :n],
                        start=(k == 0), stop=(k == KC - 1)), reads=[t_w[k], t_at], writes=[t_psd[2 + pj]])
                kb.op("act", lambda pj=pj: nc.scalar.activation(out=sg[:, :n], in_=psd[2 + pj][:, :n],
                                                                func=AF.Sigmoid),
                      reads=[t_psd[2 + pj]], writes=[t_sg])
                kb.op("dve", lambda dch=dch, pj=pj: nc.vector.tensor_tensor(
                    out=y[:, dch, :n], in0=psd[pj][:, :n], in1=sg[:, :n], op=ALU.mult),
                    reads=[t_psd[pj], t_sg], writes=[t_y])
            self.norm_post(y, t_y, xt[jx], t_xt[jx], n, i, 0, m, W, psn, t_psn)
            kb.dma(xT[:, :, c0:c0 + n], xt[jx][:, :, :n], reads=[t_xt[jx]])
        kb.barrier()


Model.phase_s5_in = phase_s5_in
Model.phase_s5_scan = phase_s5_scan
Model.phase_s5_out = phase_s5_out

def arr_vec(v):
    v = np.asarray(v)
    lead = v.shape[:-1]
    n = v.shape[-1] // 128
    return np.ascontiguousarray(np.moveaxis(v.reshape(lead + (n, 128)), -1, 0))

def host_common_x(inp, bs):
    f = np.float32
    x, ctx, c, c_ctx = inp["x"], inp["ctx"], inp["c"], inp["c_ctx"]
    cols = []
    for b in bs:
        cols.append(ctx[b].T)
        cols.append(x[b].T)
    xin = np.ascontiguousarray(np.concatenate(cols, axis=1), dtype=f)
    cvecs = [c[b] for b in bs]
    while len(cvecs) < 2:
        cvecs.append(np.zeros_like(c_ctx))
    cvecs.append(c_ctx)
    cc = np.stack(cvecs, axis=-1)
    cc = np.ascontiguousarray(cc.reshape(8, 128, 3).transpose(1, 0, 2), dtype=f)
    return {"xin": xin, "cc": cc}


def host_common(inp, bs, CTX, SEQ):
    f = np.float32
    d = host_common_x(inp, bs)
    d.update({
         "ada_w": np.ascontiguousarray(inp["ada_w"], dtype=f),
         "ada_b": arr_vec(inp["ada_b"]).astype(f),
         "norm_g": arr_vec(inp["norm_g"]).astype(f),
         "mlp_w1": np.ascontiguousarray(inp["mlp_w1"], dtype=f),
         "mlp_w2": np.ascontiguousarray(
             np.asarray(inp["mlp_w2"]).reshape(-1, 32, 128, 8, 128).transpose(0, 3, 2, 1, 4), dtype=f),
         })
    return d

def rope_table(SEQ, GRID_W=64):
    t = np.arange(SEQ)
    row, col = t // GRID_W, t % GRID_W
    inv = (10000.0 ** (-np.arange(16, dtype=np.float32) / 16)).astype(np.float32)
    ang = np.concatenate([row[:, None].astype(np.float32) * inv, col[:, None].astype(np.float32) * inv], axis=-1)
    tab = np.stack([np.cos(ang), np.sin(ang)], axis=1).astype(np.float32)
    return np.ascontiguousarray(tab.reshape(SEQ // 128, 128, 2, 32).transpose(1, 0, 2, 3))

def host_attn(inp, SEQ):
    f = np.float32
    return {"attn_w_qkv": np.ascontiguousarray(inp["attn_w_qkv"], dtype=f),
            "attn_w_o": np.ascontiguousarray(inp["attn_w_o"], dtype=f),
            "attn_lambda": np.ascontiguousarray(inp["attn_lambda"], dtype=f),
            "attn_subln": np.ascontiguousarray(inp["attn_subln"], dtype=f),
            "rope": rope_table(SEQ), "ident": np.eye(128, dtype=f)}

def host_lru(inp):
    f = np.float32
    wg = np.asarray(inp["lru_w_gate"])
    nc_ = wg.shape[0]
    wg = wg.reshape(nc_, 2, 2, 6, 2, 128, 256).transpose(0, 5, 1, 2, 3, 4, 6)
    return {"lru_w_in": np.ascontiguousarray(inp["lru_w_in"], dtype=f),
            "lru_w_out": np.ascontiguousarray(inp["lru_w_out"], dtype=f),
            "lru_w_gate": np.ascontiguousarray(wg, dtype=f),
            "lru_conv_w": np.ascontiguousarray(
                np.asarray(inp["lru_conv_w"]).reshape(nc_, 4, 12, 128).transpose(0, 3, 2, 1), dtype=f),
            "lru_conv_b": np.ascontiguousarray(
                np.asarray(inp["lru_conv_b"]).reshape(nc_, 12, 128).transpose(0, 2, 1), dtype=f),
            "lru_b_gate": np.ascontiguousarray(
                np.asarray(inp["lru_b_gate"]).reshape(nc_, 2, 2, 12, 128).transpose(0, 4, 1, 2, 3), dtype=f),
            "lru_a_param": np.ascontiguousarray(
                np.asarray(inp["lru_a_param"]).reshape(nc_, 2, 12, 128).transpose(0, 3, 1, 2), dtype=f)}

def host_s5(inp):
    f = np.float32
    def dup(a):
        return np.concatenate([a, a], axis=2)
    a_re = np.asarray(inp["s5_a_re"]).transpose(0, 1, 3, 2)
    a_im = np.asarray(inp["s5_a_im"]).transpose(0, 1, 3, 2)
    b_re = np.asarray(inp["s5_b_re"]).transpose(0, 1, 3, 2, 4)
    b_im = np.asarray(inp["s5_b_im"]).transpose(0, 1, 3, 2, 4)
    c_re = np.asarray(inp["s5_c_re"]).transpose(0, 1, 4, 2, 3)
    c_im = np.asarray(inp["s5_c_im"]).transpose(0, 1, 4, 2, 3)
    sh = np.roll(np.eye(128, dtype=f), 64, axis=1)
    return {"s5_a_re": np.ascontiguousarray(dup(a_re), dtype=f), "s5_a_im": np.ascontiguousarray(dup(a_im), dtype=f),
            "s5_b_re": np.ascontiguousarray(dup(b_re), dtype=f), "s5_b_im": np.ascontiguousarray(dup(b_im), dtype=f),
            "s5_c_re": np.ascontiguousarray(dup(c_re), dtype=f), "s5_c_im": np.ascontiguousarray(dup(c_im), dtype=f),
            "s5_log_dt": np.ascontiguousarray(inp["s5_log_dt"], dtype=f),
            "s5_d": np.ascontiguousarray(np.asarray(inp["s5_d"]).reshape(-1, 8, 128).transpose(0, 2, 1), dtype=f),
            "s5_w_glu": np.ascontiguousarray(inp["s5_w_glu"], dtype=f),
            "shift64": sh, "ident": np.eye(128, dtype=f)}


import math


def build_program(NB, CTX, SEQ, shapes, depth=4, layer_list=None):
    M = Model(NB, CTX, SEQ, layers=list(range(depth)) if layer_list is None else layer_list, depth=4)
    nc = M.nc
    for k, shp in shapes.items():
        M.din(k, shp)
    M.dram["xT"] = nc.dram_tensor("xT", [1024, M.T], F32, kind="ExternalOutput").ap()
    with Pool(M.kb) as G:
        M.setup(G)
        M.kb.dma(M.dram["xT"], M.dram["xin"])
        M.kb.barrier()
        M.phase_mod()
        for i in M.layers:
            need_ctx = i < depth - 1
            kind, j = i % 3, i // 3
            if kind == 0:
                lambda_init = 0.8 - 0.6 * math.exp(-0.3 * i)
                M.phase_attn_qkv(i, j)
                M.phase_attn_core(i, j, lambda_init, need_ctx)
                M.phase_proj_post(i, "attn_w_o", j, 8, need_ctx)
            elif kind == 1:
                M.phase_s5_in(i)
                M.phase_s5_scan(i, j)
                M.phase_s5_out(i, j, need_ctx)
            else:
                M.phase_lru_in(i, j)
                M.phase_lru_scan(i, j)
                M.phase_proj_post(i, "lru_w_out", j, 12, need_ctx)
            M.phase_mlp(i, need_ctx)
        M.kb.finish()
    return M


def host_all(inp, bs, CTX, SEQ):
    d = host_common(inp, bs, CTX, SEQ)
    d.update(host_attn(inp, SEQ))
    d.update(host_s5(inp))
    d.update(host_lru(inp))
    return d


def kernel(**inputs):
    inp = {k: np.asarray(v) for k, v in inputs.items()}
    B, SEQ, _ = inp["x"].shape
    CTX = inp["ctx"].shape[1]
    ncores = 8
    NB = B // ncores
    shared = None
    in_maps = []
    for c in range(ncores):
        bs = list(range(c * NB, (c + 1) * NB))
        if shared is None:
            d = host_all(inp, bs, CTX, SEQ)
            shared = {k: v for k, v in d.items() if k not in ("xin", "cc")}
        else:
            d = dict(shared)
            d.update({k: v for k, v in host_common_x(inp, bs).items()})
        in_maps.append(d)
    shapes = {k: v.shape for k, v in in_maps[0].items()}
    M = build_program(NB, CTX, SEQ, shapes)
    res = run_bass_kernel_spmd(M.nc, in_maps, core_ids=list(range(ncores)))
    TB = CTX + SEQ
    out = np.empty((B, SEQ, 1024), np.float32)
    for c in range(ncores):
        o = np.asarray(res.results[c]["xT"])
        for bl in range(NB):
            out[c * NB + bl] = o[:, bl * TB + CTX:(bl + 1) * TB].T
    return out
```

```python
from concourse.bass_utils import run_bass_kernel_spmd
import contextlib
import numpy as np
import concourse.bass as bass
import concourse.mybir as mybir

F32 = mybir.dt.float32
BF16 = mybir.dt.bfloat16
I32 = mybir.dt.int32
AF = mybir.ActivationFunctionType
ALU = mybir.AluOpType
AX = mybir.AxisListType

NSTREAM = 12


class Tok:
    __slots__ = ("w", "r", "name", "excl")

    def __init__(self, name="", excl=False):
        self.w = None
        self.r = {}
        self.name = name
        self.excl = excl


class KB:
    def __init__(self):
        nc = bass.Bass("TRN2", target_bir_lowering=False)
        self.nc = nc
        self.E = {"pe": nc.tensor, "act": nc.scalar, "dve": nc.vector,
                  "pool": nc.gpsimd, "sp": nc.sync}
        self.sem = {}
        self.cnt = {}
        for e in ("pe", "act", "dve", "pool"):
            self.sem[e] = nc.alloc_semaphore("s_" + e)
            self.cnt[e] = 0
        for j in range(NSTREAM):
            e = ("d", j)
            self.sem[e] = nc.alloc_semaphore("s_d%d" % j)
            self.cnt[e] = 0
        self.seen = {e: {} for e in ("pe", "act", "dve", "pool", "sp")}
        self.ndma = 0
        self.nins = 0
        self.nwait = 0

    def _val(self, src, c):
        return c * 16 if isinstance(src, tuple) else c

    def _wait(self, eng, src, c):
        if c <= 0:
            return
        if self.seen[eng].get(src, 0) >= c:
            return
        self.seen[eng][src] = c
        self.E[eng].wait_ge(self.sem[src], self._val(src, c))
        self.nins += 1
        self.nwait += 1

    def _waits_attach(self, eng, need, fn):
        todo = [(src, c) for src, c in need.items() if c > 0 and self.seen[eng].get(src, 0) < c]
        for src, c in todo[:-1]:
            self._wait(eng, src, c)
        ins = fn()
        if todo:
            src, c = todo[-1]
            self.seen[eng][src] = c
            ins._wait_ge(self.sem[src], self._val(src, c))
        return ins

    def _deps(self, eng, reads, writes, same_ok=False):
        need = {}

        def add(src, c):
            if same_ok and src == eng:
                return
            if need.get(src, 0) < c:
                need[src] = c
        for t in reads:
            if t.w is not None:
                add(*t.w)
            if t.excl:
                for src, c in t.r.items():
                    if src != eng:
                        add(src, c)
        for t in writes:
            if t.w is not None:
                add(*t.w)
            for src, c in t.r.items():
                add(src, c)
        return need

    def _commit(self, me, reads, writes):
        c = self.cnt[me]
        for t in reads:
            t.r[me] = c
        for t in writes:
            t.w = (me, c)
            t.r = {}

    def op(self, eng, fn, reads=(), writes=()):
        need = self._deps(eng, reads, writes, same_ok=(eng == "pe"))
        ins = self._waits_attach(eng, need, fn)
        self.cnt[eng] += 1
        ins.then_inc(self.sem[eng], 1)
        self.nins += 1
        self._commit(eng, reads, writes)
        return ins

    def dma(self, out, in_, reads=(), writes=(), q="sp", **kw):
        j = self.ndma % NSTREAM
        self.ndma += 1
        me = ("d", j)
        need = self._deps(q, reads, writes)
        if need.get(me, 0) < self.cnt[me]:
            need[me] = self.cnt[me]
        ins = self._waits_attach(q, need, lambda: self.E[q].dma_start(out=out, in_=in_, **kw))
        self.cnt[me] += 1
        ins.then_inc(self.sem[me], 16)
        self.nins += 1
        self._commit(me, reads, writes)
        return ins

    def barrier(self):
        for eng in ("pe", "act", "dve", "pool", "sp"):
            for src, c in self.cnt.items():
                if src == eng:
                    continue
                self._wait(eng, src, c)

    def finish(self):
        for src, c in self.cnt.items():
            self._wait("sp", src, c)


class Pool:
    _uid = [0]

    def __init__(self, kb):
        self.kb = kb
        self.st = contextlib.ExitStack()
        self.n = 0
        Pool._uid[0] += 1
        self.uid = Pool._uid[0]

    def __enter__(self):
        self.st.__enter__()
        return self

    def __exit__(self, *a):
        return self.st.__exit__(*a)

    def sb(self, shape, dtype, name=None):
        self.n += 1
        nm = "%s_%d_%d" % (name or "t", self.uid, self.n)
        t = self.st.enter_context(self.kb.nc.sbuf_tensor(nm, list(shape), dtype))
        return t

    def ps(self, shape, dtype, name=None):
        self.n += 1
        nm = "%s_%d_%d" % (name or "p", self.uid, self.n)
        t = self.st.enter_context(self.kb.nc.psum_tensor(nm, list(shape), dtype))
        return t

D = 1024
KC = 8
DFF = 4096
EPS = 1e-6


class Model:
    def __init__(self, NB, CTX, SEQ, layers=(0, 1, 2, 3), depth=4, NT=256):
        self.NB, self.CTX, self.SEQ = NB, CTX, SEQ
        self.TB = CTX + SEQ
        self.T = NB * self.TB
        self.layers = list(layers)
        self.depth = depth
        self.NT = NT
        self.kb = KB()
        self.nc = self.kb.nc
        self.dram = {}

    def din(self, name, shape, dtype=F32):
        t = self.nc.dram_tensor(name, list(shape), dtype, kind="ExternalInput").ap()
        self.dram[name] = t
        return t

    def dscratch(self, name, shape, dtype):
        t = self.nc.dram_tensor(name, list(shape), dtype, kind="Internal").ap()
        self.dram[name] = t
        return t

    def tiles(self, with_ctx=True, nt=None):
        nt = nt or self.NT
        out = []
        for b in range(self.NB):
            base = b * self.TB
            if with_ctx:
                for c0 in range(0, self.CTX, nt):
                    out.append((base + c0, min(nt, self.CTX - c0), 2))
            for c0 in range(0, self.SEQ, nt):
                out.append((base + self.CTX + c0, min(nt, self.SEQ - c0), b))
        return out

    def setup(self, G):
        kb, nc = self.kb, self.nc
        self.G = G
        self.onesm = G.sb([128, 128], BF16, "onesm")
        self.t_const = Tok("const")
        kb.op("dve", lambda: nc.vector.memset(self.onesm[:], 1.0 / D), writes=[self.t_const])
        self.epsv = G.sb([128, 1], F32, "epsv")
        kb.op("dve", lambda: nc.vector.memset(self.epsv[:], EPS), writes=[self.t_const])
        self.mv = G.sb([128, self.depth, 4, KC, 3], F32, "mv")
        self.modT = G.sb([128, self.depth, 48, 3], F32, "modT")
        self.t_mv = Tok("mv")

    def phase_mod(self):
        kb, nc = self.kb, self.nc
        cc, ada_w, ada_b, ng = (self.dram[k] for k in ("cc", "ada_w", "ada_b", "norm_g"))
        with Pool(kb) as P:
            sc = P.sb([128, KC, 3], F32, "sc")
            adab = P.sb([128, self.depth, 48], F32, "adab")
            ngt = P.sb([128, self.depth, 4, KC], F32, "ngt")
            tmp = P.sb([128, KC, 3], F32, "tmp")
            wt = [P.sb([128, KC, 512], F32, "adaw%d" % j) for j in range(2)]
            ps = [P.ps([128, 512], F32, "psmod%d" % j) for j in range(2)]
            t_sc, t_ab, t_ng, t_tmp = Tok(), Tok(), Tok(), Tok()
            t_wt = [Tok(), Tok()]
            t_ps = [Tok(excl=True), Tok(excl=True)]
            kb.dma(sc[:], cc, writes=[t_sc])
            kb.dma(adab[:], ada_b, writes=[t_ab])
            kb.dma(ngt[:], ng, writes=[t_ng])
            kb.op("act", lambda: nc.scalar.activation(out=sc[:], in_=sc[:], func=AF.Silu),
                  reads=[t_sc], writes=[t_sc])
            n = 0
            for i in self.layers:
                wv = ada_w[i].rearrange("(k p) f -> p k f", p=128)
                for cg in range(12):
                    j = n % 2
                    n += 1
                    kb.dma(wt[j][:], wv[:, :, cg * 512:(cg + 1) * 512], writes=[t_wt[j]])
                    for jj in range(4):
                        for k in range(KC):
                            kb.op("pe", lambda k=k, jj=jj, j=j: nc.tensor.matmul(
                                ps[j][:, jj * 3:jj * 3 + 3], wt[j][:, k, jj * 128:(jj + 1) * 128],
                                sc[:, k, :], start=(k == 0), stop=(k == KC - 1)),
                                reads=[t_wt[j], t_sc], writes=[t_ps[j]])
                    kb.op("dve", lambda j=j, cg=cg, i=i: nc.vector.tensor_tensor(
                        out=self.modT[:, i, cg * 4:(cg + 1) * 4, :],
                        in0=ps[j][:, 0:12].rearrange("p (a b) -> p a b", b=3),
                        in1=adab[:, i, cg * 4:(cg + 1) * 4].unsqueeze(2).broadcast_to([128, 4, 3]),
                        op=ALU.add), reads=[t_ps[j], t_ab], writes=[self.t_mv])
                for kind, (c0, gi, plus1) in enumerate([(8, 0, True), (16, 1, False),
                                                        (32, 2, True), (40, 3, False)]):
                    src = self.modT[:, i, c0:c0 + 8, :]
                    if plus1:
                        kb.op("dve", lambda src=src: nc.vector.tensor_scalar(
                            out=tmp[:], in0=src, scalar1=1.0, scalar2=None, op0=ALU.add),
                            reads=[self.t_mv], writes=[t_tmp])
                        src = tmp[:]
                    kb.op("dve", lambda src=src, i=i, kind=kind, gi=gi: nc.vector.tensor_tensor(
                        out=self.mv[:, i, kind, :, :], in0=src,
                        in1=ngt[:, i, gi, :].unsqueeze(2).broadcast_to([128, KC, 3]),
                        op=ALU.mult), reads=[self.t_mv, t_tmp, t_ng], writes=[self.t_mv])
            kb.barrier()

    def mvec(self, i, kind, k, m):
        return self.mv[:, i, kind, k, m:m + 1]

    def shvec(self, i, which, k, m):
        c0 = 0 if which == 0 else 24
        return self.modT[:, i, c0 + k, m:m + 1]

    def stats_a(self, src, n, W, t_src):
        kb, nc = self.kb, self.nc
        sq = W["sq"]
        kb.op("act", lambda: nc.scalar.activation(out=sq[:, :, :n], in_=src, func=AF.Square),
              reads=[t_src], writes=[W["t_sq"]])

    def stats_b(self, n, W, ps, t_ps):
        kb, nc = self.kb, self.nc
        sq = W["sq"]
        for k in range(KC):
            kb.op("pe", lambda k=k: nc.tensor.matmul(ps[:, :n], self.onesm[:], sq[:, k, :n],
                                                     start=(k == 0), stop=(k == KC - 1)),
                  reads=[W["t_sq"], self.t_const], writes=[t_ps])

    def stats_c(self, n, W, ps, t_ps):
        kb, nc = self.kb, self.nc
        sd, rstd = W["sd"], W["rstd"]
        kb.op("act", lambda: nc.scalar.activation(out=sd[:, :n], in_=ps[:, :n], func=AF.Sqrt,
                                                  bias=self.epsv[:], scale=1.0),
              reads=[t_ps, self.t_const], writes=[W["t_sd"]])
        kb.op("dve", lambda: nc.vector.reciprocal(out=rstd[:, :n], in_=sd[:, :n]),
              reads=[W["t_sd"]], writes=[W["t_rstd"]])
        return rstd

    def rstd_of(self, src, n, W, t_src, ps, t_ps):
        self.stats_a(src, n, W, t_src)
        self.stats_b(n, W, ps, t_ps)
        return self.stats_c(n, W, ps, t_ps)

    def norm_work(self, P, NT):
        return dict(sq=P.sb([128, KC, NT], BF16, "sq"), sd=P.sb([128, NT], F32, "sd"),
                    rstd=P.sb([128, NT], F32, "rstd"), xh=P.sb([128, KC, NT], F32, "xh"),
                    t_sq=Tok(), t_sd=Tok(), t_rstd=Tok(), t_xh=Tok())

    def norm_pre(self, xt, t_x, n, i, which, m, h, t_h, W, ps, t_ps, stats_done=False):
        kb, nc = self.kb, self.nc
        rstd = W["rstd"] if stats_done else self.rstd_of(xt[:, :, :n], n, W, t_x, ps, t_ps)
        xh = W["xh"]
        kb.op("dve", lambda: nc.vector.tensor_tensor(
            out=xh[:, :, :n], in0=xt[:, :, :n],
            in1=rstd[:, :n].unsqueeze(1).broadcast_to([128, KC, n]), op=ALU.mult),
            reads=[t_x, W["t_rstd"]], writes=[W["t_xh"]])
        kindA = 0 if which == 0 else 2
        for k in range(KC):
            if k % 2 == 0:
                kb.op("act", lambda k=k: nc.scalar.activation(
                    out=h[:, k, :n], in_=xh[:, k, :n], func=AF.Identity, scale=self.mvec(i, kindA, k, m),
                    bias=self.shvec(i, which, k, m)), reads=[W["t_xh"], self.t_mv], writes=[t_h])
            else:
                kb.op("dve", lambda k=k: nc.vector.tensor_scalar(
                    out=h[:, k, :n], in0=xh[:, k, :n], scalar1=self.mvec(i, kindA, k, m),
                    scalar2=self.shvec(i, which, k, m), op0=ALU.mult, op1=ALU.add),
                    reads=[W["t_xh"], self.t_mv], writes=[t_h])

    def norm_post(self, y, t_y, xt, t_x, n, i, which, m, W, ps, t_ps, stats_done=False):
        kb, nc = self.kb, self.nc
        rstd = W["rstd"] if stats_done else self.rstd_of(y[:, :, :n], n, W, t_y, ps, t_ps)
        kb.op("dve", lambda: nc.vector.tensor_tensor(
            out=y[:, :, :n], in0=y[:, :, :n],
            in1=rstd[:, :n].unsqueeze(1).broadcast_to([128, KC, n]), op=ALU.mult),
            reads=[W["t_rstd"]], writes=[t_y])
        kindG = 1 if which == 0 else 3
        for k in range(KC):
            kb.op("dve", lambda k=k: nc.vector.scalar_tensor_tensor(
                out=xt[:, k, :n], in0=y[:, k, :n], scalar=self.mvec(i, kindG, k, m),
                in1=xt[:, k, :n], op0=ALU.mult, op1=ALU.add),
                reads=[t_y, self.t_mv], writes=[t_x])

    def phase_mlp(self, i, with_ctx):
        kb, nc = self.kb, self.nc
        NT = self.NT
        xT = self.dram["xT"].rearrange("(k p) t -> p k t", p=128)
        w1 = self.dram["mlp_w1"][i].rearrange("(k p) f -> p k f", p=128)
        w2 = self.dram["mlp_w2"][i]
        with Pool(kb) as P:
            w1s = P.sb([128, KC, DFF], BF16, "w1s")
            w2s = P.sb([128, KC, 32, 128], BF16, "w2s")
            t_w1 = [Tok() for _ in range(8)]
            t_w2 = [Tok() for _ in range(8)]
            for g in range(8):
                kb.dma(w1s[:, :, g * 512:(g + 1) * 512], w1[:, :, g * 512:(g + 1) * 512],
                       writes=[t_w1[g]], q="pool")
            for d in range(8):
                kb.dma(w2s[:, d, :, :], w2[d], writes=[t_w2[d]], q="pool")
            xt = [P.sb([128, KC, NT], F32, "xt%d" % j) for j in range(2)]
            t_xt = [Tok(), Tok()]
            h = [P.sb([128, KC, NT], BF16, "h%d" % j) for j in range(2)]
            t_h = [Tok(), Tok()]
            hid = P.sb([128, 32, NT], BF16, "hid")
            t_hid = [Tok() for _ in range(32)]
            y1 = P.sb([128, KC, NT], F32, "y")
            y = [y1, y1]
            t_y1 = Tok()
            t_y = [t_y1, t_y1]
            rl = [P.sb([128, NT], F32, "rl%d" % j) for j in range(2)]
            t_rl = [Tok(), Tok()]
            W = self.norm_work(P, NT)
            W2 = dict(sq=P.sb([128, KC, NT], BF16, "sq2"), sd=P.sb([128, NT], F32, "sd2"),
                      rstd=P.sb([128, NT], F32, "rstd2"), t_sq=Tok(), t_sd=Tok(), t_rstd=Tok())
            psn = P.ps([128, 512], F32, "psn")
            t_psn = Tok(excl=True)
            psn2 = P.ps([128, 512], F32, "psn2")
            t_psn2 = Tok(excl=True)
            psu = [P.ps([128, 512], F32, "psu%d" % j) for j in range(3)]
            t_psu = [Tok(excl=True) for _ in range(3)]
            psd = [P.ps([128, 512], F32, "psd%d" % j) for j in range(2)]
            t_psd = [Tok(excl=True) for _ in range(2)]
            tl = self.tiles(with_ctx)

            def load(ti):
                c0, n, m = tl[ti]
                kb.dma(xt[ti % 2][:, :, :n], xT[:, :, c0:c0 + n], writes=[t_xt[ti % 2]])

            def pre_a(ti):
                c0, n, m = tl[ti]
                self.stats_a(xt[ti % 2][:, :, :n], n, W, t_xt[ti % 2])

            def pre_bc(ti):
                c0, n, m = tl[ti]
                self.stats_b(n, W, psn, t_psn)
                self.stats_c(n, W, psn, t_psn)
                self.norm_pre(xt[ti % 2], t_xt[ti % 2], n, i, 1, m, h[ti % 2], t_h[ti % 2], W, psn, t_psn,
                              stats_done=True)

            def post_bc(ti):
                c0, n, m = tl[ti]
                self.stats_b(n, W2, psn2, t_psn2)
                self.stats_c(n, W2, psn2, t_psn2)
                self.norm_post(y[ti % 2], t_y[ti % 2], xt[ti % 2], t_xt[ti % 2], n, i, 1, m, W2, psn2, t_psn2,
                               stats_done=True)
                kb.dma(xT[:, :, c0:c0 + n], xt[ti % 2][:, :, :n], reads=[t_xt[ti % 2]])
            if tl:
                load(0)
                pre_a(0)
                pre_bc(0)
                if len(tl) > 1:
                    load(1)
            nu = 0
            nd = 0
            for ti, (c0, n, m) in enumerate(tl):
                j = ti % 2
                for f in range(32):
                    pj = nu % 3
                    nu += 1
                    for k in range(KC):
                        kb.op("pe", lambda k=k, f=f, pj=pj: nc.tensor.matmul(
                            psu[pj][:, :n], w1s[:, k, f * 128:(f + 1) * 128], h[j][:, k, :n],
                            start=(k == 0), stop=(k == KC - 1)),
                            reads=[t_w1[f // 4], t_h[j]], writes=[t_psu[pj]])
                    rj = f % 2
                    kb.op("act", lambda pj=pj, rj=rj: nc.scalar.activation(
                        out=rl[rj][:, :n], in_=psu[pj][:, :n], func=AF.Relu),
                        reads=[t_psu[pj]], writes=[t_rl[rj]])
                    eng = "dve" if f % 2 == 0 else "pool"
                    E = nc.vector if eng == "dve" else nc.gpsimd
                    kb.op(eng, lambda rj=rj, f=f, E=E: E.tensor_tensor(
                        out=hid[:, f, :n], in0=rl[rj][:, :n], in1=rl[rj][:, :n], op=ALU.mult),
                        reads=[t_rl[rj]], writes=[t_hid[f]])
                    if f == 3 and ti >= 1:
                        post_bc(ti - 1)
                        if ti + 1 < len(tl):
                            load(ti + 1)
                    if f == 14 and ti + 1 < len(tl):
                        pre_a(ti + 1)
                    if f == 22 and ti + 1 < len(tl):
                        pre_bc(ti + 1)
                for d in range(KC):
                    pj = nd % 2
                    nd += 1
                    for f in range(32):
                        kb.op("pe", lambda d=d, f=f, pj=pj: nc.tensor.matmul(
                            psd[pj][:, :n], w2s[:, d, f, :], hid[:, f, :n],
                            start=(f == 0), stop=(f == 31)),
                            reads=[t_w2[d], t_hid[f]], writes=[t_psd[pj]])
                    kb.op("act", lambda d=d, pj=pj, j=j: nc.scalar.copy(out=y[j][:, d, :n], in_=psd[pj][:, :n]),
                          reads=[t_psd[pj]], writes=[t_y[j]])
                self.stats_a(y[j][:, :, :n], n, W2, t_y[j])
            if tl:
                post_bc(len(tl) - 1)
            kb.barrier()


HEADS = 8


def _attn_scratch(self):
    if "QT" in self.dram:
        return
    self.dscratch("QT", [self.NB, HEADS, 2, 128, self.TB], BF16)
    self.dscratch("KT", [self.NB, HEADS, 128, self.TB], BF16)
    self.dscratch("V", [self.NB, self.TB, 1024], BF16)
    self.dscratch("OT", [1536, self.T], BF16)


def phase_attn_qkv(self, i, j):
    kb, nc = self.kb, self.nc
    _attn_scratch(self)
    NT = self.NT
    xT = self.dram["xT"].rearrange("(k p) t -> p k t", p=128)
    wq = self.dram["attn_w_qkv"][j].rearrange("(k p) f -> p k f", p=128)
    rope = self.dram["rope"]
    QT, KT, V = self.dram["QT"], self.dram["KT"], self.dram["V"]
    with Pool(kb) as P:
        wqs = P.sb([128, KC, 3072], BF16, "wqs")
        t_wq = [Tok() for _ in range(6)]
        for g in range(6):
            kb.dma(wqs[:, :, g * 512:(g + 1) * 512], wq[:, :, g * 512:(g + 1) * 512],
                   writes=[t_wq[g]], q="pool")
        nrt = self.SEQ // 128
        rp = P.sb([128, nrt, 2, 32], F32, "rp")
        rpq = P.sb([128, nrt, 2, 32], F32, "rpq")
        t_rp = Tok()
        kb.dma(rp[:], rope, writes=[t_rp])
        kb.op("act", lambda: nc.scalar.mul(out=rpq[:], in_=rp[:], mul=0.125), reads=[t_rp], writes=[t_rp])
        ident = P.sb([128, 128], BF16, "ident")
        t_id = Tok()
        kb.dma(ident[:], self.dram["ident"], writes=[t_id], q="pool")
        xt = [P.sb([128, KC, NT], F32, "xt%d" % q) for q in range(2)]
        t_xt = [Tok(), Tok()]
        h = P.sb([128, KC, NT], BF16, "h")
        t_h = Tok()
        W = self.norm_work(P, NT)
        psn = P.ps([128, 512], F32, "psn")
        t_psn = Tok(excl=True)
        psqk = [P.ps([128, 1024], F32, "psqk%d" % q) for q in range(2)]
        t_psqk = [Tok(excl=True), Tok(excl=True)]
        psv = [P.ps([128, 512], F32, "psv%d" % q) for q in range(2)]
        t_psv = [Tok(excl=True), Tok(excl=True)]
        pst = P.ps([128, 8, 128], BF16, "pst")
        t_pst = Tok(excl=True)
        qk = [P.sb([128, 1024], BF16, "qk%d" % q) for q in range(2)]
        t_qk = [Tok(), Tok()]
        ta = P.sb([128, 16, 32], F32, "ta")
        tb = P.sb([128, 16, 32], F32, "tb")
        t_ta, t_tb = Tok(), Tok()
        vst = [P.sb([128, 1024], BF16, "vst%d" % q) for q in range(2)]
        t_vst = [Tok(), Tok()]
        qz = [P.sb([128, HEADS, NT], BF16, "qz%d" % c) for c in range(2)]
        t_qz = [Tok(), Tok()]
        kz = P.sb([128, HEADS, NT], BF16, "kz")
        t_kz = Tok()
        kb.op("pool", lambda: nc.gpsimd.memset(qz[0][:], 0.0), writes=[t_qz[0]])
        kb.op("pool", lambda: nc.gpsimd.memset(qz[1][:], 0.0), writes=[t_qz[1]])
        tl = self.tiles(True)
        c0, n, m = tl[0]
        kb.dma(xt[0][:, :, :n], xT[:, :, c0:c0 + n], writes=[t_xt[0]])
        nv = 0
        for ti, (c0, n, m) in enumerate(tl):
            jx = ti % 2
            if ti + 1 < len(tl):
                c1, n1, _ = tl[ti + 1]
                kb.dma(xt[1 - jx][:, :, :n1], xT[:, :, c1:c1 + n1], writes=[t_xt[1 - jx]])
            self.norm_pre(xt[jx], t_xt[jx], n, i, 0, m, h, t_h, W, psn, t_psn)
            b = c0 // self.TB
            pos0 = c0 - b * self.TB
            is_ctx = (m == 2)
            for s in range(n // 128):
                hs = lambda k: h[:, k, s * 128:(s + 1) * 128]
                for half in range(2):
                    pj = nv % 2
                    nv += 1
                    for k in range(KC):
                        kb.op("pe", lambda k=k, half=half, pj=pj: nc.tensor.matmul(
                            psv[pj][:, :], hs(k), wqs[:, k, 2048 + half * 512:2048 + (half + 1) * 512],
                            start=(k == 0), stop=(k == KC - 1)),
                            reads=[t_h, t_wq[4 + half]], writes=[t_psv[pj]])
                    kb.op("act", lambda half=half, pj=pj, s=s: nc.scalar.copy(
                        out=vst[s % 2][:, half * 512:(half + 1) * 512], in_=psv[pj][:, :]),
                        reads=[t_psv[pj]], writes=[t_vst[s % 2]])
                r0 = b * self.TB + pos0 + s * 128
                kb.dma(V[b, pos0 + s * 128:pos0 + (s + 1) * 128, :], vst[s % 2][:], reads=[t_vst[s % 2]])
                for which in range(2):
                    ps = psqk[which]
                    tps = t_psqk[which]
                    for half in range(2):
                        for k in range(KC):
                            kb.op("pe", lambda k=k, half=half, which=which, ps=ps: nc.tensor.matmul(
                                ps[:, half * 512:(half + 1) * 512], hs(k),
                                wqs[:, k, which * 1024 + half * 512:which * 1024 + (half + 1) * 512],
                                start=(k == 0), stop=(k == KC - 1)),
                                reads=[t_h, t_wq[which * 2 + half]], writes=[tps])
                    dst = qk[which]
                    if is_ctx:
                        kb.op("act", lambda ps=ps, dst=dst, which=which: nc.scalar.mul(
                            out=dst[:], in_=ps[:], mul=(0.125 if which == 0 else 1.0)),
                            reads=[tps], writes=[t_qk[which]])
                    else:
                        lt = (pos0 - self.CTX) // 128 + s
                        tab = rpq if which == 0 else rp
                        cs = tab[:, lt, 0, :].unsqueeze(1).broadcast_to([128, 16, 32])
                        sn = tab[:, lt, 1, :].unsqueeze(1).broadcast_to([128, 16, 32])
                        pv = ps[:, :].rearrange("p (a two f) -> p a two f", two=2, f=32)
                        dv = dst[:, :].rearrange("p (a two f) -> p a two f", two=2, f=32)
                        t1, t2 = pv[:, :, 0, :], pv[:, :, 1, :]
                        kb.op("dve", lambda: nc.vector.tensor_tensor(out=ta[:], in0=t1, in1=cs, op=ALU.mult),
                              reads=[tps, t_rp], writes=[t_ta])
                        kb.op("dve", lambda: nc.vector.tensor_tensor(out=tb[:], in0=t2, in1=sn, op=ALU.mult),
                              reads=[tps, t_rp], writes=[t_tb])
                        kb.op("dve", lambda: nc.vector.tensor_tensor(out=dv[:, :, 0, :], in0=ta[:], in1=tb[:],
                                                                     op=ALU.subtract),
                              reads=[t_ta, t_tb], writes=[t_qk[which]])
                        kb.op("dve", lambda: nc.vector.tensor_tensor(out=ta[:], in0=t1, in1=sn, op=ALU.mult),
                              reads=[tps, t_rp], writes=[t_ta])
                        kb.op("dve", lambda: nc.vector.tensor_tensor(out=tb[:], in0=t2, in1=cs, op=ALU.mult),
                              reads=[tps, t_rp], writes=[t_tb])
                        kb.op("dve", lambda: nc.vector.tensor_tensor(out=dv[:, :, 1, :], in0=ta[:], in1=tb[:],
                                                                     op=ALU.add),
                              reads=[t_ta, t_tb], writes=[t_qk[which]])
                    for hd in range(HEADS):
                        kb.op("pe", lambda hd=hd, dst=dst: nc.tensor.transpose(
                            out=pst[:, hd, :], in_=dst[:, hd * 128:(hd + 1) * 128], identity=ident[:]),
                            reads=[t_qk[which], t_id], writes=[t_pst])
                    sl = slice(s * 128, (s + 1) * 128)
                    if which == 0:
                        kb.op("act", lambda sl=sl: nc.scalar.copy(out=qz[0][0:64, :, sl], in_=pst[0:64, :, :]),
                              reads=[t_pst], writes=[t_qz[0]])
                        kb.op("act", lambda sl=sl: nc.scalar.copy(out=qz[1][64:128, :, sl], in_=pst[64:128, :, :]),
                              reads=[t_pst], writes=[t_qz[1]])
                    else:
                        kb.op("act", lambda sl=sl: nc.scalar.copy(out=kz[:, :, sl], in_=pst[:, :, :]),
                              reads=[t_pst], writes=[t_kz])
            for c in range(2):
                kb.dma(QT[b, :, c, :, pos0:pos0 + n].rearrange("h p t -> p h t"), qz[c][:, :, :n],
                       reads=[t_qz[c]])
            kb.dma(KT[b, :, :, pos0:pos0 + n].rearrange("h p t -> p h t"), kz[:, :, :n], reads=[t_kz])
        kb.barrier()


def phase_attn_core(self, i, j, lambda_init, need_ctx):
    kb, nc = self.kb, self.nc
    QT, KT, V, OT = self.dram["QT"], self.dram["KT"], self.dram["V"], self.dram["OT"]
    TB, CTX = self.TB, self.CTX
    nkt = TB // 128
    QG = 256
    with Pool(kb) as P:
        kts = P.sb([128, HEADS, TB], BF16, "kts")
        vs = P.sb([128, nkt, HEADS, 130], BF16, "vs")
        t_kts, t_vs = Tok(), Tok()
        ident = P.sb([128, 128], BF16, "ident")
        t_id = Tok()
        kb.dma(ident[:], self.dram["ident"], writes=[t_id], q="pool")
        lam = P.sb([128, 4, 64], F32, "lam")
        lt = P.sb([128, 2, 64], F32, "lt")
        ls = P.sb([128, 2], F32, "ls")
        nlam = P.sb([128, 1], F32, "nlam")
        gsb = P.sb([128, 128], F32, "gsb")
        t_l = Tok()
        kb.dma(lam[:], self.dram["attn_lambda"][j].partition_broadcast(128), writes=[t_l])
        kb.dma(gsb[:], self.dram["attn_subln"][j].partition_broadcast(128), writes=[t_l])
        kb.op("dve", lambda: nc.vector.tensor_tensor(out=lt[:], in0=lam[:, 0::2, :], in1=lam[:, 1::2, :],
                                                     op=ALU.mult), reads=[t_l], writes=[t_l])
        kb.op("dve", lambda: nc.vector.tensor_reduce(out=ls[:], in_=lt[:], op=ALU.add, axis=AX.X),
              reads=[t_l], writes=[t_l])
        kb.op("act", lambda: nc.scalar.activation(out=ls[:], in_=ls[:], func=AF.Exp), reads=[t_l], writes=[t_l])
        kb.op("dve", lambda: nc.vector.tensor_tensor(out=nlam[:], in0=ls[:, 1:2], in1=ls[:, 0:1], op=ALU.subtract),
              reads=[t_l], writes=[t_l])
        kb.op("dve", lambda: nc.vector.tensor_scalar(out=nlam[:], in0=nlam[:], scalar1=-lambda_init, scalar2=None,
                                                     op0=ALU.add), reads=[t_l], writes=[t_l])
        kb.op("act", lambda: nc.scalar.mul(out=gsb[:], in_=gsb[:], mul=1.0 - lambda_init), reads=[t_l], writes=[t_l])
        epsv = self.epsv
        qs_ = [P.sb([128, HEADS, 2, QG], BF16, "qs%d" % q) for q in range(2)]
        t_qs = [Tok(), Tok()]
        NPS = 3
        pss = [P.ps([128, 2, QG], F32, "pss%d" % q) for q in range(NPS)]
        t_pss = [Tok(excl=True) for _ in range(NPS)]
        accs = P.sb([128, 2, 2, 129], F32, "accs")
        t_accs = Tok()
        psa = [[P.ps([128, 512], F32, "psa%d%d" % (c, q)) for q in range(2)] for c in range(2)]
        t_psa = [[Tok(excl=True), Tok(excl=True)], [Tok(excl=True), Tok(excl=True)]]
        pst = P.ps([128, 128], BF16, "pst")
        t_pst = Tok(excl=True)
        es = [P.sb([128, 2, QG], BF16, "es%d" % q) for q in range(3)]
        t_es = [Tok() for _ in range(3)]
        mhalf = P.sb([128, 1], F32, "mhalf")
        kb.op("dve", lambda: nc.vector.memset(mhalf[:], -0.5), writes=[t_l])
        rz = P.sb([128, 2], F32, "rz")
        t_rz = Tok()
        o1 = P.sb([128, 128], F32, "o1")
        o2 = P.sb([128, 128], F32, "o2")
        junk = P.sb([128, 128], F32, "junk")
        ss = P.sb([128, 1], F32, "ss")
        on = P.sb([128, 128], BF16, "on")
        t_o1, t_o2, t_ss, t_on = Tok(), Tok(), Tok(), Tok()
        ots = [P.sb([128, HEADS, QG], BF16, "ots%d" % q) for q in range(2)]
        t_ots = [Tok(), Tok()]
        ne = 0
        ng = 0
        for b in range(self.NB):
            kb.dma(kts[:], KT[b].rearrange("h p t -> p h t"), writes=[t_kts])
            for hh in range(HEADS):
                kb.dma(vs[:, :, hh, 0:128],
                       V[b, :, hh * 128:(hh + 1) * 128].rearrange("(kt p) e -> p kt e", p=128), writes=[t_vs])
            kb.op("pool", lambda: nc.gpsimd.memset(vs[:, :, :, 128:129], 1.0), writes=[t_vs])
            groups = []
            if need_ctx:
                for q0 in range(0, CTX, QG):
                    groups.append((q0, min(QG, CTX - q0), list(range(CTX // 128))))
            for q0 in range(0, self.SEQ, QG):
                groups.append((CTX + q0, QG, list(range(nkt))))
            for (q0, nq, ktl) in groups:
                gj = ng % 2
                ng += 1
                kb.dma(qs_[gj][:, :, :, :nq], QT[b, :, :, :, q0:q0 + nq].rearrange("h c p t -> p h c t"),
                       writes=[t_qs[gj]])
                nsub = nq // 128
                its = [(hh, ki, kt) for hh in range(HEADS) for ki, kt in enumerate(ktl)]
                nk = len(ktl)

                def emit_scores(idx, gj=gj, nq=nq, its=its):
                    hh, ki, kt = its[idx]
                    pj = (ne0 + idx) % NPS
                    for c in range(2):
                        kb.op("pe", lambda c=c: nc.tensor.matmul(
                            pss[pj][:, c, :nq], kts[:, hh, kt * 128:(kt + 1) * 128], qs_[gj][:, hh, c, :nq],
                            start=True, stop=True), reads=[t_kts, t_qs[gj]], writes=[t_pss[pj]])

                def emit_exp(idx, nq=nq):
                    pj = (ne0 + idx) % NPS
                    ej = (ne0 + idx) % 3
                    kb.op("act", lambda: nc.scalar.activation(
                        out=es[ej][:, :, :nq], in_=pss[pj][:, :, :nq], func=AF.Exp),
                        reads=[t_pss[pj]], writes=[t_es[ej]])

                def emit_pv(idx, nsub=nsub, its=its, nk=nk):
                    hh, ki, kt = its[idx]
                    ej = (ne0 + idx) % 3
                    for c in range(2):
                        for sq in range(nsub):
                            kb.op("pe", lambda c=c, sq=sq: nc.tensor.matmul(
                                psa[c][sq][:, 0:129], es[ej][:, c, sq * 128:(sq + 1) * 128],
                                vs[:, kt, hh, 0:129], start=(ki == 0), stop=(ki == nk - 1)),
                                reads=[t_es[ej], t_vs], writes=[t_psa[c][sq]])

                def finalize(hh, gj=gj, nsub=nsub):
                    for c in range(2):
                        for sq in range(nsub):
                            kb.op("dve", lambda c=c, sq=sq: nc.vector.tensor_copy(
                                out=accs[:, c, sq, :], in_=psa[c][sq][:, 0:129]),
                                reads=[t_psa[c][sq]], writes=[t_accs])
                    for sq in range(nsub):
                        kb.op("dve", lambda sq=sq: nc.vector.reciprocal(out=rz[:, 0:2], in_=accs[:, :, sq, 128]),
                              reads=[t_accs], writes=[t_rz])
                        kb.op("dve", lambda: nc.vector.tensor_tensor(out=rz[:, 1:2], in0=rz[:, 1:2], in1=nlam[:],
                                                                     op=ALU.mult), reads=[t_l], writes=[t_rz])
                        kb.op("dve", lambda sq=sq: nc.vector.tensor_scalar(
                            out=o1[:], in0=accs[:, 0, sq, 0:128], scalar1=rz[:, 0:1], scalar2=None, op0=ALU.mult),
                            reads=[t_accs, t_rz], writes=[t_o1])
                        kb.op("dve", lambda sq=sq: nc.vector.scalar_tensor_tensor(
                            out=o2[:], in0=accs[:, 1, sq, 0:128], scalar=rz[:, 1:2], in1=o1[:],
                            op0=ALU.mult, op1=ALU.add), reads=[t_accs, t_rz, t_o1], writes=[t_o2])
                        kb.op("dve", lambda: nc.vector.scalar_tensor_tensor(
                            out=junk[:], in0=o2[:], scalar=1.0, in1=o2[:], op0=ALU.mult, op1=ALU.mult,
                            accum_out=ss[:]), reads=[t_o2], writes=[t_ss])
                        kb.op("dve", lambda: nc.vector.tensor_scalar(
                            out=ss[:], in0=ss[:], scalar1=1.0 / 128.0, scalar2=EPS, op0=ALU.mult, op1=ALU.add),
                            writes=[t_ss])
                        kb.op("pool", lambda: nc.gpsimd.tensor_tensor(out=ss[:], in0=ss[:], in1=mhalf[:], op=ALU.pow),
                              reads=[t_l], writes=[t_ss])
                        kb.op("dve", lambda: nc.vector.scalar_tensor_tensor(
                            out=on[:], in0=o2[:], scalar=ss[:, 0:1], in1=gsb[:], op0=ALU.mult, op1=ALU.mult),
                            reads=[t_o2, t_ss, t_l], writes=[t_on])
                        kb.op("pe", lambda: nc.tensor.transpose(out=pst[:], in_=on[:], identity=ident[:]),
                              reads=[t_on, t_id], writes=[t_pst])
                        kb.op("dve", lambda sq=sq: nc.vector.tensor_copy(
                            out=ots[gj][:, hh, sq * 128:(sq + 1) * 128], in_=pst[:]),
                            reads=[t_pst], writes=[t_ots[gj]])

                ne0 = ne
                AH = 2
                for a in range(min(AH, len(its))):
                    emit_scores(a)
                for idx in range(len(its)):
                    emit_exp(idx)
                    if idx + AH < len(its):
                        emit_scores(idx + AH)
                    emit_pv(idx)
                    if its[idx][1] == nk - 1:
                        finalize(its[idx][0])
                ne += len(its)
                col = b * TB + q0
                kb.dma(OT[0:1024, col:col + nq].rearrange("(h p) t -> p h t", p=128), ots[gj][:, :, :nq],
                       reads=[t_ots[gj]])
        kb.barrier()


def phase_proj_post(self, i, wname, widx, kc, with_ctx, glu=False):
    kb, nc = self.kb, self.nc
    NT = self.NT
    xT = self.dram["xT"].rearrange("(k p) t -> p k t", p=128)
    OT = self.dram["OT"].rearrange("(k p) t -> p k t", p=128)
    wd = self.dram[wname][widx].rearrange("(k p) d -> p k d", p=128)
    ncol = 2048 if glu else 1024
    with Pool(kb) as P:
        ws = P.sb([128, kc, ncol], BF16, "ws")
        t_w = [Tok() for _ in range(kc)]
        for k in range(kc):
            for hf in range(ncol // 1024):
                kb.dma(ws[:, k, hf * 1024:(hf + 1) * 1024], wd[:, k, hf * 1024:(hf + 1) * 1024],
                       writes=[t_w[k]], q="pool")
        xt = [P.sb([128, KC, NT], F32, "xt%d" % q) for q in range(2)]
        t_xt = [Tok(), Tok()]
        at = [P.sb([128, kc, NT], BF16, "at%d" % q) for q in range(2)]
        t_at = [Tok(), Tok()]
        y = P.sb([128, KC, NT], F32, "y")
        t_y = Tok()
        sg = P.sb([128, NT], F32, "sg")
        t_sg = Tok()
        W = self.norm_work(P, NT)
        psn = P.ps([128, 512], F32, "psn")
        t_psn = Tok(excl=True)
        psd = [P.ps([128, 512], F32, "psd%d" % q) for q in range(4)]
        t_psd = [Tok(excl=True) for _ in range(4)]
        tl = self.tiles(with_ctx)
        c0, n, m = tl[0]
        kb.dma(xt[0][:, :, :n], xT[:, :, c0:c0 + n], writes=[t_xt[0]])
        kb.dma(at[0][:, :, :n], OT[:, 0:kc, c0:c0 + n], writes=[t_at[0]])
        nd = 0
        for ti, (c0, n, m) in enumerate(tl):
            jx = ti % 2
            if ti + 1 < len(tl):
                c1, n1, _ = tl[ti + 1]
                kb.dma(xt[1 - jx][:, :, :n1], xT[:, :, c1:c1 + n1], writes=[t_xt[1 - jx]])
                kb.dma(at[1 - jx][:, :, :n1], OT[:, 0:kc, c1:c1 + n1], writes=[t_at[1 - jx]])
            for d in range(KC):
                pj = nd % 2
                nd += 1
                for k in range(kc):
                    kb.op("pe", lambda d=d, k=k, pj=pj: nc.tensor.matmul(
                        psd[pj][:, :n], ws[:, k, d * 128:(d + 1) * 128], at[jx][:, k, :n],
                        start=(k == 0), stop=(k == kc - 1)), reads=[t_w[k], t_at[jx]], writes=[t_psd[pj]])
                if glu:
                    for k in range(kc):
                        kb.op("pe", lambda d=d, k=k, pj=pj: nc.tensor.matmul(
                            psd[2 + pj][:, :n], ws[:, k, 1024 + d * 128:1024 + (d + 1) * 128], at[jx][:, k, :n],
                            start=(k == 0), stop=(k == kc - 1)), reads=[t_w[k], t_at[jx]], writes=[t_psd[2 + pj]])
                    kb.op("act", lambda pj=pj: nc.scalar.activation(out=sg[:, :n], in_=psd[2 + pj][:, :n],
                                                                    func=AF.Sigmoid),
                          reads=[t_psd[2 + pj]], writes=[t_sg])
                    kb.op("dve", lambda d=d, pj=pj: nc.vector.tensor_tensor(
                        out=y[:, d, :n], in0=psd[pj][:, :n], in1=sg[:, :n], op=ALU.mult),
                        reads=[t_psd[pj], t_sg], writes=[t_y])
                else:
                    kb.op("act", lambda d=d, pj=pj: nc.scalar.copy(out=y[:, d, :n], in_=psd[pj][:, :n]),
                          reads=[t_psd[pj]], writes=[t_y])
            self.norm_post(y, t_y, xt[jx], t_xt[jx], n, i, 0, m, W, psn, t_psn)
            kb.dma(xT[:, :, c0:c0 + n], xt[jx][:, :, :n], reads=[t_xt[jx]])
        kb.barrier()


Model.phase_attn_qkv = phase_attn_qkv
Model.phase_attn_core = phase_attn_core
Model.phase_proj_post = phase_proj_post


LW = 1536
LC = 12


def gelu_tanh(self, dst, src, t_src, shape, Wg, t_dst):
    kb, nc = self.kb, self.nc
    a, b_ = Wg["a"], Wg["b"]
    va = a[:, :shape[1]] if len(shape) == 2 else a[:, :shape[1], :shape[2]]
    vb = b_[:, :shape[1]] if len(shape) == 2 else b_[:, :shape[1], :shape[2]]
    kb.op("act", lambda: nc.scalar.activation(out=va, in_=src, func=AF.Square), reads=[t_src], writes=[Wg["ta"]])
    kb.op("dve", lambda: nc.vector.tensor_scalar(out=va, in0=va, scalar1=0.044715, scalar2=1.0,
                                                 op0=ALU.mult, op1=ALU.add), writes=[Wg["ta"]])
    kb.op("dve", lambda: nc.vector.tensor_tensor(out=va, in0=va, in1=src, op=ALU.mult),
          reads=[t_src], writes=[Wg["ta"]])
    kb.op("act", lambda: nc.scalar.activation(out=vb, in_=va, func=AF.Sigmoid, scale=1.5957691216057308),
          reads=[Wg["ta"]], writes=[Wg["tb"]])
    kb.op("dve", lambda: nc.vector.tensor_tensor(out=dst, in0=vb, in1=src, op=ALU.mult),
          reads=[Wg["tb"], t_src], writes=[t_dst])


def _lru_scratch(self):
    _attn_scratch(self)
    if "RT" not in self.dram:
        self.dscratch("RT", [LW, self.T], F32)
        self.dscratch("GT", [LW, self.T], BF16)


def phase_lru_in(self, i, j):
    kb, nc = self.kb, self.nc
    _lru_scratch(self)
    NT = self.NT
    xT = self.dram["xT"].rearrange("(k p) t -> p k t", p=128)
    win = self.dram["lru_w_in"][j].rearrange("(k p) f -> p k f", p=128)
    RT = self.dram["RT"].rearrange("(k p) t -> p k t", p=128)
    GT = self.dram["GT"].rearrange("(k p) t -> p k t", p=128)
    with Pool(kb) as P:
        ws = P.sb([128, KC, 2 * LW], BF16, "wins")
        t_w = [Tok() for _ in range(6)]
        for g in range(6):
            kb.dma(ws[:, :, g * 512:(g + 1) * 512], win[:, :, g * 512:(g + 1) * 512], writes=[t_w[g]], q="pool")
        xt = [P.sb([128, KC, NT], F32, "xt%d" % q) for q in range(2)]
        t_xt = [Tok(), Tok()]
        h = P.sb([128, KC, NT], BF16, "h")
        t_h = Tok()
        W = self.norm_work(P, NT)
        psn = P.ps([128, 512], F32, "psn")
        t_psn = Tok(excl=True)
        pso = [P.ps([128, 2, 256], F32, "pso%d" % q) for q in range(3)]
        t_pso = [Tok(excl=True) for _ in range(3)]
        Wg = dict(a=P.sb([128, 2, 256], F32, "ga"), b=P.sb([128, 2, 256], F32, "gb"), ta=Tok(), tb=Tok())
        gs = [P.sb([128, LC, NT], BF16, "gs%d" % q) for q in range(2)]
        t_gs = [Tok(), Tok()]
        rs = [P.sb([128, LC, NT], F32, "rs%d" % q) for q in range(2)]
        t_rs = [Tok(), Tok()]
        tl = self.tiles(True)
        c0, n, m = tl[0]
        kb.dma(xt[0][:, :, :n], xT[:, :, c0:c0 + n], writes=[t_xt[0]])
        np_ = 0
        for ti, (c0, n, m) in enumerate(tl):
            jx = ti % 2
            if ti + 1 < len(tl):
                c1, n1, _ = tl[ti + 1]
                kb.dma(xt[1 - jx][:, :, :n1], xT[:, :, c1:c1 + n1], writes=[t_xt[1 - jx]])
            self.norm_pre(xt[jx], t_xt[jx], n, i, 0, m, h, t_h, W, psn, t_psn)
            for op in range(LC):
                pj = np_ % 3
                np_ += 1
                for q2 in range(2):
                    oc = op * 2 + q2
                    for k in range(KC):
                        kb.op("pe", lambda k=k, oc=oc, q2=q2, pj=pj: nc.tensor.matmul(
                            pso[pj][:, q2, :n], ws[:, k, oc * 128:(oc + 1) * 128], h[:, k, :n],
                            start=(k == 0), stop=(k == KC - 1)), reads=[t_w[oc // 4], t_h], writes=[t_pso[pj]])
                if op < 6:
                    gelu_tanh(self, gs[jx][:, 2 * op:2 * op + 2, :n], pso[pj][:, :, :n], t_pso[pj],
                              [128, 2, n], Wg, t_gs[jx])
                else:
                    kb.op("act", lambda op=op, pj=pj: nc.scalar.copy(
                        out=rs[jx][:, 2 * (op - 6):2 * (op - 6) + 2, :n], in_=pso[pj][:, :, :n]),
                        reads=[t_pso[pj]], writes=[t_rs[jx]])
            kb.dma(GT[:, :, c0:c0 + n], gs[jx][:, :, :n], reads=[t_gs[jx]])
            kb.dma(RT[:, :, c0:c0 + n], rs[jx][:, :, :n], reads=[t_rs[jx]])
        kb.barrier()


def phase_lru_scan(self, i, j):
    kb, nc = self.kb, self.nc
    TB, CTX, SEQ = self.TB, self.CTX, self.SEQ
    RT = self.dram["RT"].rearrange("(k p) t -> p k t", p=128)
    GT = self.dram["GT"].rearrange("(k p) t -> p k t", p=128)
    OT = self.dram["OT"].rearrange("(k p) t -> p k t", p=128)
    CW = 512
    with Pool(kb) as P:
        wg = P.sb([128, 2, 2, 6, 2, 256], BF16, "wg")
        cw = P.sb([128, LC, 4], F32, "cw")
        cb = P.sb([128, LC], F32, "cb")
        bg = P.sb([128, 2, 2, LC], F32, "bg")
        ap_ = P.sb([128, 2, LC], F32, "ap")
        cv = P.sb([128, 2, LC], F32, "cv")
        cv2 = P.sb([128, 2, LC], F32, "cv2")
        one = P.sb([128, 1], F32, "one")
        t_s = Tok()
        kb.dma(wg[:], self.dram["lru_w_gate"][j], writes=[t_s], q="pool")
        kb.dma(cw[:], self.dram["lru_conv_w"][j], writes=[t_s])
        kb.dma(cb[:], self.dram["lru_conv_b"][j], writes=[t_s])
        kb.dma(bg[:], self.dram["lru_b_gate"][j], writes=[t_s])
        kb.dma(ap_[:], self.dram["lru_a_param"][j], writes=[t_s])
        kb.op("dve", lambda: nc.vector.memset(one[:], 1.0), writes=[t_s])
        kb.op("act", lambda: nc.scalar.activation(out=cv[:], in_=ap_[:], func=AF.Exp, scale=-1.0),
              reads=[t_s], writes=[t_s])
        kb.op("act", lambda: nc.scalar.activation(out=cv[:], in_=cv[:], func=AF.Ln, bias=one[:], scale=1.0),
              reads=[t_s], writes=[t_s])
        cvh = cv2
        bgh = P.sb([128, 2, 2, LC], F32, "bgh")
        kb.op("dve", lambda: nc.vector.tensor_scalar(out=cvh[:], in0=cv[:], scalar1=-4.0, scalar2=None,
                                                     op0=ALU.mult), reads=[t_s], writes=[t_s])
        kb.op("dve", lambda: nc.vector.tensor_scalar(out=cv[:], in0=cv[:], scalar1=-8.0, scalar2=None,
                                                     op0=ALU.mult), reads=[t_s], writes=[t_s])
        kb.op("dve", lambda: nc.vector.tensor_scalar(out=bgh[:], in0=bg[:], scalar1=0.5, scalar2=None,
                                                     op0=ALU.mult), reads=[t_s], writes=[t_s])
        rt = P.sb([128, TB], F32, "rt")
        t_rt = Tok()
        u = P.sb([128, 2, TB], F32, "u")
        t_u = [Tok(), Tok()]
        ub = P.sb([128, 2, TB], BF16, "ub")
        t_ub = [Tok(), Tok()]
        av = P.sb([128, TB], F32, "av")
        bv = P.sb([128, TB], F32, "bv")
        t_av, t_bv = Tok(), Tok()
        hf = P.sb([128, TB], F32, "hf")
        hb = P.sb([128, TB], F32, "hb")
        t_hf, t_hb = Tok(), Tok()
        gt = P.sb([128, TB], BF16, "gt")
        t_gt = Tok()
        mo = P.sb([128, TB], BF16, "mo")
        t_mo = Tok()
        psg = [P.ps([128, 2, CW], F32, "psg%d" % q) for q in range(2)]
        t_psg = [Tok(excl=True), Tok(excl=True)]
        sr = [P.sb([128, 2, CW], F32, "sr%d" % q) for q in range(2)]
        t_sr = [Tok(), Tok()]
        a2 = [P.sb([128, CW], F32, "a2%d" % q) for q in range(2)]
        t_a2 = [Tok(), Tok()]
        segs = [(0, CTX), (CTX, TB)]
        ng = 0
        for b in range(self.NB):
            cb0 = b * TB
            for n6 in range(6):
                for q2 in range(2):
                    ch = n6 * 2 + q2
                    kb.dma(rt[:], RT[:, ch, cb0:cb0 + TB], writes=[t_rt])
                    for (s0, s1) in segs:
                        kb.op("dve", lambda s0=s0, s1=s1, ch=ch, q2=q2: nc.vector.tensor_scalar(
                            out=u[:, q2, s0:s1], in0=rt[:, s0:s1], scalar1=cw[:, ch, 2:3], scalar2=cb[:, ch:ch + 1],
                            op0=ALU.mult, op1=ALU.add), reads=[t_rt, t_s], writes=[t_u[q2]])
                        for (tap, off) in ((0, -2), (1, -1), (3, 1)):
                            if off < 0:
                                o_sl, i_sl = slice(s0 - off, s1), slice(s0, s1 + off)
                            else:
                                o_sl, i_sl = slice(s0, s1 - off), slice(s0 + off, s1)
                            kb.op("dve", lambda o_sl=o_sl, i_sl=i_sl, ch=ch, q2=q2, tap=tap:
                                  nc.vector.scalar_tensor_tensor(
                                      out=u[:, q2, o_sl], in0=rt[:, i_sl], scalar=cw[:, ch, tap:tap + 1],
                                      in1=u[:, q2, o_sl], op0=ALU.mult, op1=ALU.add),
                                  reads=[t_rt, t_s], writes=[t_u[q2]])
                    kb.op("act", lambda q2=q2: nc.scalar.copy(out=ub[:, q2, :], in_=u[:, q2, :]),
                          reads=[t_u[q2]], writes=[t_ub[q2]])
                    kb.op("pool", lambda q2=q2: nc.gpsimd.tensor_scalar(
                        out=u[:, q2, :], in0=u[:, q2, :], scalar1=0.5, scalar2=None, op0=ALU.mult),
                        reads=[t_ub[q2]], writes=[t_u[q2]])
                for q2 in range(2):
                    ch = n6 * 2 + q2
                    kb.dma(gt[:], GT[:, ch, cb0:cb0 + TB], writes=[t_gt])
                    for d in range(2):
                        for c0 in range(0, TB, CW):
                            n = min(CW, TB - c0)
                            pj = ng % 2
                            ng += 1
                            for kk in range(2):
                                for kc in range(2):
                                    kb.op("pe", lambda kk=kk, kc=kc, d=d, c0=c0, n=n, pj=pj: nc.tensor.matmul(
                                        psg[pj][:, kk, :n], wg[:, d, kk, n6, kc, q2 * 128:(q2 + 1) * 128],
                                        ub[:, kc, c0:c0 + n], start=(kc == 0), stop=(kc == 1)),
                                        reads=[t_s] + t_ub, writes=[t_psg[pj]])
                            for kk in range(2):
                                kb.op("act", lambda kk=kk, d=d, n=n, pj=pj: nc.scalar.activation(
                                    out=sr[pj][:, kk, :n], in_=psg[pj][:, kk, :n], func=AF.Tanh,
                                    bias=bgh[:, d, kk, ch:ch + 1], scale=0.5),
                                    reads=[t_psg[pj], t_s], writes=[t_sr[pj]])
                            kb.op("act", lambda d=d, c0=c0, n=n, pj=pj: nc.scalar.activation(
                                out=av[:, c0:c0 + n], in_=sr[pj][:, 0, :n], func=AF.Exp,
                                scale=cvh[:, d, ch:ch + 1], bias=cvh[:, d, ch:ch + 1]),
                                reads=[t_sr[pj], t_s], writes=[t_av])
                            kb.op("act", lambda d=d, n=n, pj=pj: nc.scalar.activation(
                                out=a2[pj][:, :n], in_=sr[pj][:, 0, :n], func=AF.Exp,
                                scale=cv[:, d, ch:ch + 1], bias=cv[:, d, ch:ch + 1]),
                                reads=[t_sr[pj], t_s], writes=[t_a2[pj]])
                            kb.op("pool", lambda n=n, pj=pj, c0=c0: nc.gpsimd.tensor_scalar(
                                out=bv[:, c0:c0 + n], in0=a2[pj][:, :n], scalar1=-1.0, scalar2=1.0,
                                op0=ALU.mult, op1=ALU.add), reads=[t_a2[pj]], writes=[t_bv])
                            kb.op("dve", lambda n=n, pj=pj, c0=c0: nc.vector.scalar_tensor_tensor(
                                out=rt[:, c0:c0 + n], in0=sr[pj][:, 1, :n], scalar=1.0, in1=u[:, q2, c0:c0 + n],
                                op0=ALU.add, op1=ALU.mult), reads=[t_sr[pj], t_u[q2]], writes=[t_rt])
                        kb.op("act", lambda: nc.scalar.activation(out=bv[:, :], in_=bv[:, :], func=AF.Sqrt),
                              writes=[t_bv])
                        kb.op("dve", lambda: nc.vector.tensor_tensor(out=bv[:, :], in0=bv[:, :], in1=rt[:, :],
                                                                     op=ALU.mult), reads=[t_rt], writes=[t_bv])
                        if d == 0:
                            kb.op("dve", lambda: nc.vector.tensor_tensor_scan(
                                out=hf[:, :], data0=av[:, :], data1=bv[:, :], initial=0.0,
                                op0=ALU.mult, op1=ALU.add), reads=[t_av, t_bv], writes=[t_hf])
                        else:
                            kb.op("dve", lambda: nc.vector.tensor_tensor_scan(
                                out=hb[:, 0:CTX][:, ::-1], data0=av[:, 0:CTX][:, ::-1], data1=bv[:, 0:CTX][:, ::-1],
                                initial=0.0, op0=ALU.mult, op1=ALU.add), reads=[t_av, t_bv], writes=[t_hb])
                            kb.op("dve", lambda: nc.vector.tensor_tensor_scan(
                                out=hb[:, CTX:TB][:, ::-1], data0=av[:, CTX:TB][:, ::-1],
                                data1=bv[:, CTX:TB][:, ::-1], initial=hb[:, 0:1], op0=ALU.mult, op1=ALU.add),
                                reads=[t_av, t_bv], writes=[t_hb])
                    kb.op("dve", lambda: nc.vector.tensor_tensor(out=hf[:], in0=hf[:], in1=hb[:], op=ALU.add),
                          reads=[t_hb], writes=[t_hf])
                    kb.op("dve", lambda: nc.vector.tensor_tensor(out=mo[:], in0=hf[:], in1=gt[:], op=ALU.mult),
                          reads=[t_hf, t_gt], writes=[t_mo])
                    kb.dma(OT[:, ch, cb0:cb0 + TB], mo[:], reads=[t_mo])
        kb.barrier()


Model.phase_lru_in = phase_lru_in
Model.phase_lru_scan = phase_lru_scan


import math as _math
NG = 64
LL = 8


def _s5_scratch(self):
    _attn_scratch(self)
    if "UT" not in self.dram:
        self.dscratch("UT", [1024, self.T], BF16)
        self.dscratch("YF", [2, 1024, self.T], F32)


def phase_s5_in(self, i):
    kb, nc = self.kb, self.nc
    _s5_scratch(self)
    NT = self.NT
    xT = self.dram["xT"].rearrange("(k p) t -> p k t", p=128)
    UT = self.dram["UT"].rearrange("(k p) t -> p k t", p=128)
    with Pool(kb) as P:
        xt = [P.sb([128, KC, NT], F32, "xt%d" % q) for q in range(2)]
        t_xt = [Tok(), Tok()]
        h = [P.sb([128, KC, NT], BF16, "h%d" % q) for q in range(2)]
        t_h = [Tok(), Tok()]
        W = self.norm_work(P, NT)
        psn = P.ps([128, 512], F32, "psn")
        t_psn = Tok(excl=True)
        tl = self.tiles(True)
        c0, n, m = tl[0]
        kb.dma(xt[0][:, :, :n], xT[:, :, c0:c0 + n], writes=[t_xt[0]])
        for ti, (c0, n, m) in enumerate(tl):
            jx = ti % 2
            if ti + 1 < len(tl):
                c1, n1, _ = tl[ti + 1]
                kb.dma(xt[1 - jx][:, :, :n1], xT[:, :, c1:c1 + n1], writes=[t_xt[1 - jx]])
            self.norm_pre(xt[jx], t_xt[jx], n, i, 0, m, h[jx], t_h[jx], W, psn, t_psn)
            kb.dma(UT[:, :, c0:c0 + n], h[jx][:, :, :n], reads=[t_h[jx]])
        kb.barrier()


def phase_s5_scan(self, i, jb):
    kb, nc = self.kb, self.nc
    TB, CTX, SEQ = self.TB, self.CTX, self.SEQ
    J = TB // LL
    npass = 0
    while (1 << npass) < J:
        npass += 1
    pmax = getattr(Model, "s5_piece_max", 512)
    pieces = [(0, J)]
    if J > pmax:
        pieces = [(0, pmax), (pmax, J)]
    PB = pmax
    UT = self.dram["UT"].rearrange("(k p) t -> p k t", p=128)
    YF = self.dram["YF"]
    TWO_PI = 2.0 * _math.pi
    V = nc.vector
    for d in getattr(self, 's5_dirs', (0, 1)):
        with Pool(kb) as P:
            t_st = Tok()

            def tt(o, a, b, op):
                kb.op("dve", lambda: V.tensor_tensor(out=o, in0=a, in1=b, op=op), writes=[t_st])

            def ts(o, a, s1, op0, s2=None, op1=None):
                if op1 is None:
                    kb.op("dve", lambda: V.tensor_scalar(out=o, in0=a, scalar1=s1, scalar2=None, op0=op0),
                          writes=[t_st])
                else:
                    kb.op("dve", lambda: V.tensor_scalar(out=o, in0=a, scalar1=s1, scalar2=s2, op0=op0, op1=op1),
                          writes=[t_st])

            def act(o, a, func, scale=1.0, bias=None):
                if bias is None:
                    kb.op("act", lambda: nc.scalar.activation(out=o, in_=a, func=func, scale=scale), writes=[t_st])
                else:
                    kb.op("act", lambda: nc.scalar.activation(out=o, in_=a, func=func, scale=scale, bias=bias),
                          writes=[t_st])
            BS = P.sb([128, 8, NG, 16], BF16, "BS")
            CS = P.sb([128, 9, NG, 16], BF16, "CS")
            PW = P.sb([128, 9, 2, NG], F32, "PW")
            QW = P.sb([128, npass, 2, NG], F32, "QW")
            qos = P.sb([128, npass, NG], F32, "qos")
            identb = P.sb([128, 128], BF16, "identb")
            identf = P.sb([128, 128], F32, "identf")
            shf = P.sb([128, 128], F32, "shf")
            dsk = P.sb([128, KC], F32, "dsk")
            kb.dma(identb[:], self.dram["ident"], writes=[t_st], q="pool")
            kb.dma(identf[:], self.dram["ident"], writes=[t_st])
            kb.dma(shf[:], self.dram["shift64"], writes=[t_st])
            with Pool(kb) as PS:
                pg = lambda nm: PS.sb([128, NG], F32, nm)
                are, aim, dt, xr, xi, mag, yy, ff, mm, sn, cs, lr, li = [pg("pg%d" % q) for q in range(13)]
                nr, den, cr, ci, t1, t2, t3, t4 = [pg("pgb%d" % q) for q in range(8)]
                ki = PS.sb([128, NG], I32, "ki")
                pgc = lambda nm: PS.sb([128, NG, 16], F32, nm)
                Bre, Bim, Cre, Cim, Bbr, Bbi, T1, T2, T3, T4 = [pgc("pgc%d" % q) for q in range(10)]
                kb.dma(are[:], self.dram["s5_a_re"][jb, d], writes=[t_st])
                kb.dma(aim[:], self.dram["s5_a_im"][jb, d], writes=[t_st])
                kb.dma(dt[:], self.dram["s5_log_dt"][jb, d].partition_broadcast(128), writes=[t_st])
                kb.dma(Bre[:], self.dram["s5_b_re"][jb, d], writes=[t_st])
                kb.dma(Bim[:], self.dram["s5_b_im"][jb, d], writes=[t_st])
                kb.dma(Cre[:], self.dram["s5_c_re"][jb, d], writes=[t_st])
                kb.dma(Cim[:], self.dram["s5_c_im"][jb, d], writes=[t_st])
                act(dt[:], dt[:], AF.Exp)
                tt(xr[:], are[:], dt[:], ALU.mult)
                tt(xi[:], aim[:], dt[:], ALU.mult)
                ts(yy[:], xi[:], 1.0 / TWO_PI, ALU.mult)
                kb.op("dve", lambda: V.tensor_copy(out=ki[:], in_=yy[:]), writes=[t_st])
                kb.op("dve", lambda: V.tensor_copy(out=ff[:], in_=ki[:]), writes=[t_st])
                tt(ff[:], yy[:], ff[:], ALU.subtract)
                ts(mm[:], ff[:], 0.5, ALU.is_gt)
                tt(ff[:], ff[:], mm[:], ALU.subtract)
                ts(mm[:], ff[:], -0.5, ALU.is_lt)
                tt(ff[:], ff[:], mm[:], ALU.add)
                ts(ff[:], ff[:], TWO_PI / 16.0, ALU.mult)
                tt(yy[:], ff[:], ff[:], ALU.mult)

                def horner(o, z, coefs):
                    kb.op("dve", lambda: V.memset(o, 0.0), writes=[t_st])
                    for c in reversed(coefs[1:]):
                        kb.op("dve", lambda c=c: V.scalar_tensor_tensor(out=o, in0=o, scalar=float(c), in1=z,
                                                                        op0=ALU.add, op1=ALU.mult), writes=[t_st])
                    ts(o, o, float(coefs[0]), ALU.add)
                fact = [1.0]
                for q in range(1, 16):
                    fact.append(fact[-1] * q)
                horner(cs[:], yy[:], [(-1) ** q / fact[2 * q] for q in range(6)])
                horner(sn[:], yy[:], [(-1) ** q / fact[2 * q + 1] for q in range(6)])
                tt(sn[:], sn[:], ff[:], ALU.mult)
                ts(mm[:], xr[:], 1.0 / 16.0, ALU.mult)
                horner(mag[:], mm[:], [1.0 / fact[q] for q in range(10)])
                tt(lr[:], mag[:], cs[:], ALU.mult)
                tt(li[:], mag[:], sn[:], ALU.mult)
                for _sq in range(4):
                    tt(t1[:], lr[:], lr[:], ALU.mult)
                    tt(t2[:], li[:], li[:], ALU.mult)
                    tt(t3[:], lr[:], li[:], ALU.mult)
                    tt(lr[:], t1[:], t2[:], ALU.subtract)
                    ts(li[:], t3[:], 2.0, ALU.mult)
                ts(nr[:], lr[:], -1.0, ALU.add)
                tt(t1[:], are[:], are[:], ALU.mult)
                tt(t2[:], aim[:], aim[:], ALU.mult)
                tt(den[:], t1[:], t2[:], ALU.add)
                kb.op("dve", lambda: V.reciprocal(out=den[:], in_=den[:]), writes=[t_st])
                tt(t1[:], nr[:], are[:], ALU.mult)
                tt(t2[:], li[:], aim[:], ALU.mult)
                tt(cr[:], t1[:], t2[:], ALU.add)
                tt(cr[:], cr[:], den[:], ALU.mult)
                tt(t1[:], li[:], are[:], ALU.mult)
                tt(t2[:], nr[:], aim[:], ALU.mult)
                tt(ci[:], t1[:], t2[:], ALU.subtract)
                tt(ci[:], ci[:], den[:], ALU.mult)
                bc = lambda v: v.unsqueeze(2).broadcast_to([128, NG, 16])
                tt(T1[:], Bre[:], bc(cr[:]), ALU.mult)
                tt(T2[:], Bim[:], bc(ci[:]), ALU.mult)
                tt(Bbr[:], T1[:], T2[:], ALU.subtract)
                tt(T1[:], Bim[:], bc(cr[:]), ALU.mult)
                tt(T2[:], Bre[:], bc(ci[:]), ALU.mult)
                tt(Bbi[:], T1[:], T2[:], ALU.add)
                kb.op("dve", lambda: V.memset(PW[:, 0, 0, :], 1.0), writes=[t_st])
                kb.op("dve", lambda: V.memset(PW[:, 0, 1, :], 0.0), writes=[t_st])
                kb.op("dve", lambda: V.tensor_copy(out=PW[:, 1, 0, :], in_=lr[:]), writes=[t_st])
                kb.op("dve", lambda: V.tensor_copy(out=PW[:, 1, 1, :], in_=li[:]), writes=[t_st])

                def cmul(o_r, o_i, a_r, a_i, b_r, b_i):
                    tt(t1[:], a_r, b_r, ALU.mult)
                    tt(t2[:], a_i, b_i, ALU.mult)
                    tt(t3[:], a_r, b_i, ALU.mult)
                    tt(t4[:], a_i, b_r, ALU.mult)
                    tt(o_r, t1[:], t2[:], ALU.subtract)
                    tt(o_i, t3[:], t4[:], ALU.add)
                for e in range(2, 9):
                    cmul(PW[:, e, 0, :], PW[:, e, 1, :], PW[:, e - 1, 0, :], PW[:, e - 1, 1, :], lr[:], li[:])
                kb.op("dve", lambda: V.tensor_copy(out=QW[:, 0, :, :], in_=PW[:, 8, :, :]), writes=[t_st])
                for k in range(1, npass):
                    cmul(QW[:, k, 0, :], QW[:, k, 1, :], QW[:, k - 1, 0, :], QW[:, k - 1, 1, :],
                         QW[:, k - 1, 0, :], QW[:, k - 1, 1, :])
                kb.op("dve", lambda: V.tensor_copy(out=qos[0:64, :, :], in_=QW[0:64, :, 1, :]), writes=[t_st])
                kb.op("dve", lambda: V.tensor_scalar(out=qos[64:128, :, :], in0=QW[64:128, :, 1, :], scalar1=-1.0,
                                                     scalar2=None, op0=ALU.mult), writes=[t_st])
                for e in range(9):
                    p_r, p_i = bc(PW[:, e, 0, :]), bc(PW[:, e, 1, :])
                    if e < 8:
                        tt(T1[:], Bbr[:], p_r, ALU.mult)
                        tt(T2[:], Bbi[:], p_i, ALU.mult)
                        tt(T3[:], Bbi[:], p_r, ALU.mult)
                        tt(T4[:], Bbr[:], p_i, ALU.mult)
                        tt(BS[0:64, e], T1[0:64], T2[0:64], ALU.subtract)
                        tt(BS[64:128, e], T3[64:128], T4[64:128], ALU.add)
                    tt(T1[:], Cre[:], p_r, ALU.mult)
                    tt(T2[:], Cim[:], p_i, ALU.mult)
                    tt(T3[:], Cre[:], p_i, ALU.mult)
                    tt(T4[:], Cim[:], p_r, ALU.mult)
                    tt(CS[0:64, e], T1[0:64], T2[0:64], ALU.subtract)
                    kb.op("dve", lambda: V.scalar_tensor_tensor(
                        out=CS[64:128, e], in0=T3[64:128], scalar=-1.0, in1=T4[64:128],
                        op0=ALU.mult, op1=ALU.subtract), writes=[t_st])
            if getattr(self, "s5_stop", 0) == 1:
                if d == getattr(self, "s5_dbg_dir", 0):
                    kb.dma(self.dram["dbgBS"], BS[:], reads=[t_st])
                    kb.dma(self.dram["dbgCS"], CS[:], reads=[t_st])
                    kb.dma(self.dram["dbgPW"], PW[:], reads=[t_st])
                    kb.dma(self.dram["dbgQW"], QW[:], reads=[t_st])
                kb.barrier()
                continue
            ECt = P.sb([128, 9, 1152], BF16, "ECt")
            LBt = P.sb([128, 8, 8, 128], BF16, "LBt")
            Kdt = P.sb([128, 8, 128], BF16, "Kdt")
            AMt = P.sb([128, npass, 8, 128], BF16, "AMt")
            t_EC, t_LB, t_Kd, t_AM = Tok(), Tok(), Tok(), Tok()
            kb.op("pool", lambda: nc.gpsimd.memset(ECt[:], 0.0), reads=[t_st], writes=[t_EC])
            ub = [P.sb([128, TB + CTX], BF16, "ub%d" % q) for q in range(2)]
            t_ub = [Tok(), Tok()]
            S32 = P.sb([128, 8, J], F32, "S32")
            Sb = P.sb([128, 8, J], BF16, "Sb")
            Hb = P.sb([128, 8, J], BF16, "Hb")
            t_S32 = [Tok() for _ in range(8)]
            t_Sb = [Tok() for _ in range(8)]
            t_Hb = Tok()
            kb.op("pool", lambda: nc.gpsimd.memset(Hb[:, :, 0:1], 0.0), writes=[t_Hb])
            yb1 = P.sb([128, TB], F32, "yb")
            yb = [yb1, yb1]
            t_yb1 = Tok()
            t_yb = [t_yb1, t_yb1]
            _a = P.sb([128, 8, 128], F32, "amt")
            amt = [_a, _a]
            _ta = Tok()
            t_amt = [_ta, _ta]
            _b = P.sb([128, 8, 128], F32, "amtb")
            amt2 = [_b, _b]
            _tb = Tok()
            t_amt2 = [_tb, _tb]
            nu = 0
            for cidx in getattr(self, 's5_chunks', range(KC)):
                g0 = cidx * 8
                with Pool(kb) as PC:
                    EBt = PC.sb([128, 8, 1152], BF16, "EBt")
                    t_EB = Tok()
                    pst = [PC.ps([128, 8, 128], BF16, "pst%d" % q) for q in range(2)]
                    t_pst = [Tok(excl=True), Tok(excl=True)]
                    psk = [PC.ps([128, 4, 128], F32, "psk%d" % q) for q in range(2)]
                    t_psk = [Tok(excl=True), Tok(excl=True)]
                    kb.op("pool", lambda: nc.gpsimd.memset(EBt[:], 0.0), writes=[t_EB])
                    for e in range(9):
                        if e < 8:
                            kb.op("dve", lambda e=e: V.tensor_copy(
                                out=EBt[:, e, :].rearrange("p (g s) -> p g s", s=144)[:, :, 0:16],
                                in_=BS[:, e, g0:g0 + 8, :]), reads=[t_st], writes=[t_EB])
                        kb.op("dve", lambda e=e: V.tensor_copy(
                            out=ECt[:, e, :].rearrange("p (g s) -> p g s", s=144)[:, :, 0:16],
                            in_=CS[:, e, g0:g0 + 8, :]), reads=[t_st], writes=[t_EC])
                    for e in range(8):
                        pj = e % 2
                        for g in range(8):
                            kb.op("pe", lambda e=e, g=g, pj=pj: nc.tensor.transpose(
                                out=pst[pj][:, g, :], in_=EBt[:, e, g * 128:(g + 1) * 128], identity=identb[:]),
                                reads=[t_EB, t_st], writes=[t_pst[pj]])
                        kb.op("act", lambda e=e, pj=pj: nc.scalar.copy(out=LBt[:, e, :, :], in_=pst[pj][:, :, :]),
                              reads=[t_pst[pj]], writes=[t_LB])
                    for hb_ in range(2):
                        for e4 in range(4):
                            e = hb_ * 4 + e4
                            for g in range(8):
                                kb.op("pe", lambda e=e, e4=e4, g=g, hb_=hb_: nc.tensor.matmul(
                                    psk[hb_][:, e4, :], EBt[:, e, g * 128:(g + 1) * 128],
                                    ECt[:, 0, g * 128:(g + 1) * 128], start=(g == 0), stop=(g == 7)),
                                    reads=[t_EB, t_EC], writes=[t_psk[hb_]])
                        kb.op("act", lambda hb_=hb_: nc.scalar.copy(out=Kdt[:, hb_ * 4:(hb_ + 1) * 4, :],
                                                                    in_=psk[hb_][:, :, :]),
                              reads=[t_psk[hb_]], writes=[t_Kd])
                    for k in range(npass):
                        aj = k % 2
                        kb.op("dve", lambda k=k, aj=aj: V.tensor_tensor(
                            out=amt[aj][:], in0=identf[:, :].unsqueeze(1).broadcast_to([128, 8, 128]),
                            in1=QW[:, k, 0, g0:g0 + 8].unsqueeze(2).broadcast_to([128, 8, 128]), op=ALU.mult),
                            reads=[t_st], writes=[t_amt[aj]])
                        kb.op("pool", lambda k=k, aj=aj: nc.gpsimd.tensor_tensor(
                            out=amt2[aj][:], in0=shf[:, :].unsqueeze(1).broadcast_to([128, 8, 128]),
                            in1=qos[:, k, g0:g0 + 8].unsqueeze(2).broadcast_to([128, 8, 128]), op=ALU.mult),
                            reads=[t_st], writes=[t_amt2[aj]])
                        kb.op("dve", lambda k=k, aj=aj: V.tensor_tensor(
                            out=AMt[:, k, :, :], in0=amt[aj][:], in1=amt2[aj][:], op=ALU.add),
                            reads=[t_amt[aj], t_amt2[aj]], writes=[t_AM])
                if getattr(self, "s5_stop", 0) == 2:
                    if d == 0 and cidx == 0:
                        kb.dma(self.dram["dbgLB"], LBt[:], reads=[t_LB])
                        kb.dma(self.dram["dbgKd"], Kdt[:], reads=[t_Kd])
                        kb.dma(self.dram["dbgAM"], AMt[:], reads=[t_AM])
                    continue
                with Pool(kb) as PM:
                    psv = [PM.ps([128, 1024], F32, "psv%d" % q) for q in range(2)]
                    t_psv = [Tok(excl=True), Tok(excl=True)]
                    psy = [PM.ps([128, 1024], F32, "psy%d" % q) for q in range(2)]
                    t_psy = [Tok(excl=True), Tok(excl=True)]
                    npv = 0
                    npy = 0
                    for b in range(self.NB):
                        uj = nu % 2
                        nu += 1
                        ubt = ub[uj]
                        kb.dma(ubt[:, 0:TB], UT[:, cidx, b * TB:(b + 1) * TB], writes=[t_ub[uj]])
                        kb.dma(ubt[:, TB:TB + CTX], UT[:, cidx, b * TB:b * TB + CTX], writes=[t_ub[uj]])
                        ybt = yb[uj]
                        if d == 0:
                            useq = lambda s: ubt[:, s:TB:LL]
                            yseq = lambda r: ybt[:, r:TB:LL]
                        else:
                            useq = lambda s: ubt[:, CTX:CTX + TB][:, (TB - 1 - s)::-LL]
                            yseq = lambda r: ybt[:, (TB - 1 - r)::-LL]
                        for g in range(8):
                            pj = npv % 2
                            npv += 1
                            for (c0, c1) in pieces:
                                pc = pieces.index((c0, c1))
                                for s in range(LL):
                                    kb.op("pe", lambda s=s, g=g, c0=c0, c1=c1, pc=pc, pj=pj: nc.tensor.matmul(
                                        psv[pj][:, c0:c1], LBt[:, LL - 1 - s, g, :],
                                        useq(s)[:, c0:c1], start=(s == 0), stop=(s == LL - 1)),
                                        reads=[t_LB, t_ub[uj]], writes=[t_psv[pj]])
                            kb.op("act", lambda g=g, pj=pj: nc.scalar.copy(
                                out=S32[:, g, :], in_=psv[pj][:, 0:J]),
                                reads=[t_psv[pj]], writes=[t_S32[g]])
                            kb.op("dve", lambda g=g: V.tensor_copy(out=Sb[:, g, :], in_=S32[:, g, :]),
                                  reads=[t_S32[g]], writes=[t_Sb[g]])
                        _stop = getattr(self, "s5_stop", 0)
                        if _stop == 3:
                            kb.dma(self.dram["dbgS32"], S32[:], reads=t_S32)
                            continue
                        for k in range(npass):
                            dd = 1 << k
                            for g in range(8):
                                pj = npv % 2
                                npv += 1
                                segs = []
                                for (c0, c1) in pieces:
                                    if c1 <= dd:
                                        continue
                                    lo = max(c0, dd)
                                    pc = pieces.index((c0, c1))
                                    segs.append((lo, c1, lo))
                                for (lo, c1, po) in segs:
                                    kb.op("pe", lambda k=k, g=g, lo=lo, c1=c1, po=po, pj=pj, dd=dd: nc.tensor.matmul(
                                        psv[pj][:, po:po + (c1 - lo)], AMt[:, k, g, :], Sb[:, g, lo - dd:c1 - dd],
                                        start=True, stop=True), reads=[t_AM, t_Sb[g]], writes=[t_psv[pj]])
                                kb.op("dve", lambda g=g, pj=pj, dd=dd: V.tensor_tensor(
                                    out=S32[:, g, dd:J], in0=S32[:, g, dd:J], in1=psv[pj][:, dd:J],
                                    op=ALU.add), reads=[t_psv[pj]], writes=[t_S32[g]])
                                kb.op("act", lambda g=g, dd=dd: nc.scalar.copy(out=Sb[:, g, dd:J], in_=S32[:, g, dd:J]),
                                      reads=[t_S32[g]], writes=[t_Sb[g]])
                        kb.op("act", lambda: nc.scalar.copy(out=Hb[:, :, 1:J], in_=S32[:, :, 0:J - 1]),
                              reads=t_S32, writes=[t_Hb])
                        if _stop == 4:
                            kb.dma(self.dram["dbgS32"], S32[:], reads=t_S32)
                            continue
                        for r in range(LL):
                            pj = npy % 2
                            npy += 1
                            for (c0, c1) in pieces:
                                pc = pieces.index((c0, c1))
                                o_ap = psy[pj][:, c0:c1]
                                nmm = (r + 1) + 8
                                im = 0
                                for q in range(r + 1):
                                    kb.op("pe", lambda q=q, r=r, c0=c0, c1=c1, o_ap=o_ap, im=im, nmm=nmm:
                                          nc.tensor.matmul(o_ap, Kdt[:, q, :], useq(r - q)[:, c0:c1],
                                                           start=(im == 0), stop=(im == nmm - 1)),
                                          reads=[t_Kd, t_ub[uj]], writes=[t_psy[pj]])
                                    im += 1
                                for g in range(8):
                                    kb.op("pe", lambda g=g, r=r, c0=c0, c1=c1, o_ap=o_ap, im=im, nmm=nmm:
                                          nc.tensor.matmul(o_ap, ECt[:, r + 1, g * 128:(g + 1) * 128], Hb[:, g, c0:c1],
                                                           start=(im == 0), stop=(im == nmm - 1)),
                                          reads=[t_EC, t_Hb], writes=[t_psy[pj]])
                                    im += 1
                            kb.op("act", lambda r=r, pj=pj: nc.scalar.copy(
                                out=yseq(r), in_=psy[pj][:, 0:J]),
                                reads=[t_psy[pj]], writes=[t_yb[uj]])
                        col = b * TB
                        if d == 0:
                            kb.dma(YF[0, cidx * 128:(cidx + 1) * 128, col:col + TB], ybt[:, :], reads=[t_yb[uj]])
                        else:
                            kb.dma(YF[1, cidx * 128:(cidx + 1) * 128, col + CTX:col + TB], ybt[:, 0:SEQ],
                                   reads=[t_yb[uj]])
                            kb.dma(YF[1, cidx * 128:(cidx + 1) * 128, col:col + CTX], ybt[:, SEQ:TB],
                                   reads=[t_yb[uj]])
            kb.barrier()


def phase_s5_out(self, i, jb, with_ctx):
    kb, nc = self.kb, self.nc
    NT = self.NT
    xT = self.dram["xT"].rearrange("(k p) t -> p k t", p=128)
    UT = self.dram["UT"].rearrange("(k p) t -> p k t", p=128)
    YF0 = self.dram["YF"][0].rearrange("(k p) t -> p k t", p=128)
    YF1 = self.dram["YF"][1].rearrange("(k p) t -> p k t", p=128)
    wd = self.dram["s5_w_glu"][jb].rearrange("(k p) d -> p k d", p=128)
    with Pool(kb) as P:
        ws = P.sb([128, KC, 2048], BF16, "ws")
        t_w = [Tok() for _ in range(KC)]
        for k in range(KC):
            for hf in range(2):
                kb.dma(ws[:, k, hf * 1024:(hf + 1) * 1024], wd[:, k, hf * 1024:(hf + 1) * 1024],
                       writes=[t_w[k]], q="pool")
        dsk = P.sb([128, KC], F32, "dsk")
        t_d = Tok()
        kb.dma(dsk[:], self.dram["s5_d"][jb], writes=[t_d])
        xt = [P.sb([128, KC, NT], F32, "xt%d" % q) for q in range(2)]
        t_xt = [Tok(), Tok()]
        y0 = [P.sb([128, KC, NT], F32, "y0%d" % q) for q in range(2)]
        y1 = [P.sb([128, KC, NT], F32, "y1%d" % q) for q in range(2)]
        ut = [P.sb([128, KC, NT], BF16, "ut%d" % q) for q in range(2)]
        t_in = [Tok(), Tok()]
        at = P.sb([128, KC, NT], BF16, "at")
        t_at = Tok()
        Wg = dict(a=P.sb([128, KC, NT], F32, "ga"), b=P.sb([128, KC, NT], F32, "gb"), ta=Tok(), tb=Tok())
        y = P.sb([128, KC, NT], F32, "y")
        t_y = Tok()
        sg = P.sb([128, NT], F32, "sg")
        t_sg = Tok()
        W = self.norm_work(P, NT)
        psn = P.ps([128, 512], F32, "psn")
        t_psn = Tok(excl=True)
        psd = [P.ps([128, 512], F32, "psd%d" % q) for q in range(4)]
        t_psd = [Tok(excl=True) for _ in range(4)]
        tl = self.tiles(with_ctx)

        def loads(jx, c0, n):
            kb.dma(xt[jx][:, :, :n], xT[:, :, c0:c0 + n], writes=[t_xt[jx]])
            kb.dma(y0[jx][:, :, :n], YF0[:, :, c0:c0 + n], writes=[t_in[jx]])
            kb.dma(y1[jx][:, :, :n], YF1[:, :, c0:c0 + n], writes=[t_in[jx]])
            kb.dma(ut[jx][:, :, :n], UT[:, :, c0:c0 + n], writes=[t_in[jx]])
        loads(0, tl[0][0], tl[0][1])
        nd = 0
        for ti, (c0, n, m) in enumerate(tl):
            jx = ti % 2
            if ti + 1 < len(tl):
                loads(1 - jx, tl[ti + 1][0], tl[ti + 1][1])
            kb.op("dve", lambda jx=jx, n=n: nc.vector.tensor_tensor(
                out=y0[jx][:, :, :n], in0=y0[jx][:, :, :n], in1=y1[jx][:, :, :n], op=ALU.add),
                writes=[t_in[jx]])
            for k in range(KC):
                kb.op("dve", lambda jx=jx, n=n, k=k: nc.vector.scalar_tensor_tensor(
                    out=y0[jx][:, k, :n], in0=ut[jx][:, k, :n], scalar=dsk[:, k:k + 1], in1=y0[jx][:, k, :n],
                    op0=ALU.mult, op1=ALU.add), reads=[t_d], writes=[t_in[jx]])
            gelu_tanh(self, at[:, :, :n], y0[jx][:, :, :n], t_in[jx], [128, KC, n], Wg, t_at)
            for dch in range(KC):
                pj = nd % 2
                nd += 1
                for k in range(KC):
                    kb.op("pe", lambda dch=dch, k=k, pj=pj: nc.tensor.matmul(
                        psd[pj][:, :n], ws[:, k, dch * 128:(dch + 1) * 128], at[:, k, :n],
                        start=(k == 0), stop=(k == KC - 1)), reads=[t_w[k], t_at], writes=[t_psd[pj]])
                for k in range(KC):
                    kb.op("pe", lambda dch=dch, k=k, pj=pj: nc.tensor.matmul(
                        psd[2 + pj][:, :n], ws[:, k, 1024 + dch * 128:1024 + (dch + 1) * 128], at[:, k, :n],
                        start=(k == 0), stop=(k == KC - 1)), reads=[t_w[k], t_at], writes=[t_psd[2 + pj]])
                kb.op("act", lambda pj=pj: nc.scalar.activation(out=sg[:, :n], in_=psd[2 + pj][:, :n],
                                                                func=AF.Sigmoid),
                      reads=[t_psd[2 + pj]], writes=[t_sg])
                kb.op("dve", lambda dch=dch, pj=pj: nc.vector.tensor_tensor(
                    out=y[:, dch, :n], in0=psd[pj][:, :n], in1=sg[:, :n], op=ALU.mult),
                    reads=[t_psd[pj], t_sg], writes=[t_y])
            self.norm_post(y, t_y, xt[jx], t_xt[jx], n, i, 0, m, W, psn, t_psn)
            kb.dma(xT[:, :, c0:c0 + n], xt[jx][:, :, :n], reads=[t_xt[jx]])
        kb.barrier()


Model.phase_s5_in = phase_s5_in
Model.phase_s5_scan = phase_s5_scan
Model.phase_s5_out = phase_s5_out

def arr_vec(v):
    v = np.asarray(v)
    lead = v.shape[:-1]
    n = v.shape[-1] // 128
    return np.ascontiguousarray(np.moveaxis(v.reshape(lead + (n, 128)), -1, 0))

def host_common_x(inp, bs):
    f = np.float32
    x, ctx, c, c_ctx = inp["x"], inp["ctx"], inp["c"], inp["c_ctx"]
    cols = []
    for b in bs:
        cols.append(ctx[b].T)
        cols.append(x[b].T)
    xin = np.ascontiguousarray(np.concatenate(cols, axis=1), dtype=f)
    cvecs = [c[b] for b in bs]
    while len(cvecs) < 2:
        cvecs.append(np.zeros_like(c_ctx))
    cvecs.append(c_ctx)
    cc = np.stack(cvecs, axis=-1)
    cc = np.ascontiguousarray(cc.reshape(8, 128, 3).transpose(1, 0, 2), dtype=f)
    return {"xin": xin, "cc": cc}


def host_common(inp, bs, CTX, SEQ):
    f = np.float32
    d = host_common_x(inp, bs)
    d.update({
         "ada_w": np.ascontiguousarray(inp["ada_w"], dtype=f),
         "ada_b": arr_vec(inp["ada_b"]).astype(f),
         "norm_g": arr_vec(inp["norm_g"]).astype(f),
         "mlp_w1": np.ascontiguousarray(inp["mlp_w1"], dtype=f),
         "mlp_w2": np.ascontiguousarray(
             np.asarray(inp["mlp_w2"]).reshape(-1, 32, 128, 8, 128).transpose(0, 3, 2, 1, 4), dtype=f),
         })
    return d

def rope_table(SEQ, GRID_W=64):
    t = np.arange(SEQ)
    row, col = t // GRID_W, t % GRID_W
    inv = (10000.0 ** (-np.arange(16, dtype=np.float32) / 16)).astype(np.float32)
    ang = np.concatenate([row[:, None].astype(np.float32) * inv, col[:, None].astype(np.float32) * inv], axis=-1)
    tab = np.stack([np.cos(ang), np.sin(ang)], axis=1).astype(np.float32)
    return np.ascontiguousarray(tab.reshape(SEQ // 128, 128, 2, 32).transpose(1, 0, 2, 3))

def host_attn(inp, SEQ):
    f = np.float32
    return {"attn_w_qkv": np.ascontiguousarray(inp["attn_w_qkv"], dtype=f),
            "attn_w_o": np.ascontiguousarray(inp["attn_w_o"], dtype=f),
            "attn_lambda": np.ascontiguousarray(inp["attn_lambda"], dtype=f),
            "attn_subln": np.ascontiguousarray(inp["attn_subln"], dtype=f),
            "rope": rope_table(SEQ), "ident": np.eye(128, dtype=f)}

def host_lru(inp):
    f = np.float32
    wg = np.asarray(inp["lru_w_gate"])
    nc_ = wg.shape[0]
    wg = wg.reshape(nc_, 2, 2, 6, 2, 128, 256).transpose(0, 5, 1, 2, 3, 4, 6)
    return {"lru_w_in": np.ascontiguousarray(inp["lru_w_in"], dtype=f),
            "lru_w_out": np.ascontiguousarray(inp["lru_w_out"], dtype=f),
            "lru_w_gate": np.ascontiguousarray(wg, dtype=f),
            "lru_conv_w": np.ascontiguousarray(
                np.asarray(inp["lru_conv_w"]).reshape(nc_, 4, 12, 128).transpose(0, 3, 2, 1), dtype=f),
            "lru_conv_b": np.ascontiguousarray(
                np.asarray(inp["lru_conv_b"]).reshape(nc_, 12, 128).transpose(0, 2, 1), dtype=f),
            "lru_b_gate": np.ascontiguousarray(
                np.asarray(inp["lru_b_gate"]).reshape(nc_, 2, 2, 12, 128).transpose(0, 4, 1, 2, 3), dtype=f),
            "lru_a_param": np.ascontiguousarray(
                np.asarray(inp["lru_a_param"]).reshape(nc_, 2, 12, 128).transpose(0, 3, 1, 2), dtype=f)}

def host_s5(inp):
    f = np.float32
    def dup(a):
        return np.concatenate([a, a], axis=2)
    a_re = np.asarray(inp["s5_a_re"]).transpose(0, 1, 3, 2)
    a_im = np.asarray(inp["s5_a_im"]).transpose(0, 1, 3, 2)
    b_re = np.asarray(inp["s5_b_re"]).transpose(0, 1, 3, 2, 4)
    b_im = np.asarray(inp["s5_b_im"]).transpose(0, 1, 3, 2, 4)
    c_re = np.asarray(inp["s5_c_re"]).transpose(0, 1, 4, 2, 3)
    c_im = np.asarray(inp["s5_c_im"]).transpose(0, 1, 4, 2, 3)
    sh = np.roll(np.eye(128, dtype=f), 64, axis=1)
    return {"s5_a_re": np.ascontiguousarray(dup(a_re), dtype=f), "s5_a_im": np.ascontiguousarray(dup(a_im), dtype=f),
            "s5_b_re": np.ascontiguousarray(dup(b_re), dtype=f), "s5_b_im": np.ascontiguousarray(dup(b_im), dtype=f),
            "s5_c_re": np.ascontiguousarray(dup(c_re), dtype=f), "s5_c_im": np.ascontiguousarray(dup(c_im), dtype=f),
            "s5_log_dt": np.ascontiguousarray(inp["s5_log_dt"], dtype=f),
            "s5_d": np.ascontiguousarray(np.asarray(inp["s5_d"]).reshape(-1, 8, 128).transpose(0, 2, 1), dtype=f),
            "s5_w_glu": np.ascontiguousarray(inp["s5_w_glu"], dtype=f),
            "shift64": sh, "ident": np.eye(128, dtype=f)}


import math


def build_program(NB, CTX, SEQ, shapes, depth=4, layer_list=None):
    M = Model(NB, CTX, SEQ, layers=list(range(depth)) if layer_list is None else layer_list, depth=4)
    nc = M.nc
    for k, shp in shapes.items():
        M.din(k, shp)
    M.dram["xT"] = nc.dram_tensor("xT", [1024, M.T], F32, kind="ExternalOutput").ap()
    with Pool(M.kb) as G:
        M.setup(G)
        M.kb.dma(M.dram["xT"], M.dram["xin"])
        M.kb.barrier()
        M.phase_mod()
        for i in M.layers:
            need_ctx = i < depth - 1
            kind, j = i % 3, i // 3
            if kind == 0:
                lambda_init = 0.8 - 0.6 * math.exp(-0.3 * i)
                M.phase_attn_qkv(i, j)
                M.phase_attn_core(i, j, lambda_init, need_ctx)
                M.phase_proj_post(i, "attn_w_o", j, 8, need_ctx)
            elif kind == 1:
                M.phase_s5_in(i)
                M.phase_s5_scan(i, j)
                M.phase_s5_out(i, j, need_ctx)
            else:
                M.phase_lru_in(i, j)
                M.phase_lru_scan(i, j)
                M.phase_proj_post(i, "lru_w_out", j, 12, need_ctx)
            M.phase_mlp(i, need_ctx)
        M.kb.finish()
    return M


def host_all(inp, bs, CTX, SEQ):
    d = host_common(inp, bs, CTX, SEQ)
    d.update(host_attn(inp, SEQ))
    d.update(host_s5(inp))
    d.update(host_lru(inp))
    return d


def kernel(**inputs):
    inp = {k: np.asarray(v) for k, v in inputs.items()}
    B, SEQ, _ = inp["x"].shape
    CTX = inp["ctx"].shape[1]
    ncores = 8
    NB = B // ncores
    shared = None
    in_maps = []
    for c in range(ncores):
        bs = list(range(c * NB, (c + 1) * NB))
        if shared is None:
            d = host_all(inp, bs, CTX, SEQ)
            shared = {k: v for k, v in d.items() if k not in ("xin", "cc")}
        else:
            d = dict(shared)
            d.update({k: v for k, v in host_common_x(inp, bs).items()})
        in_maps.append(d)
    shapes = {k: v.shape for k, v in in_maps[0].items()}
    M = build_program(NB, CTX, SEQ, shapes)
    res = run_bass_kernel_spmd(M.nc, in_maps, core_ids=list(range(ncores)))
    TB = CTX + SEQ
    out = np.empty((B, SEQ, 1024), np.float32)
    for c in range(ncores):
        o = np.asarray(res.results[c]["xT"])
        for bl in range(NB):
            out[c * NB + bl] = o[:, bl * TB + CTX:(bl + 1) * TB].T
    return out
```

```python
from concourse.bass_utils import run_bass_kernel_spmd
import contextlib
import numpy as np
import concourse.bass as bass
import concourse.mybir as mybir

F32 = mybir.dt.float32
BF16 = mybir.dt.bfloat16
I32 = mybir.dt.int32
AF = mybir.ActivationFunctionType
ALU = mybir.AluOpType
AX = mybir.AxisListType

NSTREAM = 12


class Tok:
    __slots__ = ("w", "r", "name", "excl")

    def __init__(self, name="", excl=False):
        self.w = None
        self.r = {}
        self.name = name
        self.excl = excl


class KB:
    def __init__(self):
        nc = bass.Bass("TRN2", target_bir_lowering=False)
        self.nc = nc
        self.E = {"pe": nc.tensor, "act": nc.scalar, "dve": nc.vector,
                  "pool": nc.gpsimd, "sp": nc.sync}
        self.sem = {}
        self.cnt = {}
        for e in ("pe", "act", "dve", "pool"):
            self.sem[e] = nc.alloc_semaphore("s_" + e)
            self.cnt[e] = 0
        for j in range(NSTREAM):
            e = ("d", j)
            self.sem[e] = nc.alloc_semaphore("s_d%d" % j)
            self.cnt[e] = 0
        self.seen = {e: {} for e in ("pe", "act", "dve", "pool", "sp")}
        self.ndma = 0
        self.nins = 0
        self.nwait = 0

    def _val(self, src, c):
        return c * 16 if isinstance(src, tuple) else c

    def _wait(self, eng, src, c):
        if c <= 0:
            return
        if self.seen[eng].get(src, 0) >= c:
            return
        self.seen[eng][src] = c
        self.E[eng].wait_ge(self.sem[src], self._val(src, c))
        self.nins += 1
        self.nwait += 1

    def _waits_attach(self, eng, need, fn):
        todo = [(src, c) for src, c in need.items() if c > 0 and self.seen[eng].get(src, 0) < c]
        for src, c in todo[:-1]:
            self._wait(eng, src, c)
        ins = fn()
        if todo:
            src, c = todo[-1]
            self.seen[eng][src] = c
            ins._wait_ge(self.sem[src], self._val(src, c))
        return ins

    def _deps(self, eng, reads, writes, same_ok=False):
        need = {}

        def add(src, c):
            if same_ok and src == eng:
                return
            if need.get(src, 0) < c:
                need[src] = c
        for t in reads:
            if t.w is not None:
                add(*t.w)
            if t.excl:
                for src, c in t.r.items():
                    if src != eng:
                        add(src, c)
        for t in writes:
            if t.w is not None:
                add(*t.w)
            for src, c in t.r.items():
                add(src, c)
        return need

    def _commit(self, me, reads, writes):
        c = self.cnt[me]
        for t in reads:
            t.r[me] = c
        for t in writes:
            t.w = (me, c)
            t.r = {}

    def op(self, eng, fn, reads=(), writes=()):
        need = self._deps(eng, reads, writes, same_ok=(eng == "pe"))
        ins = self._waits_attach(eng, need, fn)
        self.cnt[eng] += 1
        ins.then_inc(self.sem[eng], 1)
        self.nins += 1
        self._commit(eng, reads, writes)
        return ins

    def dma(self, out, in_, reads=(), writes=(), q="sp", **kw):
        j = self.ndma % NSTREAM
        self.ndma += 1
        me = ("d", j)
        need = self._deps(q, reads, writes)
        if need.get(me, 0) < self.cnt[me]:
            need[me] = self.cnt[me]
        ins = self._waits_attach(q, need, lambda: self.E[q].dma_start(out=out, in_=in_, **kw))
        self.cnt[me] += 1
        ins.then_inc(self.sem[me], 16)
        self.nins += 1
        self._commit(me, reads, writes)
        return ins

    def barrier(self):
        for eng in ("pe", "act", "dve", "pool", "sp"):
            for src, c in self.cnt.items():
                if src == eng:
                    continue
                self._wait(eng, src, c)

    def finish(self):
        for src, c in self.cnt.items():
            self._wait("sp", src, c)


class Pool:
    _uid = [0]

    def __init__(self, kb):
        self.kb = kb
        self.st = contextlib.ExitStack()
        self.n = 0
        Pool._uid[0] += 1
        self.uid = Pool._uid[0]

    def __enter__(self):
        self.st.__enter__()
        return self

    def __exit__(self, *a):
        return self.st.__exit__(*a)

    def sb(self, shape, dtype, name=None):
        self.n += 1
        nm = "%s_%d_%d" % (name or "t", self.uid, self.n)
        t = self.st.enter_context(self.kb.nc.sbuf_tensor(nm, list(shape), dtype))
        return t

    def ps(self, shape, dtype, name=None):
        self.n += 1
        nm = "%s_%d_%d" % (name or "p", self.uid, self.n)
        t = self.st.enter_context(self.kb.nc.psum_tensor(nm, list(shape), dtype))
        return t

D = 1024
KC = 8
DFF = 4096
EPS = 1e-6


class Model:
    def __init__(self, NB, CTX, SEQ, layers=(0, 1, 2, 3), depth=4, NT=256):
        self.NB, self.CTX, self.SEQ = NB, CTX, SEQ
        self.TB = CTX + SEQ
        self.T = NB * self.TB
        self.layers = list(layers)
        self.depth = depth
        self.NT = NT
        self.kb = KB()
        self.nc = self.kb.nc
        self.dram = {}

    def din(self, name, shape, dtype=F32):
        t = self.nc.dram_tensor(name, list(shape), dtype, kind="ExternalInput").ap()
        self.dram[name] = t
        return t

    def dscratch(self, name, shape, dtype):
        t = self.nc.dram_tensor(name, list(shape), dtype, kind="Internal").ap()
        self.dram[name] = t
        return t

    def tiles(self, with_ctx=True, nt=None):
        nt = nt or self.NT
        out = []
        for b in range(self.NB):
            base = b * self.TB
            if with_ctx:
                for c0 in range(0, self.CTX, nt):
                    out.append((base + c0, min(nt, self.CTX - c0), 2))
            for c0 in range(0, self.SEQ, nt):
                out.append((base + self.CTX + c0, min(nt, self.SEQ - c0), b))
        return out

    def setup(self, G):
        kb, nc = self.kb, self.nc
        self.G = G
        self.onesm = G.sb([128, 128], BF16, "onesm")
        self.t_const = Tok("const")
        kb.op("dve", lambda: nc.vector.memset(self.onesm[:], 1.0 / D), writes=[self.t_const])
        self.epsv = G.sb([128, 1], F32, "epsv")
        kb.op("dve", lambda: nc.vector.memset(self.epsv[:], EPS), writes=[self.t_const])
        self.mv = G.sb([128, self.depth, 4, KC, 3], F32, "mv")
        self.modT = G.sb([128, self.depth, 48, 3], F32, "modT")
        self.t_mv = Tok("mv")

    def phase_mod(self):
        kb, nc = self.kb, self.nc
        cc, ada_w, ada_b, ng = (self.dram[k] for k in ("cc", "ada_w", "ada_b", "norm_g"))
        with Pool(kb) as P:
            sc = P.sb([128, KC, 3], F32, "sc")
            adab = P.sb([128, self.depth, 48], F32, "adab")
            ngt = P.sb([128, self.depth, 4, KC], F32, "ngt")
            tmp = P.sb([128, KC, 3], F32, "tmp")
            wt = [P.sb([128, KC, 512], F32, "adaw%d" % j) for j in range(2)]
            ps = [P.ps([128, 512], F32, "psmod%d" % j) for j in range(2)]
            t_sc, t_ab, t_ng, t_tmp = Tok(), Tok(), Tok(), Tok()
            t_wt = [Tok(), Tok()]
            t_ps = [Tok(excl=True), Tok(excl=True)]
            kb.dma(sc[:], cc, writes=[t_sc])
            kb.dma(adab[:], ada_b, writes=[t_ab])
            kb.dma(ngt[:], ng, writes=[t_ng])
            kb.op("act", lambda: nc.scalar.activation(out=sc[:], in_=sc[:], func=AF.Silu),
                  reads=[t_sc], writes=[t_sc])
            n = 0
            for i in self.layers:
                wv = ada_w[i].rearrange("(k p) f -> p k f", p=128)
                for cg in range(12):
                    j = n % 2
                    n += 1
                    kb.dma(wt[j][:], wv[:, :, cg * 512:(cg + 1) * 512], writes=[t_wt[j]])
                    for jj in range(4):
                        for k in range(KC):
                            kb.op("pe", lambda k=k, jj=jj, j=j: nc.tensor.matmul(
                                ps[j][:, jj * 3:jj * 3 + 3], wt[j][:, k, jj * 128:(jj + 1) * 128],
                                sc[:, k, :], start=(k == 0), stop=(k == KC - 1)),
                                reads=[t_wt[j], t_sc], writes=[t_ps[j]])
                    kb.op("dve", lambda j=j, cg=cg, i=i: nc.vector.tensor_tensor(
                        out=self.modT[:, i, cg * 4:(cg + 1) * 4, :],
                        in0=ps[j][:, 0:12].rearrange("p (a b) -> p a b", b=3),
                        in1=adab[:, i, cg * 4:(cg + 1) * 4].unsqueeze(2).broadcast_to([128, 4, 3]),
                        op=ALU.add), reads=[t_ps[j], t_ab], writes=[self.t_mv])
                for kind, (c0, gi, plus1) in enumerate([(8, 0, True), (16, 1, False),
                                                        (32, 2, True), (40, 3, False)]):
                    src = self.modT[:, i, c0:c0 + 8, :]
                    if plus1:
                        kb.op("dve", lambda src=src: nc.vector.tensor_scalar(
                            out=tmp[:], in0=src, scalar1=1.0, scalar2=None, op0=ALU.add),
                            reads=[self.t_mv], writes=[t_tmp])
                        src = tmp[:]
                    kb.op("dve", lambda src=src, i=i, kind=kind, gi=gi: nc.vector.tensor_tensor(
                        out=self.mv[:, i, kind, :, :], in0=src,
                        in1=ngt[:, i, gi, :].unsqueeze(2).broadcast_to([128, KC, 3]),
                        op=ALU.mult), reads=[self.t_mv, t_tmp, t_ng], writes=[self.t_mv])
            kb.barrier()

    def mvec(self, i, kind, k, m):
        return self.mv[:, i, kind, k, m:m + 1]

    def shvec(self, i, which, k, m):
        c0 = 0 if which == 0 else 24
        return self.modT[:, i, c0 + k, m:m + 1]

    def stats_a(self, src, n, W, t_src):
        kb, nc = self.kb, self.nc
        sq = W["sq"]
        kb.op("act", lambda: nc.scalar.activation(out=sq[:, :, :n], in_=src, func=AF.Square),
              reads=[t_src], writes=[W["t_sq"]])

    def stats_b(self, n, W, ps, t_ps):
        kb, nc = self.kb, self.nc
        sq = W["sq"]
        for k in range(KC):
            kb.op("pe", lambda k=k: nc.tensor.matmul(ps[:, :n], self.onesm[:], sq[:, k, :n],
                                                     start=(k == 0), stop=(k == KC - 1)),
                  reads=[W["t_sq"], self.t_const], writes=[t_ps])

    def stats_c(self, n, W, ps, t_ps):
        kb, nc = self.kb, self.nc
        sd, rstd = W["sd"], W["rstd"]
        kb.op("act", lambda: nc.scalar.activation(out=sd[:, :n], in_=ps[:, :n], func=AF.Sqrt,
                                                  bias=self.epsv[:], scale=1.0),
              reads=[t_ps, self.t_const], writes=[W["t_sd"]])
        kb.op("dve", lambda: nc.vector.reciprocal(out=rstd[:, :n], in_=sd[:, :n]),
              reads=[W["t_sd"]], writes=[W["t_rstd"]])
        return rstd

    def rstd_of(self, src, n, W, t_src, ps, t_ps):
        self.stats_a(src, n, W, t_src)
        self.stats_b(n, W, ps, t_ps)
        return self.stats_c(n, W, ps, t_ps)

    def norm_work(self, P, NT):
        return dict(sq=P.sb([128, KC, NT], BF16, "sq"), sd=P.sb([128, NT], F32, "sd"),
                    rstd=P.sb([128, NT], F32, "rstd"), xh=P.sb([128, KC, NT], F32, "xh"),
                    t_sq=Tok(), t_sd=Tok(), t_rstd=Tok(), t_xh=Tok())

    def norm_pre(self, xt, t_x, n, i, which, m, h, t_h, W, ps, t_ps, stats_done=False):
        kb, nc = self.kb, self.nc
        rstd = W["rstd"] if stats_done else self.rstd_of(xt[:, :, :n], n, W, t_x, ps, t_ps)
        xh = W["xh"]
        kb.op("dve", lambda: nc.vector.tensor_tensor(
            out=xh[:, :, :n], in0=xt[:, :, :n],
            in1=rstd[:, :n].unsqueeze(1).broadcast_to([128, KC, n]), op=ALU.mult),
            reads=[t_x, W["t_rstd"]], writes=[W["t_xh"]])
        kindA = 0 if which == 0 else 2
        for k in range(KC):
            if k % 2 == 0:
                kb.op("act", lambda k=k: nc.scalar.activation(
                    out=h[:, k, :n], in_=xh[:, k, :n], func=AF.Identity, scale=self.mvec(i, kindA, k, m),
                    bias=self.shvec(i, which, k, m)), reads=[W["t_xh"], self.t_mv], writes=[t_h])
            else:
                kb.op("dve", lambda k=k: nc.vector.tensor_scalar(
                    out=h[:, k, :n], in0=xh[:, k, :n], scalar1=self.mvec(i, kindA, k, m),
                    scalar2=self.shvec(i, which, k, m), op0=ALU.mult, op1=ALU.add),
                    reads=[W["t_xh"], self.t_mv], writes=[t_h])

    def norm_post(self, y, t_y, xt, t_x, n, i, which, m, W, ps, t_ps, stats_done=False):
        kb, nc = self.kb, self.nc
        rstd = W["rstd"] if stats_done else self.rstd_of(y[:, :, :n], n, W, t_y, ps, t_ps)
        kb.op("dve", lambda: nc.vector.tensor_tensor(
            out=y[:, :, :n], in0=y[:, :, :n],
            in1=rstd[:, :n].unsqueeze(1).broadcast_to([128, KC, n]), op=ALU.mult),
            reads=[W["t_rstd"]], writes=[t_y])
        kindG = 1 if which == 0 else 3
        for k in range(KC):
            kb.op("dve", lambda k=k: nc.vector.scalar_tensor_tensor(
                out=xt[:, k, :n], in0=y[:, k, :n], scalar=self.mvec(i, kindG, k, m),
                in1=xt[:, k, :n], op0=ALU.mult, op1=ALU.add),
                reads=[t_y, self.t_mv], writes=[t_x])

    def phase_mlp(self, i, with_ctx):
        kb, nc = self.kb, self.nc
        NT = self.NT
        xT = self.dram["xT"].rearrange("(k p) t -> p k t", p=128)
        w1 = self.dram["mlp_w1"][i].rearrange("(k p) f -> p k f", p=128)
        w2 = self.dram["mlp_w2"][i]
        with Pool(kb) as P:
            w1s = P.sb([128, KC, DFF], BF16, "w1s")
            w2s = P.sb([128, KC, 32, 128], BF16, "w2s")
            t_w1 = [Tok() for _ in range(8)]
            t_w2 = [Tok() for _ in range(8)]
            for g in range(8):
                kb.dma(w1s[:, :, g * 512:(g + 1) * 512], w1[:, :, g * 512:(g + 1) * 512],
                       writes=[t_w1[g]], q="pool")
            for d in range(8):
                kb.dma(w2s[:, d, :, :], w2[d], writes=[t_w2[d]], q="pool")
            xt = [P.sb([128, KC, NT], F32, "xt%d" % j) for j in range(2)]
            t_xt = [Tok(), Tok()]
            h = [P.sb([128, KC, NT], BF16, "h%d" % j) for j in range(2)]
            t_h = [Tok(), Tok()]
            hid = P.sb([128, 32, NT], BF16, "hid")
            t_hid = [Tok() for _ in range(32)]
            y1 = P.sb([128, KC, NT], F32, "y")
            y = [y1, y1]
            t_y1 = Tok()
            t_y = [t_y1, t_y1]
            rl = [P.sb([128, NT], F32, "rl%d" % j) for j in range(2)]
            t_rl = [Tok(), Tok()]
            W = self.norm_work(P, NT)
            W2 = dict(sq=P.sb([128, KC, NT], BF16, "sq2"), sd=P.sb([128, NT], F32, "sd2"),
                      rstd=P.sb([128, NT], F32, "rstd2"), t_sq=Tok(), t_sd=Tok(), t_rstd=Tok())
            psn = P.ps([128, 512], F32, "psn")
            t_psn = Tok(excl=True)
            psn2 = P.ps([128, 512], F32, "psn2")
            t_psn2 = Tok(excl=True)
            psu = [P.ps([128, 512], F32, "psu%d" % j) for j in range(3)]
            t_psu = [Tok(excl=True) for _ in range(3)]
            psd = [P.ps([128, 512], F32, "psd%d" % j) for j in range(2)]
            t_psd = [Tok(excl=True) for _ in range(2)]
            tl = self.tiles(with_ctx)

            def load(ti):
                c0, n, m = tl[ti]
                kb.dma(xt[ti % 2][:, :, :n], xT[:, :, c0:c0 + n], writes=[t_xt[ti % 2]])

            def pre_a(ti):
                c0, n, m = tl[ti]
                self.stats_a(xt[ti % 2][:, :, :n], n, W, t_xt[ti % 2])

            def pre_bc(ti):
                c0, n, m = tl[ti]
                self.stats_b(n, W, psn, t_psn)
                self.stats_c(n, W, psn, t_psn)
                self.norm_pre(xt[ti % 2], t_xt[ti % 2], n, i, 1, m, h[ti % 2], t_h[ti % 2], W, psn, t_psn,
                              stats_done=True)

            def post_bc(ti):
                c0, n, m = tl[ti]
                self.stats_b(n, W2, psn2, t_psn2)
                self.stats_c(n, W2, psn2, t_psn2)
                self.norm_post(y[ti % 2], t_y[ti % 2], xt[ti % 2], t_xt[ti % 2], n, i, 1, m, W2, psn2, t_psn2,
                               stats_done=True)
                kb.dma(xT[:, :, c0:c0 + n], xt[ti % 2][:, :, :n], reads=[t_xt[ti % 2]])
            if tl:
                load(0)
                pre_a(0)
                pre_bc(0)
                if len(tl) > 1:
                    load(1)
            nu = 0
            nd = 0
            for ti, (c0, n, m) in enumerate(tl):
                j = ti % 2
                for f in range(32):
                    pj = nu % 3
                    nu += 1
                    for k in range(KC):
                        kb.op("pe", lambda k=k, f=f, pj=pj: nc.tensor.matmul(
                            psu[pj][:, :n], w1s[:, k, f * 128:(f + 1) * 128], h[j][:, k, :n],
                            start=(k == 0), stop=(k == KC - 1)),
                            reads=[t_w1[f // 4], t_h[j]], writes=[t_psu[pj]])
                    rj = f % 2
                    kb.op("act", lambda pj=pj, rj=rj: nc.scalar.activation(
                        out=rl[rj][:, :n], in_=psu[pj][:, :n], func=AF.Relu),
                        reads=[t_psu[pj]], writes=[t_rl[rj]])
                    eng = "dve" if f % 2 == 0 else "pool"
                    E = nc.vector if eng == "dve" else nc.gpsimd
                    kb.op(eng, lambda rj=rj, f=f, E=E: E.tensor_tensor(
                        out=hid[:, f, :n], in0=rl[rj][:, :n], in1=rl[rj][:, :n], op=ALU.mult),
                        reads=[t_rl[rj]], writes=[t_hid[f]])
                    if f == 3 and ti >= 1:
                        post_bc(ti - 1)
                        if ti + 1 < len(tl):
                            load(ti + 1)
                    if f == 14 and ti + 1 < len(tl):
                        pre_a(ti + 1)
                    if f == 22 and ti + 1 < len(tl):
                        pre_bc(ti + 1)
                for d in range(KC):
                    pj = nd % 2
                    nd += 1
                    for f in range(32):
                        kb.op("pe", lambda d=d, f=f, pj=pj: nc.tensor.matmul(
                            psd[pj][:, :n], w2s[:, d, f, :], hid[:, f, :n],
                            start=(f == 0), stop=(f == 31)),
                            reads=[t_w2[d], t_hid[f]], writes=[t_psd[pj]])
                    kb.op("act", lambda d=d, pj=pj, j=j: nc.scalar.copy(out=y[j][:, d, :n], in_=psd[pj][:, :n]),
                          reads=[t_psd[pj]], writes=[t_y[j]])
                self.stats_a(y[j][:, :, :n], n, W2, t_y[j])
            if tl:
                post_bc(len(tl) - 1)
            kb.barrier()


HEADS = 8


def _attn_scratch(self):
    if "QT" in self.dram:
        return
    self.dscratch("QT", [self.NB, HEADS, 2, 128, self.TB], BF16)
    self.dscratch("KT", [self.NB, HEADS, 128, self.TB], BF16)
    self.dscratch("V", [self.NB, self.TB, 1024], BF16)
    self.dscratch("OT", [1536, self.T], BF16)


def phase_attn_qkv(self, i, j):
    kb, nc = self.kb, self.nc
    _attn_scratch(self)
    NT = self.NT
    xT = self.dram["xT"].rearrange("(k p) t -> p k t", p=128)
    wq = self.dram["attn_w_qkv"][j].rearrange("(k p) f -> p k f", p=128)
    rope = self.dram["rope"]
    QT, KT, V = self.dram["QT"], self.dram["KT"], self.dram["V"]
    with Pool(kb) as P:
        wqs = P.sb([128, KC, 3072], BF16, "wqs")
        t_wq = [Tok() for _ in range(6)]
        for g in range(6):
            kb.dma(wqs[:, :, g * 512:(g + 1) * 512], wq[:, :, g * 512:(g + 1) * 512],
                   writes=[t_wq[g]], q="pool")
        nrt = self.SEQ // 128
        rp = P.sb([128, nrt, 2, 32], F32, "rp")
        rpq = P.sb([128, nrt, 2, 32], F32, "rpq")
        t_rp = Tok()
        kb.dma(rp[:], rope, writes=[t_rp])
        kb.op("act", lambda: nc.scalar.mul(out=rpq[:], in_=rp[:], mul=0.125), reads=[t_rp], writes=[t_rp])
        ident = P.sb([128, 128], BF16, "ident")
        t_id = Tok()
        kb.dma(ident[:], self.dram["ident"], writes=[t_id], q="pool")
        xt = [P.sb([128, KC, NT], F32, "xt%d" % q) for q in range(2)]
        t_xt = [Tok(), Tok()]
        h = P.sb([128, KC, NT], BF16, "h")
        t_h = Tok()
        W = self.norm_work(P, NT)
        psn = P.ps([128, 512], F32, "psn")
        t_psn = Tok(excl=True)
        psqk = [P.ps([128, 1024], F32, "psqk%d" % q) for q in range(2)]
        t_psqk = [Tok(excl=True), Tok(excl=True)]
        psv = [P.ps([128, 512], F32, "psv%d" % q) for q in range(2)]
        t_psv = [Tok(excl=True), Tok(excl=True)]
        pst = P.ps([128, 8, 128], BF16, "pst")
        t_pst = Tok(excl=True)
        qk = [P.sb([128, 1024], BF16, "qk%d" % q) for q in range(2)]
        t_qk = [Tok(), Tok()]
        ta = P.sb([128, 16, 32], F32, "ta")
        tb = P.sb([128, 16, 32], F32, "tb")
        t_ta, t_tb = Tok(), Tok()
        vst = [P.sb([128, 1024], BF16, "vst%d" % q) for q in range(2)]
        t_vst = [Tok(), Tok()]
        qz = [P.sb([128, HEADS, NT], BF16, "qz%d" % c) for c in range(2)]
        t_qz = [Tok(), Tok()]
        kz = P.sb([128, HEADS, NT], BF16, "kz")
        t_kz = Tok()
        kb.op("pool", lambda: nc.gpsimd.memset(qz[0][:], 0.0), writes=[t_qz[0]])
        kb.op("pool", lambda: nc.gpsimd.memset(qz[1][:], 0.0), writes=[t_qz[1]])
        tl = self.tiles(True)
        c0, n, m = tl[0]
        kb.dma(xt[0][:, :, :n], xT[:, :, c0:c0 + n], writes=[t_xt[0]])
        nv = 0
        for ti, (c0, n, m) in enumerate(tl):
            jx = ti % 2
            if ti + 1 < len(tl):
                c1, n1, _ = tl[ti + 1]
                kb.dma(xt[1 - jx][:, :, :n1], xT[:, :, c1:c1 + n1], writes=[t_xt[1 - jx]])
            self.norm_pre(xt[jx], t_xt[jx], n, i, 0, m, h, t_h, W, psn, t_psn)
            b = c0 // self.TB
            pos0 = c0 - b * self.TB
            is_ctx = (m == 2)
            for s in range(n // 128):
                hs = lambda k: h[:, k, s * 128:(s + 1) * 128]
                for half in range(2):
                    pj = nv % 2
                    nv += 1
                    for k in range(KC):
                        kb.op("pe", lambda k=k, half=half, pj=pj: nc.tensor.matmul(
                            psv[pj][:, :], hs(k), wqs[:, k, 2048 + half * 512:2048 + (half + 1) * 512],
                            start=(k == 0), stop=(k == KC - 1)),
                            reads=[t_h, t_wq[4 + half]], writes=[t_psv[pj]])
                    kb.op("act", lambda half=half, pj=pj, s=s: nc.scalar.copy(
                        out=vst[s % 2][:, half * 512:(half + 1) * 512], in_=psv[pj][:, :]),
                        reads=[t_psv[pj]], writes=[t_vst[s % 2]])
                r0 = b * self.TB + pos0 + s * 128
                kb.dma(V[b, pos0 + s * 128:pos0 + (s + 1) * 128, :], vst[s % 2][:], reads=[t_vst[s % 2]])
                for which in range(2):
                    ps = psqk[which]
                    tps = t_psqk[which]
                    for half in range(2):
                        for k in range(KC):
                            kb.op("pe", lambda k=k, half=half, which=which, ps=ps: nc.tensor.matmul(
                                ps[:, half * 512:(half + 1) * 512], hs(k),
                                wqs[:, k, which * 1024 + half * 512:which * 1024 + (half + 1) * 512],
                                start=(k == 0), stop=(k == KC - 1)),
                                reads=[t_h, t_wq[which * 2 + half]], writes=[tps])
                    dst = qk[which]
                    if is_ctx:
                        kb.op("act", lambda ps=ps, dst=dst, which=which: nc.scalar.mul(
                            out=dst[:], in_=ps[:], mul=(0.125 if which == 0 else 1.0)),
                            reads=[tps], writes=[t_qk[which]])
                    else:
                        lt = (pos0 - self.CTX) // 128 + s
                        tab = rpq if which == 0 else rp
                        cs = tab[:, lt, 0, :].unsqueeze(1).broadcast_to([128, 16, 32])
                        sn = tab[:, lt, 1, :].unsqueeze(1).broadcast_to([128, 16, 32])
                        pv = ps[:, :].rearrange("p (a two f) -> p a two f", two=2, f=32)
                        dv = dst[:, :].rearrange("p (a two f) -> p a two f", two=2, f=32)
                        t1, t2 = pv[:, :, 0, :], pv[:, :, 1, :]
                        kb.op("dve", lambda: nc.vector.tensor_tensor(out=ta[:], in0=t1, in1=cs, op=ALU.mult),
                              reads=[tps, t_rp], writes=[t_ta])
                        kb.op("dve", lambda: nc.vector.tensor_tensor(out=tb[:], in0=t2, in1=sn, op=ALU.mult),
                              reads=[tps, t_rp], writes=[t_tb])
                        kb.op("dve", lambda: nc.vector.tensor_tensor(out=dv[:, :, 0, :], in0=ta[:], in1=tb[:],
                                                                     op=ALU.subtract),
                              reads=[t_ta, t_tb], writes=[t_qk[which]])
                        kb.op("dve", lambda: nc.vector.tensor_tensor(out=ta[:], in0=t1, in1=sn, op=ALU.mult),
                              reads=[tps, t_rp], writes=[t_ta])
                        kb.op("dve", lambda: nc.vector.tensor_tensor(out=tb[:], in0=t2, in1=cs, op=ALU.mult),
                              reads=[tps, t_rp], writes=[t_tb])
                        kb.op("dve", lambda: nc.vector.tensor_tensor(out=dv[:, :, 1, :], in0=ta[:], in1=tb[:],
                                                                     op=ALU.add),
                              reads=[t_ta, t_tb], writes=[t_qk[which]])
                    for hd in range(HEADS):
                        kb.op("pe", lambda hd=hd, dst=dst: nc.tensor.transpose(
                            out=pst[:, hd, :], in_=dst[:, hd * 128:(hd + 1) * 128], identity=ident[:]),
                            reads=[t_qk[which], t_id], writes=[t_pst])
                    sl = slice(s * 128, (s + 1) * 128)
                    if which == 0:
                        kb.op("act", lambda sl=sl: nc.scalar.copy(out=qz[0][0:64, :, sl], in_=pst[0:64, :, :]),
                              reads=[t_pst], writes=[t_qz[0]])
                        kb.op("act", lambda sl=sl: nc.scalar.copy(out=qz[1][64:128, :, sl], in_=pst[64:128, :, :]),
                              reads=[t_pst], writes=[t_qz[1]])
                    else:
                        kb.op("act", lambda sl=sl: nc.scalar.copy(out=kz[:, :, sl], in_=pst[:, :, :]),
                              reads=[t_pst], writes=[t_kz])
            for c in range(2):
                kb.dma(QT[b, :, c, :, pos0:pos0 + n].rearrange("h p t -> p h t"), qz[c][:, :, :n],
                       reads=[t_qz[c]])
            kb.dma(KT[b, :, :, pos0:pos0 + n].rearrange("h p t -> p h t"), kz[:, :, :n], reads=[t_kz])
        kb.barrier()


def phase_attn_core(self, i, j, lambda_init, need_ctx):
    kb, nc = self.kb, self.nc
    QT, KT, V, OT = self.dram["QT"], self.dram["KT"], self.dram["V"], self.dram["OT"]
    TB, CTX = self.TB, self.CTX
    nkt = TB // 128
    QG = 256
    with Pool(kb) as P:
        kts = P.sb([128, HEADS, TB], BF16, "kts")
        vs = P.sb([128, nkt, HEADS, 130], BF16, "vs")
        t_kts, t_vs = Tok(), Tok()
        ident = P.sb([128, 128], BF16, "ident")
        t_id = Tok()
        kb.dma(ident[:], self.dram["ident"], writes=[t_id], q="pool")
        lam = P.sb([128, 4, 64], F32, "lam")
        lt = P.sb([128, 2, 64], F32, "lt")
        ls = P.sb([128, 2], F32, "ls")
        nlam = P.sb([128, 1], F32, "nlam")
        gsb = P.sb([128, 128], F32, "gsb")
        t_l = Tok()
        kb.dma(lam[:], self.dram["attn_lambda"][j].partition_broadcast(128), writes=[t_l])
        kb.dma(gsb[:], self.dram["attn_subln"][j].partition_broadcast(128), writes=[t_l])
        kb.op("dve", lambda: nc.vector.tensor_tensor(out=lt[:], in0=lam[:, 0::2, :], in1=lam[:, 1::2, :],
                                                     op=ALU.mult), reads=[t_l], writes=[t_l])
        kb.op("dve", lambda: nc.vector.tensor_reduce(out=ls[:], in_=lt[:], op=ALU.add, axis=AX.X),
              reads=[t_l], writes=[t_l])
        kb.op("act", lambda: nc.scalar.activation(out=ls[:], in_=ls[:], func=AF.Exp), reads=[t_l], writes=[t_l])
        kb.op("dve", lambda: nc.vector.tensor_tensor(out=nlam[:], in0=ls[:, 1:2], in1=ls[:, 0:1], op=ALU.subtract),
              reads=[t_l], writes=[t_l])
        kb.op("dve", lambda: nc.vector.tensor_scalar(out=nlam[:], in0=nlam[:], scalar1=-lambda_init, scalar2=None,
                                                     op0=ALU.add), reads=[t_l], writes=[t_l])
        kb.op("act", lambda: nc.scalar.mul(out=gsb[:], in_=gsb[:], mul=1.0 - lambda_init), reads=[t_l], writes=[t_l])
        epsv = self.epsv
        qs_ = [P.sb([128, HEADS, 2, QG], BF16, "qs%d" % q) for q in range(2)]
        t_qs = [Tok(), Tok()]
        NPS = 3
        pss = [P.ps([128, 2, QG], F32, "pss%d" % q) for q in range(NPS)]
        t_pss = [Tok(excl=True) for _ in range(NPS)]
        accs = P.sb([128, 2, 2, 129], F32, "accs")
        t_accs = Tok()
        psa = [[P.ps([128, 512], F32, "psa%d%d" % (c, q)) for q in range(2)] for c in range(2)]
        t_psa = [[Tok(excl=True), Tok(excl=True)], [Tok(excl=True), Tok(excl=True)]]
        pst = P.ps([128, 128], BF16, "pst")
        t_pst = Tok(excl=True)
        es = [P.sb([128, 2, QG], BF16, "es%d" % q) for q in range(3)]
        t_es = [Tok() for _ in range(3)]
        mhalf = P.sb([128, 1], F32, "mhalf")
        kb.op("dve", lambda: nc.vector.memset(mhalf[:], -0.5), writes=[t_l])
        rz = P.sb([128, 2], F32, "rz")
        t_rz = Tok()
        o1 = P.sb([128, 128], F32, "o1")
        o2 = P.sb([128, 128], F32, "o2")
        junk = P.sb([128, 128], F32, "junk")
        ss = P.sb([128, 1], F32, "ss")
        on = P.sb([128, 128], BF16, "on")
        t_o1, t_o2, t_ss, t_on = Tok(), Tok(), Tok(), Tok()
        ots = [P.sb([128, HEADS, QG], BF16, "ots%d" % q) for q in range(2)]
        t_ots = [Tok(), Tok()]
        ne = 0
        ng = 0
        for b in range(self.NB):
            kb.dma(kts[:], KT[b].rearrange("h p t -> p h t"), writes=[t_kts])
            for hh in range(HEADS):
                kb.dma(vs[:, :, hh, 0:128],
                       V[b, :, hh * 128:(hh + 1) * 128].rearrange("(kt p) e -> p kt e", p=128), writes=[t_vs])
            kb.op("pool", lambda: nc.gpsimd.memset(vs[:, :, :, 128:129], 1.0), writes=[t_vs])
            groups = []
            if need_ctx:
                for q0 in range(0, CTX, QG):
                    groups.append((q0, min(QG, CTX - q0), list(range(CTX // 128))))
            for q0 in range(0, self.SEQ, QG):
                groups.append((CTX + q0, QG, list(range(nkt))))
            for (q0, nq, ktl) in groups:
                gj = ng % 2
                ng += 1
                kb.dma(qs_[gj][:, :, :, :nq], QT[b, :, :, :, q0:q0 + nq].rearrange("h c p t -> p h c t"),
                       writes=[t_qs[gj]])
                nsub = nq // 128
                its = [(hh, ki, kt) for hh in range(HEADS) for ki, kt in enumerate(ktl)]
                nk = len(ktl)

                def emit_scores(idx, gj=gj, nq=nq, its=its):
                    hh, ki, kt = its[idx]
                    pj = (ne0 + idx) % NPS
                    for c in range(2):
                        kb.op("pe", lambda c=c: nc.tensor.matmul(
                            pss[pj][:, c, :nq], kts[:, hh, kt * 128:(kt + 1) * 128], qs_[gj][:, hh, c, :nq],
                            start=True, stop=True), reads=[t_kts, t_qs[gj]], writes=[t_pss[pj]])

                def emit_exp(idx, nq=nq):
                    pj = (ne0 + idx) % NPS
                    ej = (ne0 + idx) % 3
                    kb.op("act", lambda: nc.scalar.activation(
                        out=es[ej][:, :, :nq], in_=pss[pj][:, :, :nq], func=AF.Exp),
                        reads=[t_pss[pj]], writes=[t_es[ej]])

                def emit_pv(idx, nsub=nsub, its=its, nk=nk):
                    hh, ki, kt = its[idx]
                    ej = (ne0 + idx) % 3
                    for c in range(2):
                        for sq in range(nsub):
                            kb.op("pe", lambda c=c, sq=sq: nc.tensor.matmul(
                                psa[c][sq][:, 0:129], es[ej][:, c, sq * 128:(sq + 1) * 128],
                                vs[:, kt, hh, 0:129], start=(ki == 0), stop=(ki == nk - 1)),
                                reads=[t_es[ej], t_vs], writes=[t_psa[c][sq]])

                def finalize(hh, gj=gj, nsub=nsub):
                    for c in range(2):
                        for sq in range(nsub):
                            kb.op("dve", lambda c=c, sq=sq: nc.vector.tensor_copy(
                                out=accs[:, c, sq, :], in_=psa[c][sq][:, 0:129]),
                                reads=[t_psa[c][sq]], writes=[t_accs])
                    for sq in range(nsub):
                        kb.op("dve", lambda sq=sq: nc.vector.reciprocal(out=rz[:, 0:2], in_=accs[:, :, sq, 128]),
                              reads=[t_accs], writes=[t_rz])
                        kb.op("dve", lambda: nc.vector.tensor_tensor(out=rz[:, 1:2], in0=rz[:, 1:2], in1=nlam[:],
                                                                     op=ALU.mult), reads=[t_l], writes=[t_rz])
                        kb.op("dve", lambda sq=sq: nc.vector.tensor_scalar(
                            out=o1[:], in0=accs[:, 0, sq, 0:128], scalar1=rz[:, 0:1], scalar2=None, op0=ALU.mult),
                            reads=[t_accs, t_rz], writes=[t_o1])
                        kb.op("dve", lambda sq=sq: nc.vector.scalar_tensor_tensor(
                            out=o2[:], in0=accs[:, 1, sq, 0:128], scalar=rz[:, 1:2], in1=o1[:],
                            op0=ALU.mult, op1=ALU.add), reads=[t_accs, t_rz, t_o1], writes=[t_o2])
                        kb.op("dve", lambda: nc.vector.scalar_tensor_tensor(
                            out=junk[:], in0=o2[:], scalar=1.0, in1=o2[:], op0=ALU.mult, op1=ALU.mult,
                            accum_out=ss[:]), reads=[t_o2], writes=[t_ss])
                        kb.op("dve", lambda: nc.vector.tensor_scalar(
                            out=ss[:], in0=ss[:], scalar1=1.0 / 128.0, scalar2=EPS, op0=ALU.mult, op1=ALU.add),
                            writes=[t_ss])
                        kb.op("pool", lambda: nc.gpsimd.tensor_tensor(out=ss[:], in0=ss[:], in1=mhalf[:], op=ALU.pow),
                              reads=[t_l], writes=[t_ss])
                        kb.op("dve", lambda: nc.vector.scalar_tensor_tensor(
                            out=on[:], in0=o2[:], scalar=ss[:, 0:1], in1=gsb[:], op0=ALU.mult, op1=ALU.mult),
                            reads=[t_o2, t_ss, t_l], writes=[t_on])
                        kb.op("pe", lambda: nc.tensor.transpose(out=pst[:], in_=on[:], identity=ident[:]),
                              reads=[t_on, t_id], writes=[t_pst])
                        kb.op("dve", lambda sq=sq: nc.vector.tensor_copy(
                            out=ots[gj][:, hh, sq * 128:(sq + 1) * 128], in_=pst[:]),
                            reads=[t_pst], writes=[t_ots[gj]])

                ne0 = ne
                AH = 2
                for a in range(min(AH, len(its))):
                    emit_scores(a)
                for idx in range(len(its)):
                    emit_exp(idx)
                    if idx + AH < len(its):
                        emit_scores(idx + AH)
                    emit_pv(idx)
                    if its[idx][1] == nk - 1:
                        finalize(its[idx][0])
                ne += len(its)
                col = b * TB + q0
                kb.dma(OT[0:1024, col:col + nq].rearrange("(h p) t -> p h t", p=128), ots[gj][:, :, :nq],
                       reads=[t_ots[gj]])
        kb.barrier()


def phase_proj_post(self, i, wname, widx, kc, with_ctx, glu=False):
    kb, nc = self.kb, self.nc
    NT = self.NT
    xT = self.dram["xT"].rearrange("(k p) t -> p k t", p=128)
    OT = self.dram["OT"].rearrange("(k p) t -> p k t", p=128)
    wd = self.dram[wname][widx].rearrange("(k p) d -> p k d", p=128)
    ncol = 2048 if glu else 1024
    with Pool(kb) as P:
        ws = P.sb([128, kc, ncol], BF16, "ws")
        t_w = [Tok() for _ in range(kc)]
        for k in range(kc):
            for hf in range(ncol // 1024):
                kb.dma(ws[:, k, hf * 1024:(hf + 1) * 1024], wd[:, k, hf * 1024:(hf + 1) * 1024],
                       writes=[t_w[k]], q="pool")
        xt = [P.sb([128, KC, NT], F32, "xt%d" % q) for q in range(2)]
        t_xt = [Tok(), Tok()]
        at = [P.sb([128, kc, NT], BF16, "at%d" % q) for q in range(2)]
        t_at = [Tok(), Tok()]
        y = P.sb([128, KC, NT], F32, "y")
        t_y = Tok()
        sg = P.sb([128, NT], F32, "sg")
        t_sg = Tok()
        W = self.norm_work(P, NT)
        psn = P.ps([128, 512], F32, "psn")
        t_psn = Tok(excl=True)
        psd = [P.ps([128, 512], F32, "psd%d" % q) for q in range(4)]
        t_psd = [Tok(excl=True) for _ in range(4)]
        tl = self.tiles(with_ctx)
        c0, n, m = tl[0]
        kb.dma(xt[0][:, :, :n], xT[:, :, c0:c0 + n], writes=[t_xt[0]])
        kb.dma(at[0][:, :, :n], OT[:, 0:kc, c0:c0 + n], writes=[t_at[0]])
        nd = 0
        for ti, (c0, n, m) in enumerate(tl):
            jx = ti % 2
            if ti + 1 < len(tl):
                c1, n1, _ = tl[ti + 1]
                kb.dma(xt[1 - jx][:, :, :n1], xT[:, :, c1:c1 + n1], writes=[t_xt[1 - jx]])
                kb.dma(at[1 - jx][:, :, :n1], OT[:, 0:kc, c1:c1 + n1], writes=[t_at[1 - jx]])
            for d in range(KC):
                pj = nd % 2
                nd += 1
                for k in range(kc):
                    kb.op("pe", lambda d=d, k=k, pj=pj: nc.tensor.matmul(
                        psd[pj][:, :n], ws[:, k, d * 128:(d + 1) * 128], at[jx][:, k, :n],
                        start=(k == 0), stop=(k == kc - 1)), reads=[t_w[k], t_at[jx]], writes=[t_psd[pj]])
                if glu:
                    for k in range(kc):
                        kb.op("pe", lambda d=d, k=k, pj=pj: nc.tensor.matmul(
                            psd[2 + pj][:, :n], ws[:, k, 1024 + d * 128:1024 + (d + 1) * 128], at[jx][:, k, :n],
                            start=(k == 0), stop=(k == kc - 1)), reads=[t_w[k], t_at[jx]], writes=[t_psd[2 + pj]])
                    kb.op("act", lambda pj=pj: nc.scalar.activation(out=sg[:, :n], in_=psd[2 + pj][:, :n],
                                                                    func=AF.Sigmoid),
                          reads=[t_psd[2 + pj]], writes=[t_sg])
                    kb.op("dve", lambda d=d, pj=pj: nc.vector.tensor_tensor(
                        out=y[:, d, :n], in0=psd[pj][:, :n], in1=sg[:, :n], op=ALU.mult),
                        reads=[t_psd[pj], t_sg], writes=[t_y])
                else:
                    kb.op("act", lambda d=d, pj=pj: nc.scalar.copy(out=y[:, d, :n], in_=psd[pj][:, :n]),
                          reads=[t_psd[pj]], writes=[t_y])
            self.norm_post(y, t_y, xt[jx], t_xt[jx], n, i, 0, m, W, psn, t_psn)
            kb.dma(xT[:, :, c0:c0 + n], xt[jx][:, :, :n], reads=[t_xt[jx]])
        kb.barrier()


Model.phase_attn_qkv = phase_attn_qkv
Model.phase_attn_core = phase_attn_core
Model.phase_proj_post = phase_proj_post


LW = 1536
LC = 12


def gelu_tanh(self, dst, src, t_src, shape, Wg, t_dst):
    kb, nc = self.kb, self.nc
    a, b_ = Wg["a"], Wg["b"]
    va = a[:, :shape[1]] if len(shape) == 2 else a[:, :shape[1], :shape[2]]
    vb = b_[:, :shape[1]] if len(shape) == 2 else b_[:, :shape[1], :shape[2]]
    kb.op("act", lambda: nc.scalar.activation(out=va, in_=src, func=AF.Square), reads=[t_src], writes=[Wg["ta"]])
    kb.op("dve", lambda: nc.vector.tensor_scalar(out=va, in0=va, scalar1=0.044715, scalar2=1.0,
                                                 op0=ALU.mult, op1=ALU.add), writes=[Wg["ta"]])
    kb.op("dve", lambda: nc.vector.tensor_tensor(out=va, in0=va, in1=src, op=ALU.mult),
          reads=[t_src], writes=[Wg["ta"]])
    kb.op("act", lambda: nc.scalar.activation(out=vb, in_=va, func=AF.Sigmoid, scale=1.5957691216057308),
          reads=[Wg["ta"]], writes=[Wg["tb"]])
    kb.op("dve", lambda: nc.vector.tensor_tensor(out=dst, in0=vb, in1=src, op=ALU.mult),
          reads=[Wg["tb"], t_src], writes=[t_dst])


def _lru_scratch(self):
    _attn_scratch(self)
    if "RT" not in self.dram:
        self.dscratch("RT", [LW, self.T], F32)
        self.dscratch("GT", [LW, self.T], BF16)


def phase_lru_in(self, i, j):
    kb, nc = self.kb, self.nc
    _lru_scratch(self)
    NT = self.NT
    xT = self.dram["xT"].rearrange("(k p) t -> p k t", p=128)
    win = self.dram["lru_w_in"][j].rearrange("(k p) f -> p k f", p=128)
    RT = self.dram["RT"].rearrange("(k p) t -> p k t", p=128)
    GT = self.dram["GT"].rearrange("(k p) t -> p k t", p=128)
    with Pool(kb) as P:
        ws = P.sb([128, KC, 2 * LW], BF16, "wins")
        t_w = [Tok() for _ in range(6)]
        for g in range(6):
            kb.dma(ws[:, :, g * 512:(g + 1) * 512], win[:, :, g * 512:(g + 1) * 512], writes=[t_w[g]], q="pool")
        xt = [P.sb([128, KC, NT], F32, "xt%d" % q) for q in range(2)]
        t_xt = [Tok(), Tok()]
        h = P.sb([128, KC, NT], BF16, "h")
        t_h = Tok()
        W = self.norm_work(P, NT)
        psn = P.ps([128, 512], F32, "psn")
        t_psn = Tok(excl=True)
        pso = [P.ps([128, 2, 256], F32, "pso%d" % q) for q in range(3)]
        t_pso = [Tok(excl=True) for _ in range(3)]
        Wg = dict(a=P.sb([128, 2, 256], F32, "ga"), b=P.sb([128, 2, 256], F32, "gb"), ta=Tok(), tb=Tok())
        gs = [P.sb([128, LC, NT], BF16, "gs%d" % q) for q in range(2)]
        t_gs = [Tok(), Tok()]
        rs = [P.sb([128, LC, NT], F32, "rs%d" % q) for q in range(2)]
        t_rs = [Tok(), Tok()]
        tl = self.tiles(True)
        c0, n, m = tl[0]
        kb.dma(xt[0][:, :, :n], xT[:, :, c0:c0 + n], writes=[t_xt[0]])
        np_ = 0
        for ti, (c0, n, m) in enumerate(tl):
            jx = ti % 2
            if ti + 1 < len(tl):
                c1, n1, _ = tl[ti + 1]
                kb.dma(xt[1 - jx][:, :, :n1], xT[:, :, c1:c1 + n1], writes=[t_xt[1 - jx]])
            self.norm_pre(xt[jx], t_xt[jx], n, i, 0, m, h, t_h, W, psn, t_psn)
            for op in range(LC):
                pj = np_ % 3
                np_ += 1
                for q2 in range(2):
                    oc = op * 2 + q2
                    for k in range(KC):
                        kb.op("pe", lambda k=k, oc=oc, q2=q2, pj=pj: nc.tensor.matmul(
                            pso[pj][:, q2, :n], ws[:, k, oc * 128:(oc + 1) * 128], h[:, k, :n],
                            start=(k == 0), stop=(k == KC - 1)), reads=[t_w[oc // 4], t_h], writes=[t_pso[pj]])
                if op < 6:
                    gelu_tanh(self, gs[jx][:, 2 * op:2 * op + 2, :n], pso[pj][:, :, :n], t_pso[pj],
                              [128, 2, n], Wg, t_gs[jx])
                else:
                    kb.op("act", lambda op=op, pj=pj: nc.scalar.copy(
                        out=rs[jx][:, 2 * (op - 6):2 * (op - 6) + 2, :n], in_=pso[pj][:, :, :n]),
                        reads=[t_pso[pj]], writes=[t_rs[jx]])
            kb.dma(GT[:, :, c0:c0 + n], gs[jx][:, :, :n], reads=[t_gs[jx]])
            kb.dma(RT[:, :, c0:c0 + n], rs[jx][:, :, :n], reads=[t_rs[jx]])
        kb.barrier()


def phase_lru_scan(self, i, j):
    kb, nc = self.kb, self.nc
    TB, CTX, SEQ = self.TB, self.CTX, self.SEQ
    RT = self.dram["RT"].rearrange("(k p) t -> p k t", p=128)
    GT = self.dram["GT"].rearrange("(k p) t -> p k t", p=128)
    OT = self.dram["OT"].rearrange("(k p) t -> p k t", p=128)
    CW = 512
    with Pool(kb) as P:
        wg = P.sb([128, 2, 2, 6, 2, 256], BF16, "wg")
        cw = P.sb([128, LC, 4], F32, "cw")
        cb = P.sb([128, LC], F32, "cb")
        bg = P.sb([128, 2, 2, LC], F32, "bg")
        ap_ = P.sb([128, 2, LC], F32, "ap")
        cv = P.sb([128, 2, LC], F32, "cv")
        cv2 = P.sb([128, 2, LC], F32, "cv2")
        one = P.sb([128, 1], F32, "one")
        t_s = Tok()
        kb.dma(wg[:], self.dram["lru_w_gate"][j], writes=[t_s], q="pool")
        kb.dma(cw[:], self.dram["lru_conv_w"][j], writes=[t_s])
        kb.dma(cb[:], self.dram["lru_conv_b"][j], writes=[t_s])
        kb.dma(bg[:], self.dram["lru_b_gate"][j], writes=[t_s])
        kb.dma(ap_[:], self.dram["lru_a_param"][j], writes=[t_s])
        kb.op("dve", lambda: nc.vector.memset(one[:], 1.0), writes=[t_s])
        kb.op("act", lambda: nc.scalar.activation(out=cv[:], in_=ap_[:], func=AF.Exp, scale=-1.0),
              reads=[t_s], writes=[t_s])
        kb.op("act", lambda: nc.scalar.activation(out=cv[:], in_=cv[:], func=AF.Ln, bias=one[:], scale=1.0),
              reads=[t_s], writes=[t_s])
        cvh = cv2
        bgh = P.sb([128, 2, 2, LC], F32, "bgh")
        kb.op("dve", lambda: nc.vector.tensor_scalar(out=cvh[:], in0=cv[:], scalar1=-4.0, scalar2=None,
                                                     op0=ALU.mult), reads=[t_s], writes=[t_s])
        kb.op("dve", lambda: nc.vector.tensor_scalar(out=cv[:], in0=cv[:], scalar1=-8.0, scalar2=None,
                                                     op0=ALU.mult), reads=[t_s], writes=[t_s])
        kb.op("dve", lambda: nc.vector.tensor_scalar(out=bgh[:], in0=bg[:], scalar1=0.5, scalar2=None,
                                                     op0=ALU.mult), reads=[t_s], writes=[t_s])
        rt = P.sb([128, TB], F32, "rt")
        t_rt = Tok()
        u = P.sb([128, 2, TB], F32, "u")
        t_u = [Tok(), Tok()]
        ub = P.sb([128, 2, TB], BF16, "ub")
        t_ub = [Tok(), Tok()]
        av = P.sb([128, TB], F32, "av")
        bv = P.sb([128, TB], F32, "bv")
        t_av, t_bv = Tok(), Tok()
        hf = P.sb([128, TB], F32, "hf")
        hb = P.sb([128, TB], F32, "hb")
        t_hf, t_hb = Tok(), Tok()
        gt = P.sb([128, TB], BF16, "gt")
        t_gt = Tok()
        mo = P.sb([128, TB], BF16, "mo")
        t_mo = Tok()
        psg = [P.ps([128, 2, CW], F32, "psg%d" % q) for q in range(2)]
        t_psg = [Tok(excl=True), Tok(excl=True)]
        sr = [P.sb([128, 2, CW], F32, "sr%d" % q) for q in range(2)]
        t_sr = [Tok(), Tok()]
        a2 = [P.sb([128, CW], F32, "a2%d" % q) for q in range(2)]
        t_a2 = [Tok(), Tok()]
        segs = [(0, CTX), (CTX, TB)]
        ng = 0
        for b in range(self.NB):
            cb0 = b * TB
            for n6 in range(6):
                for q2 in range(2):
                    ch = n6 * 2 + q2
                    kb.dma(rt[:], RT[:, ch, cb0:cb0 + TB], writes=[t_rt])
                    for (s0, s1) in segs:
                        kb.op("dve", lambda s0=s0, s1=s1, ch=ch, q2=q2: nc.vector.tensor_scalar(
                            out=u[:, q2, s0:s1], in0=rt[:, s0:s1], scalar1=cw[:, ch, 2:3], scalar2=cb[:, ch:ch + 1],
                            op0=ALU.mult, op1=ALU.add), reads=[t_rt, t_s], writes=[t_u[q2]])
                        for (tap, off) in ((0, -2), (1, -1), (3, 1)):
                            if off < 0:
                                o_sl, i_sl = slice(s0 - off, s1), slice(s0, s1 + off)
                            else:
                                o_sl, i_sl = slice(s0, s1 - off), slice(s0 + off, s1)
                            kb.op("dve", lambda o_sl=o_sl, i_sl=i_sl, ch=ch, q2=q2, tap=tap:
                                  nc.vector.scalar_tensor_tensor(
                                      out=u[:, q2, o_sl], in0=rt[:, i_sl], scalar=cw[:, ch, tap:tap + 1],
                                      in1=u[:, q2, o_sl], op0=ALU.mult, op1=ALU.add),
                                  reads=[t_rt, t_s], writes=[t_u[q2]])
                    kb.op("act", lambda q2=q2: nc.scalar.copy(out=ub[:, q2, :], in_=u[:, q2, :]),
                          reads=[t_u[q2]], writes=[t_ub[q2]])
                    kb.op("pool", lambda q2=q2: nc.gpsimd.tensor_scalar(
                        out=u[:, q2, :], in0=u[:, q2, :], scalar1=0.5, scalar2=None, op0=ALU.mult),
                        reads=[t_ub[q2]], writes=[t_u[q2]])
                for q2 in range(2):
                    ch = n6 * 2 + q2
                    kb.dma(gt[:], GT[:, ch, cb0:cb0 + TB], writes=[t_gt])
                    for d in range(2):
                        for c0 in range(0, TB, CW):
                            n = min(CW, TB - c0)
                            pj = ng % 2
                            ng += 1
                            for kk in range(2):
                                for kc in range(2):
                                    kb.op("pe", lambda kk=kk, kc=kc, d=d, c0=c0, n=n, pj=pj: nc.tensor.matmul(
                                        psg[pj][:, kk, :n], wg[:, d, kk, n6, kc, q2 * 128:(q2 + 1) * 128],
                                        ub[:, kc, c0:c0 + n], start=(kc == 0), stop=(kc == 1)),
                                        reads=[t_s] + t_ub, writes=[t_psg[pj]])
                            for kk in range(2):
                                kb.op("act", lambda kk=kk, d=d, n=n, pj=pj: nc.scalar.activation(
                                    out=sr[pj][:, kk, :n], in_=psg[pj][:, kk, :n], func=AF.Tanh,
                                    bias=bgh[:, d, kk, ch:ch + 1], scale=0.5),
                                    reads=[t_psg[pj], t_s], writes=[t_sr[pj]])
                            kb.op("act", lambda d=d, c0=c0, n=n, pj=pj: nc.scalar.activation(
                                out=av[:, c0:c0 + n], in_=sr[pj][:, 0, :n], func=AF.Exp,
                                scale=cvh[:, d, ch:ch + 1], bias=cvh[:, d, ch:ch + 1]),
                                reads=[t_sr[pj], t_s], writes=[t_av])
                            kb.op("act", lambda d=d, n=n, pj=pj: nc.scalar.activation(
                                out=a2[pj][:, :n], in_=sr[pj][:, 0, :n], func=AF.Exp,
                                scale=cv[:, d, ch:ch + 1], bias=cv[:, d, ch:ch + 1]),
                                reads=[t_sr[pj], t_s], writes=[t_a2[pj]])
                            kb.op("pool", lambda n=n, pj=pj, c0=c0: nc.gpsimd.tensor_scalar(
                                out=bv[:, c0:c0 + n], in0=a2[pj][:, :n], scalar1=-1.0, scalar2=1.0,
                                op0=ALU.mult, op1=ALU.add), reads=[t_a2[pj]], writes=[t_bv])
                            kb.op("dve", lambda n=n, pj=pj, c0=c0: nc.vector.scalar_tensor_tensor(
                                out=rt[:, c0:c0 + n], in0=sr[pj][:, 1, :n], scalar=1.0, in1=u[:, q2, c0:c0 + n],
                                op0=ALU.add, op1=ALU.mult), reads=[t_sr[pj], t_u[q2]], writes=[t_rt])
                        kb.op("act", lambda: nc.scalar.activation(out=bv[:, :], in_=bv[:, :], func=AF.Sqrt),
                              writes=[t_bv])
                        kb.op("dve", lambda: nc.vector.tensor_tensor(out=bv[:, :], in0=bv[:, :], in1=rt[:, :],
                                                                     op=ALU.mult), reads=[t_rt], writes=[t_bv])
                        if d == 0:
                            kb.op("dve", lambda: nc.vector.tensor_tensor_scan(
                                out=hf[:, :], data0=av[:, :], data1=bv[:, :], initial=0.0,
                                op0=ALU.mult, op1=ALU.add), reads=[t_av, t_bv], writes=[t_hf])
                        else:
                            kb.op("dve", lambda: nc.vector.tensor_tensor_scan(
                                out=hb[:, 0:CTX][:, ::-1], data0=av[:, 0:CTX][:, ::-1], data1=bv[:, 0:CTX][:, ::-1],
                                initial=0.0, op0=ALU.mult, op1=ALU.add), reads=[t_av, t_bv], writes=[t_hb])
                            kb.op("dve", lambda: nc.vector.tensor_tensor_scan(
                                out=hb[:, CTX:TB][:, ::-1], data0=av[:, CTX:TB][:, ::-1],
                                data1=bv[:, CTX:TB][:, ::-1], initial=hb[:, 0:1], op0=ALU.mult, op1=ALU.add),
                                reads=[t_av, t_bv], writes=[t_hb])
                    kb.op("dve", lambda: nc.vector.tensor_tensor(out=hf[:], in0=hf[:], in1=hb[:], op=ALU.add),
                          reads=[t_hb], writes=[t_hf])
                    kb.op("dve", lambda: nc.vector.tensor_tensor(out=mo[:], in0=hf[:], in1=gt[:], op=ALU.mult),
                          reads=[t_hf, t_gt], writes=[t_mo])
                    kb.dma(OT[:, ch, cb0:cb0 + TB], mo[:], reads=[t_mo])
        kb.barrier()


Model.phase_lru_in = phase_lru_in
Model.phase_lru_scan = phase_lru_scan


import math as _math
NG = 64
LL = 8


def _s5_scratch(self):
    _attn_scratch(self)
    if "UT" not in self.dram:
        self.dscratch("UT", [1024, self.T], BF16)
        self.dscratch("YF", [2, 1024, self.T], F32)


def phase_s5_in(self, i):
    kb, nc = self.kb, self.nc
    _s5_scratch(self)
    NT = self.NT
    xT = self.dram["xT"].rearrange("(k p) t -> p k t", p=128)
    UT = self.dram["UT"].rearrange("(k p) t -> p k t", p=128)
    with Pool(kb) as P:
        xt = [P.sb([128, KC, NT], F32, "xt%d" % q) for q in range(2)]
        t_xt = [Tok(), Tok()]
        h = [P.sb([128, KC, NT], BF16, "h%d" % q) for q in range(2)]
        t_h = [Tok(), Tok()]
        W = self.norm_work(P, NT)
        psn = P.ps([128, 512], F32, "psn")
        t_psn = Tok(excl=True)
        tl = self.tiles(True)
        c0, n, m = tl[0]
        kb.dma(xt[0][:, :, :n], xT[:, :, c0:c0 + n], writes=[t_xt[0]])
        for ti, (c0, n, m) in enumerate(tl):
            jx = ti % 2
            if ti + 1 < len(tl):
                c1, n1, _ = tl[ti + 1]
                kb.dma(xt[1 - jx][:, :, :n1], xT[:, :, c1:c1 + n1], writes=[t_xt[1 - jx]])
            self.norm_pre(xt[jx], t_xt[jx], n, i, 0, m, h[jx], t_h[jx], W, psn, t_psn)
            kb.dma(UT[:, :, c0:c0 + n], h[jx][:, :, :n], reads=[t_h[jx]])
        kb.barrier()


def phase_s5_scan(self, i, jb):
    kb, nc = self.kb, self.nc
    TB, CTX, SEQ = self.TB, self.CTX, self.SEQ
    J = TB // LL
    npass = 0
    while (1 << npass) < J:
        npass += 1
    pmax = getattr(Model, "s5_piece_max", 512)
    pieces = [(0, J)]
    if J > pmax:
        pieces = [(0, pmax), (pmax, J)]
    PB = pmax
    UT = self.dram["UT"].rearrange("(k p) t -> p k t", p=128)
    YF = self.dram["YF"]
    TWO_PI = 2.0 * _math.pi
    V = nc.vector
    for d in getattr(self, 's5_dirs', (0, 1)):
        with Pool(kb) as P:
            t_st = Tok()

            def tt(o, a, b, op):
                kb.op("dve", lambda: V.tensor_tensor(out=o, in0=a, in1=b, op=op), writes=[t_st])

            def ts(o, a, s1, op0, s2=None, op1=None):
                if op1 is None:
                    kb.op("dve", lambda: V.tensor_scalar(out=o, in0=a, scalar1=s1, scalar2=None, op0=op0),
                          writes=[t_st])
                else:
                    kb.op("dve", lambda: V.tensor_scalar(out=o, in0=a, scalar1=s1, scalar2=s2, op0=op0, op1=op1),
                          writes=[t_st])

            def act(o, a, func, scale=1.0, bias=None):
                if bias is None:
                    kb.op("act", lambda: nc.scalar.activation(out=o, in_=a, func=func, scale=scale), writes=[t_st])
                else:
                    kb.op("act", lambda: nc.scalar.activation(out=o, in_=a, func=func, scale=scale, bias=bias),
                          writes=[t_st])
            BS = P.sb([128, 8, NG, 16], BF16, "BS")
            CS = P.sb([128, 9, NG, 16], BF16, "CS")
            PW = P.sb([128, 9, 2, NG], F32, "PW")
            QW = P.sb([128, npass, 2, NG], F32, "QW")
            qos = P.sb([128, npass, NG], F32, "qos")
            identb = P.sb([128, 128], BF16, "identb")
            identf = P.sb([128, 128], F32, "identf")
            shf = P.sb([128, 128], F32, "shf")
            dsk = P.sb([128, KC], F32, "dsk")
            kb.dma(identb[:], self.dram["ident"], writes=[t_st], q="pool")
            kb.dma(identf[:], self.dram["ident"], writes=[t_st])
            kb.dma(shf[:], self.dram["shift64"], writes=[t_st])
            with Pool(kb) as PS:
                pg = lambda nm: PS.sb([128, NG], F32, nm)
                are, aim, dt, xr, xi, mag, yy, ff, mm, sn, cs, lr, li = [pg("pg%d" % q) for q in range(13)]
                nr, den, cr, ci, t1, t2, t3, t4 = [pg("pgb%d" % q) for q in range(8)]
                ki = PS.sb([128, NG], I32, "ki")
                pgc = lambda nm: PS.sb([128, NG, 16], F32, nm)
                Bre, Bim, Cre, Cim, Bbr, Bbi, T1, T2, T3, T4 = [pgc("pgc%d" % q) for q in range(10)]
                kb.dma(are[:], self.dram["s5_a_re"][jb, d], writes=[t_st])
                kb.dma(aim[:], self.dram["s5_a_im"][jb, d], writes=[t_st])
                kb.dma(dt[:], self.dram["s5_log_dt"][jb, d].partition_broadcast(128), writes=[t_st])
                kb.dma(Bre[:], self.dram["s5_b_re"][jb, d], writes=[t_st])
                kb.dma(Bim[:], self.dram["s5_b_im"][jb, d], writes=[t_st])
                kb.dma(Cre[:], self.dram["s5_c_re"][jb, d], writes=[t_st])
                kb.dma(Cim[:], self.dram["s5_c_im"][jb, d], writes=[t_st])
                act(dt[:], dt[:], AF.Exp)
                tt(xr[:], are[:], dt[:], ALU.mult)
                tt(xi[:], aim[:], dt[:], ALU.mult)
                ts(yy[:], xi[:], 1.0 / TWO_PI, ALU.mult)
                kb.op("dve", lambda: V.tensor_copy(out=ki[:], in_=yy[:]), writes=[t_st])
                kb.op("dve", lambda: V.tensor_copy(out=ff[:], in_=ki[:]), writes=[t_st])
                tt(ff[:], yy[:], ff[:], ALU.subtract)
                ts(mm[:], ff[:], 0.5, ALU.is_gt)
                tt(ff[:], ff[:], mm[:], ALU.subtract)
                ts(mm[:], ff[:], -0.5, ALU.is_lt)
                tt(ff[:], ff[:], mm[:], ALU.add)
                ts(ff[:], ff[:], TWO_PI / 16.0, ALU.mult)
                tt(yy[:], ff[:], ff[:], ALU.mult)

                def horner(o, z, coefs):
                    kb.op("dve", lambda: V.memset(o, 0.0), writes=[t_st])
                    for c in reversed(coefs[1:]):
                        kb.op("dve", lambda c=c: V.scalar_tensor_tensor(out=o, in0=o, scalar=float(c), in1=z,
                                                                        op0=ALU.add, op1=ALU.mult), writes=[t_st])
                    ts(o, o, float(coefs[0]), ALU.add)
                fact = [1.0]
                for q in range(1, 16):
                    fact.append(fact[-1] * q)
                horner(cs[:], yy[:], [(-1) ** q / fact[2 * q] for q in range(6)])
                horner(sn[:], yy[:], [(-1) ** q / fact[2 * q + 1] for q in range(6)])
                tt(sn[:], sn[:], ff[:], ALU.mult)
                ts(mm[:], xr[:], 1.0 / 16.0, ALU.mult)
                horner(mag[:], mm[:], [1.0 / fact[q] for q in range(10)])
                tt(lr[:], mag[:], cs[:], ALU.mult)
                tt(li[:], mag[:], sn[:], ALU.mult)
                for _sq in range(4):
                    tt(t1[:], lr[:], lr[:], ALU.mult)
                    tt(t2[:], li[:], li[:], ALU.mult)
                    tt(t3[:], lr[:], li[:], ALU.mult)
                    tt(lr[:], t1[:], t2[:], ALU.subtract)
                    ts(li[:], t3[:], 2.0, ALU.mult)
                ts(nr[:], lr[:], -1.0, ALU.add)
                tt(t1[:], are[:], are[:], ALU.mult)
                tt(t2[:], aim[:], aim[:], ALU.mult)
                tt(den[:], t1[:], t2[:], ALU.add)
                kb.op("dve", lambda: V.reciprocal(out=den[:], in_=den[:]), writes=[t_st])
                tt(t1[:], nr[:], are[:], ALU.mult)
                tt(t2[:], li[:], aim[:], ALU.mult)
                tt(cr[:], t1[:], t2[:], ALU.add)
                tt(cr[:], cr[:], den[:], ALU.mult)
                tt(t1[:], li[:], are[:], ALU.mult)
                tt(t2[:], nr[:], aim[:], ALU.mult)
                tt(ci[:], t1[:], t2[:], ALU.subtract)
                tt(ci[:], ci[:], den[:], ALU.mult)
                bc = lambda v: v.unsqueeze(2).broadcast_to([128, NG, 16])
                tt(T1[:], Bre[:], bc(cr[:]), ALU.mult)
                tt(T2[:], Bim[:], bc(ci[:]), ALU.mult)
                tt(Bbr[:], T1[:], T2[:], ALU.subtract)
                tt(T1[:], Bim[:], bc(cr[:]), ALU.mult)
                tt(T2[:], Bre[:], bc(ci[:]), ALU.mult)
                tt(Bbi[:], T1[:], T2[:], ALU.add)
                kb.op("dve", lambda: V.memset(PW[:, 0, 0, :], 1.0), writes=[t_st])
                kb.op("dve", lambda: V.memset(PW[:, 0, 1, :], 0.0), writes=[t_st])
                kb.op("dve", lambda: V.tensor_copy(out=PW[:, 1, 0, :], in_=lr[:]), writes=[t_st])
                kb.op("dve", lambda: V.tensor_copy(out=PW[:, 1, 1, :], in_=li[:]), writes=[t_st])

                def cmul(o_r, o_i, a_r, a_i, b_r, b_i):
                    tt(t1[:], a_r, b_r, ALU.mult)
                    tt(t2[:], a_i, b_i, ALU.mult)
                    tt(t3[:], a_r, b_i, ALU.mult)
                    tt(t4[:], a_i, b_r, ALU.mult)
                    tt(o_r, t1[:], t2[:], ALU.subtract)
                    tt(o_i, t3[:], t4[:], ALU.add)
                for e in range(2, 9):
                    cmul(PW[:, e, 0, :], PW[:, e, 1, :], PW[:, e - 1, 0, :], PW[:, e - 1, 1, :], lr[:], li[:])
                kb.op("dve", lambda: V.tensor_copy(out=QW[:, 0, :, :], in_=PW[:, 8, :, :]), writes=[t_st])
                for k in range(1, npass):
                    cmul(QW[:, k, 0, :], QW[:, k, 1, :], QW[:, k - 1, 0, :], QW[:, k - 1, 1, :],
                         QW[:, k - 1, 0, :], QW[:, k - 1, 1, :])
                kb.op("dve", lambda: V.tensor_copy(out=qos[0:64, :, :], in_=QW[0:64, :, 1, :]), writes=[t_st])
                kb.op("dve", lambda: V.tensor_scalar(out=qos[64:128, :, :], in0=QW[64:128, :, 1, :], scalar1=-1.0,
                                                     scalar2=None, op0=ALU.mult), writes=[t_st])
                for e in range(9):
                    p_r, p_i = bc(PW[:, e, 0, :]), bc(PW[:, e, 1, :])
                    if e < 8:
                        tt(T1[:], Bbr[:], p_r, ALU.mult)
                        tt(T2[:], Bbi[:], p_i, ALU.mult)
                        tt(T3[:], Bbi[:], p_r, ALU.mult)
                        tt(T4[:], Bbr[:], p_i, ALU.mult)
                        tt(BS[0:64, e], T1[0:64], T2[0:64], ALU.subtract)
                        tt(BS[64:128, e], T3[64:128], T4[64:128], ALU.add)
                    tt(T1[:], Cre[:], p_r, ALU.mult)
                    tt(T2[:], Cim[:], p_i, ALU.mult)
                    tt(T3[:], Cre[:], p_i, ALU.mult)
                    tt(T4[:], Cim[:], p_r, ALU.mult)
                    tt(CS[0:64, e], T1[0:64], T2[0:64], ALU.subtract)
                    kb.op("dve", lambda: V.scalar_tensor_tensor(
                        out=CS[64:128, e], in0=T3[64:128], scalar=-1.0, in1=T4[64:128],
                        op0=ALU.mult, op1=ALU.subtract), writes=[t_st])
            if getattr(self, "s5_stop", 0) == 1:
                if d == getattr(self, "s5_dbg_dir", 0):
                    kb.dma(self.dram["dbgBS"], BS[:], reads=[t_st])
                    kb.dma(self.dram["dbgCS"], CS[:], reads=[t_st])
                    kb.dma(self.dram["dbgPW"], PW[:], reads=[t_st])
                    kb.dma(self.dram["dbgQW"], QW[:], reads=[t_st])
                kb.barrier()
                continue
            ECt = P.sb([128, 9, 1152], BF16, "ECt")
            LBt = P.sb([128, 8, 8, 128], BF16, "LBt")
            Kdt = P.sb([128, 8, 128], BF16, "Kdt")
            AMt = P.sb([128, npass, 8, 128], BF16, "AMt")
            t_EC, t_LB, t_Kd, t_AM = Tok(), Tok(), Tok(), Tok()
            kb.op("pool", lambda: nc.gpsimd.memset(ECt[:], 0.0), reads=[t_st], writes=[t_EC])
            _ub = P.sb([128, TB + CTX], BF16, "ub")
            ub = [_ub, _ub]
            _tub = Tok()
            t_ub = [_tub, _tub]
            ud = P.sb([128, LL, J], BF16, "ud")
            t_ud = Tok()
            S32 = P.sb([128, 8, J], F32, "S32")
            Sb = P.sb([128, 8, J], BF16, "Sb")
            Hb = P.sb([128, 8, J], BF16, "Hb")
            t_S32 = [Tok() for _ in range(8)]
            t_Sb = [Tok() for _ in range(8)]
            t_Hb = Tok()
            kb.op("pool", lambda: nc.gpsimd.memset(Hb[:, :, 0:1], 0.0), writes=[t_Hb])
            yb1 = P.sb([128, TB], F32, "yb")
            yb = [yb1, yb1]
            t_yb1 = Tok()
            t_yb = [t_yb1, t_yb1]
            _a = P.sb([128, 8, 128], F32, "amt")
            amt = [_a, _a]
            _ta = Tok()
            t_amt = [_ta, _ta]
            _b = P.sb([128, 8, 128], F32, "amtb")
            amt2 = [_b, _b]
            _tb = Tok()
            t_amt2 = [_tb, _tb]
            nu = 0
            for cidx in getattr(self, 's5_chunks', range(KC)):
                g0 = cidx * 8
                with Pool(kb) as PC:
                    EBt = PC.sb([128, 8, 1152], BF16, "EBt")
                    t_EB = Tok()
                    pst = [PC.ps([128, 8, 128], BF16, "pst%d" % q) for q in range(2)]
                    t_pst = [Tok(excl=True), Tok(excl=True)]
                    psk = [PC.ps([128, 4, 128], F32, "psk%d" % q) for q in range(2)]
                    t_psk = [Tok(excl=True), Tok(excl=True)]
                    kb.op("pool", lambda: nc.gpsimd.memset(EBt[:], 0.0), writes=[t_EB])
                    for e in range(9):
                        if e < 8:
                            kb.op("dve", lambda e=e: V.tensor_copy(
                                out=EBt[:, e, :].rearrange("p (g s) -> p g s", s=144)[:, :, 0:16],
                                in_=BS[:, e, g0:g0 + 8, :]), reads=[t_st], writes=[t_EB])
                        kb.op("dve", lambda e=e: V.tensor_copy(
                            out=ECt[:, e, :].rearrange("p (g s) -> p g s", s=144)[:, :, 0:16],
                            in_=CS[:, e, g0:g0 + 8, :]), reads=[t_st], writes=[t_EC])
                    for e in range(8):
                        pj = e % 2
                        for g in range(8):
                            kb.op("pe", lambda e=e, g=g, pj=pj: nc.tensor.transpose(
                                out=pst[pj][:, g, :], in_=EBt[:, e, g * 128:(g + 1) * 128], identity=identb[:]),
                                reads=[t_EB, t_st], writes=[t_pst[pj]])
                        kb.op("act", lambda e=e, pj=pj: nc.scalar.copy(out=LBt[:, e, :, :], in_=pst[pj][:, :, :]),
                              reads=[t_pst[pj]], writes=[t_LB])
                    for hb_ in range(2):
                        for e4 in range(4):
                            e = hb_ * 4 + e4
                            for g in range(8):
                                kb.op("pe", lambda e=e, e4=e4, g=g, hb_=hb_: nc.tensor.matmul(
                                    psk[hb_][:, e4, :], EBt[:, e, g * 128:(g + 1) * 128],
                                    ECt[:, 0, g * 128:(g + 1) * 128], start=(g == 0), stop=(g == 7)),
                                    reads=[t_EB, t_EC], writes=[t_psk[hb_]])
                        kb.op("act", lambda hb_=hb_: nc.scalar.copy(out=Kdt[:, hb_ * 4:(hb_ + 1) * 4, :],
                                                                    in_=psk[hb_][:, :, :]),
                              reads=[t_psk[hb_]], writes=[t_Kd])
                    for k in range(npass):
                        aj = k % 2
                        kb.op("dve", lambda k=k, aj=aj: V.tensor_tensor(
                            out=amt[aj][:], in0=identf[:, :].unsqueeze(1).broadcast_to([128, 8, 128]),
                            in1=QW[:, k, 0, g0:g0 + 8].unsqueeze(2).broadcast_to([128, 8, 128]), op=ALU.mult),
                            reads=[t_st], writes=[t_amt[aj]])
                        kb.op("pool", lambda k=k, aj=aj: nc.gpsimd.tensor_tensor(
                            out=amt2[aj][:], in0=shf[:, :].unsqueeze(1).broadcast_to([128, 8, 128]),
                            in1=qos[:, k, g0:g0 + 8].unsqueeze(2).broadcast_to([128, 8, 128]), op=ALU.mult),
                            reads=[t_st], writes=[t_amt2[aj]])
                        kb.op("dve", lambda k=k, aj=aj: V.tensor_tensor(
                            out=AMt[:, k, :, :], in0=amt[aj][:], in1=amt2[aj][:], op=ALU.add),
                            reads=[t_amt[aj], t_amt2[aj]], writes=[t_AM])
                if getattr(self, "s5_stop", 0) == 2:
                    if d == 0 and cidx == 0:
                        kb.dma(self.dram["dbgLB"], LBt[:], reads=[t_LB])
                        kb.dma(self.dram["dbgKd"], Kdt[:], reads=[t_Kd])
                        kb.dma(self.dram["dbgAM"], AMt[:], reads=[t_AM])
                    continue
                with Pool(kb) as PM:
                    psv = [PM.ps([128, 1024], F32, "psv%d" % q) for q in range(2)]
                    t_psv = [Tok(excl=True), Tok(excl=True)]
                    psy = [PM.ps([128, 1024], F32, "psy%d" % q) for q in range(2)]
                    t_psy = [Tok(excl=True), Tok(excl=True)]
                    npv = 0
                    npy = 0
                    for b in range(self.NB):
                        uj = nu % 2
                        nu += 1
                        ubt = ub[uj]
                        kb.dma(ubt[:, 0:TB], UT[:, cidx, b * TB:(b + 1) * TB], writes=[t_ub[uj]])
                        kb.dma(ubt[:, TB:TB + CTX], UT[:, cidx, b * TB:b * TB + CTX], writes=[t_ub[uj]])
                        ybt = yb[uj]
                        if d == 0:
                            useq = lambda s: ubt[:, s:TB:LL]
                            yseq = lambda r: ybt[:, r:TB:LL]
                        else:
                            useq = lambda s: ubt[:, CTX:CTX + TB][:, (TB - 1 - s)::-LL]
                            yseq = lambda r: ybt[:, (TB - 1 - r)::-LL]
                        for s_ in range(LL):
                            kb.op("act", lambda s_=s_: nc.scalar.copy(out=ud[:, s_, :], in_=useq(s_)),
                                  reads=[t_ub[uj]], writes=[t_ud])
                        for g in range(8):
                            pj = npv % 2
                            npv += 1
                            for (c0, c1) in pieces:
                                pc = pieces.index((c0, c1))
                                for s in range(LL):
                                    kb.op("pe", lambda s=s, g=g, c0=c0, c1=c1, pc=pc, pj=pj: nc.tensor.matmul(
                                        psv[pj][:, c0:c1], LBt[:, LL - 1 - s, g, :],
                                        ud[:, s, c0:c1], start=(s == 0), stop=(s == LL - 1)),
                                        reads=[t_LB, t_ud], writes=[t_psv[pj]])
                            kb.op("act", lambda g=g, pj=pj: nc.scalar.copy(
                                out=S32[:, g, :], in_=psv[pj][:, 0:J]),
                                reads=[t_psv[pj]], writes=[t_S32[g]])
                            kb.op("dve", lambda g=g: V.tensor_copy(out=Sb[:, g, :], in_=S32[:, g, :]),
                                  reads=[t_S32[g]], writes=[t_Sb[g]])
                        _stop = getattr(self, "s5_stop", 0)
                        if _stop == 3:
                            kb.dma(self.dram["dbgS32"], S32[:], reads=t_S32)
                            continue
                        for k in range(npass):
                            dd = 1 << k
                            for g in range(8):
                                pj = npv % 2
                                npv += 1
                                segs = []
                                for (c0, c1) in pieces:
                                    if c1 <= dd:
                                        continue
                                    lo = max(c0, dd)
                                    pc = pieces.index((c0, c1))
                                    segs.append((lo, c1, lo))
                                for (lo, c1, po) in segs:
                                    kb.op("pe", lambda k=k, g=g, lo=lo, c1=c1, po=po, pj=pj, dd=dd: nc.tensor.matmul(
                                        psv[pj][:, po:po + (c1 - lo)], AMt[:, k, g, :], Sb[:, g, lo - dd:c1 - dd],
                                        start=True, stop=True), reads=[t_AM, t_Sb[g]], writes=[t_psv[pj]])
                                kb.op("dve", lambda g=g, pj=pj, dd=dd: V.tensor_tensor(
                                    out=S32[:, g, dd:J], in0=S32[:, g, dd:J], in1=psv[pj][:, dd:J],
                                    op=ALU.add), reads=[t_psv[pj]], writes=[t_S32[g]])
                                kb.op("act", lambda g=g, dd=dd: nc.scalar.copy(out=Sb[:, g, dd:J], in_=S32[:, g, dd:J]),
                                      reads=[t_S32[g]], writes=[t_Sb[g]])
                        kb.op("act", lambda: nc.scalar.copy(out=Hb[:, :, 1:J], in_=S32[:, :, 0:J - 1]),
                              reads=t_S32, writes=[t_Hb])
                        if _stop == 4:
                            kb.dma(self.dram["dbgS32"], S32[:], reads=t_S32)
                            continue
                        for r in range(LL):
                            pj = npy % 2
                            npy += 1
                            for (c0, c1) in pieces:
                                pc = pieces.index((c0, c1))
                                o_ap = psy[pj][:, c0:c1]
                                nmm = (r + 1) + 8
                                im = 0
                                for q in range(r + 1):
                                    kb.op("pe", lambda q=q, r=r, c0=c0, c1=c1, o_ap=o_ap, im=im, nmm=nmm:
                                          nc.tensor.matmul(o_ap, Kdt[:, q, :], ud[:, r - q, c0:c1],
                                                           start=(im == 0), stop=(im == nmm - 1)),
                                          reads=[t_Kd, t_ud], writes=[t_psy[pj]])
                                    im += 1
                                for g in range(8):
                                    kb.op("pe", lambda g=g, r=r, c0=c0, c1=c1, o_ap=o_ap, im=im, nmm=nmm:
                                          nc.tensor.matmul(o_ap, ECt[:, r + 1, g * 128:(g + 1) * 128], Hb[:, g, c0:c1],
                                                           start=(im == 0), stop=(im == nmm - 1)),
                                          reads=[t_EC, t_Hb], writes=[t_psy[pj]])
                                    im += 1
                            kb.op("act", lambda r=r, pj=pj: nc.scalar.copy(
                                out=yseq(r), in_=psy[pj][:, 0:J]),
                                reads=[t_psy[pj]], writes=[t_yb[uj]])
                        col = b * TB
                        if d == 0:
                            kb.dma(YF[0, cidx * 128:(cidx + 1) * 128, col:col + TB], ybt[:, :], reads=[t_yb[uj]])
                        else:
                            kb.dma(YF[1, cidx * 128:(cidx + 1) * 128, col + CTX:col + TB], ybt[:, 0:SEQ],
                                   reads=[t_yb[uj]])
                            kb.dma(YF[1, cidx * 128:(cidx + 1) * 128, col:col + CTX], ybt[:, SEQ:TB],
                                   reads=[t_yb[uj]])
            kb.barrier()


def phase_s5_out(self, i, jb, with_ctx):
    kb, nc = self.kb, self.nc
    NT = self.NT
    xT = self.dram["xT"].rearrange("(k p) t -> p k t", p=128)
    UT = self.dram["UT"].rearrange("(k p) t -> p k t", p=128)
    YF0 = self.dram["YF"][0].rearrange("(k p) t -> p k t", p=128)
    YF1 = self.dram["YF"][1].rearrange("(k p) t -> p k t", p=128)
    wd = self.dram["s5_w_glu"][jb].rearrange("(k p) d -> p k d", p=128)
    with Pool(kb) as P:
        ws = P.sb([128, KC, 2048], BF16, "ws")
        t_w = [Tok() for _ in range(KC)]
        for k in range(KC):
            for hf in range(2):
                kb.dma(ws[:, k, hf * 1024:(hf + 1) * 1024], wd[:, k, hf * 1024:(hf + 1) * 1024],
                       writes=[t_w[k]], q="pool")
        dsk = P.sb([128, KC], F32, "dsk")
        t_d = Tok()
        kb.dma(dsk[:], self.dram["s5_d"][jb], writes=[t_d])
        xt = [P.sb([128, KC, NT], F32, "xt%d" % q) for q in range(2)]
        t_xt = [Tok(), Tok()]
        y0 = [P.sb([128, KC, NT], F32, "y0%d" % q) for q in range(2)]
        y1 = [P.sb([128, KC, NT], F32, "y1%d" % q) for q in range(2)]
        ut = [P.sb([128, KC, NT], BF16, "ut%d" % q) for q in range(2)]
        t_in = [Tok(), Tok()]
        at = P.sb([128, KC, NT], BF16, "at")
        t_at = Tok()
        Wg = dict(a=P.sb([128, KC, NT], F32, "ga"), b=P.sb([128, KC, NT], F32, "gb"), ta=Tok(), tb=Tok())
        y = P.sb([128, KC, NT], F32, "y")
        t_y = Tok()
        sg = P.sb([128, NT], F32, "sg")
        t_sg = Tok()
        W = self.norm_work(P, NT)
        psn = P.ps([128, 512], F32, "psn")
        t_psn = Tok(excl=True)
        psd = [P.ps([128, 512], F32, "psd%d" % q) for q in range(4)]
        t_psd = [Tok(excl=True) for _ in range(4)]
        tl = self.tiles(with_ctx)

        def loads(jx, c0, n):
            kb.dma(xt[jx][:, :, :n], xT[:, :, c0:c0 + n], writes=[t_xt[jx]])
            kb.dma(y0[jx][:, :, :n], YF0[:, :, c0:c0 + n], writes=[t_in[jx]])
            kb.dma(y1[jx][:, :, :n], YF1[:, :, c0:c0 + n], writes=[t_in[jx]])
            kb.dma(ut[jx][:, :, :n], UT[:, :, c0:c0 + n], writes=[t_in[jx]])
        loads(0, tl[0][0], tl[0][1])
        nd = 0
        for ti, (c0, n, m) in enumerate(tl):
            jx = ti % 2
            if ti + 1 < len(tl):
                loads(1 - jx, tl[ti + 1][0], tl[ti + 1][1])
            kb.op("dve", lambda jx=jx, n=n: nc.vector.tensor_tensor(
                out=y0[jx][:, :, :n], in0=y0[jx][:, :, :n], in1=y1[jx][:, :, :n], op=ALU.add),
                writes=[t_in[jx]])
            for k in range(KC):
                kb.op("dve", lambda jx=jx, n=n, k=k: nc.vector.scalar_tensor_tensor(
                    out=y0[jx][:, k, :n], in0=ut[jx][:, k, :n], scalar=dsk[:, k:k + 1], in1=y0[jx][:, k, :n],
                    op0=ALU.mult, op1=ALU.add), reads=[t_d], writes=[t_in[jx]])
            gelu_tanh(self, at[:, :, :n], y0[jx][:, :, :n], t_in[jx], [128, KC, n], Wg, t_at)
            for dch in range(KC):
                pj = nd % 2
                nd += 1
                for k in range(KC):
                    kb.op("pe", lambda dch=dch, k=k, pj=pj: nc.tensor.matmul(
                        psd[pj][:, :n], ws[:, k, dch * 128:(dch + 1) * 128], at[:, k, :n],
                        start=(k == 0), stop=(k == KC - 1)), reads=[t_w[k], t_at], writes=[t_psd[pj]])
                for k in range(KC):
                    kb.op("pe", lambda dch=dch, k=k, pj=pj: nc.tensor.matmul(
                        psd[2 + pj][:, :n], ws[:, k, 1024 + dch * 128:1024 + (dch + 1) * 128], at[:, k, :n],
                        start=(k == 0), stop=(k == KC - 1)), reads=[t_w[k], t_at], writes=[t_psd[2 + pj]])
                kb.op("act", lambda pj=pj: nc.scalar.activation(out=sg[:, :n], in_=psd[2 + pj][:, :n],
                                                                func=AF.Sigmoid),
                      reads=[t_psd[2 + pj]], writes=[t_sg])
                kb.op("dve", lambda dch=dch, pj=pj: nc.vector.tensor_tensor(
                    out=y[:, dch, :n], in0=psd[pj][:, :n], in1=sg[:, :n], op=ALU.mult),
                    reads=[t_psd[pj], t_sg], writes=[t_y])
            self.norm_post(y, t_y, xt[jx], t_xt[jx], n, i, 0, m, W, psn, t_psn)
            kb.dma(xT[:, :, c0:c0 + n], xt[jx][:, :, :n], reads=[t_xt[jx]])
        kb.barrier()


Model.phase_s5_in = phase_s5_in
Model.phase_s5_scan = phase_s5_scan
Model.phase_s5_out = phase_s5_out

def arr_vec(v):
    v = np.asarray(v)
    lead = v.shape[:-1]
    n = v.shape[-1] // 128
    return np.ascontiguousarray(np.moveaxis(v.reshape(lead + (n, 128)), -1, 0))

def host_common_x(inp, bs):
    f = np.float32
    x, ctx, c, c_ctx = inp["x"], inp["ctx"], inp["c"], inp["c_ctx"]
    cols = []
    for b in bs:
        cols.append(ctx[b].T)
        cols.append(x[b].T)
    xin = np.ascontiguousarray(np.concatenate(cols, axis=1), dtype=f)
    cvecs = [c[b] for b in bs]
    while len(cvecs) < 2:
        cvecs.append(np.zeros_like(c_ctx))
    cvecs.append(c_ctx)
    cc = np.stack(cvecs, axis=-1)
    cc = np.ascontiguousarray(cc.reshape(8, 128, 3).transpose(1, 0, 2), dtype=f)
    return {"xin": xin, "cc": cc}


def host_common(inp, bs, CTX, SEQ):
    f = np.float32
    d = host_common_x(inp, bs)
    d.update({
         "ada_w": np.ascontiguousarray(inp["ada_w"], dtype=f),
         "ada_b": arr_vec(inp["ada_b"]).astype(f),
         "norm_g": arr_vec(inp["norm_g"]).astype(f),
         "mlp_w1": np.ascontiguousarray(inp["mlp_w1"], dtype=f),
         "mlp_w2": np.ascontiguousarray(
             np.asarray(inp["mlp_w2"]).reshape(-1, 32, 128, 8, 128).transpose(0, 3, 2, 1, 4), dtype=f),
         })
    return d

def rope_table(SEQ, GRID_W=64):
    t = np.arange(SEQ)
    row, col = t // GRID_W, t % GRID_W
    inv = (10000.0 ** (-np.arange(16, dtype=np.float32) / 16)).astype(np.float32)
    ang = np.concatenate([row[:, None].astype(np.float32) * inv, col[:, None].astype(np.float32) * inv], axis=-1)
    tab = np.stack([np.cos(ang), np.sin(ang)], axis=1).astype(np.float32)
    return np.ascontiguousarray(tab.reshape(SEQ // 128, 128, 2, 32).transpose(1, 0, 2, 3))

def host_attn(inp, SEQ):
    f = np.float32
    return {"attn_w_qkv": np.ascontiguousarray(inp["attn_w_qkv"], dtype=f),
            "attn_w_o": np.ascontiguousarray(inp["attn_w_o"], dtype=f),
            "attn_lambda": np.ascontiguousarray(inp["attn_lambda"], dtype=f),
            "attn_subln": np.ascontiguousarray(inp["attn_subln"], dtype=f),
            "rope": rope_table(SEQ), "ident": np.eye(128, dtype=f)}

def host_lru(inp):
    f = np.float32
    wg = np.asarray(inp["lru_w_gate"])
    nc_ = wg.shape[0]
    wg = wg.reshape(nc_, 2, 2, 6, 2, 128, 256).transpose(0, 5, 1, 2, 3, 4, 6)
    return {"lru_w_in": np.ascontiguousarray(inp["lru_w_in"], dtype=f),
            "lru_w_out": np.ascontiguousarray(inp["lru_w_out"], dtype=f),
            "lru_w_gate": np.ascontiguousarray(wg, dtype=f),
            "lru_conv_w": np.ascontiguousarray(
                np.asarray(inp["lru_conv_w"]).reshape(nc_, 4, 12, 128).transpose(0, 3, 2, 1), dtype=f),
            "lru_conv_b": np.ascontiguousarray(
                np.asarray(inp["lru_conv_b"]).reshape(nc_, 12, 128).transpose(0, 2, 1), dtype=f),
            "lru_b_gate": np.ascontiguousarray(
                np.asarray(inp["lru_b_gate"]).reshape(nc_, 2, 2, 12, 128).transpose(0, 4, 1, 2, 3), dtype=f),
            "lru_a_param": np.ascontiguousarray(
                np.asarray(inp["lru_a_param"]).reshape(nc_, 2, 12, 128).transpose(0, 3, 1, 2), dtype=f)}

def host_s5(inp):
    f = np.float32
    def dup(a):
        return np.concatenate([a, a], axis=2)
    a_re = np.asarray(inp["s5_a_re"]).transpose(0, 1, 3, 2)
    a_im = np.asarray(inp["s5_a_im"]).transpose(0, 1, 3, 2)
    b_re = np.asarray(inp["s5_b_re"]).transpose(0, 1, 3, 2, 4)
    b_im = np.asarray(inp["s5_b_im"]).transpose(0, 1, 3, 2, 4)
    c_re = np.asarray(inp["s5_c_re"]).transpose(0, 1, 4, 2, 3)
    c_im = np.asarray(inp["s5_c_im"]).transpose(0, 1, 4, 2, 3)
    sh = np.roll(np.eye(128, dtype=f), 64, axis=1)
    return {"s5_a_re": np.ascontiguousarray(dup(a_re), dtype=f), "s5_a_im": np.ascontiguousarray(dup(a_im), dtype=f),
            "s5_b_re": np.ascontiguousarray(dup(b_re), dtype=f), "s5_b_im": np.ascontiguousarray(dup(b_im), dtype=f),
            "s5_c_re": np.ascontiguousarray(dup(c_re), dtype=f), "s5_c_im": np.ascontiguousarray(dup(c_im), dtype=f),
            "s5_log_dt": np.ascontiguousarray(inp["s5_log_dt"], dtype=f),
            "s5_d": np.ascontiguousarray(np.asarray(inp["s5_d"]).reshape(-1, 8, 128).transpose(0, 2, 1), dtype=f),
            "s5_w_glu": np.ascontiguousarray(inp["s5_w_glu"], dtype=f),
            "shift64": sh, "ident": np.eye(128, dtype=f)}


import math


def build_program(NB, CTX, SEQ, shapes, depth=4, layer_list=None):
    M = Model(NB, CTX, SEQ, layers=list(range(depth)) if layer_list is None else layer_list, depth=4)
    nc = M.nc
    for k, shp in shapes.items():
        M.din(k, shp)
    M.dram["xT"] = nc.dram_tensor("xT", [1024, M.T], F32, kind="ExternalOutput").ap()
    with Pool(M.kb) as G:
        M.setup(G)
        M.kb.dma(M.dram["xT"], M.dram["xin"])
        M.kb.barrier()
        M.phase_mod()
        for i in M.layers:
            need_ctx = i < depth - 1
            kind, j = i % 3, i // 3
            if kind == 0:
                lambda_init = 0.8 - 0.6 * math.exp(-0.3 * i)
                M.phase_attn_qkv(i, j)
                M.phase_attn_core(i, j, lambda_init, need_ctx)
                M.phase_proj_post(i, "attn_w_o", j, 8, need_ctx)
            elif kind == 1:
                M.phase_s5_in(i)
                M.phase_s5_scan(i, j)
                M.phase_s5_out(i, j, need_ctx)
            else:
                M.phase_lru_in(i, j)
                M.phase_lru_scan(i, j)
                M.phase_proj_post(i, "lru_w_out", j, 12, need_ctx)
            M.phase_mlp(i, need_ctx)
        M.kb.finish()
    return M


def host_all(inp, bs, CTX, SEQ):
    d = host_common(inp, bs, CTX, SEQ)
    d.update(host_attn(inp, SEQ))
    d.update(host_s5(inp))
    d.update(host_lru(inp))
    return d


def kernel(**inputs):
    inp = {k: np.asarray(v) for k, v in inputs.items()}
    B, SEQ, _ = inp["x"].shape
    CTX = inp["ctx"].shape[1]
    ncores = 8
    NB = B // ncores
    shared = None
    in_maps = []
    for c in range(ncores):
        bs = list(range(c * NB, (c + 1) * NB))
        if shared is None:
            d = host_all(inp, bs, CTX, SEQ)
            shared = {k: v for k, v in d.items() if k not in ("xin", "cc")}
        else:
            d = dict(shared)
            d.update({k: v for k, v in host_common_x(inp, bs).items()})
        in_maps.append(d)
    shapes = {k: v.shape for k, v in in_maps[0].items()}
    M = build_program(NB, CTX, SEQ, shapes)
    res = run_bass_kernel_spmd(M.nc, in_maps, core_ids=list(range(ncores)))
    TB = CTX + SEQ
    out = np.empty((B, SEQ, 1024), np.float32)
    for c in range(ncores):
        o = np.asarray(res.results[c]["xT"])
        for bl in range(NB):
            out[c * NB + bl] = o[:, bl * TB + CTX:(bl + 1) * TB].T
    return out
```

```python
from concourse.bass_utils import run_bass_kernel_spmd
import contextlib
import numpy as np
import concourse.bass as bass
import concourse.mybir as mybir

F32 = mybir.dt.float32
BF16 = mybir.dt.bfloat16
I32 = mybir.dt.int32
AF = mybir.ActivationFunctionType
ALU = mybir.AluOpType
AX = mybir.AxisListType

NSTREAM = 12


class Tok:
    __slots__ = ("w", "r", "name", "excl")

    def __init__(self, name="", excl=False):
        self.w = None
        self.r = {}
        self.name = name
        self.excl = excl


class KB:
    def __init__(self):
        nc = bass.Bass("TRN2", target_bir_lowering=False)
        self.nc = nc
        self.E = {"pe": nc.tensor, "act": nc.scalar, "dve": nc.vector,
                  "pool": nc.gpsimd, "sp": nc.sync}
        self.sem = {}
        self.cnt = {}
        for e in ("pe", "act", "dve", "pool"):
            self.sem[e] = nc.alloc_semaphore("s_" + e)
            self.cnt[e] = 0
        for j in range(NSTREAM):
            e = ("d", j)
            self.sem[e] = nc.alloc_semaphore("s_d%d" % j)
            self.cnt[e] = 0
        self.seen = {e: {} for e in ("pe", "act", "dve", "pool", "sp")}
        self.ndma = 0
        self.nins = 0
        self.nwait = 0

    def _val(self, src, c):
        return c * 16 if isinstance(src, tuple) else c

    def _wait(self, eng, src, c):
        if c <= 0:
            return
        if self.seen[eng].get(src, 0) >= c:
            return
        self.seen[eng][src] = c
        self.E[eng].wait_ge(self.sem[src], self._val(src, c))
        self.nins += 1
        self.nwait += 1

    def _waits_attach(self, eng, need, fn):
        todo = [(src, c) for src, c in need.items() if c > 0 and self.seen[eng].get(src, 0) < c]
        for src, c in todo[:-1]:
            self._wait(eng, src, c)
        ins = fn()
        if todo:
            src, c = todo[-1]
            self.seen[eng][src] = c
            ins._wait_ge(self.sem[src], self._val(src, c))
        return ins

    def _deps(self, eng, reads, writes, same_ok=False):
        need = {}

        def add(src, c):
            if same_ok and src == eng:
                return
            if need.get(src, 0) < c:
                need[src] = c
        for t in reads:
            if t.w is not None:
                add(*t.w)
            if t.excl:
                for src, c in t.r.items():
                    if src != eng:
                        add(src, c)
        for t in writes:
            if t.w is not None:
                add(*t.w)
            for src, c in t.r.items():
                add(src, c)
        return need

    def _commit(self, me, reads, writes):
        c = self.cnt[me]
        for t in reads:
            t.r[me] = c
        for t in writes:
            t.w = (me, c)
            t.r = {}

    def op(self, eng, fn, reads=(), writes=()):
        need = self._deps(eng, reads, writes, same_ok=(eng == "pe"))
        ins = self._waits_attach(eng, need, fn)
        self.cnt[eng] += 1
        ins.then_inc(self.sem[eng], 1)
        self.nins += 1
        self._commit(eng, reads, writes)
        return ins

    def dma(self, out, in_, reads=(), writes=(), q="sp", **kw):
        j = self.ndma % NSTREAM
        self.ndma += 1
        me = ("d", j)
        need = self._deps(q, reads, writes)
        if need.get(me, 0) < self.cnt[me]:
            need[me] = self.cnt[me]
        ins = self._waits_attach(q, need, lambda: self.E[q].dma_start(out=out, in_=in_, **kw))
        self.cnt[me] += 1
        ins.then_inc(self.sem[me], 16)
        self.nins += 1
        self._commit(me, reads, writes)
        return ins

    def barrier(self):
        for eng in ("pe", "act", "dve", "pool", "sp"):
            for src, c in self.cnt.items():
                if src == eng:
                    continue
                self._wait(eng, src, c)

    def finish(self):
        for src, c in self.cnt.items():
            self._wait("sp", src, c)


class Pool:
    _uid = [0]

    def __init__(self, kb):
        self.kb = kb
        self.st = contextlib.ExitStack()
        self.n = 0
        Pool._uid[0] += 1
        self.uid = Pool._uid[0]

    def __enter__(self):
        self.st.__enter__()
        return self

    def __exit__(self, *a):
        return self.st.__exit__(*a)

    def sb(self, shape, dtype, name=None):
        self.n += 1
        nm = "%s_%d_%d" % (name or "t", self.uid, self.n)
        t = self.st.enter_context(self.kb.nc.sbuf_tensor(nm, list(shape), dtype))
        return t

    def ps(self, shape, dtype, name=None):
        self.n += 1
        nm = "%s_%d_%d" % (name or "p", self.uid, self.n)
        t = self.st.enter_context(self.kb.nc.psum_tensor(nm, list(shape), dtype))
        return t

D = 1024
KC = 8
DFF = 4096
EPS = 1e-6


class Model:
    def __init__(self, NB, CTX, SEQ, layers=(0, 1, 2, 3), depth=4, NT=256):
        self.NB, self.CTX, self.SEQ = NB, CTX, SEQ
        self.TB = CTX + SEQ
        self.T = NB * self.TB
        self.layers = list(layers)
        self.depth = depth
        self.NT = NT
        self.kb = KB()
        self.nc = self.kb.nc
        self.dram = {}

    def din(self, name, shape, dtype=F32):
        t = self.nc.dram_tensor(name, list(shape), dtype, kind="ExternalInput").ap()
        self.dram[name] = t
        return t

    def dscratch(self, name, shape, dtype):
        t = self.nc.dram_tensor(name, list(shape), dtype, kind="Internal").ap()
        self.dram[name] = t
        return t

    def tiles(self, with_ctx=True, nt=None):
        nt = nt or self.NT
        out = []
        for b in range(self.NB):
            base = b * self.TB
            if with_ctx:
                for c0 in range(0, self.CTX, nt):
                    out.append((base + c0, min(nt, self.CTX - c0), 2))
            for c0 in range(0, self.SEQ, nt):
                out.append((base + self.CTX + c0, min(nt, self.SEQ - c0), b))
        return out

    def setup(self, G):
        kb, nc = self.kb, self.nc
        self.G = G
        self.onesm = G.sb([128, 128], BF16, "onesm")
        self.t_const = Tok("const")
        kb.op("dve", lambda: nc.vector.memset(self.onesm[:], 1.0 / D), writes=[self.t_const])
        self.epsv = G.sb([128, 1], F32, "epsv")
        kb.op("dve", lambda: nc.vector.memset(self.epsv[:], EPS), writes=[self.t_const])
        self.mv = G.sb([128, self.depth, 4, KC, 3], F32, "mv")
        self.modT = G.sb([128, self.depth, 48, 3], F32, "modT")
        self.t_mv = Tok("mv")

    def phase_mod(self):
        kb, nc = self.kb, self.nc
        cc, ada_w, ada_b, ng = (self.dram[k] for k in ("cc", "ada_w", "ada_b", "norm_g"))
        with Pool(kb) as P:
            sc = P.sb([128, KC, 3], F32, "sc")
            adab = P.sb([128, self.depth, 48], F32, "adab")
            ngt = P.sb([128, self.depth, 4, KC], F32, "ngt")
            tmp = P.sb([128, KC, 3], F32, "tmp")
            wt = [P.sb([128, KC, 512], F32, "adaw%d" % j) for j in range(2)]
            ps = [P.ps([128, 512], F32, "psmod%d" % j) for j in range(2)]
            t_sc, t_ab, t_ng, t_tmp = Tok(), Tok(), Tok(), Tok()
            t_wt = [Tok(), Tok()]
            t_ps = [Tok(excl=True), Tok(excl=True)]
            kb.dma(sc[:], cc, writes=[t_sc])
            kb.dma(adab[:], ada_b, writes=[t_ab])
            kb.dma(ngt[:], ng, writes=[t_ng])
            kb.op("act", lambda: nc.scalar.activation(out=sc[:], in_=sc[:], func=AF.Silu),
                  reads=[t_sc], writes=[t_sc])
            n = 0
            for i in self.layers:
                wv = ada_w[i].rearrange("(k p) f -> p k f", p=128)
                for cg in range(12):
                    j = n % 2
                    n += 1
                    kb.dma(wt[j][:], wv[:, :, cg * 512:(cg + 1) * 512], writes=[t_wt[j]])
                    for jj in range(4):
                        for k in range(KC):
                            kb.op("pe", lambda k=k, jj=jj, j=j: nc.tensor.matmul(
                                ps[j][:, jj * 3:jj * 3 + 3], wt[j][:, k, jj * 128:(jj + 1) * 128],
                                sc[:, k, :], start=(k == 0), stop=(k == KC - 1)),
                                reads=[t_wt[j], t_sc], writes=[t_ps[j]])
                    kb.op("dve", lambda j=j, cg=cg, i=i: nc.vector.tensor_tensor(
                        out=self.modT[:, i, cg * 4:(cg + 1) * 4, :],
                        in0=ps[j][:, 0:12].rearrange("p (a b) -> p a b", b=3),
                        in1=adab[:, i, cg * 4:(cg + 1) * 4].unsqueeze(2).broadcast_to([128, 4, 3]),
                        op=ALU.add), reads=[t_ps[j], t_ab], writes=[self.t_mv])
                for kind, (c0, gi, plus1) in enumerate([(8, 0, True), (16, 1, False),
                                                        (32, 2, True), (40, 3, False)]):
                    src = self.modT[:, i, c0:c0 + 8, :]
                    if plus1:
                        kb.op("dve", lambda src=src: nc.vector.tensor_scalar(
                            out=tmp[:], in0=src, scalar1=1.0, scalar2=None, op0=ALU.add),
                            reads=[self.t_mv], writes=[t_tmp])
                        src = tmp[:]
                    kb.op("dve", lambda src=src, i=i, kind=kind, gi=gi: nc.vector.tensor_tensor(
                        out=self.mv[:, i, kind, :, :], in0=src,
                        in1=ngt[:, i, gi, :].unsqueeze(2).broadcast_to([128, KC, 3]),
                        op=ALU.mult), reads=[self.t_mv, t_tmp, t_ng], writes=[self.t_mv])
            kb.barrier()

    def mvec(self, i, kind, k, m):
        return self.mv[:, i, kind, k, m:m + 1]

    def shvec(self, i, which, k, m):
        c0 = 0 if which == 0 else 24
        return self.modT[:, i, c0 + k, m:m + 1]

    def stats_a(self, src, n, W, t_src):
        kb, nc = self.kb, self.nc
        sq = W["sq"]
        kb.op("act", lambda: nc.scalar.activation(out=sq[:, :, :n], in_=src, func=AF.Square),
              reads=[t_src], writes=[W["t_sq"]])

    def stats_b(self, n, W, ps, t_ps):
        kb, nc = self.kb, self.nc
        sq = W["sq"]
        for k in range(KC):
            kb.op("pe", lambda k=k: nc.tensor.matmul(ps[:, :n], self.onesm[:], sq[:, k, :n],
                                                     start=(k == 0), stop=(k == KC - 1)),
                  reads=[W["t_sq"], self.t_const], writes=[t_ps])

    def stats_c(self, n, W, ps, t_ps):
        kb, nc = self.kb, self.nc
        sd, rstd = W["sd"], W["rstd"]
        kb.op("act", lambda: nc.scalar.activation(out=sd[:, :n], in_=ps[:, :n], func=AF.Sqrt,
                                                  bias=self.epsv[:], scale=1.0),
              reads=[t_ps, self.t_const], writes=[W["t_sd"]])
        kb.op("dve", lambda: nc.vector.reciprocal(out=rstd[:, :n], in_=sd[:, :n]),
              reads=[W["t_sd"]], writes=[W["t_rstd"]])
        return rstd

    def rstd_of(self, src, n, W, t_src, ps, t_ps):
        self.stats_a(src, n, W, t_src)
        self.stats_b(n, W, ps, t_ps)
        return self.stats_c(n, W, ps, t_ps)

    def norm_work(self, P, NT):
        return dict(sq=P.sb([128, KC, NT], BF16, "sq"), sd=P.sb([128, NT], F32, "sd"),
                    rstd=P.sb([128, NT], F32, "rstd"), xh=P.sb([128, KC, NT], F32, "xh"),
                    t_sq=Tok(), t_sd=Tok(), t_rstd=Tok(), t_xh=Tok())

    def norm_pre(self, xt, t_x, n, i, which, m, h, t_h, W, ps, t_ps, stats_done=False):
        kb, nc = self.kb, self.nc
        rstd = W["rstd"] if stats_done else self.rstd_of(xt[:, :, :n], n, W, t_x, ps, t_ps)
        xh = W["xh"]
        kb.op("dve", lambda: nc.vector.tensor_tensor(
            out=xh[:, :, :n], in0=xt[:, :, :n],
            in1=rstd[:, :n].unsqueeze(1).broadcast_to([128, KC, n]), op=ALU.mult),
            reads=[t_x, W["t_rstd"]], writes=[W["t_xh"]])
        kindA = 0 if which == 0 else 2
        for k in range(KC):
            if k % 2 == 0:
                kb.op("act", lambda k=k: nc.scalar.activation(
                    out=h[:, k, :n], in_=xh[:, k, :n], func=AF.Identity, scale=self.mvec(i, kindA, k, m),
                    bias=self.shvec(i, which, k, m)), reads=[W["t_xh"], self.t_mv], writes=[t_h])
            else:
                kb.op("dve", lambda k=k: nc.vector.tensor_scalar(
                    out=h[:, k, :n], in0=xh[:, k, :n], scalar1=self.mvec(i, kindA, k, m),
                    scalar2=self.shvec(i, which, k, m), op0=ALU.mult, op1=ALU.add),
                    reads=[W["t_xh"], self.t_mv], writes=[t_h])

    def norm_post(self, y, t_y, xt, t_x, n, i, which, m, W, ps, t_ps, stats_done=False):
        kb, nc = self.kb, self.nc
        rstd = W["rstd"] if stats_done else self.rstd_of(y[:, :, :n], n, W, t_y, ps, t_ps)
        kb.op("dve", lambda: nc.vector.tensor_tensor(
            out=y[:, :, :n], in0=y[:, :, :n],
            in1=rstd[:, :n].unsqueeze(1).broadcast_to([128, KC, n]), op=ALU.mult),
            reads=[W["t_rstd"]], writes=[t_y])
        kindG = 1 if which == 0 else 3
        for k in range(KC):
            kb.op("dve", lambda k=k: nc.vector.scalar_tensor_tensor(
                out=xt[:, k, :n], in0=y[:, k, :n], scalar=self.mvec(i, kindG, k, m),
                in1=xt[:, k, :n], op0=ALU.mult, op1=ALU.add),
                reads=[t_y, self.t_mv], writes=[t_x])

    def phase_mlp(self, i, with_ctx):
        kb, nc = self.kb, self.nc
        NT = self.NT
        xT = self.dram["xT"].rearrange("(k p) t -> p k t", p=128)
        w1 = self.dram["mlp_w1"][i].rearrange("(k p) f -> p k f", p=128)
        w2 = self.dram["mlp_w2"][i]
        with Pool(kb) as P:
            w1s = P.sb([128, KC, DFF], BF16, "w1s")
            w2s = P.sb([128, KC, 32, 128], BF16, "w2s")
            t_w1 = [Tok() for _ in range(8)]
            t_w2 = [Tok() for _ in range(8)]
            for g in range(8):
                kb.dma(w1s[:, :, g * 512:(g + 1) * 512], w1[:, :, g * 512:(g + 1) * 512],
                       writes=[t_w1[g]], q="pool")
            for d in range(8):
                kb.dma(w2s[:, d, :, :], w2[d], writes=[t_w2[d]], q="pool")
            xt = [P.sb([128, KC, NT], F32, "xt%d" % j) for j in range(2)]
            t_xt = [Tok(), Tok()]
            h = [P.sb([128, KC, NT], BF16, "h%d" % j) for j in range(2)]
            t_h = [Tok(), Tok()]
            hid = P.sb([128, 32, NT], BF16, "hid")
            t_hid = [Tok() for _ in range(32)]
            y1 = P.sb([128, KC, NT], F32, "y")
            y = [y1, y1]
            t_y1 = Tok()
            t_y = [t_y1, t_y1]
            rl = [P.sb([128, NT], F32, "rl%d" % j) for j in range(2)]
            t_rl = [Tok(), Tok()]
            W = self.norm_work(P, NT)
            W2 = dict(sq=P.sb([128, KC, NT], BF16, "sq2"), sd=P.sb([128, NT], F32, "sd2"),
                      rstd=P.sb([128, NT], F32, "rstd2"), t_sq=Tok(), t_sd=Tok(), t_rstd=Tok())
            psn = P.ps([128, 512], F32, "psn")
            t_psn = Tok(excl=True)
            psn2 = P.ps([128, 512], F32, "psn2")
            t_psn2 = Tok(excl=True)
            psu = [P.ps([128, 512], F32, "psu%d" % j) for j in range(3)]
            t_psu = [Tok(excl=True) for _ in range(3)]
            psd = [P.ps([128, 512], F32, "psd%d" % j) for j in range(2)]
            t_psd = [Tok(excl=True) for _ in range(2)]
            tl = self.tiles(with_ctx)

            def load(ti):
                c0, n, m = tl[ti]
                kb.dma(xt[ti % 2][:, :, :n], xT[:, :, c0:c0 + n], writes=[t_xt[ti % 2]])

            def pre_a(ti):
                c0, n, m = tl[ti]
                self.stats_a(xt[ti % 2][:, :, :n], n, W, t_xt[ti % 2])

            def pre_bc(ti):
                c0, n, m = tl[ti]
                self.stats_b(n, W, psn, t_psn)
                self.stats_c(n, W, psn, t_psn)
                self.norm_pre(xt[ti % 2], t_xt[ti % 2], n, i, 1, m, h[ti % 2], t_h[ti % 2], W, psn, t_psn,
                              stats_done=True)

            def post_bc(ti):
                c0, n, m = tl[ti]
                self.stats_b(n, W2, psn2, t_psn2)
                self.stats_c(n, W2, psn2, t_psn2)
                self.norm_post(y[ti % 2], t_y[ti % 2], xt[ti % 2], t_xt[ti % 2], n, i, 1, m, W2, psn2, t_psn2,
                               stats_done=True)
                kb.dma(xT[:, :, c0:c0 + n], xt[ti % 2][:, :, :n], reads=[t_xt[ti % 2]])
            if tl:
                load(0)
                pre_a(0)
                pre_bc(0)
                if len(tl) > 1:
                    load(1)
            nu = 0
            nd = 0
            for ti, (c0, n, m) in enumerate(tl):
                j = ti % 2
                for f in range(32):
                    pj = nu % 3
                    nu += 1
                    for k in range(KC):
                        kb.op("pe", lambda k=k, f=f, pj=pj: nc.tensor.matmul(
                            psu[pj][:, :n], w1s[:, k, f * 128:(f + 1) * 128], h[j][:, k, :n],
                            start=(k == 0), stop=(k == KC - 1)),
                            reads=[t_w1[f // 4], t_h[j]], writes=[t_psu[pj]])
                    rj = f % 2
                    kb.op("act", lambda pj=pj, rj=rj: nc.scalar.activation(
                        out=rl[rj][:, :n], in_=psu[pj][:, :n], func=AF.Relu),
                        reads=[t_psu[pj]], writes=[t_rl[rj]])
                    eng = "dve" if f % 2 == 0 else "pool"
                    E = nc.vector if eng == "dve" else nc.gpsimd
                    kb.op(eng, lambda rj=rj, f=f, E=E: E.tensor_tensor(
                        out=hid[:, f, :n], in0=rl[rj][:, :n], in1=rl[rj][:, :n], op=ALU.mult),
                        reads=[t_rl[rj]], writes=[t_hid[f]])
                    if f == 3 and ti >= 1:
                        post_bc(ti - 1)
                        if ti + 1 < len(tl):
                            load(ti + 1)
                    if f == 14 and ti + 1 < len(tl):
                        pre_a(ti + 1)
                    if f == 22 and ti + 1 < len(tl):
                        pre_bc(ti + 1)
                for d in range(KC):
                    pj = nd % 2
                    nd += 1
                    for f in range(32):
                        kb.op("pe", lambda d=d, f=f, pj=pj: nc.tensor.matmul(
                            psd[pj][:, :n], w2s[:, d, f, :], hid[:, f, :n],
                            start=(f == 0), stop=(f == 31)),
                            reads=[t_w2[d], t_hid[f]], writes=[t_psd[pj]])
                    kb.op("act", lambda d=d, pj=pj, j=j: nc.scalar.copy(out=y[j][:, d, :n], in_=psd[pj][:, :n]),
                          reads=[t_psd[pj]], writes=[t_y[j]])
                self.stats_a(y[j][:, :, :n], n, W2, t_y[j])
            if tl:
                post_bc(len(tl) - 1)
            kb.barrier()


HEADS = 8


def _attn_scratch(self):
    if "QT" in self.dram:
        return
    self.dscratch("QT", [self.NB, HEADS, 2, 128, self.TB], BF16)
    self.dscratch("KT", [self.NB, HEADS, 128, self.TB], BF16)
    self.dscratch("V", [self.NB, self.TB, 1024], BF16)
    self.dscratch("OT", [1536, self.T], BF16)


def phase_attn_qkv(self, i, j):
    kb, nc = self.kb, self.nc
    _attn_scratch(self)
    NT = self.NT
    xT = self.dram["xT"].rearrange("(k p) t -> p k t", p=128)
    wq = self.dram["attn_w_qkv"][j].rearrange("(k p) f -> p k f", p=128)
    rope = self.dram["rope"]
    QT, KT, V = self.dram["QT"], self.dram["KT"], self.dram["V"]
    with Pool(kb) as P:
        wqs = P.sb([128, KC, 3072], BF16, "wqs")
        t_wq = [Tok() for _ in range(6)]
        for g in range(6):
            kb.dma(wqs[:, :, g * 512:(g + 1) * 512], wq[:, :, g * 512:(g + 1) * 512],
                   writes=[t_wq[g]], q="pool")
        nrt = self.SEQ // 128
        rp = P.sb([128, nrt, 2, 32], F32, "rp")
        rpq = P.sb([128, nrt, 2, 32], F32, "rpq")
        t_rp = Tok()
        kb.dma(rp[:], rope, writes=[t_rp])
        kb.op("act", lambda: nc.scalar.mul(out=rpq[:], in_=rp[:], mul=0.125), reads=[t_rp], writes=[t_rp])
        ident = P.sb([128, 128], BF16, "ident")
        t_id = Tok()
        kb.dma(ident[:], self.dram["ident"], writes=[t_id], q="pool")
        xt = [P.sb([128, KC, NT], F32, "xt%d" % q) for q in range(2)]
        t_xt = [Tok(), Tok()]
        h = P.sb([128, KC, NT], BF16, "h")
        t_h = Tok()
        W = self.norm_work(P, NT)
        psn = P.ps([128, 512], F32, "psn")
        t_psn = Tok(excl=True)
        psqk = [P.ps([128, 1024], F32, "psqk%d" % q) for q in range(2)]
        t_psqk = [Tok(excl=True), Tok(excl=True)]
        psv = [P.ps([128, 512], F32, "psv%d" % q) for q in range(2)]
        t_psv = [Tok(excl=True), Tok(excl=True)]
        pst = P.ps([128, 8, 128], BF16, "pst")
        t_pst = Tok(excl=True)
        qk = [P.sb([128, 1024], BF16, "qk%d" % q) for q in range(2)]
        t_qk = [Tok(), Tok()]
        ta = P.sb([128, 16, 32], F32, "ta")
        tb = P.sb([128, 16, 32], F32, "tb")
        t_ta, t_tb = Tok(), Tok()
        vst = [P.sb([128, 1024], BF16, "vst%d" % q) for q in range(2)]
        t_vst = [Tok(), Tok()]
        qz = [P.sb([128, HEADS, NT], BF16, "qz%d" % c) for c in range(2)]
        t_qz = [Tok(), Tok()]
        kz = P.sb([128, HEADS, NT], BF16, "kz")
        t_kz = Tok()
        kb.op("pool", lambda: nc.gpsimd.memset(qz[0][:], 0.0), writes=[t_qz[0]])
        kb.op("pool", lambda: nc.gpsimd.memset(qz[1][:], 0.0), writes=[t_qz[1]])
        tl = self.tiles(True)
        c0, n, m = tl[0]
        kb.dma(xt[0][:, :, :n], xT[:, :, c0:c0 + n], writes=[t_xt[0]])
        nv = 0
        for ti, (c0, n, m) in enumerate(tl):
            jx = ti % 2
            if ti + 1 < len(tl):
                c1, n1, _ = tl[ti + 1]
                kb.dma(xt[1 - jx][:, :, :n1], xT[:, :, c1:c1 + n1], writes=[t_xt[1 - jx]])
            self.norm_pre(xt[jx], t_xt[jx], n, i, 0, m, h, t_h, W, psn, t_psn)
            b = c0 // self.TB
            pos0 = c0 - b * self.TB
            is_ctx = (m == 2)
            for s in range(n // 128):
                hs = lambda k: h[:, k, s * 128:(s + 1) * 128]
                for which in range(2):
                    ps = psqk[which]
                    tps = t_psqk[which]
                    for half in range(2):
                        for k in range(KC):
                            kb.op("pe", lambda k=k, half=half, which=which, ps=ps: nc.tensor.matmul(
                                ps[:, half * 512:(half + 1) * 512], hs(k),
                                wqs[:, k, which * 1024 + half * 512:which * 1024 + (half + 1) * 512],
                                start=(k == 0), stop=(k == KC - 1)),
                                reads=[t_h, t_wq[which * 2 + half]], writes=[tps])
                for half in range(2):
                    pj = nv % 2
                    nv += 1
                    for k in range(KC):
                        kb.op("pe", lambda k=k, half=half, pj=pj: nc.tensor.matmul(
                            psv[pj][:, :], hs(k), wqs[:, k, 2048 + half * 512:2048 + (half + 1) * 512],
                            start=(k == 0), stop=(k == KC - 1)),
                            reads=[t_h, t_wq[4 + half]], writes=[t_psv[pj]])
                    kb.op("act", lambda half=half, pj=pj, s=s: nc.scalar.copy(
                        out=vst[s % 2][:, half * 512:(half + 1) * 512], in_=psv[pj][:, :]),
                        reads=[t_psv[pj]], writes=[t_vst[s % 2]])
                r0 = b * self.TB + pos0 + s * 128
                kb.dma(V[b, pos0 + s * 128:pos0 + (s + 1) * 128, :], vst[s % 2][:], reads=[t_vst[s % 2]])
                for which in range(2):
                    ps = psqk[which]
                    tps = t_psqk[which]
                    dst = qk[which]
                    if is_ctx:
                        kb.op("act", lambda ps=ps, dst=dst, which=which: nc.scalar.mul(
                            out=dst[:], in_=ps[:], mul=(0.125 if which == 0 else 1.0)),
                            reads=[tps], writes=[t_qk[which]])
                    else:
                        lt = (pos0 - self.CTX) // 128 + s
                        tab = rpq if which == 0 else rp
                        cs = tab[:, lt, 0, :].unsqueeze(1).broadcast_to([128, 16, 32])
                        sn = tab[:, lt, 1, :].unsqueeze(1).broadcast_to([128, 16, 32])
                        pv = ps[:, :].rearrange("p (a two f) -> p a two f", two=2, f=32)
                        dv = dst[:, :].rearrange("p (a two f) -> p a two f", two=2, f=32)
                        t1, t2 = pv[:, :, 0, :], pv[:, :, 1, :]
                        kb.op("dve", lambda: nc.vector.tensor_tensor(out=ta[:], in0=t1, in1=cs, op=ALU.mult),
                              reads=[tps, t_rp], writes=[t_ta])
                        kb.op("dve", lambda: nc.vector.tensor_tensor(out=tb[:], in0=t2, in1=sn, op=ALU.mult),
                              reads=[tps, t_rp], writes=[t_tb])
                        kb.op("dve", lambda: nc.vector.tensor_tensor(out=dv[:, :, 0, :], in0=ta[:], in1=tb[:],
                                                                     op=ALU.subtract),
                              reads=[t_ta, t_tb], writes=[t_qk[which]])
                        kb.op("dve", lambda: nc.vector.tensor_tensor(out=ta[:], in0=t1, in1=sn, op=ALU.mult),
                              reads=[tps, t_rp], writes=[t_ta])
                        kb.op("dve", lambda: nc.vector.tensor_tensor(out=tb[:], in0=t2, in1=cs, op=ALU.mult),
                              reads=[tps, t_rp], writes=[t_tb])
                        kb.op("dve", lambda: nc.vector.tensor_tensor(out=dv[:, :, 1, :], in0=ta[:], in1=tb[:],
                                                                     op=ALU.add),
                              reads=[t_ta, t_tb], writes=[t_qk[which]])
                    for hd in range(HEADS):
                        kb.op("pe", lambda hd=hd, dst=dst: nc.tensor.transpose(
                            out=pst[:, hd, :], in_=dst[:, hd * 128:(hd + 1) * 128], identity=ident[:]),
                            reads=[t_qk[which], t_id], writes=[t_pst])
                    sl = slice(s * 128, (s + 1) * 128)
                    if which == 0:
                        kb.op("act", lambda sl=sl: nc.scalar.copy(out=qz[0][0:64, :, sl], in_=pst[0:64, :, :]),
                              reads=[t_pst], writes=[t_qz[0]])
                        kb.op("act", lambda sl=sl: nc.scalar.copy(out=qz[1][64:128, :, sl], in_=pst[64:128, :, :]),
                              reads=[t_pst], writes=[t_qz[1]])
                    else:
                        kb.op("act", lambda sl=sl: nc.scalar.copy(out=kz[:, :, sl], in_=pst[:, :, :]),
                              reads=[t_pst], writes=[t_kz])
            for c in range(2):
                kb.dma(QT[b, :, c, :, pos0:pos0 + n].rearrange("h p t -> p h t"), qz[c][:, :, :n],
                       reads=[t_qz[c]])
            kb.dma(KT[b, :, :, pos0:pos0 + n].rearrange("h p t -> p h t"), kz[:, :, :n], reads=[t_kz])
        kb.barrier()


def phase_attn_core(self, i, j, lambda_init, need_ctx):
    kb, nc = self.kb, self.nc
    QT, KT, V, OT = self.dram["QT"], self.dram["KT"], self.dram["V"], self.dram["OT"]
    TB, CTX = self.TB, self.CTX
    nkt = TB // 128
    QG = 256
    with Pool(kb) as P:
        kts = P.sb([128, HEADS, TB], BF16, "kts")
        vs = P.sb([128, nkt, HEADS, 130], BF16, "vs")
        t_kts, t_vs = Tok(), Tok()
        ident = P.sb([128, 128], BF16, "ident")
        t_id = Tok()
        kb.dma(ident[:], self.dram["ident"], writes=[t_id], q="pool")
        lam = P.sb([128, 4, 64], F32, "lam")
        lt = P.sb([128, 2, 64], F32, "lt")
        ls = P.sb([128, 2], F32, "ls")
        nlam = P.sb([128, 1], F32, "nlam")
        gsb = P.sb([128, 128], F32, "gsb")
        t_l = Tok()
        kb.dma(lam[:], self.dram["attn_lambda"][j].partition_broadcast(128), writes=[t_l])
        kb.dma(gsb[:], self.dram["attn_subln"][j].partition_broadcast(128), writes=[t_l])
        kb.op("dve", lambda: nc.vector.tensor_tensor(out=lt[:], in0=lam[:, 0::2, :], in1=lam[:, 1::2, :],
                                                     op=ALU.mult), reads=[t_l], writes=[t_l])
        kb.op("dve", lambda: nc.vector.tensor_reduce(out=ls[:], in_=lt[:], op=ALU.add, axis=AX.X),
              reads=[t_l], writes=[t_l])
        kb.op("act", lambda: nc.scalar.activation(out=ls[:], in_=ls[:], func=AF.Exp), reads=[t_l], writes=[t_l])
        kb.op("dve", lambda: nc.vector.tensor_tensor(out=nlam[:], in0=ls[:, 1:2], in1=ls[:, 0:1], op=ALU.subtract),
              reads=[t_l], writes=[t_l])
        kb.op("dve", lambda: nc.vector.tensor_scalar(out=nlam[:], in0=nlam[:], scalar1=-lambda_init, scalar2=None,
                                                     op0=ALU.add), reads=[t_l], writes=[t_l])
        kb.op("act", lambda: nc.scalar.mul(out=gsb[:], in_=gsb[:], mul=1.0 - lambda_init), reads=[t_l], writes=[t_l])
        epsv = self.epsv
        qs_ = [P.sb([128, HEADS, 2, QG], BF16, "qs%d" % q) for q in range(2)]
        t_qs = [Tok(), Tok()]
        NPS = 3
        pss = [P.ps([128, 2, QG], F32, "pss%d" % q) for q in range(NPS)]
        t_pss = [Tok(excl=True) for _ in range(NPS)]
        accs = P.sb([128, 2, 2, 129], F32, "accs")
        t_accs = Tok()
        psa = [[P.ps([128, 512], F32, "psa%d%d" % (c, q)) for q in range(2)] for c in range(2)]
        t_psa = [[Tok(excl=True), Tok(excl=True)], [Tok(excl=True), Tok(excl=True)]]
        pst = P.ps([128, 128], BF16, "pst")
        t_pst = Tok(excl=True)
        es = [P.sb([128, 2, QG], BF16, "es%d" % q) for q in range(3)]
        t_es = [Tok() for _ in range(3)]
        mhalf = P.sb([128, 1], F32, "mhalf")
        kb.op("dve", lambda: nc.vector.memset(mhalf[:], -0.5), writes=[t_l])
        rz = P.sb([128, 2], F32, "rz")
        t_rz = Tok()
        o1 = P.sb([128, 128], F32, "o1")
        o2 = P.sb([128, 128], F32, "o2")
        junk = P.sb([128, 128], F32, "junk")
        ss = P.sb([128, 1], F32, "ss")
        on = P.sb([128, 128], BF16, "on")
        t_o1, t_o2, t_ss, t_on = Tok(), Tok(), Tok(), Tok()
        ots = [P.sb([128, HEADS, QG], BF16, "ots%d" % q) for q in range(2)]
        t_ots = [Tok(), Tok()]
        ne = 0
        ng = 0
        for b in range(self.NB):
            kb.dma(kts[:], KT[b].rearrange("h p t -> p h t"), writes=[t_kts])
            for hh in range(HEADS):
                kb.dma(vs[:, :, hh, 0:128],
                       V[b, :, hh * 128:(hh + 1) * 128].rearrange("(kt p) e -> p kt e", p=128), writes=[t_vs])
            kb.op("pool", lambda: nc.gpsimd.memset(vs[:, :, :, 128:129], 1.0), writes=[t_vs])
            groups = []
            if need_ctx:
                for q0 in range(0, CTX, QG):
                    groups.append((q0, min(QG, CTX - q0), list(range(CTX // 128))))
            for q0 in range(0, self.SEQ, QG):
                groups.append((CTX + q0, QG, list(range(nkt))))
            for (q0, nq, ktl) in groups:
                gj = ng % 2
                ng += 1
                kb.dma(qs_[gj][:, :, :, :nq], QT[b, :, :, :, q0:q0 + nq].rearrange("h c p t -> p h c t"),
                       writes=[t_qs[gj]])
                nsub = nq // 128
                its = [(hh, ki, kt) for hh in range(HEADS) for ki, kt in enumerate(ktl)]
                nk = len(ktl)

                def emit_scores(idx, gj=gj, nq=nq, its=its):
                    hh, ki, kt = its[idx]
                    pj = (ne0 + idx) % NPS
                    for c in range(2):
                        kb.op("pe", lambda c=c: nc.tensor.matmul(
                            pss[pj][:, c, :nq], kts[:, hh, kt * 128:(kt + 1) * 128], qs_[gj][:, hh, c, :nq],
                            start=True, stop=True), reads=[t_kts, t_qs[gj]], writes=[t_pss[pj]])

                def emit_exp(idx, nq=nq):
                    pj = (ne0 + idx) % NPS
                    ej = (ne0 + idx) % 3
                    kb.op("act", lambda: nc.scalar.activation(
                        out=es[ej][:, :, :nq], in_=pss[pj][:, :, :nq], func=AF.Exp),
                        reads=[t_pss[pj]], writes=[t_es[ej]])

                def emit_pv(idx, nsub=nsub, its=its, nk=nk):
                    hh, ki, kt = its[idx]
                    ej = (ne0 + idx) % 3
                    for c in range(2):
                        for sq in range(nsub):
                            kb.op("pe", lambda c=c, sq=sq: nc.tensor.matmul(
                                psa[c][sq][:, 0:129], es[ej][:, c, sq * 128:(sq + 1) * 128],
                                vs[:, kt, hh, 0:129], start=(ki == 0), stop=(ki == nk - 1)),
                                reads=[t_es[ej], t_vs], writes=[t_psa[c][sq]])

                def finalize(hh, gj=gj, nsub=nsub):
                    for c in range(2):
                        for sq in range(nsub):
                            kb.op("dve", lambda c=c, sq=sq: nc.vector.tensor_copy(
                                out=accs[:, c, sq, :], in_=psa[c][sq][:, 0:129]),
                                reads=[t_psa[c][sq]], writes=[t_accs])
                    for sq in range(nsub):
                        kb.op("dve", lambda sq=sq: nc.vector.reciprocal(out=rz[:, 0:2], in_=accs[:, :, sq, 128]),
                              reads=[t_accs], writes=[t_rz])
                        kb.op("dve", lambda: nc.vector.tensor_tensor(out=rz[:, 1:2], in0=rz[:, 1:2], in1=nlam[:],
                                                                     op=ALU.mult), reads=[t_l], writes=[t_rz])
                        kb.op("dve", lambda sq=sq: nc.vector.tensor_scalar(
                            out=o1[:], in0=accs[:, 0, sq, 0:128], scalar1=rz[:, 0:1], scalar2=None, op0=ALU.mult),
                            reads=[t_accs, t_rz], writes=[t_o1])
                        kb.op("dve", lambda sq=sq: nc.vector.scalar_tensor_tensor(
                            out=o2[:], in0=accs[:, 1, sq, 0:128], scalar=rz[:, 1:2], in1=o1[:],
                            op0=ALU.mult, op1=ALU.add), reads=[t_accs, t_rz, t_o1], writes=[t_o2])
                        kb.op("dve", lambda: nc.vector.scalar_tensor_tensor(
                            out=junk[:], in0=o2[:], scalar=1.0, in1=o2[:], op0=ALU.mult, op1=ALU.mult,
                            accum_out=ss[:]), reads=[t_o2], writes=[t_ss])
                        kb.op("dve", lambda: nc.vector.tensor_scalar(
                            out=ss[:], in0=ss[:], scalar1=1.0 / 128.0, scalar2=EPS, op0=ALU.mult, op1=ALU.add),
                            writes=[t_ss])
                        kb.op("pool", lambda: nc.gpsimd.tensor_tensor(out=ss[:], in0=ss[:], in1=mhalf[:], op=ALU.pow),
                              reads=[t_l], writes=[t_ss])
                        kb.op("dve", lambda: nc.vector.scalar_tensor_tensor(
                            out=on[:], in0=o2[:], scalar=ss[:, 0:1], in1=gsb[:], op0=ALU.mult, op1=ALU.mult),
                            reads=[t_o2, t_ss, t_l], writes=[t_on])
                        kb.op("pe", lambda: nc.tensor.transpose(out=pst[:], in_=on[:], identity=ident[:]),
                              reads=[t_on, t_id], writes=[t_pst])
                        kb.op("dve", lambda sq=sq: nc.vector.tensor_copy(
                            out=ots[gj][:, hh, sq * 128:(sq + 1) * 128], in_=pst[:]),
                            reads=[t_pst], writes=[t_ots[gj]])

                ne0 = ne
                AH = 2
                for a in range(min(AH, len(its))):
                    emit_scores(a)
                for idx in range(len(its)):
                    emit_exp(idx)
                    if idx + AH < len(its):
                        emit_scores(idx + AH)
                    emit_pv(idx)
                    if its[idx][1] == nk - 1:
                        finalize(its[idx][0])
                ne += len(its)
                col = b * TB + q0
                kb.dma(OT[0:1024, col:col + nq].rearrange("(h p) t -> p h t", p=128), ots[gj][:, :, :nq],
                       reads=[t_ots[gj]])
        kb.barrier()


def phase_proj_post(self, i, wname, widx, kc, with_ctx, glu=False):
    kb, nc = self.kb, self.nc
    NT = self.NT
    xT = self.dram["xT"].rearrange("(k p) t -> p k t", p=128)
    OT = self.dram["OT"].rearrange("(k p) t -> p k t", p=128)
    wd = self.dram[wname][widx].rearrange("(k p) d -> p k d", p=128)
    ncol = 2048 if glu else 1024
    with Pool(kb) as P:
        ws = P.sb([128, kc, ncol], BF16, "ws")
        t_w = [Tok() for _ in range(kc)]
        for k in range(kc):
            for hf in range(ncol // 1024):
                kb.dma(ws[:, k, hf * 1024:(hf + 1) * 1024], wd[:, k, hf * 1024:(hf + 1) * 1024],
                       writes=[t_w[k]], q="pool")
        xt = [P.sb([128, KC, NT], F32, "xt%d" % q) for q in range(2)]
        t_xt = [Tok(), Tok()]
        at = [P.sb([128, kc, NT], BF16, "at%d" % q) for q in range(2)]
        t_at = [Tok(), Tok()]
        y = P.sb([128, KC, NT], F32, "y")
        t_y = Tok()
        sg = P.sb([128, NT], F32, "sg")
        t_sg = Tok()
        W = self.norm_work(P, NT)
        psn = P.ps([128, 512], F32, "psn")
        t_psn = Tok(excl=True)
        psd = [P.ps([128, 512], F32, "psd%d" % q) for q in range(4)]
        t_psd = [Tok(excl=True) for _ in range(4)]
        tl = self.tiles(with_ctx)
        c0, n, m = tl[0]
        kb.dma(xt[0][:, :, :n], xT[:, :, c0:c0 + n], writes=[t_xt[0]])
        kb.dma(at[0][:, :, :n], OT[:, 0:kc, c0:c0 + n], writes=[t_at[0]])
        nd = 0
        for ti, (c0, n, m) in enumerate(tl):
            jx = ti % 2
            if ti + 1 < len(tl):
                c1, n1, _ = tl[ti + 1]
                kb.dma(xt[1 - jx][:, :, :n1], xT[:, :, c1:c1 + n1], writes=[t_xt[1 - jx]])
                kb.dma(at[1 - jx][:, :, :n1], OT[:, 0:kc, c1:c1 + n1], writes=[t_at[1 - jx]])
            for d in range(KC):
                pj = nd % 2
                nd += 1
                for k in range(kc):
                    kb.op("pe", lambda d=d, k=k, pj=pj: nc.tensor.matmul(
                        psd[pj][:, :n], ws[:, k, d * 128:(d + 1) * 128], at[jx][:, k, :n],
                        start=(k == 0), stop=(k == kc - 1)), reads=[t_w[k], t_at[jx]], writes=[t_psd[pj]])
                if glu:
                    for k in range(kc):
                        kb.op("pe", lambda d=d, k=k, pj=pj: nc.tensor.matmul(
                            psd[2 + pj][:, :n], ws[:, k, 1024 + d * 128:1024 + (d + 1) * 128], at[jx][:, k, :n],
                            start=(k == 0), stop=(k == kc - 1)), reads=[t_w[k], t_at[jx]], writes=[t_psd[2 + pj]])
                    kb.op("act", lambda pj=pj: nc.scalar.activation(out=sg[:, :n], in_=psd[2 + pj][:, :n],
                                                                    func=AF.Sigmoid),
                          reads=[t_psd[2 + pj]], writes=[t_sg])
                    kb.op("dve", lambda d=d, pj=pj: nc.vector.tensor_tensor(
                        out=y[:, d, :n], in0=psd[pj][:, :n], in1=sg[:, :n], op=ALU.mult),
                        reads=[t_psd[pj], t_sg], writes=[t_y])
                else:
                    kb.op("act", lambda d=d, pj=pj: nc.scalar.copy(out=y[:, d, :n], in_=psd[pj][:, :n]),
                          reads=[t_psd[pj]], writes=[t_y])
            self.norm_post(y, t_y, xt[jx], t_xt[jx], n, i, 0, m, W, psn, t_psn)
            kb.dma(xT[:, :, c0:c0 + n], xt[jx][:, :, :n], reads=[t_xt[jx]])
        kb.barrier()


Model.phase_attn_qkv = phase_attn_qkv
Model.phase_attn_core = phase_attn_core
Model.phase_proj_post = phase_proj_post


LW = 1536
LC = 12


def gelu_tanh(self, dst, src, t_src, shape, Wg, t_dst):
    kb, nc = self.kb, self.nc
    a, b_ = Wg["a"], Wg["b"]
    va = a[:, :shape[1]] if len(shape) == 2 else a[:, :shape[1], :shape[2]]
    vb = b_[:, :shape[1]] if len(shape) == 2 else b_[:, :shape[1], :shape[2]]
    kb.op("act", lambda: nc.scalar.activation(out=va, in_=src, func=AF.Square), reads=[t_src], writes=[Wg["ta"]])
    kb.op("dve", lambda: nc.vector.tensor_scalar(out=va, in0=va, scalar1=0.044715, scalar2=1.0,
                                                 op0=ALU.mult, op1=ALU.add), writes=[Wg["ta"]])
    kb.op("dve", lambda: nc.vector.tensor_tensor(out=va, in0=va, in1=src, op=ALU.mult),
          reads=[t_src], writes=[Wg["ta"]])
    kb.op("act", lambda: nc.scalar.activation(out=vb, in_=va, func=AF.Sigmoid, scale=1.5957691216057308),
          reads=[Wg["ta"]], writes=[Wg["tb"]])
    kb.op("dve", lambda: nc.vector.tensor_tensor(out=dst, in0=vb, in1=src, op=ALU.mult),
          reads=[Wg["tb"], t_src], writes=[t_dst])


def _lru_scratch(self):
    _attn_scratch(self)
    if "RT" not in self.dram:
        self.dscratch("RT", [LW, self.T], F32)
        self.dscratch("GT", [LW, self.T], BF16)


def phase_lru_in(self, i, j):
    kb, nc = self.kb, self.nc
    _lru_scratch(self)
    NT = self.NT
    xT = self.dram["xT"].rearrange("(k p) t -> p k t", p=128)
    win = self.dram["lru_w_in"][j].rearrange("(k p) f -> p k f", p=128)
    RT = self.dram["RT"].rearrange("(k p) t -> p k t", p=128)
    GT = self.dram["GT"].rearrange("(k p) t -> p k t", p=128)
    with Pool(kb) as P:
        ws = P.sb([128, KC, 2 * LW], BF16, "wins")
        t_w = [Tok() for _ in range(6)]
        for g in range(6):
            kb.dma(ws[:, :, g * 512:(g + 1) * 512], win[:, :, g * 512:(g + 1) * 512], writes=[t_w[g]], q="pool")
        xt = [P.sb([128, KC, NT], F32, "xt%d" % q) for q in range(2)]
        t_xt = [Tok(), Tok()]
        h = P.sb([128, KC, NT], BF16, "h")
        t_h = Tok()
        W = self.norm_work(P, NT)
        psn = P.ps([128, 512], F32, "psn")
        t_psn = Tok(excl=True)
        pso = [P.ps([128, 2, 256], F32, "pso%d" % q) for q in range(3)]
        t_pso = [Tok(excl=True) for _ in range(3)]
        Wg = dict(a=P.sb([128, 2, 256], F32, "ga"), b=P.sb([128, 2, 256], F32, "gb"), ta=Tok(), tb=Tok())
        gs = [P.sb([128, LC, NT], BF16, "gs%d" % q) for q in range(2)]
        t_gs = [Tok(), Tok()]
        rs = [P.sb([128, LC, NT], F32, "rs%d" % q) for q in range(2)]
        t_rs = [Tok(), Tok()]
        tl = self.tiles(True)
        c0, n, m = tl[0]
        kb.dma(xt[0][:, :, :n], xT[:, :, c0:c0 + n], writes=[t_xt[0]])
        np_ = 0
        for ti, (c0, n, m) in enumerate(tl):
            jx = ti % 2
            if ti + 1 < len(tl):
                c1, n1, _ = tl[ti + 1]
                kb.dma(xt[1 - jx][:, :, :n1], xT[:, :, c1:c1 + n1], writes=[t_xt[1 - jx]])
            self.norm_pre(xt[jx], t_xt[jx], n, i, 0, m, h, t_h, W, psn, t_psn)
            for op in range(LC):
                pj = np_ % 3
                np_ += 1
                for q2 in range(2):
                    oc = op * 2 + q2
                    for k in range(KC):
                        kb.op("pe", lambda k=k, oc=oc, q2=q2, pj=pj: nc.tensor.matmul(
                            pso[pj][:, q2, :n], ws[:, k, oc * 128:(oc + 1) * 128], h[:, k, :n],
                            start=(k == 0), stop=(k == KC - 1)), reads=[t_w[oc // 4], t_h], writes=[t_pso[pj]])
                if op < 6:
                    gelu_tanh(self, gs[jx][:, 2 * op:2 * op + 2, :n], pso[pj][:, :, :n], t_pso[pj],
                              [128, 2, n], Wg, t_gs[jx])
                else:
                    kb.op("act", lambda op=op, pj=pj: nc.scalar.copy(
                        out=rs[jx][:, 2 * (op - 6):2 * (op - 6) + 2, :n], in_=pso[pj][:, :, :n]),
                        reads=[t_pso[pj]], writes=[t_rs[jx]])
            kb.dma(GT[:, :, c0:c0 + n], gs[jx][:, :, :n], reads=[t_gs[jx]])
            kb.dma(RT[:, :, c0:c0 + n], rs[jx][:, :, :n], reads=[t_rs[jx]])
        kb.barrier()


def phase_lru_scan(self, i, j):
    kb, nc = self.kb, self.nc
    TB, CTX, SEQ = self.TB, self.CTX, self.SEQ
    RT = self.dram["RT"].rearrange("(k p) t -> p k t", p=128)
    GT = self.dram["GT"].rearrange("(k p) t -> p k t", p=128)
    OT = self.dram["OT"].rearrange("(k p) t -> p k t", p=128)
    CW = 512
    with Pool(kb) as P:
        wg = P.sb([128, 2, 2, 6, 2, 256], BF16, "wg")
        cw = P.sb([128, LC, 4], F32, "cw")
        cb = P.sb([128, LC], F32, "cb")
        bg = P.sb([128, 2, 2, LC], F32, "bg")
        ap_ = P.sb([128, 2, LC], F32, "ap")
        cv = P.sb([128, 2, LC], F32, "cv")
        cv2 = P.sb([128, 2, LC], F32, "cv2")
        one = P.sb([128, 1], F32, "one")
        t_s = Tok()
        kb.dma(wg[:], self.dram["lru_w_gate"][j], writes=[t_s], q="pool")
        kb.dma(cw[:], self.dram["lru_conv_w"][j], writes=[t_s])
        kb.dma(cb[:], self.dram["lru_conv_b"][j], writes=[t_s])
        kb.dma(bg[:], self.dram["lru_b_gate"][j], writes=[t_s])
        kb.dma(ap_[:], self.dram["lru_a_param"][j], writes=[t_s])
        kb.op("dve", lambda: nc.vector.memset(one[:], 1.0), writes=[t_s])
        kb.op("act", lambda: nc.scalar.activation(out=cv[:], in_=ap_[:], func=AF.Exp, scale=-1.0),
              reads=[t_s], writes=[t_s])
        kb.op("act", lambda: nc.scalar.activation(out=cv[:], in_=cv[:], func=AF.Ln, bias=one[:], scale=1.0),
              reads=[t_s], writes=[t_s])
        cvh = cv2
        bgh = P.sb([128, 2, 2, LC], F32, "bgh")
        kb.op("dve", lambda: nc.vector.tensor_scalar(out=cvh[:], in0=cv[:], scalar1=-4.0, scalar2=None,
                                                     op0=ALU.mult), reads=[t_s], writes=[t_s])
        kb.op("dve", lambda: nc.vector.tensor_scalar(out=cv[:], in0=cv[:], scalar1=-8.0, scalar2=None,
                                                     op0=ALU.mult), reads=[t_s], writes=[t_s])
        kb.op("dve", lambda: nc.vector.tensor_scalar(out=bgh[:], in0=bg[:], scalar1=0.5, scalar2=None,
                                                     op0=ALU.mult), reads=[t_s], writes=[t_s])
        rt = P.sb([128, TB], F32, "rt")
        t_rt = Tok()
        u = P.sb([128, 2, TB], F32, "u")
        t_u = [Tok(), Tok()]
        ub = P.sb([128, 2, TB], BF16, "ub")
        t_ub = [Tok(), Tok()]
        av = P.sb([128, TB], F32, "av")
        bv = P.sb([128, TB], F32, "bv")
        t_av, t_bv = Tok(), Tok()
        hf = P.sb([128, TB], F32, "hf")
        hb = P.sb([128, TB], F32, "hb")
        t_hf, t_hb = Tok(), Tok()
        gt = P.sb([128, TB], BF16, "gt")
        t_gt = Tok()
        mo = P.sb([128, TB], BF16, "mo")
        t_mo = Tok()
        psg = [P.ps([128, 2, CW], F32, "psg%d" % q) for q in range(2)]
        t_psg = [Tok(excl=True), Tok(excl=True)]
        sr = [P.sb([128, 2, CW], F32, "sr%d" % q) for q in range(2)]
        t_sr = [Tok(), Tok()]
        a2 = [P.sb([128, CW], F32, "a2%d" % q) for q in range(2)]
        t_a2 = [Tok(), Tok()]
        segs = [(0, CTX), (CTX, TB)]
        ng = 0
        for b in range(self.NB):
            cb0 = b * TB
            for n6 in range(6):
                for q2 in range(2):
                    ch = n6 * 2 + q2
                    kb.dma(rt[:], RT[:, ch, cb0:cb0 + TB], writes=[t_rt])
                    for (s0, s1) in segs:
                        kb.op("dve", lambda s0=s0, s1=s1, ch=ch, q2=q2: nc.vector.tensor_scalar(
                            out=u[:, q2, s0:s1], in0=rt[:, s0:s1], scalar1=cw[:, ch, 2:3], scalar2=cb[:, ch:ch + 1],
                            op0=ALU.mult, op1=ALU.add), reads=[t_rt, t_s], writes=[t_u[q2]])
                        for (tap, off) in ((0, -2), (1, -1), (3, 1)):
                            if off < 0:
                                o_sl, i_sl = slice(s0 - off, s1), slice(s0, s1 + off)
                            else:
                                o_sl, i_sl = slice(s0, s1 - off), slice(s0 + off, s1)
                            kb.op("dve", lambda o_sl=o_sl, i_sl=i_sl, ch=ch, q2=q2, tap=tap:
                                  nc.vector.scalar_tensor_tensor(
                                      out=u[:, q2, o_sl], in0=rt[:, i_sl], scalar=cw[:, ch, tap:tap + 1],
                                      in1=u[:, q2, o_sl], op0=ALU.mult, op1=ALU.add),
                                  reads=[t_rt, t_s], writes=[t_u[q2]])
                    kb.op("act", lambda q2=q2: nc.scalar.copy(out=ub[:, q2, :], in_=u[:, q2, :]),
                          reads=[t_u[q2]], writes=[t_ub[q2]])
                    kb.op("pool", lambda q2=q2: nc.gpsimd.tensor_scalar(
                        out=u[:, q2, :], in0=u[:, q2, :], scalar1=0.5, scalar2=None, op0=ALU.mult),
                        reads=[t_ub[q2]], writes=[t_u[q2]])
                for q2 in range(2):
                    ch = n6 * 2 + q2
                    kb.dma(gt[:], GT[:, ch, cb0:cb0 + TB], writes=[t_gt])
                    for d in range(2):
                        for c0 in range(0, TB, CW):
                            n = min(CW, TB - c0)
                            pj = ng % 2
                            ng += 1
                            for kk in range(2):
                                for kc in range(2):
                                    kb.op("pe", lambda kk=kk, kc=kc, d=d, c0=c0, n=n, pj=pj: nc.tensor.matmul(
                                        psg[pj][:, kk, :n], wg[:, d, kk, n6, kc, q2 * 128:(q2 + 1) * 128],
                                        ub[:, kc, c0:c0 + n], start=(kc == 0), stop=(kc == 1)),
                                        reads=[t_s] + t_ub, writes=[t_psg[pj]])
                            for kk in range(2):
                                kb.op("act", lambda kk=kk, d=d, n=n, pj=pj: nc.scalar.activation(
                                    out=sr[pj][:, kk, :n], in_=psg[pj][:, kk, :n], func=AF.Tanh,
                                    bias=bgh[:, d, kk, ch:ch + 1], scale=0.5),
                                    reads=[t_psg[pj], t_s], writes=[t_sr[pj]])
                            kb.op("act", lambda d=d, c0=c0, n=n, pj=pj: nc.scalar.activation(
                                out=av[:, c0:c0 + n], in_=sr[pj][:, 0, :n], func=AF.Exp,
                                scale=cvh[:, d, ch:ch + 1], bias=cvh[:, d, ch:ch + 1]),
                                reads=[t_sr[pj], t_s], writes=[t_av])
                            kb.op("act", lambda d=d, n=n, pj=pj: nc.scalar.activation(
                                out=a2[pj][:, :n], in_=sr[pj][:, 0, :n], func=AF.Exp,
                                scale=cv[:, d, ch:ch + 1], bias=cv[:, d, ch:ch + 1]),
                                reads=[t_sr[pj], t_s], writes=[t_a2[pj]])
                            kb.op("pool", lambda n=n, pj=pj, c0=c0: nc.gpsimd.tensor_scalar(
                                out=bv[:, c0:c0 + n], in0=a2[pj][:, :n], scalar1=-1.0, scalar2=1.0,
                                op0=ALU.mult, op1=ALU.add), reads=[t_a2[pj]], writes=[t_bv])
                            kb.op("dve", lambda n=n, pj=pj, c0=c0: nc.vector.scalar_tensor_tensor(
                                out=rt[:, c0:c0 + n], in0=sr[pj][:, 1, :n], scalar=1.0, in1=u[:, q2, c0:c0 + n],
                                op0=ALU.add, op1=ALU.mult), reads=[t_sr[pj], t_u[q2]], writes=[t_rt])
                        kb.op("act", lambda: nc.scalar.activation(out=bv[:, :], in_=bv[:, :], func=AF.Sqrt),
                              writes=[t_bv])
                        kb.op("pool", lambda: nc.gpsimd.tensor_tensor(out=bv[:, :], in0=bv[:, :], in1=rt[:, :],
                                                                      op=ALU.mult), reads=[t_rt], writes=[t_bv])
                        if d == 0:
                            kb.op("dve", lambda: nc.vector.tensor_tensor_scan(
                                out=hf[:, :], data0=av[:, :], data1=bv[:, :], initial=0.0,
                                op0=ALU.mult, op1=ALU.add), reads=[t_av, t_bv], writes=[t_hf])
                        else:
                            kb.op("dve", lambda: nc.vector.tensor_tensor_scan(
                                out=hb[:, 0:CTX][:, ::-1], data0=av[:, 0:CTX][:, ::-1], data1=bv[:, 0:CTX][:, ::-1],
                                initial=0.0, op0=ALU.mult, op1=ALU.add), reads=[t_av, t_bv], writes=[t_hb])
                            kb.op("dve", lambda: nc.vector.tensor_tensor_scan(
                                out=hb[:, CTX:TB][:, ::-1], data0=av[:, CTX:TB][:, ::-1],
                                data1=bv[:, CTX:TB][:, ::-1], initial=hb[:, 0:1], op0=ALU.mult, op1=ALU.add),
                                reads=[t_av, t_bv], writes=[t_hb])
                    kb.op("pool", lambda: nc.gpsimd.tensor_tensor(out=hf[:], in0=hf[:], in1=hb[:], op=ALU.add),
                          reads=[t_hb], writes=[t_hf])
                    kb.op("pool", lambda: nc.gpsimd.tensor_tensor(out=mo[:], in0=hf[:], in1=gt[:], op=ALU.mult),
                          reads=[t_hf, t_gt], writes=[t_mo])
                    kb.dma(OT[:, ch, cb0:cb0 + TB], mo[:], reads=[t_mo])
        kb.barrier()


Model.phase_lru_in = phase_lru_in
Model.phase_lru_scan = phase_lru_scan


import math as _math
NG = 64
LL = 8


def _s5_scratch(self):
    _attn_scratch(self)
    if "UT" not in self.dram:
        self.dscratch("UT", [1024, self.T], BF16)
        self.dscratch("YF", [2, 1024, self.T], F32)


def phase_s5_in(self, i):
    kb, nc = self.kb, self.nc
    _s5_scratch(self)
    NT = self.NT
    xT = self.dram["xT"].rearrange("(k p) t -> p k t", p=128)
    UT = self.dram["UT"].rearrange("(k p) t -> p k t", p=128)
    with Pool(kb) as P:
        xt = [P.sb([128, KC, NT], F32, "xt%d" % q) for q in range(2)]
        t_xt = [Tok(), Tok()]
        h = [P.sb([128, KC, NT], BF16, "h%d" % q) for q in range(2)]
        t_h = [Tok(), Tok()]
        W = self.norm_work(P, NT)
        psn = P.ps([128, 512], F32, "psn")
        t_psn = Tok(excl=True)
        tl = self.tiles(True)
        c0, n, m = tl[0]
        kb.dma(xt[0][:, :, :n], xT[:, :, c0:c0 + n], writes=[t_xt[0]])
        for ti, (c0, n, m) in enumerate(tl):
            jx = ti % 2
            if ti + 1 < len(tl):
                c1, n1, _ = tl[ti + 1]
                kb.dma(xt[1 - jx][:, :, :n1], xT[:, :, c1:c1 + n1], writes=[t_xt[1 - jx]])
            self.norm_pre(xt[jx], t_xt[jx], n, i, 0, m, h[jx], t_h[jx], W, psn, t_psn)
            kb.dma(UT[:, :, c0:c0 + n], h[jx][:, :, :n], reads=[t_h[jx]])
        kb.barrier()


def phase_s5_scan(self, i, jb):
    kb, nc = self.kb, self.nc
    TB, CTX, SEQ = self.TB, self.CTX, self.SEQ
    J = TB // LL
    npass = 0
    while (1 << npass) < J:
        npass += 1
    pmax = getattr(Model, "s5_piece_max", 512)
    pieces = [(0, J)]
    if J > pmax:
        pieces = [(0, pmax), (pmax, J)]
    PB = pmax
    UT = self.dram["UT"].rearrange("(k p) t -> p k t", p=128)
    YF = self.dram["YF"]
    TWO_PI = 2.0 * _math.pi
    V = nc.vector
    for d in getattr(self, 's5_dirs', (0, 1)):
        with Pool(kb) as P:
            t_st = Tok()

            def tt(o, a, b, op):
                kb.op("dve", lambda: V.tensor_tensor(out=o, in0=a, in1=b, op=op), writes=[t_st])

            def ts(o, a, s1, op0, s2=None, op1=None):
                if op1 is None:
                    kb.op("dve", lambda: V.tensor_scalar(out=o, in0=a, scalar1=s1, scalar2=None, op0=op0),
                          writes=[t_st])
                else:
                    kb.op("dve", lambda: V.tensor_scalar(out=o, in0=a, scalar1=s1, scalar2=s2, op0=op0, op1=op1),
                          writes=[t_st])

            def act(o, a, func, scale=1.0, bias=None):
                if bias is None:
                    kb.op("act", lambda: nc.scalar.activation(out=o, in_=a, func=func, scale=scale), writes=[t_st])
                else:
                    kb.op("act", lambda: nc.scalar.activation(out=o, in_=a, func=func, scale=scale, bias=bias),
                          writes=[t_st])
            BS = P.sb([128, 8, NG, 16], BF16, "BS")
            CS = P.sb([128, 9, NG, 16], BF16, "CS")
            PW = P.sb([128, 9, 2, NG], F32, "PW")
            QW = P.sb([128, npass, 2, NG], F32, "QW")
            qos = P.sb([128, npass, NG], F32, "qos")
            identb = P.sb([128, 128], BF16, "identb")
            identf = P.sb([128, 128], F32, "identf")
            shf = P.sb([128, 128], F32, "shf")
            dsk = P.sb([128, KC], F32, "dsk")
            kb.dma(identb[:], self.dram["ident"], writes=[t_st], q="pool")
            kb.dma(identf[:], self.dram["ident"], writes=[t_st])
            kb.dma(shf[:], self.dram["shift64"], writes=[t_st])
            with Pool(kb) as PS:
                pg = lambda nm: PS.sb([128, NG], F32, nm)
                are, aim, dt, xr, xi, mag, yy, ff, mm, sn, cs, lr, li = [pg("pg%d" % q) for q in range(13)]
                nr, den, cr, ci, t1, t2, t3, t4 = [pg("pgb%d" % q) for q in range(8)]
                ki = PS.sb([128, NG], I32, "ki")
                pgc = lambda nm: PS.sb([128, NG, 16], F32, nm)
                Bre, Bim, Cre, Cim, Bbr, Bbi, T1, T2, T3, T4 = [pgc("pgc%d" % q) for q in range(10)]
                kb.dma(are[:], self.dram["s5_a_re"][jb, d], writes=[t_st])
                kb.dma(aim[:], self.dram["s5_a_im"][jb, d], writes=[t_st])
                kb.dma(dt[:], self.dram["s5_log_dt"][jb, d].partition_broadcast(128), writes=[t_st])
                kb.dma(Bre[:], self.dram["s5_b_re"][jb, d], writes=[t_st])
                kb.dma(Bim[:], self.dram["s5_b_im"][jb, d], writes=[t_st])
                kb.dma(Cre[:], self.dram["s5_c_re"][jb, d], writes=[t_st])
                kb.dma(Cim[:], self.dram["s5_c_im"][jb, d], writes=[t_st])
                act(dt[:], dt[:], AF.Exp)
                tt(xr[:], are[:], dt[:], ALU.mult)
                tt(xi[:], aim[:], dt[:], ALU.mult)
                ts(yy[:], xi[:], 1.0 / TWO_PI, ALU.mult)
                kb.op("dve", lambda: V.tensor_copy(out=ki[:], in_=yy[:]), writes=[t_st])
                kb.op("dve", lambda: V.tensor_copy(out=ff[:], in_=ki[:]), writes=[t_st])
                tt(ff[:], yy[:], ff[:], ALU.subtract)
                ts(mm[:], ff[:], 0.5, ALU.is_gt)
                tt(ff[:], ff[:], mm[:], ALU.subtract)
                ts(mm[:], ff[:], -0.5, ALU.is_lt)
                tt(ff[:], ff[:], mm[:], ALU.add)
                ts(ff[:], ff[:], TWO_PI / 16.0, ALU.mult)
                tt(yy[:], ff[:], ff[:], ALU.mult)

                def horner(o, z, coefs):
                    kb.op("dve", lambda: V.memset(o, 0.0), writes=[t_st])
                    for c in reversed(coefs[1:]):
                        kb.op("dve", lambda c=c: V.scalar_tensor_tensor(out=o, in0=o, scalar=float(c), in1=z,
                                                                        op0=ALU.add, op1=ALU.mult), writes=[t_st])
                    ts(o, o, float(coefs[0]), ALU.add)
                fact = [1.0]
                for q in range(1, 16):
                    fact.append(fact[-1] * q)
                horner(cs[:], yy[:], [(-1) ** q / fact[2 * q] for q in range(6)])
                horner(sn[:], yy[:], [(-1) ** q / fact[2 * q + 1] for q in range(6)])
                tt(sn[:], sn[:], ff[:], ALU.mult)
                ts(mm[:], xr[:], 1.0 / 16.0, ALU.mult)
                horner(mag[:], mm[:], [1.0 / fact[q] for q in range(10)])
                tt(lr[:], mag[:], cs[:], ALU.mult)
                tt(li[:], mag[:], sn[:], ALU.mult)
                for _sq in range(4):
                    tt(t1[:], lr[:], lr[:], ALU.mult)
                    tt(t2[:], li[:], li[:], ALU.mult)
                    tt(t3[:], lr[:], li[:], ALU.mult)
                    tt(lr[:], t1[:], t2[:], ALU.subtract)
                    ts(li[:], t3[:], 2.0, ALU.mult)
                ts(nr[:], lr[:], -1.0, ALU.add)
                tt(t1[:], are[:], are[:], ALU.mult)
                tt(t2[:], aim[:], aim[:], ALU.mult)
                tt(den[:], t1[:], t2[:], ALU.add)
                kb.op("dve", lambda: V.reciprocal(out=den[:], in_=den[:]), writes=[t_st])
                tt(t1[:], nr[:], are[:], ALU.mult)
                tt(t2[:], li[:], aim[:], ALU.mult)
                tt(cr[:], t1[:], t2[:], ALU.add)
                tt(cr[:], cr[:], den[:], ALU.mult)
                tt(t1[:], li[:], are[:], ALU.mult)
                tt(t2[:], nr[:], aim[:], ALU.mult)
                tt(ci[:], t1[:], t2[:], ALU.subtract)
                tt(ci[:], ci[:], den[:], ALU.mult)
                bc = lambda v: v.unsqueeze(2).broadcast_to([128, NG, 16])
                tt(T1[:], Bre[:], bc(cr[:]), ALU.mult)
                tt(T2[:], Bim[:], bc(ci[:]), ALU.mult)
                tt(Bbr[:], T1[:], T2[:], ALU.subtract)
                tt(T1[:], Bim[:], bc(cr[:]), ALU.mult)
                tt(T2[:], Bre[:], bc(ci[:]), ALU.mult)
                tt(Bbi[:], T1[:], T2[:], ALU.add)
                kb.op("dve", lambda: V.memset(PW[:, 0, 0, :], 1.0), writes=[t_st])
                kb.op("dve", lambda: V.memset(PW[:, 0, 1, :], 0.0), writes=[t_st])
                kb.op("dve", lambda: V.tensor_copy(out=PW[:, 1, 0, :], in_=lr[:]), writes=[t_st])
                kb.op("dve", lambda: V.tensor_copy(out=PW[:, 1, 1, :], in_=li[:]), writes=[t_st])

                def cmul(o_r, o_i, a_r, a_i, b_r, b_i):
                    tt(t1[:], a_r, b_r, ALU.mult)
                    tt(t2[:], a_i, b_i, ALU.mult)
                    tt(t3[:], a_r, b_i, ALU.mult)
                    tt(t4[:], a_i, b_r, ALU.mult)
                    tt(o_r, t1[:], t2[:], ALU.subtract)
                    tt(o_i, t3[:], t4[:], ALU.add)
                for e in range(2, 9):
                    cmul(PW[:, e, 0, :], PW[:, e, 1, :], PW[:, e - 1, 0, :], PW[:, e - 1, 1, :], lr[:], li[:])
                kb.op("dve", lambda: V.tensor_copy(out=QW[:, 0, :, :], in_=PW[:, 8, :, :]), writes=[t_st])
                for k in range(1, npass):
                    cmul(QW[:, k, 0, :], QW[:, k, 1, :], QW[:, k - 1, 0, :], QW[:, k - 1, 1, :],
                         QW[:, k - 1, 0, :], QW[:, k - 1, 1, :])
                kb.op("dve", lambda: V.tensor_copy(out=qos[0:64, :, :], in_=QW[0:64, :, 1, :]), writes=[t_st])
                kb.op("dve", lambda: V.tensor_scalar(out=qos[64:128, :, :], in0=QW[64:128, :, 1, :], scalar1=-1.0,
                                                     scalar2=None, op0=ALU.mult), writes=[t_st])
                for e in range(9):
                    p_r, p_i = bc(PW[:, e, 0, :]), bc(PW[:, e, 1, :])
                    if e < 8:
                        tt(T1[:], Bbr[:], p_r, ALU.mult)
                        tt(T2[:], Bbi[:], p_i, ALU.mult)
                        tt(T3[:], Bbi[:], p_r, ALU.mult)
                        tt(T4[:], Bbr[:], p_i, ALU.mult)
                        tt(BS[0:64, e], T1[0:64], T2[0:64], ALU.subtract)
                        tt(BS[64:128, e], T3[64:128], T4[64:128], ALU.add)
                    tt(T1[:], Cre[:], p_r, ALU.mult)
                    tt(T2[:], Cim[:], p_i, ALU.mult)
                    tt(T3[:], Cre[:], p_i, ALU.mult)
                    tt(T4[:], Cim[:], p_r, ALU.mult)
                    tt(CS[0:64, e], T1[0:64], T2[0:64], ALU.subtract)
                    kb.op("dve", lambda: V.scalar_tensor_tensor(
                        out=CS[64:128, e], in0=T3[64:128], scalar=-1.0, in1=T4[64:128],
                        op0=ALU.mult, op1=ALU.subtract), writes=[t_st])
            if getattr(self, "s5_stop", 0) == 1:
                if d == getattr(self, "s5_dbg_dir", 0):
                    kb.dma(self.dram["dbgBS"], BS[:], reads=[t_st])
                    kb.dma(self.dram["dbgCS"], CS[:], reads=[t_st])
                    kb.dma(self.dram["dbgPW"], PW[:], reads=[t_st])
                    kb.dma(self.dram["dbgQW"], QW[:], reads=[t_st])
                kb.barrier()
                continue
            ECt = P.sb([128, 9, 1152], BF16, "ECt")
            LBt = P.sb([128, 8, 8, 128], BF16, "LBt")
            Kdt = P.sb([128, 8, 128], BF16, "Kdt")
            AMt = P.sb([128, npass, 8, 128], BF16, "AMt")
            t_EC, t_LB, t_Kd, t_AM = Tok(), Tok(), Tok(), Tok()
            kb.op("pool", lambda: nc.gpsimd.memset(ECt[:], 0.0), reads=[t_st], writes=[t_EC])
            _ub = P.sb([128, TB + CTX], BF16, "ub")
            ub = [_ub, _ub]
            _tub = Tok()
            t_ub = [_tub, _tub]
            ud = P.sb([128, LL, J], BF16, "ud")
            t_ud = Tok()
            S32 = P.sb([128, 8, J], F32, "S32")
            Sb = P.sb([128, 8, J], BF16, "Sb")
            Hb = P.sb([128, 8, J], BF16, "Hb")
            t_S32 = [Tok() for _ in range(8)]
            t_Sb = [Tok() for _ in range(8)]
            t_Hb = Tok()
            kb.op("pool", lambda: nc.gpsimd.memset(Hb[:, :, 0:1], 0.0), writes=[t_Hb])
            yb1 = P.sb([128, TB], F32, "yb")
            yb = [yb1, yb1]
            t_yb1 = Tok()
            t_yb = [t_yb1, t_yb1]
            _a = P.sb([128, 8, 128], F32, "amt")
            amt = [_a, _a]
            _ta = Tok()
            t_amt = [_ta, _ta]
            _b = P.sb([128, 8, 128], F32, "amtb")
            amt2 = [_b, _b]
            _tb = Tok()
            t_amt2 = [_tb, _tb]
            nu = 0
            for cidx in getattr(self, 's5_chunks', range(KC)):
                g0 = cidx * 8
                with Pool(kb) as PC:
                    EBt = PC.sb([128, 8, 1152], BF16, "EBt")
                    t_EB = Tok()
                    pst = [PC.ps([128, 8, 128], BF16, "pst%d" % q) for q in range(2)]
                    t_pst = [Tok(excl=True), Tok(excl=True)]
                    psk = [PC.ps([128, 4, 128], F32, "psk%d" % q) for q in range(2)]
                    t_psk = [Tok(excl=True), Tok(excl=True)]
                    kb.op("pool", lambda: nc.gpsimd.memset(EBt[:], 0.0), writes=[t_EB])
                    for e in range(9):
                        if e < 8:
                            kb.op("dve", lambda e=e: V.tensor_copy(
                                out=EBt[:, e, :].rearrange("p (g s) -> p g s", s=144)[:, :, 0:16],
                                in_=BS[:, e, g0:g0 + 8, :]), reads=[t_st], writes=[t_EB])
                        kb.op("dve", lambda e=e: V.tensor_copy(
                            out=ECt[:, e, :].rearrange("p (g s) -> p g s", s=144)[:, :, 0:16],
                            in_=CS[:, e, g0:g0 + 8, :]), reads=[t_st], writes=[t_EC])
                    for e in range(8):
                        pj = e % 2
                        for g in range(8):
                            kb.op("pe", lambda e=e, g=g, pj=pj: nc.tensor.transpose(
                                out=pst[pj][:, g, :], in_=EBt[:, e, g * 128:(g + 1) * 128], identity=identb[:]),
                                reads=[t_EB, t_st], writes=[t_pst[pj]])
                        kb.op("act", lambda e=e, pj=pj: nc.scalar.copy(out=LBt[:, e, :, :], in_=pst[pj][:, :, :]),
                              reads=[t_pst[pj]], writes=[t_LB])
                    for hb_ in range(2):
                        for e4 in range(4):
                            e = hb_ * 4 + e4
                            for g in range(8):
                                kb.op("pe", lambda e=e, e4=e4, g=g, hb_=hb_: nc.tensor.matmul(
                                    psk[hb_][:, e4, :], EBt[:, e, g * 128:(g + 1) * 128],
                                    ECt[:, 0, g * 128:(g + 1) * 128], start=(g == 0), stop=(g == 7)),
                                    reads=[t_EB, t_EC], writes=[t_psk[hb_]])
                        kb.op("act", lambda hb_=hb_: nc.scalar.copy(out=Kdt[:, hb_ * 4:(hb_ + 1) * 4, :],
                                                                    in_=psk[hb_][:, :, :]),
                              reads=[t_psk[hb_]], writes=[t_Kd])
                    for k in range(npass):
                        aj = k % 2
                        kb.op("dve", lambda k=k, aj=aj: V.tensor_tensor(
                            out=amt[aj][:], in0=identf[:, :].unsqueeze(1).broadcast_to([128, 8, 128]),
                            in1=QW[:, k, 0, g0:g0 + 8].unsqueeze(2).broadcast_to([128, 8, 128]), op=ALU.mult),
                            reads=[t_st], writes=[t_amt[aj]])
                        kb.op("pool", lambda k=k, aj=aj: nc.gpsimd.tensor_tensor(
                            out=amt2[aj][:], in0=shf[:, :].unsqueeze(1).broadcast_to([128, 8, 128]),
                            in1=qos[:, k, g0:g0 + 8].unsqueeze(2).broadcast_to([128, 8, 128]), op=ALU.mult),
                            reads=[t_st], writes=[t_amt2[aj]])
                        kb.op("dve", lambda k=k, aj=aj: V.tensor_tensor(
                            out=AMt[:, k, :, :], in0=amt[aj][:], in1=amt2[aj][:], op=ALU.add),
                            reads=[t_amt[aj], t_amt2[aj]], writes=[t_AM])
                if getattr(self, "s5_stop", 0) == 2:
                    if d == 0 and cidx == 0:
                        kb.dma(self.dram["dbgLB"], LBt[:], reads=[t_LB])
                        kb.dma(self.dram["dbgKd"], Kdt[:], reads=[t_Kd])
                        kb.dma(self.dram["dbgAM"], AMt[:], reads=[t_AM])
                    continue
                with Pool(kb) as PM:
                    psv = [PM.ps([128, 1024], F32, "psv%d" % q) for q in range(2)]
                    t_psv = [Tok(excl=True), Tok(excl=True)]
                    psy = [PM.ps([128, 1024], F32, "psy%d" % q) for q in range(2)]
                    t_psy = [Tok(excl=True), Tok(excl=True)]
                    npv = 0
                    npy = 0
                    for b in range(self.NB):
                        uj = nu % 2
                        nu += 1
                        ubt = ub[uj]
                        kb.dma(ubt[:, 0:TB], UT[:, cidx, b * TB:(b + 1) * TB], writes=[t_ub[uj]])
                        kb.dma(ubt[:, TB:TB + CTX], UT[:, cidx, b * TB:b * TB + CTX], writes=[t_ub[uj]])
                        ybt = yb[uj]
                        if d == 0:
                            useq = lambda s: ubt[:, s:TB:LL]
                            yseq = lambda r: ybt[:, r:TB:LL]
                        else:
                            useq = lambda s: ubt[:, CTX:CTX + TB][:, (TB - 1 - s)::-LL]
                            yseq = lambda r: ybt[:, (TB - 1 - r)::-LL]
                        for s_ in range(LL):
                            kb.op("act", lambda s_=s_: nc.scalar.copy(out=ud[:, s_, :], in_=useq(s_)),
                                  reads=[t_ub[uj]], writes=[t_ud])
                        for g in range(8):
                            pj = npv % 2
                            npv += 1
                            for (c0, c1) in pieces:
                                pc = pieces.index((c0, c1))
                                for s in range(LL):
                                    kb.op("pe", lambda s=s, g=g, c0=c0, c1=c1, pc=pc, pj=pj: nc.tensor.matmul(
                                        psv[pj][:, c0:c1], LBt[:, LL - 1 - s, g, :],
                                        ud[:, s, c0:c1], start=(s == 0), stop=(s == LL - 1)),
                                        reads=[t_LB, t_ud], writes=[t_psv[pj]])
                            kb.op("act", lambda g=g, pj=pj: nc.scalar.copy(
                                out=S32[:, g, :], in_=psv[pj][:, 0:J]),
                                reads=[t_psv[pj]], writes=[t_S32[g]])
                            kb.op("dve", lambda g=g: V.tensor_copy(out=Sb[:, g, :], in_=S32[:, g, :]),
                                  reads=[t_S32[g]], writes=[t_Sb[g]])
                        _stop = getattr(self, "s5_stop", 0)
                        if _stop == 3:
                            kb.dma(self.dram["dbgS32"], S32[:], reads=t_S32)
                            continue
                        for k in range(npass):
                            dd = 1 << k
                            for g in range(8):
                                pj = npv % 2
                                npv += 1
                                segs = []
                                for (c0, c1) in pieces:
                                    if c1 <= dd:
                                        continue
                                    lo = max(c0, dd)
                                    pc = pieces.index((c0, c1))
                                    segs.append((lo, c1, lo))
                                for (lo, c1, po) in segs:
                                    kb.op("pe", lambda k=k, g=g, lo=lo, c1=c1, po=po, pj=pj, dd=dd: nc.tensor.matmul(
                                        psv[pj][:, po:po + (c1 - lo)], AMt[:, k, g, :], Sb[:, g, lo - dd:c1 - dd],
                                        start=True, stop=True), reads=[t_AM, t_Sb[g]], writes=[t_psv[pj]])
                                kb.op("dve", lambda g=g, pj=pj, dd=dd: V.tensor_tensor(
                                    out=S32[:, g, dd:J], in0=S32[:, g, dd:J], in1=psv[pj][:, dd:J],
                                    op=ALU.add), reads=[t_psv[pj]], writes=[t_S32[g]])
                                kb.op("act", lambda g=g, dd=dd: nc.scalar.copy(out=Sb[:, g, dd:J], in_=S32[:, g, dd:J]),
                                      reads=[t_S32[g]], writes=[t_Sb[g]])
                        kb.op("act", lambda: nc.scalar.copy(out=Hb[:, :, 1:J], in_=S32[:, :, 0:J - 1]),
                              reads=t_S32, writes=[t_Hb])
                        if _stop == 4:
                            kb.dma(self.dram["dbgS32"], S32[:], reads=t_S32)
                            continue
                        for r in range(LL):
                            pj = npy % 2
                            npy += 1
                            for (c0, c1) in pieces:
                                pc = pieces.index((c0, c1))
                                o_ap = psy[pj][:, c0:c1]
                                nmm = (r + 1) + 8
                                im = 0
                                for q in range(r + 1):
                                    kb.op("pe", lambda q=q, r=r, c0=c0, c1=c1, o_ap=o_ap, im=im, nmm=nmm:
                                          nc.tensor.matmul(o_ap, Kdt[:, q, :], ud[:, r - q, c0:c1],
                                                           start=(im == 0), stop=(im == nmm - 1)),
                                          reads=[t_Kd, t_ud], writes=[t_psy[pj]])
                                    im += 1
                                for g in range(8):
                                    kb.op("pe", lambda g=g, r=r, c0=c0, c1=c1, o_ap=o_ap, im=im, nmm=nmm:
                                          nc.tensor.matmul(o_ap, ECt[:, r + 1, g * 128:(g + 1) * 128], Hb[:, g, c0:c1],
                                                           start=(im == 0), stop=(im == nmm - 1)),
                                          reads=[t_EC, t_Hb], writes=[t_psy[pj]])
                                    im += 1
                            kb.op("act", lambda r=r, pj=pj: nc.scalar.copy(
                                out=yseq(r), in_=psy[pj][:, 0:J]),
                                reads=[t_psy[pj]], writes=[t_yb[uj]])
                        col = b * TB
                        if d == 0:
                            kb.dma(YF[0, cidx * 128:(cidx + 1) * 128, col:col + TB], ybt[:, :], reads=[t_yb[uj]])
                        else:
                            kb.dma(YF[1, cidx * 128:(cidx + 1) * 128, col + CTX:col + TB], ybt[:, 0:SEQ],
                                   reads=[t_yb[uj]])
                            kb.dma(YF[1, cidx * 128:(cidx + 1) * 128, col:col + CTX], ybt[:, SEQ:TB],
                                   reads=[t_yb[uj]])
            kb.barrier()


def phase_s5_out(self, i, jb, with_ctx):
    kb, nc = self.kb, self.nc
    NT = self.NT
    xT = self.dram["xT"].rearrange("(k p) t -> p k t", p=128)
    UT = self.dram["UT"].rearrange("(k p) t -> p k t", p=128)
    YF0 = self.dram["YF"][0].rearrange("(k p) t -> p k t", p=128)
    YF1 = self.dram["YF"][1].rearrange("(k p) t -> p k t", p=128)
    wd = self.dram["s5_w_glu"][jb].rearrange("(k p) d -> p k d", p=128)
    with Pool(kb) as P:
        ws = P.sb([128, KC, 2048], BF16, "ws")
        t_w = [Tok() for _ in range(KC)]
        for k in range(KC):
            for hf in range(2):
                kb.dma(ws[:, k, hf * 1024:(hf + 1) * 1024], wd[:, k, hf * 1024:(hf + 1) * 1024],
                       writes=[t_w[k]], q="pool")
        dsk = P.sb([128, KC], F32, "dsk")
        t_d = Tok()
        kb.dma(dsk[:], self.dram["s5_d"][jb], writes=[t_d])
        xt = [P.sb([128, KC, NT], F32, "xt%d" % q) for q in range(2)]
        t_xt = [Tok(), Tok()]
        y0 = [P.sb([128, KC, NT], F32, "y0%d" % q) for q in range(2)]
        y1 = [P.sb([128, KC, NT], F32, "y1%d" % q) for q in range(2)]
        ut = [P.sb([128, KC, NT], BF16, "ut%d" % q) for q in range(2)]
        t_in = [Tok(), Tok()]
        at = P.sb([128, KC, NT], BF16, "at")
        t_at = Tok()
        Wg = dict(a=P.sb([128, KC, NT], F32, "ga"), b=P.sb([128, KC, NT], F32, "gb"), ta=Tok(), tb=Tok())
        y = P.sb([128, KC, NT], F32, "y")
        t_y = Tok()
        sg = P.sb([128, NT], F32, "sg")
        t_sg = Tok()
        W = self.norm_work(P, NT)
        psn = P.ps([128, 512], F32, "psn")
        t_psn = Tok(excl=True)
        psd = [P.ps([128, 512], F32, "psd%d" % q) for q in range(4)]
        t_psd = [Tok(excl=True) for _ in range(4)]
        tl = self.tiles(with_ctx)

        def loads(jx, c0, n):
            kb.dma(xt[jx][:, :, :n], xT[:, :, c0:c0 + n], writes=[t_xt[jx]])
            kb.dma(y0[jx][:, :, :n], YF0[:, :, c0:c0 + n], writes=[t_in[jx]])
            kb.dma(y1[jx][:, :, :n], YF1[:, :, c0:c0 + n], writes=[t_in[jx]])
            kb.dma(ut[jx][:, :, :n], UT[:, :, c0:c0 + n], writes=[t_in[jx]])
        loads(0, tl[0][0], tl[0][1])
        nd = 0
        for ti, (c0, n, m) in enumerate(tl):
            jx = ti % 2
            if ti + 1 < len(tl):
                loads(1 - jx, tl[ti + 1][0], tl[ti + 1][1])
            kb.op("dve", lambda jx=jx, n=n: nc.vector.tensor_tensor(
                out=y0[jx][:, :, :n], in0=y0[jx][:, :, :n], in1=y1[jx][:, :, :n], op=ALU.add),
                writes=[t_in[jx]])
            for k in range(KC):
                kb.op("dve", lambda jx=jx, n=n, k=k: nc.vector.scalar_tensor_tensor(
                    out=y0[jx][:, k, :n], in0=ut[jx][:, k, :n], scalar=dsk[:, k:k + 1], in1=y0[jx][:, k, :n],
                    op0=ALU.mult, op1=ALU.add), reads=[t_d], writes=[t_in[jx]])
            gelu_tanh(self, at[:, :, :n], y0[jx][:, :, :n], t_in[jx], [128, KC, n], Wg, t_at)
            for dch in range(KC):
                pj = nd % 2
                nd += 1
                for k in range(KC):
                    kb.op("pe", lambda dch=dch, k=k, pj=pj: nc.tensor.matmul(
                        psd[pj][:, :n], ws[:, k, dch * 128:(dch + 1) * 128], at[:, k, :n],
                        start=(k == 0), stop=(k == KC - 1)), reads=[t_w[k], t_at], writes=[t_psd[pj]])
                for k in range(KC):
                    kb.op("pe", lambda dch=dch, k=k, pj=pj: nc.tensor.matmul(
                        psd[2 + pj][:, :n], ws[:, k, 1024 + dch * 128:1024 + (dch + 1) * 128], at[:, k, :n],
                        start=(k == 0), stop=(k == KC - 1)), reads=[t_w[k], t_at], writes=[t_psd[2 + pj]])
                kb.op("act", lambda pj=pj: nc.scalar.activation(out=sg[:, :n], in_=psd[2 + pj][:, :n],
                                                                func=AF.Sigmoid),
                      reads=[t_psd[2 + pj]], writes=[t_sg])
                kb.op("dve", lambda dch=dch, pj=pj: nc.vector.tensor_tensor(
                    out=y[:, dch, :n], in0=psd[pj][:, :n], in1=sg[:, :n], op=ALU.mult),
                    reads=[t_psd[pj], t_sg], writes=[t_y])
            self.norm_post(y, t_y, xt[jx], t_xt[jx], n, i, 0, m, W, psn, t_psn)
            kb.dma(xT[:, :, c0:c0 + n], xt[jx][:, :, :n], reads=[t_xt[jx]])
        kb.barrier()


Model.phase_s5_in = phase_s5_in
Model.phase_s5_scan = phase_s5_scan
Model.phase_s5_out = phase_s5_out

def arr_vec(v):
    v = np.asarray(v)
    lead = v.shape[:-1]
    n = v.shape[-1] // 128
    return np.ascontiguousarray(np.moveaxis(v.reshape(lead + (n, 128)), -1, 0))

def host_common_x(inp, bs):
    f = np.float32
    x, ctx, c, c_ctx = inp["x"], inp["ctx"], inp["c"], inp["c_ctx"]
    cols = []
    for b in bs:
        cols.append(ctx[b].T)
        cols.append(x[b].T)
    xin = np.ascontiguousarray(np.concatenate(cols, axis=1), dtype=f)
    cvecs = [c[b] for b in bs]
    while len(cvecs) < 2:
        cvecs.append(np.zeros_like(c_ctx))
    cvecs.append(c_ctx)
    cc = np.stack(cvecs, axis=-1)
    cc = np.ascontiguousarray(cc.reshape(8, 128, 3).transpose(1, 0, 2), dtype=f)
    return {"xin": xin, "cc": cc}


def host_common(inp, bs, CTX, SEQ):
    f = np.float32
    d = host_common_x(inp, bs)
    d.update({
         "ada_w": np.ascontiguousarray(inp["ada_w"], dtype=f),
         "ada_b": arr_vec(inp["ada_b"]).astype(f),
         "norm_g": arr_vec(inp["norm_g"]).astype(f),
         "mlp_w1": np.ascontiguousarray(inp["mlp_w1"], dtype=f),
         "mlp_w2": np.ascontiguousarray(
             np.asarray(inp["mlp_w2"]).reshape(-1, 32, 128, 8, 128).transpose(0, 3, 2, 1, 4), dtype=f),
         })
    return d

def rope_table(SEQ, GRID_W=64):
    t = np.arange(SEQ)
    row, col = t // GRID_W, t % GRID_W
    inv = (10000.0 ** (-np.arange(16, dtype=np.float32) / 16)).astype(np.float32)
    ang = np.concatenate([row[:, None].astype(np.float32) * inv, col[:, None].astype(np.float32) * inv], axis=-1)
    tab = np.stack([np.cos(ang), np.sin(ang)], axis=1).astype(np.float32)
    return np.ascontiguousarray(tab.reshape(SEQ // 128, 128, 2, 32).transpose(1, 0, 2, 3))

def host_attn(inp, SEQ):
    f = np.float32
    return {"attn_w_qkv": np.ascontiguousarray(inp["attn_w_qkv"], dtype=f),
            "attn_w_o": np.ascontiguousarray(inp["attn_w_o"], dtype=f),
            "attn_lambda": np.ascontiguousarray(inp["attn_lambda"], dtype=f),
            "attn_subln": np.ascontiguousarray(inp["attn_subln"], dtype=f),
            "rope": rope_table(SEQ), "ident": np.eye(128, dtype=f)}

def host_lru(inp):
    f = np.float32
    wg = np.asarray(inp["lru_w_gate"])
    nc_ = wg.shape[0]
    wg = wg.reshape(nc_, 2, 2, 6, 2, 128, 256).transpose(0, 5, 1, 2, 3, 4, 6)
    return {"lru_w_in": np.ascontiguousarray(inp["lru_w_in"], dtype=f),
            "lru_w_out": np.ascontiguousarray(inp["lru_w_out"], dtype=f),
            "lru_w_gate": np.ascontiguousarray(wg, dtype=f),
            "lru_conv_w": np.ascontiguousarray(
                np.asarray(inp["lru_conv_w"]).reshape(nc_, 4, 12, 128).transpose(0, 3, 2, 1), dtype=f),
            "lru_conv_b": np.ascontiguousarray(
                np.asarray(inp["lru_conv_b"]).reshape(nc_, 12, 128).transpose(0, 2, 1), dtype=f),
            "lru_b_gate": np.ascontiguousarray(
                np.asarray(inp["lru_b_gate"]).reshape(nc_, 2, 2, 12, 128).transpose(0, 4, 1, 2, 3), dtype=f),
            "lru_a_param": np.ascontiguousarray(
                np.asarray(inp["lru_a_param"]).reshape(nc_, 2, 12, 128).transpose(0, 3, 1, 2), dtype=f)}

def host_s5(inp):
    f = np.float32
    def dup(a):
        return np.concatenate([a, a], axis=2)
    a_re = np.asarray(inp["s5_a_re"]).transpose(0, 1, 3, 2)
    a_im = np.asarray(inp["s5_a_im"]).transpose(0, 1, 3, 2)
    b_re = np.asarray(inp["s5_b_re"]).transpose(0, 1, 3, 2, 4)
    b_im = np.asarray(inp["s5_b_im"]).transpose(0, 1, 3, 2, 4)
    c_re = np.asarray(inp["s5_c_re"]).transpose(0, 1, 4, 2, 3)
    c_im = np.asarray(inp["s5_c_im"]).transpose(0, 1, 4, 2, 3)
    sh = np.roll(np.eye(128, dtype=f), 64, axis=1)
    return {"s5_a_re": np.ascontiguousarray(dup(a_re), dtype=f), "s5_a_im": np.ascontiguousarray(dup(a_im), dtype=f),
            "s5_b_re": np.ascontiguousarray(dup(b_re), dtype=f), "s5_b_im": np.ascontiguousarray(dup(b_im), dtype=f),
            "s5_c_re": np.ascontiguousarray(dup(c_re), dtype=f), "s5_c_im": np.ascontiguousarray(dup(c_im), dtype=f),
            "s5_log_dt": np.ascontiguousarray(inp["s5_log_dt"], dtype=f),
            "s5_d": np.ascontiguousarray(np.asarray(inp["s5_d"]).reshape(-1, 8, 128).transpose(0, 2, 1), dtype=f),
            "s5_w_glu": np.ascontiguousarray(inp["s5_w_glu"], dtype=f),
            "shift64": sh, "ident": np.eye(128, dtype=f)}


import math


def build_program(NB, CTX, SEQ, shapes, depth=4, layer_list=None):
    M = Model(NB, CTX, SEQ, layers=list(range(depth)) if layer_list is None else layer_list, depth=4)
    nc = M.nc
    for k, shp in shapes.items():
        M.din(k, shp)
    M.dram["xT"] = nc.dram_tensor("xT", [1024, M.T], F32, kind="ExternalOutput").ap()
    with Pool(M.kb) as G:
        M.setup(G)
        M.kb.dma(M.dram["xT"], M.dram["xin"])
        M.kb.barrier()
        M.phase_mod()
        for i in M.layers:
            need_ctx = i < depth - 1
            kind, j = i % 3, i // 3
            if kind == 0:
                lambda_init = 0.8 - 0.6 * math.exp(-0.3 * i)
                M.phase_attn_qkv(i, j)
                M.phase_attn_core(i, j, lambda_init, need_ctx)
                M.phase_proj_post(i, "attn_w_o", j, 8, need_ctx)
            elif kind == 1:
                M.phase_s5_in(i)
                M.phase_s5_scan(i, j)
                M.phase_s5_out(i, j, need_ctx)
            else:
                M.phase_lru_in(i, j)
                M.phase_lru_scan(i, j)
                M.phase_proj_post(i, "lru_w_out", j, 12, need_ctx)
            M.phase_mlp(i, need_ctx)
        M.kb.finish()
    return M


def host_all(inp, bs, CTX, SEQ):
    d = host_common(inp, bs, CTX, SEQ)
    d.update(host_attn(inp, SEQ))
    d.update(host_s5(inp))
    d.update(host_lru(inp))
    return d


def kernel(**inputs):
    inp = {k: np.asarray(v) for k, v in inputs.items()}
    B, SEQ, _ = inp["x"].shape
    CTX = inp["ctx"].shape[1]
    ncores = 8
    NB = B // ncores
    shared = None
    in_maps = []
    for c in range(ncores):
        bs = list(range(c * NB, (c + 1) * NB))
        if shared is None:
            d = host_all(inp, bs, CTX, SEQ)
            shared = {k: v for k, v in d.items() if k not in ("xin", "cc")}
        else:
            d = dict(shared)
            d.update({k: v for k, v in host_common_x(inp, bs).items()})
        in_maps.append(d)
    shapes = {k: v.shape for k, v in in_maps[0].items()}
    M = build_program(NB, CTX, SEQ, shapes)
    res = run_bass_kernel_spmd(M.nc, in_maps, core_ids=list(range(ncores)))
    TB = CTX + SEQ
    out = np.empty((B, SEQ, 1024), np.float32)
    for c in range(ncores):
        o = np.asarray(res.results[c]["xT"])
        for bl in range(NB):
            out[c * NB + bl] = o[:, bl * TB + CTX:(bl + 1) * TB].T
    return out
```
